# Optimizing a Trainium2 kernel written in Bass

```python
import math
import jax
import jax.numpy as jnp
from jax import lax
import numpy as np

D_MODEL = 1024
BATCH = 8
SEQ = 2048
DEPTH = 4

CTX_LEN = 256
GRID_W = 64
N_EVEN = (DEPTH + 1) // 2
N_ODD = DEPTH // 2
NORM_EPS = 1e-6
F32 = jnp.float32

LRU_WIDTH = D_MODEL
LRU_HEADS = 8
LRU_HEAD_DIM = LRU_WIDTH // LRU_HEADS
LRU_CONV = 4
LRU_C = 8.0

S5_WIDTH = D_MODEL // 2
S5_GROUP = 16
S5_GROUPS = S5_WIDTH // S5_GROUP
S5_STATE = 64
S5_DT_MIN = 1e-3
S5_DT_MAX = 1e-1

EVEN_SPLITS = (LRU_WIDTH, 2 * LRU_WIDTH, 2 * LRU_WIDTH + S5_WIDTH)
EVEN_IN = 2 * LRU_WIDTH + 2 * S5_WIDTH
EVEN_MIX = LRU_WIDTH + S5_WIDTH

MLA_HEADS = 8
MLA_Q_RANK = 384
MLA_KV_RANK = 256
MLA_NOPE = 128
MLA_ROPE = 64
MLA_V = 128
MLA_WIDTH = MLA_HEADS * MLA_V
MLA_SPLITS = (MLA_Q_RANK, MLA_Q_RANK + MLA_KV_RANK, MLA_Q_RANK + MLA_KV_RANK + MLA_ROPE)
MLA_IN = MLA_Q_RANK + MLA_KV_RANK + MLA_ROPE + MLA_WIDTH
MLA_SCALE = 1.0 / math.sqrt(MLA_NOPE + MLA_ROPE)
ROPE_AXIS = MLA_ROPE // 2
ROPE_BASE = 10000.0
Q_BLOCK = 128

kernel_name = 'hybrid_rglru_s5_mla_prefix_dit'


def rmsnorm(x, g):
    xf = x.astype(F32)
    y = xf * lax.rsqrt(jnp.mean(xf * xf, axis=-1, keepdims=True) + NORM_EPS)
    return (y * g.astype(F32)).astype(x.dtype)


def modulation(cond, w, b):
    return jnp.split(jax.nn.silu(cond) @ w + b, 3, axis=-1)


def dwconv_centred(u, w, b):
    L = u.shape[1]
    up = jnp.pad(u, ((0, 0), (LRU_CONV // 2, LRU_CONV - 1 - LRU_CONV // 2), (0, 0)))
    out = b
    for k in range(LRU_CONV):
        out = out + up[:, k:k + L] * w[k]
    return out


def _real_combine(e1, e2):
    a1, b1 = e1
    a2, b2 = e2
    return a1 * a2, a2 * b1 + b2


def real_scan(a, b, h0, reverse):
    a_cum, h = lax.associative_scan(_real_combine, (a, b), reverse=reverse, axis=1)
    if h0 is not None:
        h = h + a_cum * h0[:, None]
    return h


def _complex_combine(e1, e2):
    ar1, ai1, br1, bi1 = e1
    ar2, ai2, br2, bi2 = e2
    return (ar1 * ar2 - ai1 * ai2, ar1 * ai2 + ai1 * ar2,
            ar2 * br1 - ai2 * bi1 + br2, ar2 * bi1 + ai2 * br1 + bi2)


def complex_scan(a_re, a_im, b_re, b_im, h0, reverse):
    a_re = jnp.broadcast_to(a_re, b_re.shape)
    a_im = jnp.broadcast_to(a_im, b_re.shape)
    ar, ai, hr, hi = lax.associative_scan(_complex_combine, (a_re, a_im, b_re, b_im), reverse=reverse, axis=1)
    if h0 is not None:
        h0r, h0i = h0[0][:, None], h0[1][:, None]
        hr, hi = hr + ar * h0r - ai * h0i, hi + ar * h0i + ai * h0r
    return hr, hi


def rglru_coeffs(u, wr, br, wi, bi, lam):
    uh = u.reshape(*u.shape[:-1], LRU_HEADS, LRU_HEAD_DIM)
    r = jax.nn.sigmoid(jnp.einsum('blhi,hij->blhj', uh, wr.astype(F32)).reshape(u.shape) + br.astype(F32))
    i = jax.nn.sigmoid(jnp.einsum('blhi,hij->blhj', uh, wi.astype(F32)).reshape(u.shape) + bi.astype(F32))
    log_a = -LRU_C * r * jax.nn.softplus(-lam.astype(F32))
    a = jnp.exp(log_a)
    b = jnp.sqrt(-jnp.expm1(2.0 * log_a)) * (i * u)
    return a, b


def rglru_mix(x_ctx, x_lat, conv_w, conv_b, wr, br, wi, bi, lam, with_ctx):
    u_ctx = dwconv_centred(x_ctx, conv_w, conv_b).astype(F32)
    u_lat = dwconv_centred(x_lat, conv_w, conv_b).astype(F32)
    hs_ctx, hs_lat = [], []
    for d, rev in enumerate((False, True)):
        a_c, b_c = rglru_coeffs(u_ctx, wr[d], br[d], wi[d], bi[d], lam[d])
        h_c = real_scan(a_c, b_c, None, rev)
        h0 = h_c[:, 0] if rev else h_c[:, -1]
        a_l, b_l = rglru_coeffs(u_lat, wr[d], br[d], wi[d], bi[d], lam[d])
        hs_lat.append(real_scan(a_l, b_l, h0, rev))
        hs_ctx.append(h_c)
    y_lat = hs_lat[0] + hs_lat[1]
    y_ctx = hs_ctx[0] + hs_ctx[1] if with_ctx else None
    return y_ctx, y_lat


def s5_discretise(lam_re, lam_im, log_dt, b_re, b_im):
    lam_re = lam_re.astype(F32)
    lam_im = lam_im.astype(F32)
    dt = jnp.exp(log_dt.astype(F32))
    mag = jnp.exp(lam_re * dt)
    ab_re = mag * jnp.cos(lam_im * dt)
    ab_im = mag * jnp.sin(lam_im * dt)
    den = lam_re * lam_re + lam_im * lam_im
    nr = ab_re - 1.0
    f_re = (nr * lam_re + ab_im * lam_im) / den
    f_im = (ab_im * lam_re - nr * lam_im) / den
    b_re = b_re.astype(F32)
    b_im = b_im.astype(F32)
    bb_re = f_re[..., None] * b_re - f_im[..., None] * b_im
    bb_im = f_re[..., None] * b_im + f_im[..., None] * b_re
    return ab_re, ab_im, bb_re, bb_im


def s5_mix(u_ctx, u_lat, lam_re, lam_im, log_dt, b_re, b_im, c_re, c_im, d_skip, glu_w, glu_b, with_ctx):
    dtype = u_lat.dtype
    uc = u_ctx.astype(F32).reshape(*u_ctx.shape[:-1], S5_GROUPS, S5_GROUP)
    ul = u_lat.astype(F32).reshape(*u_lat.shape[:-1], S5_GROUPS, S5_GROUP)
    ys_ctx, ys_lat = [], []
    for d, rev in enumerate((False, True)):
        ab_re, ab_im, bb_re, bb_im = s5_discretise(lam_re[d], lam_im[d], log_dt[d], b_re[d], b_im[d])
        cr, ci = c_re[d].astype(F32), c_im[d].astype(F32)
        hc_re, hc_im = complex_scan(ab_re, ab_im,
                                    jnp.einsum('blgh,gph->blgp', uc, bb_re),
                                    jnp.einsum('blgh,gph->blgp', uc, bb_im), None, rev)
        idx = 0 if rev else -1
        hl_re, hl_im = complex_scan(ab_re, ab_im,
                                    jnp.einsum('blgh,gph->blgp', ul, bb_re),
                                    jnp.einsum('blgh,gph->blgp', ul, bb_im),
                                    (hc_re[:, idx], hc_im[:, idx]), rev)
        ys_lat.append(jnp.einsum('ghp,blgp->blgh', cr, hl_re) - jnp.einsum('ghp,blgp->blgh', ci, hl_im))
        if with_ctx:
            ys_ctx.append(jnp.einsum('ghp,blgp->blgh', cr, hc_re) - jnp.einsum('ghp,blgp->blgh', ci, hc_im))

    def finish(ys, u):
        y = ys[0] + ys[1] + d_skip.astype(F32) * u
        y = jax.nn.gelu(y.reshape(*y.shape[:2], S5_WIDTH)).astype(dtype)
        return y * jax.nn.sigmoid(y @ glu_w + glu_b)

    y_lat = finish(ys_lat, ul)
    y_ctx = finish(ys_ctx, uc) if with_ctx else None
    return y_ctx, y_lat


def even_mix(h_ctx, h_lat, w_in, conv_w, conv_b, wr, br, wi, bi, lam, lam_re, lam_im, log_dt,
             b_re, b_im, c_re, c_im, d_skip, glu_w, glu_b, w_out, with_ctx):
    xa_c, ga_c, ub_c, gb_c = jnp.split(h_ctx @ w_in, EVEN_SPLITS, axis=-1)
    xa_l, ga_l, ub_l, gb_l = jnp.split(h_lat @ w_in, EVEN_SPLITS, axis=-1)
    ya_c, ya_l = rglru_mix(xa_c, xa_l, conv_w, conv_b, wr, br, wi, bi, lam, with_ctx)
    yb_c, yb_l = s5_mix(ub_c, ub_l, lam_re, lam_im, log_dt, b_re, b_im, c_re, c_im, d_skip, glu_w, glu_b, with_ctx)

    def merge(ya, ga, yb, gb):
        return jnp.concatenate([ya.astype(ga.dtype) * jax.nn.silu(ga), yb * jax.nn.silu(gb)], axis=-1) @ w_out

    out_lat = merge(ya_l, ga_l, yb_l, gb_l)
    out_ctx = merge(ya_c, ga_c, yb_c, gb_c) if with_ctx else None
    return out_ctx, out_lat


def axial_rope(n_tokens):
    rows = n_tokens // GRID_W
    row = jnp.repeat(jnp.arange(rows, dtype=F32), GRID_W)
    col = jnp.tile(jnp.arange(GRID_W, dtype=F32), rows)
    inv = ROPE_BASE ** (-jnp.arange(0, ROPE_AXIS, 2, dtype=F32) / ROPE_AXIS)
    ang = jnp.concatenate([row[:, None] * inv, col[:, None] * inv], axis=-1)
    return jnp.cos(ang), jnp.sin(ang)


def apply_rope(x, cos, sin):
    x1, x2 = jnp.split(x.astype(F32), 2, axis=-1)
    return jnp.concatenate([x1 * cos - x2 * sin, x2 * cos + x1 * sin], axis=-1).astype(x.dtype)


def attend(qn, qr, kn, kr, v):
    s = jnp.einsum('bqhd,bkhd->bhqk', qn, kn) + jnp.einsum('bqhr,bkr->bhqk', qr, kr)
    p = jax.nn.softmax(s.astype(F32) * MLA_SCALE, axis=-1)
    return jnp.einsum('bhqk,bkhd->bqhd', p.astype(v.dtype), v)


def mla_mix(h_ctx, h_lat, w_in, q_norm, w_uq, kv_norm, w_ukv, w_out, with_ctx):
    def project(h):
        cq, ckv, kr, gate = jnp.split(h @ w_in, MLA_SPLITS, axis=-1)
        q = (rmsnorm(cq, q_norm) @ w_uq).reshape(*h.shape[:2], MLA_HEADS, MLA_NOPE + MLA_ROPE)
        kv = (rmsnorm(ckv, kv_norm) @ w_ukv).reshape(*h.shape[:2], MLA_HEADS, MLA_NOPE + MLA_V)
        return q[..., :MLA_NOPE], q[..., MLA_NOPE:], kv[..., :MLA_NOPE], kr, kv[..., MLA_NOPE:], gate

    qn_c, qr_c, kn_c, kr_c, v_c, g_c = project(h_ctx)
    qn_l, qr_l, kn_l, kr_l, v_l, g_l = project(h_lat)
    bsz, n_lat = h_lat.shape[0], h_lat.shape[1]
    cos, sin = axial_rope(n_lat)
    qr_l = apply_rope(qr_l, cos[:, None], sin[:, None])
    kr_l = apply_rope(kr_l, cos, sin)
    kn = jnp.concatenate([kn_c, kn_l], axis=1)
    kr = jnp.concatenate([kr_c, kr_l], axis=1)
    v = jnp.concatenate([v_c, v_l], axis=1)
    nb = n_lat // Q_BLOCK
    qn_b = qn_l.reshape(bsz, nb, Q_BLOCK, MLA_HEADS, MLA_NOPE).swapaxes(0, 1)
    qr_b = qr_l.reshape(bsz, nb, Q_BLOCK, MLA_HEADS, MLA_ROPE).swapaxes(0, 1)
    o_l = lax.map(lambda qs: attend(qs[0], qs[1], kn, kr, v), (qn_b, qr_b))
    o_l = o_l.swapaxes(0, 1).reshape(bsz, n_lat, MLA_WIDTH)
    out_lat = (o_l * jax.nn.silu(g_l)) @ w_out
    out_ctx = None
    if with_ctx:
        o_c = attend(qn_c, qr_c, kn_c, kr_c, v_c).reshape(bsz, h_ctx.shape[1], MLA_WIDTH)
        out_ctx = (o_c * jax.nn.silu(g_c)) @ w_out
    return out_ctx, out_lat


def setup_inputs(seed: int = 0) -> dict:
    key = jax.random.key(seed)
    ks = iter(jax.random.split(key, 48))

    def nrm(shape, scale):
        return jax.random.normal(next(ks), shape, F32) * scale

    def gain(shape):
        return 1.0 + nrm(shape, 0.05)

    D = D_MODEL
    G, P, H = S5_GROUPS, S5_STATE, S5_GROUP
    v = jax.random.uniform(next(ks), (N_EVEN, 2, LRU_WIDTH), F32, minval=0.9, maxval=0.999)
    a0 = v ** (1.0 / LRU_C)
    lru_lam = jnp.log(a0) - jnp.log1p(-a0)
    s5_log_dt = jax.random.uniform(next(ks), (N_EVEN, 2, G, P), F32,
                                   minval=math.log(S5_DT_MIN), maxval=math.log(S5_DT_MAX))
    return {
        'x': nrm((BATCH, SEQ, D), 1.0),
        'c': nrm((BATCH, D), 1.0),
        'ctx': nrm((BATCH, CTX_LEN, D), 1.0),
        'c_ctx': nrm((D,), 1.0),
        'norm_g': gain((DEPTH, D)),
        'mod_w': nrm((DEPTH, D, 3 * D), 0.5 * D ** -0.5),
        'mod_b': nrm((DEPTH, 3 * D), 0.01),
        'ev_w_in': nrm((N_EVEN, D, EVEN_IN), D ** -0.5),
        'lru_conv_w': nrm((N_EVEN, LRU_CONV, LRU_WIDTH), LRU_CONV ** -0.5),
        'lru_conv_b': nrm((N_EVEN, LRU_WIDTH), 0.01),
        'lru_wr': nrm((N_EVEN, 2, LRU_HEADS, LRU_HEAD_DIM, LRU_HEAD_DIM), LRU_HEAD_DIM ** -0.5),
        'lru_br': nrm((N_EVEN, 2, LRU_WIDTH), 0.01),
        'lru_wi': nrm((N_EVEN, 2, LRU_HEADS, LRU_HEAD_DIM, LRU_HEAD_DIM), LRU_HEAD_DIM ** -0.5),
        'lru_bi': nrm((N_EVEN, 2, LRU_WIDTH), 0.01),
        'lru_lam': lru_lam,
        's5_lam_re': -0.5 * jnp.exp(nrm((N_EVEN, 2, G, P), 0.05)),
        's5_lam_im': jnp.pi * jnp.arange(P, dtype=F32) + nrm((N_EVEN, 2, G, P), 0.01),
        's5_log_dt': s5_log_dt,
        's5_b_re': nrm((N_EVEN, 2, G, P, H), (2.0 * H) ** -0.5),
        's5_b_im': nrm((N_EVEN, 2, G, P, H), (2.0 * H) ** -0.5),
        's5_c_re': nrm((N_EVEN, 2, G, H, P), P ** -0.5),
        's5_c_im': nrm((N_EVEN, 2, G, H, P), P ** -0.5),
        's5_d': nrm((N_EVEN, G, H), 1.0),
        's5_glu_w': nrm((N_EVEN, S5_WIDTH, S5_WIDTH), S5_WIDTH ** -0.5),
        's5_glu_b': nrm((N_EVEN, S5_WIDTH), 0.01),
        'ev_w_out': nrm((N_EVEN, EVEN_MIX, D), EVEN_MIX ** -0.5),
        'mla_w_in': nrm((N_ODD, D, MLA_IN), D ** -0.5),
        'mla_q_norm': gain((N_ODD, MLA_Q_RANK)),
        'mla_w_uq': nrm((N_ODD, MLA_Q_RANK, MLA_HEADS * (MLA_NOPE + MLA_ROPE)), MLA_Q_RANK ** -0.5),
        'mla_kv_norm': gain((N_ODD, MLA_KV_RANK)),
        'mla_w_ukv': nrm((N_ODD, MLA_KV_RANK, MLA_HEADS * (MLA_NOPE + MLA_V)), MLA_KV_RANK ** -0.5),
        'mla_w_out': nrm((N_ODD, MLA_WIDTH, D), MLA_WIDTH ** -0.5),
        'final_g': gain((D,)),
    }


def reference(x, c, ctx, c_ctx, norm_g, mod_w, mod_b,
              ev_w_in, lru_conv_w, lru_conv_b, lru_wr, lru_br, lru_wi, lru_bi, lru_lam,
              s5_lam_re, s5_lam_im, s5_log_dt, s5_b_re, s5_b_im, s5_c_re, s5_c_im, s5_d,
              s5_glu_w, s5_glu_b, ev_w_out,
              mla_w_in, mla_q_norm, mla_w_uq, mla_kv_norm, mla_w_ukv, mla_w_out,
              final_g):
    ctx_s = ctx
    for l in range(DEPTH):
        with_ctx = l < DEPTH - 1
        sh_l, sc_l, gt_l = modulation(c, mod_w[l], mod_b[l])
        sh_c, sc_c, gt_c = modulation(c_ctx, mod_w[l], mod_b[l])
        n_lat = rmsnorm(x, norm_g[l]) * (1.0 + sc_l[:, None]) + sh_l[:, None]
        n_ctx = rmsnorm(ctx_s, norm_g[l]) * (1.0 + sc_c) + sh_c
        if l % 2 == 0:
            e = l // 2
            o_ctx, o_lat = even_mix(n_ctx, n_lat, ev_w_in[e], lru_conv_w[e], lru_conv_b[e],
                                    lru_wr[e], lru_br[e], lru_wi[e], lru_bi[e], lru_lam[e],
                                    s5_lam_re[e], s5_lam_im[e], s5_log_dt[e], s5_b_re[e], s5_b_im[e],
                                    s5_c_re[e], s5_c_im[e], s5_d[e], s5_glu_w[e], s5_glu_b[e],
                                    ev_w_out[e], with_ctx)
        else:
            o = l // 2
            o_ctx, o_lat = mla_mix(n_ctx, n_lat, mla_w_in[o], mla_q_norm[o], mla_w_uq[o],
                                   mla_kv_norm[o], mla_w_ukv[o], mla_w_out[o], with_ctx)
        x = x + gt_l[:, None] * o_lat
        if with_ctx:
            ctx_s = ctx_s + gt_c * o_ctx
    return rmsnorm(x, final_g)
```

```python
import math
import numpy as np
import concourse.bass as bass
import concourse.mybir as mybir
from concourse.bass_utils import run_bass_kernel_spmd

F32 = mybir.dt.float32
BF16 = mybir.dt.bfloat16
I32 = mybir.dt.int32
AF = mybir.ActivationFunctionType
ALU = mybir.AluOpType
AX = mybir.AxisListType

D = 1024
SEQ = 2048
CTX = 256
T = CTX + SEQ
NKT = 8
DEPTH = 4
EPS = 1e-6
BLKS = [(0, 256), (256, 768), (768, 1280), (1280, 1792), (1792, 2304)]
MLA_SCALE = 1.0 / math.sqrt(192.0)
TWO_PI = 2.0 * math.pi

DBG_D = 0
ENGS = ["pe", "act", "dve", "pool", "sp"]


class Prog:
    def __init__(self, nc):
        self.nc = nc
        self.streams = {e: [] for e in ENGS}
        self.count = {e: 0 for e in ENGS}
        self.sem = {e: nc.alloc_semaphore(name=f"prog_{e}") for e in ENGS}
        self.waited = {}
        self.lastw = {}
        self.readers = {}
        self.dma_sems = [nc.alloc_semaphore(name=f"dma_{i}") for i in range(16)]
        self.dma_cnt = [0] * 16
        self.dma_rr = 0

    def _deps(self, reads, writes):
        deps = []
        for k in reads:
            if k in self.lastw:
                deps.append(self.lastw[k])
        for k in writes:
            if k in self.lastw:
                deps.append(self.lastw[k])
            deps.extend(self.readers.get(k, []))
        return deps

    def _emit_waits(self, eng, deps):
        need = {}
        for (s, v) in deps:
            if eng == "pe" and s is self.sem["pe"]:
                continue
            key = id(s)
            if v > self.waited.get((eng, key), 0):
                if key not in need or need[key][1] < v:
                    need[key] = (s, v)
        for key, (s, v) in need.items():
            self.waited[(eng, key)] = v
            self.streams[eng].append(lambda e, s=s, v=v: e.wait_ge(s, v))

    def _commit(self, tok, reads, writes):
        for k in writes:
            self.lastw[k] = tok
            self.readers[k] = []
        for k in reads:
            if k not in writes:
                self.readers.setdefault(k, []).append(tok)

    def op(self, eng, fn, reads=(), writes=()):
        reads = list(reads)
        writes = list(writes)
        self._emit_waits(eng, self._deps(reads, writes))
        self.count[eng] += 1
        v = self.count[eng]
        s = self.sem[eng]
        self.streams[eng].append(lambda e, fn=fn, s=s: fn(e).then_inc(s, 1))
        tok = (s, v)
        self._commit(tok, reads, writes)
        return tok

    def dma(self, out, in_, reads=(), writes=(), eng="sp", **kw):
        reads = list(reads)
        writes = list(writes)
        i = self.dma_rr
        self.dma_rr = (self.dma_rr + 1) % len(self.dma_sems)
        s = self.dma_sems[i]
        deps = self._deps(reads, writes)
        if self.dma_cnt[i] > 0:
            deps.append((s, 16 * self.dma_cnt[i]))
        self._emit_waits(eng, deps)
        self.dma_cnt[i] += 1
        v = 16 * self.dma_cnt[i]
        self.streams[eng].append(
            lambda e, out=out, in_=in_, s=s, kw=kw: e.dma_start(out=out, in_=in_, **kw).then_inc(s, 16))
        tok = (s, v)
        self._commit(tok, reads, writes)
        return tok

    def finish(self, final_tokens):
        nc = self.nc
        self._emit_waits("sp", final_tokens)
        with nc.Block() as block:
            @block.tensor
            def _(e):
                for f in self.streams["pe"]:
                    f(e)

            @block.scalar
            def _(e):
                for f in self.streams["act"]:
                    f(e)

            @block.vector
            def _(e):
                for f in self.streams["dve"]:
                    f(e)

            @block.gpsimd
            def _(e):
                for f in self.streams["pool"]:
                    f(e)

            @block.sync
            def _(e):
                for f in self.streams["sp"]:
                    f(e)


WSPEC = [
    ("x", [SEQ, D]), ("c", [D]), ("ctx", [CTX, D]), ("c_ctx", [D]),
    ("norm_g", [4, D]), ("mod_w", [4, D, 3 * D]), ("mod_b", [4, 3 * D]),
    ("ev_w_in", [2, D, 3072]), ("lru_conv_w", [2, 4, D]), ("lru_conv_b", [2, D]),
    ("lru_wr", [2, 2, 8, 128, 128]), ("lru_br", [2, 2, D]),
    ("lru_wi", [2, 2, 8, 128, 128]), ("lru_bi", [2, 2, D]), ("lru_lam", [2, 2, D]),
    ("s5_lam_re", [2, 2, 32, 64]), ("s5_lam_im", [2, 2, 32, 64]), ("s5_log_dt", [2, 2, 32, 64]),
    ("s5_b_re", [2, 2, 32, 64, 16]), ("s5_b_im", [2, 2, 32, 64, 16]),
    ("s5_c_re", [2, 2, 32, 16, 64]), ("s5_c_im", [2, 2, 32, 16, 64]),
    ("s5_d", [2, 32, 16]), ("s5_glu_w", [2, 512, 512]), ("s5_glu_b", [2, 512]),
    ("ev_w_out", [2, 1536, D]),
    ("mla_w_in", [2, D, 1728]), ("mla_q_norm", [2, 384]), ("mla_w_uq", [2, 384, 1536]),
    ("mla_kv_norm", [2, 256]), ("mla_w_ukv", [2, 256, 2048]), ("mla_w_out", [2, D, D]),
    ("final_g", [D]),
]


class Builder:
    def __init__(self, layers=(0, 1, 2, 3), debug=False):
        self.layers = list(layers)
        nc = bass.Bass("TRN2", target_bir_lowering=False)
        self.nc = nc
        self.P = Prog(nc)
        self.W = {}
        for name, shp in WSPEC:
            self.W[name] = nc.dram_tensor(name, shp, F32, kind="ExternalInput").ap()
        self.out = nc.dram_tensor("out", [SEQ, D], F32, kind="ExternalOutput").ap()
        self.debug = debug
        if debug:
            self.dbgf = nc.dram_tensor("dbgf", [8, 128, T], F32, kind="ExternalOutput").ap()
            self.dbgb = nc.dram_tensor("dbgb", [8, 128, T], BF16, kind="ExternalOutput").ap()
        A = nc.alloc_sbuf_tensor
        self.xT = A("xT", [128, NKT, T], F32)
        self.nT = A("nT", [128, NKT, T], BF16)
        self.modT = A("modT", [128, DEPTH, 24, 2], F32)
        self.identf = A("identf", [128, 128], F32)
        self.identb = A("identb", [128, 128], BF16)
        self.onesb = A("onesb", [128, 128], BF16)
        self.ygrp = A("ygrp", [128, 4, T], BF16)
        self.F = [A(f"F{i}", [128, T], F32) for i in range(3)]
        self.B = [A(f"B{i}", [128, T + 8], BF16) for i in range(4)]
        self.stage = [A(f"stage{i}", [128, 1024], F32) for i in range(2)]
        self.stage_i = 0
        self.wbf = [A(f"wbf{i}", [128, 1024], BF16) for i in range(3)]
        self.wbf_i = 0
        self.LW = A("LW", [128, 2304], F32)
        self.small = A("small", [128, 256], F32)
        self.tmpA = [A(f"tmpA{i}", [128, 512], F32) for i in range(3)]
        self.tmpB = [A(f"tmpB{i}", [128, 512], BF16) for i in range(4)]
        self.ps = [nc.alloc_psum_tensor(f"ps{i}", [128, 512], F32) for i in range(8)]

    def load_cast(self, src_ap, kt, ncols, dst=None, dst_key=None, defer=False):
        P = self.P
        si = self.stage_i
        self.stage_i = (si + 1) % len(self.stage)
        st = self.stage[si]
        stv = st[:, 0:kt * ncols].rearrange("p (k c) -> p k c", k=kt)
        P.dma(stv, src_ap.rearrange("(k p) c -> p k c", p=128), writes=[f"stage{si}"])
        if dst is None:
            wi = self.wbf_i
            self.wbf_i = (wi + 1) % len(self.wbf)
            dst = self.wbf[wi][:, 0:kt * ncols].rearrange("p (k c) -> p k c", k=kt)
            dst_key = f"wbf{wi}"

        def cast(eng="pool"):
            if eng == "act":
                P.op("act", lambda e: e.activation(dst, stv, AF.Copy), reads=[f"stage{si}"], writes=[dst_key])
            else:
                P.op(eng, lambda e: e.tensor_copy(dst, stv), reads=[f"stage{si}"], writes=[dst_key])
        if defer:
            return dst, dst_key, cast
        cast()
        return dst, dst_key

    def mm(self, out, lhsT, rhs, start, stop, reads, writes):
        return self.P.op("pe", lambda e: e.matmul(out, lhsT, rhs, start=start, stop=stop),
                         reads=reads, writes=writes)

    def consts(self):
        P = self.P
        idf, idb, ob = self.identf, self.identb, self.onesb
        P.op("pool", lambda e: e.memset(idf[:], 0.0), writes=["identf"])
        P.op("pool", lambda e: e.affine_select(idf[:], idf[:], [[-1, 128]], ALU.not_equal, 1.0, base=0,
                                               channel_multiplier=1), reads=["identf"], writes=["identf"])
        P.op("pool", lambda e: e.tensor_copy(idb[:], idf[:]), reads=["identf"], writes=["identb"])
        P.op("pool", lambda e: e.memset(ob[:], 1.0), writes=["onesb"])

    def load_x(self):
        P = self.P
        for tt in range(T // 128):
            st = self.stage[tt % 2]
            skey = f"stage{tt % 2}"
            src = self.W["ctx"][tt * 128:(tt + 1) * 128, :] if tt < 2 else self.W["x"][(tt - 2) * 128:(tt - 1) * 128, :]
            P.dma(st[:], src, writes=[skey])
            for half in range(2):
                ps = self.ps[(tt * 2 + half) % 8]
                pkey = f"ps{(tt * 2 + half) % 8}"
                for q in range(4):
                    kt = half * 4 + q
                    P.op("pe", lambda e, ps=ps, q=q, kt=kt, st=st: e.transpose(
                        ps[:, q * 128:(q + 1) * 128], st[:, kt * 128:(kt + 1) * 128], self.identf[:]),
                        reads=[skey, "identf"], writes=[pkey])
                dst = self.xT[:, half * 4:half * 4 + 4, tt * 128:(tt + 1) * 128]
                srcp = ps[:].rearrange("p (q c) -> p q c", q=4)
                eng = "dve" if half == 0 else "act"
                if eng == "dve":
                    P.op("dve", lambda e, dst=dst, srcp=srcp: e.tensor_copy(dst, srcp), reads=[pkey],
                         writes=[("xT", half * 4 + q) for q in range(4)])
                else:
                    P.op("act", lambda e, dst=dst, srcp=srcp: e.activation(dst, srcp, AF.Copy), reads=[pkey],
                         writes=[("xT", half * 4 + q) for q in range(4)])

    def modulation(self):
        P = self.P
        sm = self.small
        cc = sm[:, 0:16]
        csb = sm[:, 16:24].bitcast(BF16)
        P.dma(cc[:, 0:8], self.W["c"].rearrange("(k p) -> p k", p=128), writes=["cc"], allow_slow_non_contiguous=True)
        P.dma(cc[:, 8:16], self.W["c_ctx"].rearrange("(k p) -> p k", p=128), writes=["cc"], allow_slow_non_contiguous=True)
        P.op("act", lambda e: e.activation(csb, cc, AF.Silu), reads=["cc"], writes=["cs"])
        csv = csb.rearrange("p (s k) -> p k s", s=2)
        nTf = self.nT[:].rearrange("p k t -> p (k t)").bitcast(F32)
        stg = [nTf[:, i * 1024:(i + 1) * 1024] for i in range(6)]
        nxt = 0
        for l in range(DEPTH):
            for kt in range(NKT):
                for h in range(3):
                    si = nxt % 6
                    nxt += 1
                    st = stg[si]
                    P.dma(st, self.W["mod_w"][l, kt * 128:(kt + 1) * 128, h * 1024:(h + 1) * 1024], writes=[f"mstg{si}"])
                    wi = self.wbf_i
                    self.wbf_i = (wi + 1) % len(self.wbf)
                    wb = self.wbf[wi]
                    eng = "act" if nxt % 2 == 0 else "dve"
                    if eng == "act":
                        P.op("act", lambda e, wb=wb, st=st: e.activation(wb[:], st, AF.Copy), reads=[f"mstg{si}"], writes=[f"wbf{wi}"])
                    else:
                        P.op("dve", lambda e, wb=wb, st=st: e.tensor_copy(wb[:], st), reads=[f"mstg{si}"], writes=[f"wbf{wi}"])
                    for q in range(2):
                        bank = h * 2 + q
                        self.mm(self.ps[bank][0:2, :], csv[:, kt, :], wb[:, q * 512:(q + 1) * 512],
                                kt == 0, kt == NKT - 1, ["cs", f"wbf{wi}"], [f"ps{bank}"])
            mrow = nTf[:, 6144:9216][0:2, :]
            bro = self.F[0][0:2, 0:2304]
            bro2 = self.F[1][0:2, 0:768]
            for s_ in range(2):
                P.dma(bro[s_:s_ + 1, :], self.W["mod_b"][l:l + 1, 0:2304], writes=["bro", "F0"])
                P.dma(bro2[s_:s_ + 1, :], self.W["mod_b"][l:l + 1, 2304:3072], writes=["bro", "F1"])
            for bank in range(6):
                c0 = bank * 512
                if c0 + 512 <= 2304:
                    bsrc = bro[:, c0:c0 + 512]
                    P.op("dve", lambda e, bank=bank, bsrc=bsrc, c0=c0: e.tensor_tensor(
                        mrow[:, c0:c0 + 512], self.ps[bank][0:2, :], bsrc, ALU.add), reads=[f"ps{bank}", "bro", "F0", "F1"], writes=["mrow"])
                else:
                    for (a0, a1) in [(c0, min(c0 + 512, 2304)), (max(c0, 2304), c0 + 512)]:
                        if a1 <= a0:
                            continue
                        bsrc = bro[:, a0:a1] if a1 <= 2304 else bro2[:, a0 - 2304:a1 - 2304]
                        P.op("dve", lambda e, bank=bank, bsrc=bsrc, a0=a0, a1=a1, c0=c0: e.tensor_tensor(
                            mrow[:, a0:a1], self.ps[bank][0:2, a0 - c0:a1 - c0], bsrc, ALU.add), reads=[f"ps{bank}", "bro", "F0", "F1"], writes=["mrow"])
            for j in range(24):
                self.mm(self.ps[6][:, 2 * j:2 * j + 2], mrow[:, j * 128:(j + 1) * 128], self.identf[0:2, 0:2],
                        True, True, ["mrow", "identf"], ["ps6"])
            P.op("dve", lambda e, l=l: e.tensor_copy(self.modT[:, l].rearrange("p j s -> p (j s)"), self.ps[6][:, 0:48]),
                 reads=["ps6"], writes=["modT"])

    def norm_mod(self, l):
        P = self.P
        sm = self.small
        g = sm[:, 32:40]
        gm = sm[:, 40:56]
        P.dma(g, self.W["norm_g"][l].rearrange("(k p) -> p k", p=128), writes=["g"], allow_slow_non_contiguous=True)
        for s in range(2):
            P.op("dve", lambda e, s=s: e.scalar_tensor_tensor(
                gm[:, s * 8:(s + 1) * 8], self.modT[:, l, 8:16, s], 1.0, g, ALU.add, ALU.mult),
                reads=["modT", "g"], writes=["gm"])
        rstd = self.F[1]
        self.rstd_into(rstd, "F1", [self.xT[:, kt, :] for kt in range(NKT)], [("xT", kt) for kt in range(NKT)], D)
        for kt in range(NKT):
            for s, (c0, c1) in enumerate([(CTX, T), (0, CTX)]):
                tmp = self.F[2]
                P.op("dve", lambda e, kt=kt, s=s, c0=c0, c1=c1: e.scalar_tensor_tensor(
                    tmp[:, c0:c1], self.xT[:, kt, c0:c1], gm[:, s * 8 + kt:s * 8 + kt + 1], rstd[:, c0:c1], ALU.mult, ALU.mult),
                    reads=[("xT", kt), "gm", "F1"], writes=["F2"])
                P.op("act", lambda e, kt=kt, s=s, c0=c0, c1=c1: e.activation(
                    self.nT[:, kt, c0:c1], tmp[:, c0:c1], AF.Identity, bias=self.modT[:, l, kt, s:s + 1], scale=1.0),
                    reads=["F2", "modT"], writes=[("nT", kt)])

    def rstd_into(self, dst, dst_key, srcs, src_keys, dim, cols=(0, T)):
        P = self.P
        blks = [b for b in BLKS if b[0] >= cols[0] and b[1] <= cols[1]]
        for bi, (c0, c1) in enumerate(blks):
            pb = self.ps[7]
            n = len(srcs)
            for i, (s_ap, sk) in enumerate(zip(srcs, src_keys)):
                sq = self.tmpB[i % 2]
                P.op("act", lambda e, sq=sq, s_ap=s_ap, c0=c0, c1=c1: e.activation(sq[:, 0:c1 - c0], s_ap[:, c0:c1], AF.Square),
                     reads=[sk], writes=[f"tmpB{i % 2}"])
                self.mm(pb[:, 0:c1 - c0], self.onesb[:], sq[:, 0:c1 - c0], i == 0, i == n - 1,
                        [f"tmpB{i % 2}", "onesb"], ["ps7"])
            P.op("act", lambda e, c0=c0, c1=c1: e.activation(dst[:, c0:c1], pb[:, 0:c1 - c0], AF.Sqrt, bias=EPS, scale=1.0 / dim),
                 reads=["ps7"], writes=[dst_key])
            P.op("dve", lambda e, c0=c0, c1=c1: e.reciprocal(dst[:, c0:c1], dst[:, c0:c1]), reads=[dst_key], writes=[dst_key])

    def final(self):
        P = self.P
        gb = self.F[0]
        P.dma(gb[:, 0:D], self.W["final_g"].partition_broadcast(128), writes=["F0"])
        ot = self.F[1]
        for tt in range(SEQ // 128):
            c0 = CTX + tt * 128
            ss = self.small[:, 64 + (tt % 2) * 2:64 + (tt % 2) * 2 + 2]
            sskey = f"ss{tt % 2}"
            pss = []
            P.op("pool", lambda e, ss=ss: e.memset(ss, 0.0), writes=[sskey + "0", sskey + "1"])
            for half in range(2):
                bank = (tt * 2 + half) % 4
                ps = self.ps[bank]
                for q in range(4):
                    kt = half * 4 + q
                    P.op("pe", lambda e, ps=ps, q=q, kt=kt, c0=c0: e.transpose(
                        ps[:, q * 128:(q + 1) * 128], self.xT[:, kt, c0:c0 + 128], self.identf[:]),
                        reads=[("xT", kt), "identf"], writes=[f"ps{bank}"])
                junk = self.tmpA[half]
                P.op("act", lambda e, ps=ps, junk=junk, half=half, ss=ss: e.activation(
                    junk[:], ps[:], AF.Square, accum_out=ss[:, half:half + 1]),
                    reads=[f"ps{bank}"], writes=[f"tmpA{half}", sskey + str(half)])
                pss.append((ps, bank))
            rs = self.small[:, 72 + (tt % 2):73 + (tt % 2)]
            rkey = f"rs{tt % 2}"
            P.op("dve", lambda e, ss=ss, rs=rs: e.tensor_tensor(rs, ss[:, 0:1], ss[:, 1:2], ALU.add),
                 reads=[sskey + "0", sskey + "1"], writes=[rkey])
            P.op("act", lambda e, rs=rs: e.activation(rs, rs, AF.Sqrt, bias=EPS, scale=1.0 / D), reads=[rkey], writes=[rkey])
            P.op("dve", lambda e, rs=rs: e.reciprocal(rs, rs), reads=[rkey], writes=[rkey])
            obuf = ot[:, (tt % 2) * 1024:(tt % 2) * 1024 + 1024]
            okey = f"ot{tt % 2}"
            for half, (ps, bank) in enumerate(pss):
                P.op("dve", lambda e, ps=ps, half=half, rs=rs, obuf=obuf: e.scalar_tensor_tensor(
                    obuf[:, half * 512:(half + 1) * 512], ps[:], rs, gb[:, half * 512:(half + 1) * 512], ALU.mult, ALU.mult),
                    reads=[f"ps{bank}", rkey, "F0"], writes=[okey + str(half)])
            self.out_toks.append(P.dma(self.out[tt * 128:(tt + 1) * 128, :], obuf, reads=[okey + "0", okey + "1"]))

    def build(self):
        self.out_toks = []
        self.consts()
        self.load_x()
        self.modulation()
        for l in self.layers:
            self.norm_mod(l)
            if l % 2 == 0:
                self.even_layer(l)
            else:
                self.mla_layer(l)
        self.final()
        self.P.finish(self.out_toks)
        return self.nc


    def even_layer(self, l):
        P = self.P
        e_ = l // 2
        Win = self.W["ev_w_in"][e_]
        Wout = self.W["ev_w_out"][e_]
        self.ps_rr = 0
        self.proj_banks = [0, 1, 2, 3, 4, 5, 6, 7]
        sm = self.small
        cw = sm[:, 104:136].rearrange("p (k t) -> p k t", k=4)
        cb = sm[:, 136:144]
        br = sm[:, 144:160].rearrange("p (d t) -> p d t", d=2)
        bi = sm[:, 160:176].rearrange("p (d t) -> p d t", d=2)
        coef = sm[:, 176:192].rearrange("p (d t) -> p d t", d=2)
        coef2 = sm[:, 192:208].rearrange("p (d t) -> p d t", d=2)
        ld = lambda dst, src, key: P.dma(dst, src, writes=[key], allow_slow_non_contiguous=True)
        for k in range(4):
            ld(cw[:, k, :], self.W["lru_conv_w"][e_, k].rearrange("(t p) -> p t", p=128), "cw")
        ld(cb, self.W["lru_conv_b"][e_].rearrange("(t p) -> p t", p=128), "cb")
        for d in range(2):
            ld(br[:, d, :], self.W["lru_br"][e_, d].rearrange("(t p) -> p t", p=128), "br")
            ld(bi[:, d, :], self.W["lru_bi"][e_, d].rearrange("(t p) -> p t", p=128), "bi")
            ld(coef[:, d, :], self.W["lru_lam"][e_, d].rearrange("(t p) -> p t", p=128), "coef")
        cf = sm[:, 176:192]
        cf2 = sm[:, 192:208]
        P.op("act", lambda e: e.activation(cf, cf, AF.Exp, scale=-1.0), reads=["coef"], writes=["coef"])
        P.op("act", lambda e: e.activation(cf, cf, AF.Ln, bias=1.0), reads=["coef"], writes=["coef"])
        P.op("dve", lambda e: e.tensor_scalar_mul(cf2, cf, -16.0), reads=["coef"], writes=["coef2"])
        P.op("dve", lambda e: e.tensor_scalar_mul(cf, cf, -8.0), reads=["coef", "coef2"], writes=["coef"])
        nsrc = lambda k: self.nT[:, k, :]
        nkeys = [("nT", k) for k in range(NKT)]
        xap, u, ib, hf = self.B
        F0, F1, F2 = self.F
        dg = [self.tmpB[0][:, k * 128:(k + 1) * 128] for k in range(4)]
        F2b = F2[:].bitcast(BF16)
        hr = F2b[:, 0:T]
        abuf = [(F1, "F1"), (self.LW, "LWa")]
        ibuf = [(ib, "B2"), (F2b[:, T:2 * T], "ibB")]
        for (a, b) in [(0, 2), (258, 262), (2310, 2312)]:
            P.op("pool", lambda e, a=a, b=b: e.memset(xap[:, a:b], 0.0), writes=["B0"])

        def xoff(c0):
            return c0 + 2 if c0 < CTX else c0 + 6

        def lru_loads(h):
            r = {"wxa": self.load_cast(Win[:, h * 128:(h + 1) * 128], 8, 128),
                 "wga": self.load_cast(Win[:, 1024 + h * 128:1024 + (h + 1) * 128], 8, 128)}
            gwb = self.tmpB[2 + h % 2]
            for d in range(2):
                for q, nm in enumerate(["lru_wr", "lru_wi"]):
                    off = (d * 2 + q) * 128
                    dst = gwb[:, off:off + 128].rearrange("p (k c) -> p k c", k=1)
                    r[(d, q)] = self.load_cast(self.W[nm][e_, d, h], 1, 128, dst=dst, dst_key=("gw", h % 2, d, q))
            return r

        pend = None
        xa_pending = None
        for h in range(8):
            if pend is None:
                pend = lru_loads(h)
            Wt = pend
            pend = None
            wxa, wxak = Wt["wxa"]
            wga, wgak = Wt["wga"]

            def ev_xa(ps, pkey, c0, c1):
                P.op("act", lambda e: e.activation(xap[:, xoff(c0):xoff(c0) + c1 - c0], ps[:, 0:c1 - c0], AF.Copy), reads=[pkey], writes=["B0"])
            if xa_pending is not None:
                for (ps_, pk_, c0_, c1_) in xa_pending:
                    ev_xa(ps_, pk_, c0_, c1_)
                xa_pending = None
                self.proj_banks = [0, 1, 2, 3, 4, 5, 6, 7]
            else:
                self.proj_tile(wxa, 8, 128, nsrc, nkeys, wxak, ev_xa)
            for k in range(4):
                P.op("pool", lambda e, k=k, h=h: e.tensor_scalar_mul(dg[k], self.identb[:], cw[:, k, h:h + 1]), reads=["identb", "cw"], writes=[f"dg{k}", "tmpB0"])
            for (c0, c1) in BLKS:
                bank = self.proj_banks[self.ps_rr % len(self.proj_banks)]
                self.ps_rr += 1
                ps = self.ps[bank]
                for k in range(4):
                    o0 = xoff(c0) + k - 2
                    self.mm(ps[:, 0:c1 - c0], dg[k], xap[:, o0:o0 + c1 - c0], k == 0, k == 3, [f"dg{k}", "B0"], [f"ps{bank}"])
                P.op("act", lambda e, ps=ps, c0=c0, c1=c1, h=h: e.activation(u[:, c0:c1], ps[:, 0:c1 - c0], AF.Identity, bias=cb[:, h:h + 1], scale=1.0),
                     reads=[f"ps{bank}", "cb"], writes=["B1"])
            def ev_g(ps, pkey, c0, c1):
                P.op("act", lambda e: e.activation(xap[:, xoff(c0):xoff(c0) + c1 - c0], ps[:, 0:c1 - c0], AF.Silu), reads=[pkey], writes=["B0"])
            usrc = lambda k: u
            for d in range(2):
                wr, wrk = Wt[(d, 0)]
                wi, wik = Wt[(d, 1)]
                ad, adk = abuf[d]
                ibd, ibk = ibuf[d]

                def ev_r(ps, pkey, c0, c1, d=d, h=h):
                    P.op("act", lambda e: e.activation(F0[:, c0:c1], ps[:, 0:c1 - c0], AF.Sigmoid, bias=br[:, d, h:h + 1], scale=1.0),
                         reads=[pkey, "br"], writes=["F0"])

                def ev_i(ps, pkey, c0, c1, d=d, h=h, ibd=ibd, ibk=ibk):
                    P.op("act", lambda e: e.activation(ibd[:, c0:c1], ps[:, 0:c1 - c0], AF.Sigmoid, bias=bi[:, d, h:h + 1], scale=1.0),
                         reads=[pkey, "bi"], writes=[ibk])
                self.proj_tile(wr, 1, 128, usrc, ["B1"], wrk, ev_r)
                self.proj_tile(wi, 1, 128, usrc, ["B1"], wik, ev_i)
                if d == 1 and pend is not None:
                    nwxa, nwxak = pend["wxa"]
                    xa_pending = []
                    for bi_, (c0_, c1_) in enumerate(BLKS):
                        bank_ = 3 + bi_
                        for k_ in range(NKT):
                            self.mm(self.ps[bank_][:, 0:c1_ - c0_], nwxa[:, k_, :], self.nT[:, k_, c0_:c1_], k_ == 0, k_ == NKT - 1,
                                    [nwxak, ("nT", k_)], [f"ps{bank_}"])
                        xa_pending.append((self.ps[bank_], f"ps{bank_}", c0_, c1_))
                    self.proj_banks = [0, 1, 2]
                P.op("act", lambda e, d=d, h=h, ad=ad: e.activation(ad[:, 0:T], F0[:, :], AF.Exp, scale=coef[:, d, h:h + 1]), reads=["F0", "coef"], writes=[adk])
                P.op("act", lambda e, d=d, h=h: e.activation(F0[:, :], F0[:, :], AF.Exp, scale=coef2[:, d, h:h + 1]), reads=["F0", "coef2"], writes=["F0"])
                P.op("act", lambda e: e.activation(F0[:, :], F0[:, :], AF.Sqrt, bias=1.0, scale=-1.0), reads=["F0"], writes=["F0"])
                P.op("dve", lambda e, ibd=ibd: e.tensor_tensor(ibd[:, 0:T], ibd[:, 0:T], u[:, 0:T], ALU.mult), reads=[ibk, "B1"], writes=[ibk])
                P.op("dve", lambda e, ibd=ibd: e.tensor_tensor(ibd[:, 0:T], F0[:, :], ibd[:, 0:T], ALU.mult), reads=["F0", ibk], writes=[ibk])
                if d == 0:
                    P.op("dve", lambda e, ad=ad, ibd=ibd: e.tensor_tensor_scan(hf[:, 0:T], ad[:, 0:T], ibd[:, 0:T], 0.0, ALU.mult, ALU.add),
                         reads=[ibk, adk], writes=["B3"])
                    self.proj_tile(wga, 8, 128, nsrc, nkeys, wgak, ev_g)
                    if h + 1 < 8 and h % 4 != 3:
                        pend = lru_loads(h + 1)
                else:
                    P.op("dve", lambda e, ad=ad, ibd=ibd: e.tensor_tensor_scan(hr[:, 0:CTX][:, ::-1], ad[:, 0:CTX][:, ::-1], ibd[:, 0:CTX][:, ::-1], 0.0,
                                                                              ALU.mult, ALU.add), reads=[ibk, adk], writes=["hr"])
                    P.op("dve", lambda e, ad=ad, ibd=ibd: e.tensor_tensor_scan(hr[:, CTX:T][:, ::-1], ad[:, CTX:T][:, ::-1], ibd[:, CTX:T][:, ::-1], hr[:, 0:1],
                                                                              ALU.mult, ALU.add), reads=[ibk, adk, "hr"], writes=["hr"])
            P.op("dve", lambda e: e.tensor_tensor(hr, hr, hf[:, 0:T], ALU.add), reads=["hr", "B3"], writes=["hr"])
            P.op("dve", lambda e, h=h: e.tensor_tensor(self.ygrp[:, h % 4, 0:CTX], hr[:, 0:CTX], xap[:, 2:2 + CTX], ALU.mult),
                 reads=["hr", "B0"], writes=[("og", h % 4)])
            P.op("dve", lambda e, h=h: e.tensor_tensor(self.ygrp[:, h % 4, CTX:T], hr[:, CTX:T], xap[:, 262:262 + SEQ], ALU.mult),
                 reads=["hr", "B0"], writes=[("og", h % 4)])
            if h % 4 == 3:
                self.out_proj(l, Wout[(h // 4) * 512:(h // 4 + 1) * 512, :], BLKS)
        self.barrier()
        self.s5_phase(l)
        self.barrier()


    def s5_phase(self, l):
        P = self.P
        e_ = l // 2
        Win = self.W["ev_w_in"][e_]
        Wout = self.W["ev_w_out"][e_]
        W = self.W
        F0, F1, F2 = self.F
        g_re, g_im, ub, B3 = self.B
        LW = self.LW
        LWb = LW[:].bitcast(BF16)
        sm = self.small
        tht, rmag = LW[:, 0:32], LW[:, 32:64]
        M16, Mrow, nMrow = LW[:, 64:72], LW[:, 72:74], LW[:, 74:76]
        Braw = [LW[:, 80:144], LW[:, 144:208]]
        Bbar = [LW[:, 208:272], LW[:, 272:336]]
        CT = LW[:, 336:464].rearrange("p (j q h) -> p j q h", j=4, q=2)
        dsk, glub = LW[:, 464:468], LW[:, 468:472]
        Eexp = LW[0:32, 472:600]
        Fc = LW[0:32, 600:856]
        lB = lambda j, q: LWb[:, 1712 + (j * 2 + q) * 128:1712 + (j * 2 + q + 1) * 128]
        lC = lambda j, v: LWb[:, 2736 + (j * 3 + v) * 128:2736 + (j * 3 + v + 1) * 128]
        Dd = LWb[:, 4272:4400]
        iota48 = sm[:, 208:256]
        hpi = sm[:, 87:88]
        tA0, tA1, tA2 = self.tmpA
        ld = lambda dst, src, key: P.dma(dst, src, writes=[key], allow_slow_non_contiguous=True)
        dve = lambda fn, r, w: P.op("dve", fn, reads=r, writes=w)
        act = lambda fn, r, w: P.op("act", fn, reads=r, writes=w)
        pool = lambda fn, r, w: P.op("pool", fn, reads=r, writes=w)
        pool(lambda e: e.iota(iota48, [[1, 48]], base=0, channel_multiplier=0, allow_small_or_imprecise_dtypes=True), [], ["iota48"])
        dve(lambda e: e.memset(hpi, math.pi / 2), [], ["hpi"])
        dve(lambda e: e.tensor_reduce(M16, self.identf[:].rearrange("p (c h) -> p c h", h=16), AX.X, ALU.add), ["identf"], ["M16"])
        dve(lambda e: e.tensor_reduce(Mrow, self.identf[:].rearrange("p (c h) -> p c h", h=64), AX.X, ALU.add), ["identf"], ["Mrow"])
        dve(lambda e: e.tensor_scalar_mul(nMrow, Mrow, -1.0), ["Mrow"], ["nMrow"])
        pool(lambda e: e.memset(LWb[:, 2736:4272], 0.0), [], ["lC"])
        ld(dsk, W["s5_d"][e_].rearrange("(t g) h -> (g h) t", g=8), "dsk")
        ld(glub, W["s5_glu_b"][e_].rearrange("(t p) -> p t", p=128), "glub")
        for i, nm in enumerate(["s5_lam_re", "s5_lam_im", "s5_log_dt"]):
            ld(tA0[:, i * 32:(i + 1) * 32], W[nm][e_].rearrange("d (gp gl) p -> (gl p) (d gp)", gl=2), "tA0")
        act(lambda e: e.activation(tA0[:, 64:96], tA0[:, 64:96], AF.Exp), ["tA0"], ["tA0"])
        dve(lambda e: e.tensor_tensor(tht, tA0[:, 32:64], tA0[:, 64:96], ALU.mult), ["tA0"], ["tht"])
        dve(lambda e: e.tensor_scalar_mul(tht, tht, 1.0 / TWO_PI), ["tht"], ["tht"])
        dve(lambda e: e.tensor_tensor(rmag, tA0[:, 0:32], tA0[:, 64:96], ALU.mult), ["tA0"], ["rmag"])
        act(lambda e: e.activation(rmag, rmag, AF.Exp), ["rmag"], ["rmag"])
        c = lambda i: F0[0:32, i * 128:(i + 1) * 128]
        ci = lambda i: F0[0:32, i * 128:(i + 1) * 128].bitcast(I32)
        for i, nm in enumerate(["s5_lam_re", "s5_lam_im", "s5_log_dt"]):
            ld(c(i).rearrange("g (d p) -> g d p", d=2), W[nm][e_].rearrange("d g p -> g d p"), "F0")
        K0 = ["F0"]
        act(lambda e: e.activation(c(2), c(2), AF.Exp), K0, K0)
        dve(lambda e: e.tensor_tensor(c(3), c(0), c(2), ALU.mult), K0, K0)
        dve(lambda e: e.tensor_tensor(c(4), c(1), c(2), ALU.mult), K0, K0)
        dve(lambda e: e.tensor_scalar_mul(c(4), c(4), 1.0 / TWO_PI), K0, K0)
        dve(lambda e: e.tensor_scalar_mul(c(10), c(4), 0.5), K0, K0)
        act(lambda e: e.activation(c(5), c(3), AF.Exp), K0, K0)
        act(lambda e: e.activation(c(6), c(3), AF.Tanh, scale=0.5), K0, K0)
        dve(lambda e: e.scalar_tensor_tensor(c(6), c(5), 1.0, c(6), ALU.add, ALU.mult), K0, K0)
        dve(lambda e: e.tensor_copy(ci(7), c(4)), K0, K0)
        dve(lambda e: e.tensor_tensor(c(4), c(4), ci(7), ALU.subtract), K0, K0)
        act(lambda e: e.activation(c(8), c(4), AF.Sin, scale=TWO_PI), K0, K0)
        dve(lambda e: e.scalar_tensor_tensor(c(9), c(4), -1.0, c(4), ALU.mult, ALU.max), K0, K0)
        act(lambda e: e.activation(c(9), c(9), AF.Sin, bias=hpi[0:32, :], scale=-TWO_PI), K0 + ["hpi"], K0)
        dve(lambda e: e.tensor_copy(ci(7), c(10)), K0, K0)
        dve(lambda e: e.tensor_tensor(c(10), c(10), ci(7), ALU.subtract), K0, K0)
        act(lambda e: e.activation(c(10), c(10), AF.Sin, scale=TWO_PI), K0, K0)
        dve(lambda e: e.tensor_tensor(c(10), c(10), c(10), ALU.mult), K0, K0)
        dve(lambda e: e.tensor_tensor(c(11), c(6), c(9), ALU.mult), K0, K0)
        dve(lambda e: e.scalar_tensor_tensor(c(11), c(10), -2.0, c(11), ALU.mult, ALU.add), K0, K0)
        dve(lambda e: e.tensor_tensor(c(12), c(5), c(8), ALU.mult), K0, K0)
        dve(lambda e: e.tensor_tensor(c(13), c(0), c(0), ALU.mult), K0, K0)
        dve(lambda e: e.tensor_tensor(c(14), c(1), c(1), ALU.mult), K0, K0)
        dve(lambda e: e.tensor_tensor(c(13), c(13), c(14), ALU.add), K0, K0)
        dve(lambda e: e.reciprocal(c(13), c(13)), K0, K0)
        dve(lambda e: e.tensor_tensor(c(14), c(11), c(0), ALU.mult), K0, K0)
        dve(lambda e: e.tensor_tensor(c(15), c(12), c(1), ALU.mult), K0, K0)
        dve(lambda e: e.tensor_tensor(c(14), c(14), c(15), ALU.add), K0, K0)
        dve(lambda e: e.tensor_tensor(Fc[:, 0:128], c(14), c(13), ALU.mult), K0, ["Fc"])
        dve(lambda e: e.tensor_tensor(c(14), c(12), c(0), ALU.mult), K0, K0)
        dve(lambda e: e.tensor_tensor(c(15), c(11), c(1), ALU.mult), K0, K0)
        dve(lambda e: e.tensor_tensor(c(14), c(14), c(15), ALU.subtract), K0, K0)
        dve(lambda e: e.tensor_tensor(Fc[:, 128:256], c(14), c(13), ALU.mult), K0, ["Fc"])
        nsrc = lambda k: self.nT[:, k, :]
        nkeys = [("nT", k) for k in range(NKT)]

        def tv(X, d, c0, c1):
            if d == 0:
                return X[:, c0:c1]
            if c0 < CTX:
                return X[:, 0:CTX][:, ::-1]
            return X[:, 2560 - c1:2560 - c0][:, ::-1]

        for ti in range(4):
            self.proj_banks = [7]
            wub, wubk = self.load_cast(Win[:, 2048 + ti * 128:2048 + (ti + 1) * 128], 8, 128)

            def ev_u(ps, pkey, c0, c1):
                act(lambda e: e.activation(ub[:, c0:c1], ps[:, 0:c1 - c0], AF.Copy), [pkey], ["B2"])
            self.proj_tile(wub, 8, 128, nsrc, nkeys, wubk, ev_u)
            pool(lambda e, ti=ti: e.tensor_copy(Eexp.rearrange("g (c h) -> g c h", h=16),
                                                self.identf[0:32, 8 * ti:8 * ti + 8].unsqueeze(2).to_broadcast([32, 8, 16])), ["identf"], ["Eexp"])
            pool(lambda e, ti=ti: e.tensor_scalar_mul(Dd, self.identb[:], dsk[:, ti:ti + 1]), ["identb", "dsk"], ["Dd"])
            for d in range(2):
                self.mm(self.ps[7][:, 0:256], Eexp, Fc, True, True, ["Eexp", "Fc"], ["ps7"])
                for q, nm in enumerate(["s5_b_re", "s5_b_im"]):
                    for g8 in range(8):
                        ld(Braw[q][16 * g8:16 * g8 + 16, :], W[nm][e_, d, 8 * ti + g8].rearrange("p h -> h p"), f"Braw{q}")
                for q, nm in enumerate(["s5_c_re", "s5_c_im"]):
                    for gl in range(2):
                        for j in range(4):
                            ld(CT[64 * gl:64 * gl + 64, j, q, :], W[nm][e_, d, 8 * ti + 2 * j + gl].rearrange("h p -> p h"), "CT")
                Fre = self.ps[7][:, d * 64:(d + 1) * 64]
                Fim = self.ps[7][:, 128 + d * 64:128 + (d + 1) * 64]
                t0_, t1_ = tA0[:, 0:64], tA0[:, 64:128]
                dve(lambda e, Fre=Fre: e.tensor_tensor(t0_, Fre, Braw[0], ALU.mult), ["ps7", "Braw0"], ["tA0", "prod0"])
                dve(lambda e, Fim=Fim: e.tensor_tensor(t1_, Fim, Braw[1], ALU.mult), ["ps7", "Braw1"], ["tA0", "prod0"])
                dve(lambda e: e.tensor_tensor(Bbar[0], t0_, t1_, ALU.subtract), ["tA0"], ["Bbar0"])
                dve(lambda e, Fre=Fre: e.tensor_tensor(t0_, Fre, Braw[1], ALU.mult), ["ps7", "Braw1"], ["tA0", "prod0"])
                dve(lambda e, Fim=Fim: e.tensor_tensor(t1_, Fim, Braw[0], ALU.mult), ["ps7", "Braw0"], ["tA0", "prod0"])
                dve(lambda e: e.tensor_tensor(Bbar[1], t0_, t1_, ALU.add), ["tA0"], ["Bbar1"])
                if ti == 0 and d == DBG_D and self.debug:
                    self.dump(3, Fc, "Fc")
                    dve(lambda e: e.tensor_copy(tA1[:, 0:256], self.ps[7][:, 0:256]), ["ps7"], ["tA1"])
                    self.dump(6, tA1[:, 0:256], "tA1")
                    self.dump(7, Braw[0], "Braw0")
                for j in range(4):
                    for q in range(2):
                        for gl in range(2):
                            act(lambda e, j=j, q=q, gl=gl: e.activation(
                                lB(j, q)[:, 64 * gl:64 * gl + 64], Bbar[q], AF.Copy, scale=M16[:, 2 * j + gl:2 * j + gl + 1]),
                                [f"Bbar{q}", "M16"], [("lB", j)])
                    for gl in range(2):
                        cs = slice(32 * j + 16 * gl, 32 * j + 16 * gl + 16)
                        act(lambda e, j=j, gl=gl, cs=cs: e.activation(lC(j, 0)[:, cs], CT[:, j, 0, :], AF.Copy, scale=Mrow[:, gl:gl + 1]), ["CT", "Mrow"], [("lC", j)])
                        act(lambda e, j=j, gl=gl, cs=cs: e.activation(lC(j, 1)[:, cs], CT[:, j, 0, :], AF.Copy, scale=nMrow[:, gl:gl + 1]), ["CT", "nMrow"], [("lC", j)])
                        act(lambda e, j=j, gl=gl, cs=cs: e.activation(lC(j, 2)[:, cs], CT[:, j, 1, :], AF.Copy, scale=nMrow[:, gl:gl + 1]), ["CT", "nMrow"], [("lC", j)])
                for j in range(4):
                    uidx = (ti * 2 + d) * 4 + j
                    if uidx == 0:
                        for st in self.s5_table_steps(0, 0, 0, 0, tht):
                            st()
                    ins = []
                    if uidx + 1 < 32:
                        n_ = uidx + 1
                        ins = self.s5_table_steps(n_ // 8, (n_ // 4) % 2, n_ % 4, n_, tht)
                    self.s5_unit_compute(ti, d, j, uidx, lB, lC, rmag, ins)
            for bi, (c0, c1) in enumerate(BLKS):
                w_ = c1 - c0
                y = self.ps[bi]
                yk = f"ps{bi}"
                self.mm(y[:, 0:w_], Dd, ub[:, c0:c1], False, True, ["Dd", "B2"], [yk])
                tg, tgk = (tA0, "tA0") if bi % 2 == 0 else (tA1, "tA1")
                act(lambda e, y=y, w_=w_, tg=tg: e.activation(tg[:, 0:w_], y[:, 0:w_], AF.Square), [yk], [tgk])
                dve(lambda e, w_=w_, tg=tg: e.tensor_scalar(tg[:, 0:w_], tg[:, 0:w_], 0.044715, 1.0, ALU.mult, ALU.add), [tgk], [tgk])
                dve(lambda e, y=y, w_=w_, tg=tg: e.tensor_tensor(tg[:, 0:w_], tg[:, 0:w_], y[:, 0:w_], ALU.mult), [tgk, yk], [tgk])
                act(lambda e, w_=w_, tg=tg: e.activation(tg[:, 0:w_], tg[:, 0:w_], AF.Sigmoid, scale=2.0 * math.sqrt(2.0 / math.pi)), [tgk], [tgk])
                dve(lambda e, y=y, w_=w_, ti=ti, c0=c0, c1=c1, tg=tg: e.tensor_tensor(self.ygrp[:, ti, c0:c1], y[:, 0:w_], tg[:, 0:w_], ALU.mult),
                    [yk, tgk], [("og", ti)])
        self.proj_banks = [4, 5, 6]
        ysrc = lambda k: self.ygrp[:, k, :]
        ykeys = [("og", k) for k in range(4)]
        for ot in range(4):
            wg, wgk = self.load_cast(W["s5_glu_w"][e_][:, ot * 128:(ot + 1) * 128], 4, 128)

            def ev_z(ps, pkey, c0, c1, ot=ot):
                act(lambda e: e.activation(self.B[ot][:, c0:c1], ps[:, 0:c1 - c0], AF.Sigmoid, bias=glub[:, ot:ot + 1], scale=1.0),
                    [pkey, "glub"], [f"B{ot}"])
            self.proj_tile(wg, 4, 128, ysrc, ykeys, wgk, ev_z)
        for ot in range(4):
            wgb, wgbk = self.load_cast(Win[:, 2560 + ot * 128:2560 + (ot + 1) * 128], 8, 128)

            def ev_gb(ps, pkey, c0, c1, ot=ot):
                sg = self.tmpB[self.ps_rr % 2]
                sk = f"tmpB{self.ps_rr % 2}"
                act(lambda e: e.activation(sg[:, 0:c1 - c0], ps[:, 0:c1 - c0], AF.Silu), [pkey], [sk])
                pool(lambda e: e.tensor_tensor(sg[:, 0:c1 - c0], sg[:, 0:c1 - c0], self.B[ot][:, c0:c1], ALU.mult), [sk, f"B{ot}"], [sk])
                pool(lambda e: e.tensor_tensor(self.ygrp[:, ot, c0:c1], self.ygrp[:, ot, c0:c1], sg[:, 0:c1 - c0], ALU.mult),
                     [sk, ("og", ot)], [("og", ot)])
            self.proj_tile(wgb, 8, 128, nsrc, nkeys, wgbk, ev_gb)
        self.out_proj(l, Wout[1024:1536, :], BLKS)


    def s5_views(self):
        F0b = self.F[0][:].bitcast(BF16)
        F1b = self.F[1][:].bitcast(BF16)
        F2b = self.F[2][:].bitcast(BF16)
        tabs = [(F0b[:, 0:T], F0b[:, T:2 * T]), (F1b[:, 0:T], F1b[:, T:2 * T])]
        gin = (F2b[:, 0:T], F2b[:, T:2 * T])
        B3f = self.B[3][:, 0:2304].bitcast(F32)
        tAb = [self.tmpA[0][:].bitcast(BF16), self.tmpA[1][:].bitcast(BF16)]
        prod = [tAb[0][:, 0:512], tAb[0][:, 512:1024], tAb[1][:, 0:512], tAb[1][:, 512:1024]]
        return tabs, gin, B3f, prod

    def s5_table_steps(self, ti, d, j, uidx, tht):
        P = self.P
        tabs, gin, B3f, prod = self.s5_views()
        cosT, sinT = tabs[uidx % 2]
        tk = f"tab{uidx % 2}"
        sm = self.small
        iota48 = sm[:, 208:256]
        hpi = sm[:, 87:88]
        tA2 = self.tmpA[2]
        idx0 = d * 16 + 4 * ti
        th4 = tht[:, idx0:idx0 + 4]
        Ap4 = tA2[:, 0:192]
        Bp4 = tA2[:, 192:384]
        t48_4 = tA2[:, 384:388]
        kk1_4 = tA2[:, 388:392].bitcast(I32)
        kk = tA2[:, 392:488].bitcast(I32)
        Ap, Bp = Ap4[:, j * 48:(j + 1) * 48], Bp4[:, j * 48:(j + 1) * 48]
        KA = ["tA2"]
        dve = lambda fn, r, w: P.op("dve", fn, reads=r, writes=w)
        sxs = [B3f[:, 0:384], B3f[:, 384:768]]
        sy = B3f[:, 768:1152].bitcast(I32)
        steps = []

        def tiny():
            io4 = iota48.unsqueeze(1).to_broadcast([128, 4, 48])
            dve(lambda e: e.tensor_scalar_mul(t48_4, th4, 48.0), ["tht"], KA)
            dve(lambda e: e.tensor_copy(kk1_4, t48_4), KA, KA)
            dve(lambda e: e.tensor_tensor(t48_4, t48_4, kk1_4, ALU.subtract), KA, KA)
            dve(lambda e: e.tensor_tensor(Ap4.rearrange("p (u i) -> p u i", u=4), io4,
                                          t48_4.unsqueeze(2).to_broadcast([128, 4, 48]), ALU.mult), KA + ["iota48"], KA)
            dve(lambda e: e.tensor_tensor(Bp4.rearrange("p (u i) -> p u i", u=4), io4,
                                          th4.unsqueeze(2).to_broadcast([128, 4, 48]), ALU.mult), ["tht", "iota48"] + KA, KA)
            for X in (Ap4, Bp4):
                for hh in range(2):
                    xs = X[:, hh * 96:(hh + 1) * 96]
                    dve(lambda e, xs=xs: e.tensor_copy(kk, xs), KA, KA)
                    dve(lambda e, xs=xs: e.tensor_tensor(xs, xs, kk, ALU.subtract), KA, KA)
        if j == 0:
            steps.append(tiny)
        for k in range(6):
            sx = sxs[k % 2]
            sk = f"sx{k % 2}"
            c0 = 384 * k

            def stepA(k=k, sx=sx, sk=sk):
                sxv = sx.rearrange("p (i j) -> p i j", j=48)
                P.op("dve", lambda e: e.tensor_tensor(sxv, Ap[:, 8 * k:8 * k + 8].unsqueeze(2).to_broadcast([128, 8, 48]),
                                                     Bp.unsqueeze(1).to_broadcast([128, 8, 48]), ALU.add), reads=KA, writes=[sk])
                dve(lambda e: e.tensor_copy(sy, sx), [sk], ["sy"])
                P.op("dve", lambda e: e.tensor_tensor(sx, sx, sy, ALU.subtract), reads=[sk, "sy"], writes=[sk])

            def stepB(sx=sx, sk=sk, c0=c0):
                P.op("act", lambda e: e.activation(sinT[:, c0:c0 + 384], sx, AF.Sin, scale=TWO_PI), reads=[sk], writes=[tk])

            def stepC(sx=sx, sk=sk, c0=c0):
                P.op("act", lambda e: e.activation(sx, sx, AF.Sin, scale=math.pi), reads=[sk], writes=[sk])
                P.op("act", lambda e: e.activation(sx, sx, AF.Square), reads=[sk], writes=[sk])
                P.op("act", lambda e: e.activation(cosT[:, c0:c0 + 384], sx, AF.Identity, bias=1.0, scale=-2.0), reads=[sk], writes=[tk])
            steps += [stepA, stepB, stepC]
        return steps

    def s5_unit_compute(self, ti, d, j, uidx, lB, lC, rmag, inserts):
        P = self.P
        tabs, gin, B3f, prod = self.s5_views()
        cosT, sinT = tabs[uidx % 2]
        tk = f"tab{uidx % 2}"
        g_re, g_im, ub, _ = self.B
        idx = d * 16 + 4 * ti + j
        rm = rmag[:, idx:idx + 1]
        bre, bim, tA, tB = self.tmpB
        dve = lambda fn, r, w: P.op("dve", fn, reads=r, writes=w)
        nslots = 12
        total = len(inserts)
        state = {"slot": 0, "done": 0}

        def slot_end():
            state["slot"] += 1
            target = (state["slot"] * total + nslots - 1) // nslots
            while state["done"] < min(target, total):
                inserts[state["done"]]()
                state["done"] += 1

        def tcols(k):
            s0, s1 = BLKS[k]
            if d == 0:
                return s0, s1, k, False
            if k == 0:
                return 0, CTX, 0, True
            return 2560 - s1, 2560 - s0, 5 - k, True

        for k, (c0, c1) in enumerate(BLKS):
            w_ = c1 - c0
            t0, t1, bt, rev = tcols(k)
            ubv = ub[:, t0:t1][:, ::-1] if rev else ub[:, t0:t1]
            br_, bi_ = 5 + (2 * k) % 3, 5 + (2 * k + 1) % 3
            if k % 2 == 0:
                cre, cim, kre, kim = bre, bim, "tmpB0", "tmpB1"
            else:
                cre, cim, kre, kim = prod[0], prod[1], "prod0", "prod1"
            self.mm(self.ps[br_][:, 0:w_], lB(j, 0), ubv, True, True, [("lB", j), "B2"], [f"ps{br_}"])
            self.mm(self.ps[bi_][:, 0:w_], lB(j, 1), ubv, True, True, [("lB", j), "B2"], [f"ps{bi_}"])
            P.op("act", lambda e, w_=w_, cre=cre, br_=br_: e.activation(cre[:, 0:w_], self.ps[br_][:, 0:w_], AF.Copy), reads=[f"ps{br_}"], writes=[kre])
            P.op("act", lambda e, w_=w_, cim=cim, bi_=bi_: e.activation(cim[:, 0:w_], self.ps[bi_][:, 0:w_], AF.Copy), reads=[f"ps{bi_}"], writes=[kim])
            cv, sv = cosT[:, c0:c1], sinT[:, c0:c1]
            dve(lambda e, w_=w_, cv=cv, cre=cre: e.tensor_tensor(tA[:, 0:w_], cre[:, 0:w_], cv, ALU.mult), [kre, tk], ["tmpB2"])
            dve(lambda e, w_=w_, sv=sv, cim=cim: e.tensor_tensor(tB[:, 0:w_], cim[:, 0:w_], sv, ALU.mult), [kim, tk], ["tmpB3"])
            dve(lambda e, w_=w_, c0=c0, c1=c1: e.tensor_tensor(gin[0][:, c0:c1], tA[:, 0:w_], tB[:, 0:w_], ALU.add), ["tmpB2", "tmpB3"], ["ginr"])
            dve(lambda e, w_=w_, cv=cv, cim=cim: e.tensor_tensor(tA[:, 0:w_], cim[:, 0:w_], cv, ALU.mult), [kim, tk], ["tmpB2"])
            dve(lambda e, w_=w_, sv=sv, cre=cre: e.tensor_tensor(tB[:, 0:w_], cre[:, 0:w_], sv, ALU.mult), [kre, tk], ["tmpB3"])
            dve(lambda e, w_=w_, c0=c0, c1=c1: e.tensor_tensor(gin[1][:, c0:c1], tA[:, 0:w_], tB[:, 0:w_], ALU.subtract), ["tmpB2", "tmpB3"], ["gini"])
            slot_end()
        for part, (dst, dk, gk) in enumerate([(g_re, "B0", "ginr"), (g_im, "B1", "gini")]):
            src = gin[part]
            dve(lambda e, dst=dst, src=src: e.tensor_tensor_scan(dst[:, 0:T], rm.to_broadcast([128, T]), src, 0.0, ALU.mult, ALU.add),
                [gk, "rmag"], [dk])
            slot_end()
        for k, (c0, c1) in enumerate(BLKS):
            w_ = c1 - c0
            t0, t1, bt, rev = tcols(k)
            cv, sv = cosT[:, c0:c1], sinT[:, c0:c1]
            plist = [(cv, g_re, "B0", 0), (sv, g_im, "B1", 1), (sv, g_re, "B0", 2), (cv, g_im, "B1", 2)]
            for pi, (tab, gg, gk, var) in enumerate(plist):
                pt = prod[pi]
                pk = f"prod{pi}"
                dve(lambda e, pt=pt, tab=tab, gg=gg, c0=c0, c1=c1, w_=w_: e.tensor_tensor(pt[:, 0:w_], tab, gg[:, c0:c1], ALU.mult),
                    [tk, gk], [pk])
                first = (d == 0 and j == 0 and pi == 0)
                rhs = pt[:, 0:w_][:, ::-1] if rev else pt[:, 0:w_]
                self.mm(self.ps[bt][:, 0:w_], lC(j, var), rhs, first, False, [("lC", j), pk], [f"ps{bt}"])
            slot_end()
        while state["done"] < total:
            inserts[state["done"]]()
            state["done"] += 1

    def dump(self, slot, ap, key, bf=False):
        if not self.debug:
            return
        dst = (self.dbgb if bf else self.dbgf)[slot, 0:ap.shape[0], 0:ap.shape[1]]
        self.out_toks.append(self.P.dma(dst, ap, reads=[key]))

    def barrier(self):
        P = self.P
        toks = [(P.sem[e], P.count[e]) for e in ENGS if P.count[e] > 0]
        toks += [(s, 16 * c) for s, c in zip(P.dma_sems, P.dma_cnt) if c > 0]
        for e in ENGS:
            P._emit_waits(e, toks)
        P.lastw.clear()
        P.readers.clear()

    def rope_tables(self):
        P = self.P
        yf = self.ygrp[:].rearrange("p s t -> p (s t)").bitcast(F32)
        posr = yf[0:64, 0:2048]
        posc = yf[0:64, 2048:4096]
        sm = self.small
        pidx = sm[0:64, 80:81].bitcast(I32)
        p16 = sm[0:64, 81:82].bitcast(I32)
        pb16 = sm[0:64, 82:83].bitcast(I32)
        invt = sm[0:64, 83:84]
        mA = sm[0:64, 84:85]
        mB = sm[0:64, 85:86]
        halfpi = sm[0:64, 86:87]
        LWb = self.LW[:].bitcast(BF16)
        self.cosT = LWb[0:64, 0:2048]
        self.sinT = LWb[0:64, 2048:4096]
        K = ["ropescr"]
        P.op("pool", lambda e: e.iota(posr.rearrange("p (r c) -> p r c", c=64), [[1, 32], [0, 64]], base=0, channel_multiplier=0,
                                      allow_small_or_imprecise_dtypes=True), writes=["posr"])
        P.op("pool", lambda e: e.iota(posc.rearrange("p (r c) -> p r c", c=64), [[0, 32], [1, 64]], base=0, channel_multiplier=0,
                                      allow_small_or_imprecise_dtypes=True), writes=["posc"])
        P.op("pool", lambda e: e.iota(pidx, [[0, 1]], base=0, channel_multiplier=1), writes=["pidx"])
        P.op("dve", lambda e: e.tensor_single_scalar(p16, pidx, 15, ALU.bitwise_and), reads=["pidx"], writes=["p16"])
        P.op("dve", lambda e: e.tensor_single_scalar(pb16, pidx, 16, ALU.bitwise_and), reads=["pidx"], writes=["pb16"])
        P.op("dve", lambda e: e.tensor_copy(invt, p16), reads=["p16"], writes=["invt"])
        P.op("act", lambda e: e.activation(invt, invt, AF.Exp, scale=-math.log(10000.0) / 16.0), reads=["invt"], writes=["invt"])
        P.op("dve", lambda e: e.tensor_scalar_mul(invt, invt, 1.0 / TWO_PI), reads=["invt"], writes=["invt"])
        P.op("dve", lambda e: e.tensor_copy(mB, pb16), reads=["pb16"], writes=["mB"])
        P.op("dve", lambda e: e.tensor_scalar_mul(mB, mB, 1.0 / 16.0), reads=["mB"], writes=["mB"])
        P.op("dve", lambda e: e.tensor_scalar(mA, mB, -1.0, 1.0, ALU.mult, ALU.add), reads=["mB"], writes=["mA"])
        P.op("dve", lambda e: e.memset(halfpi, math.pi / 2), writes=["halfpi"])
        P.op("dve", lambda e: e.tensor_scalar_mul(posr, posr, mA), reads=["posr", "mA"], writes=["posr"])
        P.op("dve", lambda e: e.scalar_tensor_tensor(posr, posc, mB, posr, ALU.mult, ALU.add), reads=["posr", "posc", "mB"], writes=["posr"])
        P.op("dve", lambda e: e.tensor_scalar_mul(posr, posr, invt), reads=["posr", "invt"], writes=["posr"])
        pci = posc.bitcast(I32)
        P.op("dve", lambda e: e.tensor_copy(pci, posr), reads=["posr"], writes=["posc"])
        P.op("dve", lambda e: e.tensor_tensor(posr, posr, pci, ALU.subtract), reads=["posr", "posc"], writes=["posr"])
        P.op("act", lambda e: e.activation(self.sinT, posr, AF.Sin, scale=TWO_PI), reads=["posr"], writes=["sinT"])
        P.op("dve", lambda e: e.scalar_tensor_tensor(posr, posr, -1.0, posr, ALU.mult, ALU.max), reads=["posr"], writes=["posr"])
        P.op("act", lambda e: e.activation(self.cosT, posr, AF.Sin, bias=halfpi, scale=-TWO_PI), reads=["posr", "halfpi"], writes=["cosT"])
        self.Rm = LWb[0:64, 4096:4160]
        P.op("pool", lambda e: e.tensor_scalar_mul(self.Rm[:, 0:32], self.identb[0:64, 32:64], -1.0), reads=["identb"], writes=["Rm"])
        P.op("pool", lambda e: e.tensor_copy(self.Rm[:, 32:64], self.identb[0:64, 0:32]), reads=["identb"], writes=["Rm"])

    def rope_apply(self, buf, key, blocks):
        P = self.P
        for (c0, c1) in blocks:
            w = c1 - c0
            ps = self.ps[7]
            self.mm(ps[0:64, 0:w], self.Rm, buf[0:64, c0:c1], True, True, [key, "Rm"], ["ps7"])
            t1 = self.tmpA[0]
            t2 = self.tmpA[1]
            P.op("pool", lambda e, c0=c0, c1=c1, w=w: e.tensor_tensor(t1[0:64, 0:w], buf[0:64, c0:c1], self.cosT[:, c0 - CTX:c1 - CTX], ALU.mult),
                 reads=[key, "cosT"], writes=["tmpA0"])
            P.op("dve", lambda e, c0=c0, c1=c1, w=w: e.tensor_tensor(t2[0:64, 0:w], ps[0:64, 0:w], self.sinT[:, c0 - CTX:c1 - CTX], ALU.mult),
                 reads=["ps7", "sinT"], writes=["tmpA1"])
            P.op("dve", lambda e, c0=c0, c1=c1, w=w: e.tensor_tensor(buf[0:64, c0:c1], t1[0:64, 0:w], t2[0:64, 0:w], ALU.add),
                 reads=["tmpA0", "tmpA1"], writes=[key])

    def proj_tile(self, w, nk, m, src_fn, src_keys, wkey, evac, blks=None):
        for bi, (c0, c1) in enumerate(blks or BLKS):
            bank = self.proj_banks[self.ps_rr % len(self.proj_banks)]
            self.ps_rr += 1
            ps = self.ps[bank]
            for k in range(nk):
                self.mm(ps[0:m, 0:c1 - c0], w[:, k, :], src_fn(k)[:, c0:c1], k == 0, k == nk - 1,
                        [wkey, src_keys[k]], [f"ps{bank}"])
            evac(ps, f"ps{bank}", c0, c1)

    def mla_layer(self, l):
        P = self.P
        o = l // 2
        with_ctx = l < DEPTH - 1
        Win = self.W["mla_w_in"][o]
        Wuq = self.W["mla_w_uq"][o]
        Wukv = self.W["mla_w_ukv"][o]
        Wout = self.W["mla_w_out"][o]
        self.ps_rr = 0
        self.proj_banks = [0, 1, 2, 3, 4, 5, 6]
        self.rope_tables()
        F0b = self.F[0][:].bitcast(BF16)
        F1b = self.F[1][:].bitcast(BF16)
        F2b = self.F[2][:].bitcast(BF16)
        cqn = [F0b[:, 0:T], F0b[:, T:2 * T], F1b[:, 0:T]]
        vh = F1b[:, T:2 * T].rearrange("p (t d) -> p t d", d=128)
        ckvn = [F2b[:, 0:T], F2b[:, T:2 * T]]
        kr, qn, qr, kn = self.B[0], self.B[1], self.B[2], self.B[3]
        yf = self.ygrp[:].rearrange("p s t -> p (s t)").bitcast(F32)
        nsrc = lambda k: self.nT[:, k, :]
        nkeys = [("nT", k) for k in range(NKT)]
        sm = self.small
        gq = sm[:, 96:99]
        gkv = sm[:, 99:101]
        P.dma(gq, self.W["mla_q_norm"][o].rearrange("(k p) -> p k", p=128), writes=["gq"], allow_slow_non_contiguous=True)
        P.dma(gkv, self.W["mla_kv_norm"][o].rearrange("(k p) -> p k", p=128), writes=["gkv"], allow_slow_non_contiguous=True)

        def copy_evac(dst, dkey, m=128, scale=None, eng="act"):
            def f(ps, pkey, c0, c1):
                if eng == "act":
                    if scale is None:
                        P.op("act", lambda e: e.activation(dst[0:m, c0:c1], ps[0:m, 0:c1 - c0], AF.Copy), reads=[pkey], writes=[dkey])
                    else:
                        P.op("act", lambda e: e.activation(dst[0:m, c0:c1], ps[0:m, 0:c1 - c0], AF.Copy, scale=scale), reads=[pkey], writes=[dkey])
                else:
                    P.op("dve", lambda e: e.tensor_copy(dst[0:m, c0:c1], ps[0:m, 0:c1 - c0]), reads=[pkey], writes=[dkey])
            return f

        for k in range(3):
            w, wk = self.load_cast(Win[:, k * 128:(k + 1) * 128], 8, 128)
            self.proj_tile(w, 8, 128, nsrc, nkeys, wk, copy_evac(cqn[k], f"cqn{k}", eng="act" if k % 2 == 0 else "dve"))
        for k in range(2):
            w, wk = self.load_cast(Win[:, 384 + k * 128:384 + (k + 1) * 128], 8, 128)
            self.proj_tile(w, 8, 128, nsrc, nkeys, wk, copy_evac(ckvn[k], f"ckvn{k}", eng="dve" if k % 2 == 0 else "act"))
        w, wk = self.load_cast(Win[:, 640:704], 8, 64)
        self.proj_tile(w, 8, 64, nsrc, nkeys, wk, copy_evac(kr, "B0", m=64))
        rq = yf[:, 0:T]
        rkv = yf[:, T:2 * T]
        self.rstd_into(rq, "rq", cqn, [f"cqn{k}" for k in range(3)], 384)
        self.rstd_into(rkv, "rkv", ckvn, [f"ckvn{k}" for k in range(2)], 256)
        for k in range(3):
            P.op("dve", lambda e, k=k: e.scalar_tensor_tensor(cqn[k], cqn[k], gq[:, k:k + 1], rq, ALU.mult, ALU.mult),
                 reads=[f"cqn{k}", "gq", "rq"], writes=[f"cqn{k}"])
        for k in range(2):
            P.op("dve", lambda e, k=k: e.scalar_tensor_tensor(ckvn[k], ckvn[k], gkv[:, k:k + 1], rkv, ALU.mult, ALU.mult),
                 reads=[f"ckvn{k}", "gkv", "rkv"], writes=[f"ckvn{k}"])
        self.rope_apply(kr, "B0", BLKS[1:])
        P.op("pool", lambda e: e.memset(kr[64:128, 0:T], 0.0), writes=["B0"])
        P.op("pool", lambda e: e.memset(qr[64:128, 0:T], 0.0), writes=["B2"])
        qblks = BLKS if with_ctx else BLKS[1:]
        cq_src = lambda k: cqn[k]
        cq_keys = [f"cqn{k}" for k in range(3)]
        kv_src = lambda k: ckvn[k]
        kv_keys = [f"ckvn{k}" for k in range(2)]
        self.proj_banks = [4, 5, 6]

        def head_loads(h, defer):
            r = [self.load_cast(Wuq[:, h * 192:(h + 1) * 192], 3, 192, defer=defer),
                 self.load_cast(Wukv[:, h * 256:(h + 1) * 256], 2, 256, defer=defer)]
            if not defer:
                r.append(self.load_cast(Win[:, 704 + h * 128:704 + (h + 1) * 128], 8, 128))
            return r
        pend = None
        for h in range(8):
            if pend is None:
                pend = head_loads(h, False)
            (wq, wqk), (wkv, wkvk), (wg, wgk) = [p[0:2] for p in pend]
            pend = None
            self.proj_tile(wq[:, :, 0:128], 3, 128, cq_src, cq_keys, wqk, copy_evac(qn, "B1", scale=MLA_SCALE))
            self.proj_tile(wq[:, :, 128:192], 3, 64, cq_src, cq_keys, wqk, copy_evac(qr, "B2", m=64, scale=MLA_SCALE))
            self.rope_apply(qr, "B2", BLKS[1:])
            self.proj_tile(wkv[:, :, 0:128], 2, 128, kv_src, kv_keys, wkvk, copy_evac(kn, "B3", eng="dve"))
            for t0 in range(0, 18, 4):
                nt = min(4, 18 - t0)
                bank = 4 + (self.ps_rr % 3)
                self.ps_rr += 1
                ps = self.ps[bank]
                for j in range(nt):
                    tt = t0 + j
                    for rk in range(2):
                        self.mm(ps[:, j * 128:(j + 1) * 128], ckvn[rk][:, tt * 128:(tt + 1) * 128], wkv[:, rk, 128:256],
                                rk == 0, rk == 1, [f"ckvn{rk}", wkvk], [f"ps{bank}"])
                P.op("dve", lambda e, ps=ps, t0=t0, nt=nt: e.tensor_copy(
                    vh[:, t0:t0 + nt, :], ps[:, 0:nt * 128].rearrange("p (t d) -> p t d", d=128)),
                    reads=[f"ps{bank}"], writes=["vh"])
            def ev_gate(ps, pkey, c0, c1, h=h):
                P.op("act", lambda e: e.activation(self.ygrp[:, h % 4, c0:c1], ps[:, 0:c1 - c0], AF.Silu), reads=[pkey],
                     writes=[("og", h % 4), "rq", "rkv"])
            self.proj_tile(wg, 8, 128, nsrc, nkeys, wgk, ev_gate, blks=qblks)
            if h + 1 < 8 and h % 4 != 3:
                pend = head_loads(h + 1, True)
            LOOK = 2
            for qi, (c0, c1) in enumerate(qblks):
                if qi == 1 and pend is not None:
                    for p in pend:
                        p[2]("act")
                    d3, k3, c3 = self.load_cast(Win[:, 704 + (h + 1) * 128:704 + (h + 2) * 128], 8, 128, defer=True)
                    c3("act")
                    pend.append((d3, k3))
                w_ = c1 - c0
                nkt = 2 if c0 == 0 else 18
                Ops = self.ps[qi % 2]
                Dps = self.ps[2 + qi % 2]
                okey, dkey = f"ps{qi % 2}", f"ps{2 + qi % 2}"
                Ebuf = {}

                def emit_S(kt, w_=w_, c0=c0, c1=c1):
                    sb = 4 + (self.ps_rr % 3)
                    self.ps_rr += 1
                    S = self.ps[sb]
                    ks = slice(kt * 128, (kt + 1) * 128)
                    self.mm(S[:, 0:w_], kn[:, ks], qn[:, c0:c1], True, False, ["B3", "B1"], [f"ps{sb}"])
                    self.mm(S[:, 0:w_], kr[:, ks], qr[:, c0:c1], False, True, ["B0", "B2"], [f"ps{sb}"])
                    ei = kt % 3
                    E = self.tmpB[ei]
                    P.op("act", lambda e, E=E, S=S, w_=w_: e.activation(E[:, 0:w_], S[:, 0:w_], AF.Exp),
                         reads=[f"ps{sb}"], writes=[f"tmpB{ei}"])
                    Ebuf[kt] = (E, f"tmpB{ei}")

                acc = self.tmpA[2]

                def emit_OD(kt, w_=w_, nkt=nkt, Ops=Ops, Dps=Dps, okey=okey, dkey=dkey):
                    E, ek = Ebuf[kt]
                    self.mm(Ops[:, 0:w_], vh[:, kt, :], E[:, 0:w_], kt == 0, kt == nkt - 1, ["vh", ek], [okey])
                    if kt % 2 == 1:
                        self.mm(Dps[:, 0:w_], self.onesb[:], E[:, 0:w_], kt == 1, False, ["onesb", ek], [dkey])
                    elif kt == 0:
                        P.op("dve", lambda e: e.tensor_copy(acc[:, 0:w_], E[:, 0:w_]), reads=[ek], writes=["tmpA2"])
                    else:
                        P.op("dve", lambda e: e.tensor_tensor(acc[:, 0:w_], acc[:, 0:w_], E[:, 0:w_], ALU.add), reads=[ek, "tmpA2"], writes=["tmpA2"])
                    if kt == nkt - 1:
                        accb = self.tmpB[3]
                        P.op("dve", lambda e: e.tensor_copy(accb[:, 0:w_], acc[:, 0:w_]), reads=["tmpA2"], writes=["tmpB3"])
                        self.mm(Dps[:, 0:w_], self.onesb[:], accb[:, 0:w_], False, True, ["onesb", "tmpB3"], [dkey])
                for kt in range(min(LOOK, nkt)):
                    emit_S(kt)
                for kt in range(nkt):
                    if kt + LOOK < nkt:
                        emit_S(kt + LOOK)
                    emit_OD(kt)
                rec = self.tmpA[0]
                o1 = self.tmpA[1]
                P.op("dve", lambda e, Dps=Dps, w_=w_: e.reciprocal(rec[:, 0:w_], Dps[:, 0:w_]), reads=[dkey], writes=["tmpA0"])
                P.op("dve", lambda e, Ops=Ops, w_=w_: e.tensor_tensor(o1[:, 0:w_], Ops[:, 0:w_], rec[:, 0:w_], ALU.mult),
                     reads=[okey, "tmpA0"], writes=["tmpA1"])
                P.op("pool", lambda e, h=h, c0=c0, c1=c1, w_=w_: e.tensor_tensor(self.ygrp[:, h % 4, c0:c1], o1[:, 0:w_], self.ygrp[:, h % 4, c0:c1], ALU.mult),
                     reads=["tmpA1", ("og", h % 4)], writes=[("og", h % 4)])
            if h % 4 == 3:
                self.out_proj(l, Wout[(h // 4) * 512:(h // 4 + 1) * 512, :], qblks)
        self.barrier()

    def out_proj(self, l, Wrows, blks, nk=4):
        P = self.P
        for ot in range(NKT):
            w, wk = self.load_cast(Wrows[:, ot * 128:(ot + 1) * 128], nk, 128)
            for (c0, c1) in blks:
                s = 1 if c0 == 0 else 0
                bank = self.proj_banks[self.ps_rr % len(self.proj_banks)]
                self.ps_rr += 1
                ps = self.ps[bank]
                for j in range(nk):
                    self.mm(ps[:, 0:c1 - c0], w[:, j, :], self.ygrp[:, j, c0:c1], j == 0, j == nk - 1, [wk, ("og", j)], [f"ps{bank}"])
                P.op("dve", lambda e, ot=ot, c0=c0, c1=c1, s=s, ps=ps: e.scalar_tensor_tensor(
                    self.xT[:, ot, c0:c1], ps[:, 0:c1 - c0], self.modT[:, l, 16 + ot, s:s + 1], self.xT[:, ot, c0:c1], ALU.mult, ALU.add),
                    reads=[f"ps{bank}", "modT", ("xT", ot)], writes=[("xT", ot)])


_NC_CACHE = {}


def _get_nc(layers, debug=False):
    key = (tuple(layers), debug)
    if key not in _NC_CACHE:
        _NC_CACHE[key] = Builder(layers, debug).build()
    return _NC_CACHE[key]


def kernel(_layers=(0, 1, 2, 3), _ncores=8, _debug=False, **inputs):
    nc = _get_nc(_layers, _debug)
    in_maps = []
    for b in range(_ncores):
        m = {}
        for name, shp in WSPEC:
            a = np.asarray(inputs[name], dtype=np.float32)
            if name in ("x", "c", "ctx"):
                a = a[b]
            m[name] = np.ascontiguousarray(a).reshape(shp)
        in_maps.append(m)
    res = run_bass_kernel_spmd(nc, in_maps, core_ids=list(range(_ncores)))
    if _debug:
        return res.results[0]
    return np.stack([np.asarray(r["out"], dtype=np.float32).reshape(SEQ, D) for r in res.results], axis=0)
```

```python
import math
import numpy as np
import concourse.bass as bass
import concourse.mybir as mybir
from concourse.bass_utils import run_bass_kernel_spmd

F32 = mybir.dt.float32
BF16 = mybir.dt.bfloat16
I32 = mybir.dt.int32
AF = mybir.ActivationFunctionType
ALU = mybir.AluOpType
AX = mybir.AxisListType

D = 1024
SEQ = 2048
CTX = 256
T = CTX + SEQ
NKT = 8
DEPTH = 4
EPS = 1e-6
BLKS = [(0, 256), (256, 768), (768, 1280), (1280, 1792), (1792, 2304)]
MLA_SCALE = 1.0 / math.sqrt(192.0)
TWO_PI = 2.0 * math.pi

DBG_D = 0
ENGS = ["pe", "act", "dve", "pool", "sp"]


class Prog:
    def __init__(self, nc):
        self.nc = nc
        self.streams = {e: [] for e in ENGS}
        self.count = {e: 0 for e in ENGS}
        self.sem = {e: nc.alloc_semaphore(name=f"prog_{e}") for e in ENGS}
        self.waited = {}
        self.lastw = {}
        self.readers = {}
        self.dma_sems = [nc.alloc_semaphore(name=f"dma_{i}") for i in range(16)]
        self.dma_cnt = [0] * 16
        self.dma_rr = 0

    def _deps(self, reads, writes):
        deps = []
        for k in reads:
            if k in self.lastw:
                deps.append(self.lastw[k])
        for k in writes:
            if k in self.lastw:
                deps.append(self.lastw[k])
            deps.extend(self.readers.get(k, []))
        return deps

    def _emit_waits(self, eng, deps):
        need = {}
        for (s, v) in deps:
            if eng == "pe" and s is self.sem["pe"]:
                continue
            key = id(s)
            if v > self.waited.get((eng, key), 0):
                if key not in need or need[key][1] < v:
                    need[key] = (s, v)
        for key, (s, v) in need.items():
            self.waited[(eng, key)] = v
            self.streams[eng].append(lambda e, s=s, v=v: e.wait_ge(s, v))

    def _commit(self, tok, reads, writes):
        for k in writes:
            self.lastw[k] = tok
            self.readers[k] = []
        for k in reads:
            if k not in writes:
                self.readers.setdefault(k, []).append(tok)

    def op(self, eng, fn, reads=(), writes=()):
        reads = list(reads)
        writes = list(writes)
        self._emit_waits(eng, self._deps(reads, writes))
        self.count[eng] += 1
        v = self.count[eng]
        s = self.sem[eng]
        self.streams[eng].append(lambda e, fn=fn, s=s: fn(e).then_inc(s, 1))
        tok = (s, v)
        self._commit(tok, reads, writes)
        return tok

    def dma(self, out, in_, reads=(), writes=(), eng="sp", **kw):
        reads = list(reads)
        writes = list(writes)
        i = self.dma_rr
        self.dma_rr = (self.dma_rr + 1) % len(self.dma_sems)
        s = self.dma_sems[i]
        deps = self._deps(reads, writes)
        if self.dma_cnt[i] > 0:
            deps.append((s, 16 * self.dma_cnt[i]))
        self._emit_waits(eng, deps)
        self.dma_cnt[i] += 1
        v = 16 * self.dma_cnt[i]
        self.streams[eng].append(
            lambda e, out=out, in_=in_, s=s, kw=kw: e.dma_start(out=out, in_=in_, **kw).then_inc(s, 16))
        tok = (s, v)
        self._commit(tok, reads, writes)
        return tok

    def finish(self, final_tokens):
        nc = self.nc
        self._emit_waits("sp", final_tokens)
        with nc.Block() as block:
            @block.tensor
            def _(e):
                for f in self.streams["pe"]:
                    f(e)

            @block.scalar
            def _(e):
                for f in self.streams["act"]:
                    f(e)

            @block.vector
            def _(e):
                for f in self.streams["dve"]:
                    f(e)

            @block.gpsimd
            def _(e):
                for f in self.streams["pool"]:
                    f(e)

            @block.sync
            def _(e):
                for f in self.streams["sp"]:
                    f(e)


WSPEC = [
    ("x", [SEQ, D]), ("c", [D]), ("ctx", [CTX, D]), ("c_ctx", [D]),
    ("norm_g", [4, D]), ("mod_w", [4, D, 3 * D]), ("mod_b", [4, 3 * D]),
    ("ev_w_in", [2, D, 3072]), ("lru_conv_w", [2, 4, D]), ("lru_conv_b", [2, D]),
    ("lru_wr", [2, 2, 8, 128, 128]), ("lru_br", [2, 2, D]),
    ("lru_wi", [2, 2, 8, 128, 128]), ("lru_bi", [2, 2, D]), ("lru_lam", [2, 2, D]),
    ("s5_lam_re", [2, 2, 32, 64]), ("s5_lam_im", [2, 2, 32, 64]), ("s5_log_dt", [2, 2, 32, 64]),
    ("s5_b_re", [2, 2, 32, 64, 16]), ("s5_b_im", [2, 2, 32, 64, 16]),
    ("s5_c_re", [2, 2, 32, 16, 64]), ("s5_c_im", [2, 2, 32, 16, 64]),
    ("s5_d", [2, 32, 16]), ("s5_glu_w", [2, 512, 512]), ("s5_glu_b", [2, 512]),
    ("ev_w_out", [2, 1536, D]),
    ("mla_w_in", [2, D, 1728]), ("mla_q_norm", [2, 384]), ("mla_w_uq", [2, 384, 1536]),
    ("mla_kv_norm", [2, 256]), ("mla_w_ukv", [2, 256, 2048]), ("mla_w_out", [2, D, D]),
    ("final_g", [D]),
]


class Builder:
    def __init__(self, layers=(0, 1, 2, 3), debug=False):
        self.layers = list(layers)
        nc = bass.Bass("TRN2", target_bir_lowering=False)
        self.nc = nc
        self.P = Prog(nc)
        self.W = {}
        for name, shp in WSPEC:
            self.W[name] = nc.dram_tensor(name, shp, F32, kind="ExternalInput").ap()
        self.out = nc.dram_tensor("out", [SEQ, D], F32, kind="ExternalOutput").ap()
        self.debug = debug
        if debug:
            self.dbgf = nc.dram_tensor("dbgf", [8, 128, T], F32, kind="ExternalOutput").ap()
            self.dbgb = nc.dram_tensor("dbgb", [8, 128, T], BF16, kind="ExternalOutput").ap()
        A = nc.alloc_sbuf_tensor
        self.xT = A("xT", [128, NKT, T], F32)
        self.nT = A("nT", [128, NKT, T], BF16)
        self.modT = A("modT", [128, DEPTH, 24, 2], F32)
        self.identf = A("identf", [128, 128], F32)
        self.identb = A("identb", [128, 128], BF16)
        self.onesb = A("onesb", [128, 128], BF16)
        self.ygrp = A("ygrp", [128, 4, T], BF16)
        self.F = [A(f"F{i}", [128, T], F32) for i in range(3)]
        self.B = [A(f"B{i}", [128, T + 8], BF16) for i in range(4)]
        self.stage = [A(f"stage{i}", [128, 1024], F32) for i in range(2)]
        self.stage_i = 0
        self.wbf = [A(f"wbf{i}", [128, 1024], BF16) for i in range(3)]
        self.wbf_i = 0
        self.LW = A("LW", [128, 2304], F32)
        self.small = A("small", [128, 256], F32)
        self.tmpA = [A(f"tmpA{i}", [128, 512], F32) for i in range(3)]
        self.tmpB = [A(f"tmpB{i}", [128, 512], BF16) for i in range(4)]
        self.ps = [nc.alloc_psum_tensor(f"ps{i}", [128, 512], F32) for i in range(8)]

    def load_cast(self, src_ap, kt, ncols, dst=None, dst_key=None, defer=False):
        P = self.P
        si = self.stage_i
        self.stage_i = (si + 1) % len(self.stage)
        st = self.stage[si]
        stv = st[:, 0:kt * ncols].rearrange("p (k c) -> p k c", k=kt)
        P.dma(stv, src_ap.rearrange("(k p) c -> p k c", p=128), writes=[f"stage{si}"])
        if dst is None:
            wi = self.wbf_i
            self.wbf_i = (wi + 1) % len(self.wbf)
            dst = self.wbf[wi][:, 0:kt * ncols].rearrange("p (k c) -> p k c", k=kt)
            dst_key = f"wbf{wi}"

        def cast(eng="pool"):
            if eng == "act":
                P.op("act", lambda e: e.activation(dst, stv, AF.Copy), reads=[f"stage{si}"], writes=[dst_key])
            else:
                P.op(eng, lambda e: e.tensor_copy(dst, stv), reads=[f"stage{si}"], writes=[dst_key])
        if defer:
            return dst, dst_key, cast
        cast()
        return dst, dst_key

    def mm(self, out, lhsT, rhs, start, stop, reads, writes):
        return self.P.op("pe", lambda e: e.matmul(out, lhsT, rhs, start=start, stop=stop),
                         reads=reads, writes=writes)

    def consts(self):
        P = self.P
        idf, idb, ob = self.identf, self.identb, self.onesb
        P.op("pool", lambda e: e.memset(idf[:], 0.0), writes=["identf"])
        P.op("pool", lambda e: e.affine_select(idf[:], idf[:], [[-1, 128]], ALU.not_equal, 1.0, base=0,
                                               channel_multiplier=1), reads=["identf"], writes=["identf"])
        P.op("pool", lambda e: e.tensor_copy(idb[:], idf[:]), reads=["identf"], writes=["identb"])
        P.op("pool", lambda e: e.memset(ob[:], 1.0), writes=["onesb"])

    def load_x(self):
        P = self.P
        for tt in range(T // 128):
            st = self.stage[tt % 2]
            skey = f"stage{tt % 2}"
            src = self.W["ctx"][tt * 128:(tt + 1) * 128, :] if tt < 2 else self.W["x"][(tt - 2) * 128:(tt - 1) * 128, :]
            P.dma(st[:], src, writes=[skey])
            for half in range(2):
                ps = self.ps[(tt * 2 + half) % 8]
                pkey = f"ps{(tt * 2 + half) % 8}"
                for q in range(4):
                    kt = half * 4 + q
                    P.op("pe", lambda e, ps=ps, q=q, kt=kt, st=st: e.transpose(
                        ps[:, q * 128:(q + 1) * 128], st[:, kt * 128:(kt + 1) * 128], self.identf[:]),
                        reads=[skey, "identf"], writes=[pkey])
                dst = self.xT[:, half * 4:half * 4 + 4, tt * 128:(tt + 1) * 128]
                srcp = ps[:].rearrange("p (q c) -> p q c", q=4)
                eng = "dve" if half == 0 else "act"
                if eng == "dve":
                    P.op("dve", lambda e, dst=dst, srcp=srcp: e.tensor_copy(dst, srcp), reads=[pkey],
                         writes=[("xT", half * 4 + q) for q in range(4)])
                else:
                    P.op("act", lambda e, dst=dst, srcp=srcp: e.activation(dst, srcp, AF.Copy), reads=[pkey],
                         writes=[("xT", half * 4 + q) for q in range(4)])

    def modulation(self):
        P = self.P
        sm = self.small
        cc = sm[:, 0:16]
        csb = sm[:, 16:24].bitcast(BF16)
        P.dma(cc[:, 0:8], self.W["c"].rearrange("(k p) -> p k", p=128), writes=["cc"], allow_slow_non_contiguous=True)
        P.dma(cc[:, 8:16], self.W["c_ctx"].rearrange("(k p) -> p k", p=128), writes=["cc"], allow_slow_non_contiguous=True)
        P.op("act", lambda e: e.activation(csb, cc, AF.Silu), reads=["cc"], writes=["cs"])
        csv = csb.rearrange("p (s k) -> p k s", s=2)
        nTf = self.nT[:].rearrange("p k t -> p (k t)").bitcast(F32)
        stg = [nTf[:, i * 1024:(i + 1) * 1024] for i in range(6)]
        nxt = 0
        for l in range(DEPTH):
            for kt in range(NKT):
                for h in range(3):
                    si = nxt % 6
                    nxt += 1
                    st = stg[si]
                    P.dma(st, self.W["mod_w"][l, kt * 128:(kt + 1) * 128, h * 1024:(h + 1) * 1024], writes=[f"mstg{si}"])
                    wi = self.wbf_i
                    self.wbf_i = (wi + 1) % len(self.wbf)
                    wb = self.wbf[wi]
                    eng = "act" if nxt % 2 == 0 else "dve"
                    if eng == "act":
                        P.op("act", lambda e, wb=wb, st=st: e.activation(wb[:], st, AF.Copy), reads=[f"mstg{si}"], writes=[f"wbf{wi}"])
                    else:
                        P.op("dve", lambda e, wb=wb, st=st: e.tensor_copy(wb[:], st), reads=[f"mstg{si}"], writes=[f"wbf{wi}"])
                    for q in range(2):
                        bank = h * 2 + q
                        self.mm(self.ps[bank][0:2, :], csv[:, kt, :], wb[:, q * 512:(q + 1) * 512],
                                kt == 0, kt == NKT - 1, ["cs", f"wbf{wi}"], [f"ps{bank}"])
            mrow = nTf[:, 6144:9216][0:2, :]
            bro = self.F[0][0:2, 0:2304]
            bro2 = self.F[1][0:2, 0:768]
            for s_ in range(2):
                P.dma(bro[s_:s_ + 1, :], self.W["mod_b"][l:l + 1, 0:2304], writes=["bro", "F0"])
                P.dma(bro2[s_:s_ + 1, :], self.W["mod_b"][l:l + 1, 2304:3072], writes=["bro", "F1"])
            for bank in range(6):
                c0 = bank * 512
                if c0 + 512 <= 2304:
                    bsrc = bro[:, c0:c0 + 512]
                    P.op("dve", lambda e, bank=bank, bsrc=bsrc, c0=c0: e.tensor_tensor(
                        mrow[:, c0:c0 + 512], self.ps[bank][0:2, :], bsrc, ALU.add), reads=[f"ps{bank}", "bro", "F0", "F1"], writes=["mrow"])
                else:
                    for (a0, a1) in [(c0, min(c0 + 512, 2304)), (max(c0, 2304), c0 + 512)]:
                        if a1 <= a0:
                            continue
                        bsrc = bro[:, a0:a1] if a1 <= 2304 else bro2[:, a0 - 2304:a1 - 2304]
                        P.op("dve", lambda e, bank=bank, bsrc=bsrc, a0=a0, a1=a1, c0=c0: e.tensor_tensor(
                            mrow[:, a0:a1], self.ps[bank][0:2, a0 - c0:a1 - c0], bsrc, ALU.add), reads=[f"ps{bank}", "bro", "F0", "F1"], writes=["mrow"])
            for j in range(24):
                self.mm(self.ps[6][:, 2 * j:2 * j + 2], mrow[:, j * 128:(j + 1) * 128], self.identf[0:2, 0:2],
                        True, True, ["mrow", "identf"], ["ps6"])
            P.op("dve", lambda e, l=l: e.tensor_copy(self.modT[:, l].rearrange("p j s -> p (j s)"), self.ps[6][:, 0:48]),
                 reads=["ps6"], writes=["modT"])

    def norm_mod(self, l):
        P = self.P
        sm = self.small
        g = sm[:, 32:40]
        gm = sm[:, 40:56]
        P.dma(g, self.W["norm_g"][l].rearrange("(k p) -> p k", p=128), writes=["g"], allow_slow_non_contiguous=True)
        for s in range(2):
            P.op("dve", lambda e, s=s: e.scalar_tensor_tensor(
                gm[:, s * 8:(s + 1) * 8], self.modT[:, l, 8:16, s], 1.0, g, ALU.add, ALU.mult),
                reads=["modT", "g"], writes=["gm"])
        rstd = self.F[1]
        self.rstd_into(rstd, "F1", [self.xT[:, kt, :] for kt in range(NKT)], [("xT", kt) for kt in range(NKT)], D)
        for kt in range(NKT):
            for s, (c0, c1) in enumerate([(CTX, T), (0, CTX)]):
                tmp = self.F[2]
                P.op("dve", lambda e, kt=kt, s=s, c0=c0, c1=c1: e.scalar_tensor_tensor(
                    tmp[:, c0:c1], self.xT[:, kt, c0:c1], gm[:, s * 8 + kt:s * 8 + kt + 1], rstd[:, c0:c1], ALU.mult, ALU.mult),
                    reads=[("xT", kt), "gm", "F1"], writes=["F2"])
                P.op("act", lambda e, kt=kt, s=s, c0=c0, c1=c1: e.activation(
                    self.nT[:, kt, c0:c1], tmp[:, c0:c1], AF.Identity, bias=self.modT[:, l, kt, s:s + 1], scale=1.0),
                    reads=["F2", "modT"], writes=[("nT", kt)])

    def rstd_into(self, dst, dst_key, srcs, src_keys, dim, cols=(0, T)):
        P = self.P
        blks = [b for b in BLKS if b[0] >= cols[0] and b[1] <= cols[1]]
        for bi, (c0, c1) in enumerate(blks):
            pb = self.ps[7]
            n = len(srcs)
            for i, (s_ap, sk) in enumerate(zip(srcs, src_keys)):
                sq = self.tmpB[i % 2]
                P.op("act", lambda e, sq=sq, s_ap=s_ap, c0=c0, c1=c1: e.activation(sq[:, 0:c1 - c0], s_ap[:, c0:c1], AF.Square),
                     reads=[sk], writes=[f"tmpB{i % 2}"])
                self.mm(pb[:, 0:c1 - c0], self.onesb[:], sq[:, 0:c1 - c0], i == 0, i == n - 1,
                        [f"tmpB{i % 2}", "onesb"], ["ps7"])
            P.op("act", lambda e, c0=c0, c1=c1: e.activation(dst[:, c0:c1], pb[:, 0:c1 - c0], AF.Sqrt, bias=EPS, scale=1.0 / dim),
                 reads=["ps7"], writes=[dst_key])
            P.op("dve", lambda e, c0=c0, c1=c1: e.reciprocal(dst[:, c0:c1], dst[:, c0:c1]), reads=[dst_key], writes=[dst_key])

    def final(self):
        P = self.P
        gb = self.F[0]
        P.dma(gb[:, 0:D], self.W["final_g"].partition_broadcast(128), writes=["F0"])
        ot = self.F[1]
        for tt in range(SEQ // 128):
            c0 = CTX + tt * 128
            ss = self.small[:, 64 + (tt % 2) * 2:64 + (tt % 2) * 2 + 2]
            sskey = f"ss{tt % 2}"
            pss = []
            P.op("pool", lambda e, ss=ss: e.memset(ss, 0.0), writes=[sskey + "0", sskey + "1"])
            for half in range(2):
                bank = (tt * 2 + half) % 4
                ps = self.ps[bank]
                for q in range(4):
                    kt = half * 4 + q
                    P.op("pe", lambda e, ps=ps, q=q, kt=kt, c0=c0: e.transpose(
                        ps[:, q * 128:(q + 1) * 128], self.xT[:, kt, c0:c0 + 128], self.identf[:]),
                        reads=[("xT", kt), "identf"], writes=[f"ps{bank}"])
                junk = self.tmpA[half]
                P.op("act", lambda e, ps=ps, junk=junk, half=half, ss=ss: e.activation(
                    junk[:], ps[:], AF.Square, accum_out=ss[:, half:half + 1]),
                    reads=[f"ps{bank}"], writes=[f"tmpA{half}", sskey + str(half)])
                pss.append((ps, bank))
            rs = self.small[:, 72 + (tt % 2):73 + (tt % 2)]
            rkey = f"rs{tt % 2}"
            P.op("dve", lambda e, ss=ss, rs=rs: e.tensor_tensor(rs, ss[:, 0:1], ss[:, 1:2], ALU.add),
                 reads=[sskey + "0", sskey + "1"], writes=[rkey])
            P.op("act", lambda e, rs=rs: e.activation(rs, rs, AF.Sqrt, bias=EPS, scale=1.0 / D), reads=[rkey], writes=[rkey])
            P.op("dve", lambda e, rs=rs: e.reciprocal(rs, rs), reads=[rkey], writes=[rkey])
            obuf = ot[:, (tt % 2) * 1024:(tt % 2) * 1024 + 1024]
            okey = f"ot{tt % 2}"
            for half, (ps, bank) in enumerate(pss):
                P.op("dve", lambda e, ps=ps, half=half, rs=rs, obuf=obuf: e.scalar_tensor_tensor(
                    obuf[:, half * 512:(half + 1) * 512], ps[:], rs, gb[:, half * 512:(half + 1) * 512], ALU.mult, ALU.mult),
                    reads=[f"ps{bank}", rkey, "F0"], writes=[okey + str(half)])
            self.out_toks.append(P.dma(self.out[tt * 128:(tt + 1) * 128, :], obuf, reads=[okey + "0", okey + "1"]))

    def build(self):
        self.out_toks = []
        self.consts()
        self.load_x()
        self.modulation()
        for l in self.layers:
            self.norm_mod(l)
            if l % 2 == 0:
                self.even_layer(l)
            else:
                self.mla_layer(l)
        self.final()
        self.P.finish(self.out_toks)
        return self.nc


    def even_layer(self, l):
        P = self.P
        e_ = l // 2
        Win = self.W["ev_w_in"][e_]
        Wout = self.W["ev_w_out"][e_]
        self.ps_rr = 0
        self.proj_banks = [0, 1, 2, 3, 4, 5, 6, 7]
        sm = self.small
        cw = sm[:, 104:136].rearrange("p (k t) -> p k t", k=4)
        cb = sm[:, 136:144]
        br = sm[:, 144:160].rearrange("p (d t) -> p d t", d=2)
        bi = sm[:, 160:176].rearrange("p (d t) -> p d t", d=2)
        coef = sm[:, 176:192].rearrange("p (d t) -> p d t", d=2)
        coef2 = sm[:, 192:208].rearrange("p (d t) -> p d t", d=2)
        ld = lambda dst, src, key: P.dma(dst, src, writes=[key], allow_slow_non_contiguous=True)
        for k in range(4):
            ld(cw[:, k, :], self.W["lru_conv_w"][e_, k].rearrange("(t p) -> p t", p=128), "cw")
        ld(cb, self.W["lru_conv_b"][e_].rearrange("(t p) -> p t", p=128), "cb")
        for d in range(2):
            ld(br[:, d, :], self.W["lru_br"][e_, d].rearrange("(t p) -> p t", p=128), "br")
            ld(bi[:, d, :], self.W["lru_bi"][e_, d].rearrange("(t p) -> p t", p=128), "bi")
            ld(coef[:, d, :], self.W["lru_lam"][e_, d].rearrange("(t p) -> p t", p=128), "coef")
        cf = sm[:, 176:192]
        cf2 = sm[:, 192:208]
        P.op("act", lambda e: e.activation(cf, cf, AF.Exp, scale=-1.0), reads=["coef"], writes=["coef"])
        P.op("act", lambda e: e.activation(cf, cf, AF.Ln, bias=1.0), reads=["coef"], writes=["coef"])
        P.op("dve", lambda e: e.tensor_scalar_mul(cf2, cf, -16.0), reads=["coef"], writes=["coef2"])
        P.op("dve", lambda e: e.tensor_scalar_mul(cf, cf, -8.0), reads=["coef", "coef2"], writes=["coef"])
        nsrc = lambda k: self.nT[:, k, :]
        nkeys = [("nT", k) for k in range(NKT)]
        xap, u, ib, hf = self.B
        F0, F1, F2 = self.F
        dg = [self.tmpB[0][:, k * 128:(k + 1) * 128] for k in range(4)]
        F2b = F2[:].bitcast(BF16)
        hr = F2b[:, 0:T]
        abuf = [(F1, "F1"), (self.LW, "LWa")]
        ibuf = [(ib, "B2"), (F2b[:, T:2 * T], "ibB")]
        for (a, b) in [(0, 2), (258, 262), (2310, 2312)]:
            P.op("pool", lambda e, a=a, b=b: e.memset(xap[:, a:b], 0.0), writes=["B0"])

        def xoff(c0):
            return c0 + 2 if c0 < CTX else c0 + 6

        def lru_loads(h):
            r = {"wxa": self.load_cast(Win[:, h * 128:(h + 1) * 128], 8, 128),
                 "wga": self.load_cast(Win[:, 1024 + h * 128:1024 + (h + 1) * 128], 8, 128)}
            gwb = self.tmpB[2 + h % 2]
            for d in range(2):
                for q, nm in enumerate(["lru_wr", "lru_wi"]):
                    off = (d * 2 + q) * 128
                    dst = gwb[:, off:off + 128].rearrange("p (k c) -> p k c", k=1)
                    r[(d, q)] = self.load_cast(self.W[nm][e_, d, h], 1, 128, dst=dst, dst_key=("gw", h % 2, d, q))
            return r

        pend = None
        xa_pending = None
        for h in range(8):
            if pend is None:
                pend = lru_loads(h)
            Wt = pend
            pend = None
            wxa, wxak = Wt["wxa"]
            wga, wgak = Wt["wga"]

            def ev_xa(ps, pkey, c0, c1):
                P.op("act", lambda e: e.activation(xap[:, xoff(c0):xoff(c0) + c1 - c0], ps[:, 0:c1 - c0], AF.Copy), reads=[pkey], writes=["B0"])
            if xa_pending is not None:
                for (ps_, pk_, c0_, c1_) in xa_pending:
                    ev_xa(ps_, pk_, c0_, c1_)
                xa_pending = None
                self.proj_banks = [0, 1, 2, 3, 4, 5, 6, 7]
            else:
                self.proj_tile(wxa, 8, 128, nsrc, nkeys, wxak, ev_xa)
            for k in range(4):
                P.op("pool", lambda e, k=k, h=h: e.tensor_scalar_mul(dg[k], self.identb[:], cw[:, k, h:h + 1]), reads=["identb", "cw"], writes=[f"dg{k}", "tmpB0"])
            for (c0, c1) in BLKS:
                bank = self.proj_banks[self.ps_rr % len(self.proj_banks)]
                self.ps_rr += 1
                ps = self.ps[bank]
                for k in range(4):
                    o0 = xoff(c0) + k - 2
                    self.mm(ps[:, 0:c1 - c0], dg[k], xap[:, o0:o0 + c1 - c0], k == 0, k == 3, [f"dg{k}", "B0"], [f"ps{bank}"])
                P.op("act", lambda e, ps=ps, c0=c0, c1=c1, h=h: e.activation(u[:, c0:c1], ps[:, 0:c1 - c0], AF.Identity, bias=cb[:, h:h + 1], scale=1.0),
                     reads=[f"ps{bank}", "cb"], writes=["B1"])
            def ev_g(ps, pkey, c0, c1):
                P.op("act", lambda e: e.activation(xap[:, xoff(c0):xoff(c0) + c1 - c0], ps[:, 0:c1 - c0], AF.Silu), reads=[pkey], writes=["B0"])
            usrc = lambda k: u
            for d in range(2):
                wr, wrk = Wt[(d, 0)]
                wi, wik = Wt[(d, 1)]
                ad, adk = abuf[d]
                ibd, ibk = ibuf[d]

                def ev_r(ps, pkey, c0, c1, d=d, h=h):
                    P.op("act", lambda e: e.activation(F0[:, c0:c1], ps[:, 0:c1 - c0], AF.Sigmoid, bias=br[:, d, h:h + 1], scale=1.0),
                         reads=[pkey, "br"], writes=["F0"])

                def ev_i(ps, pkey, c0, c1, d=d, h=h, ibd=ibd, ibk=ibk):
                    P.op("act", lambda e: e.activation(ibd[:, c0:c1], ps[:, 0:c1 - c0], AF.Sigmoid, bias=bi[:, d, h:h + 1], scale=1.0),
                         reads=[pkey, "bi"], writes=[ibk])
                self.proj_tile(wr, 1, 128, usrc, ["B1"], wrk, ev_r)
                self.proj_tile(wi, 1, 128, usrc, ["B1"], wik, ev_i)
                if d == 1 and pend is not None:
                    nwxa, nwxak = pend["wxa"]
                    xa_pending = []
                    for bi_, (c0_, c1_) in enumerate(BLKS):
                        bank_ = 3 + bi_
                        for k_ in range(NKT):
                            self.mm(self.ps[bank_][:, 0:c1_ - c0_], nwxa[:, k_, :], self.nT[:, k_, c0_:c1_], k_ == 0, k_ == NKT - 1,
                                    [nwxak, ("nT", k_)], [f"ps{bank_}"])
                        xa_pending.append((self.ps[bank_], f"ps{bank_}", c0_, c1_))
                    self.proj_banks = [0, 1, 2]
                P.op("act", lambda e, d=d, h=h, ad=ad: e.activation(ad[:, 0:T], F0[:, :], AF.Exp, scale=coef[:, d, h:h + 1]), reads=["F0", "coef"], writes=[adk])
                P.op("act", lambda e, d=d, h=h: e.activation(F0[:, :], F0[:, :], AF.Exp, scale=coef2[:, d, h:h + 1]), reads=["F0", "coef2"], writes=["F0"])
                P.op("act", lambda e: e.activation(F0[:, :], F0[:, :], AF.Sqrt, bias=1.0, scale=-1.0), reads=["F0"], writes=["F0"])
                P.op("dve", lambda e, ibd=ibd: e.tensor_tensor(ibd[:, 0:T], ibd[:, 0:T], u[:, 0:T], ALU.mult), reads=[ibk, "B1"], writes=[ibk])
                P.op("dve", lambda e, ibd=ibd: e.tensor_tensor(ibd[:, 0:T], F0[:, :], ibd[:, 0:T], ALU.mult), reads=["F0", ibk], writes=[ibk])
                if d == 0:
                    P.op("dve", lambda e, ad=ad, ibd=ibd: e.tensor_tensor_scan(hf[:, 0:T], ad[:, 0:T], ibd[:, 0:T], 0.0, ALU.mult, ALU.add),
                         reads=[ibk, adk], writes=["B3"])
                    self.proj_tile(wga, 8, 128, nsrc, nkeys, wgak, ev_g)
                    if h + 1 < 8 and h % 4 != 3:
                        pend = lru_loads(h + 1)
                else:
                    P.op("dve", lambda e, ad=ad, ibd=ibd: e.tensor_tensor_scan(hr[:, 0:CTX][:, ::-1], ad[:, 0:CTX][:, ::-1], ibd[:, 0:CTX][:, ::-1], 0.0,
                                                                              ALU.mult, ALU.add), reads=[ibk, adk], writes=["hr"])
                    P.op("dve", lambda e, ad=ad, ibd=ibd: e.tensor_tensor_scan(hr[:, CTX:T][:, ::-1], ad[:, CTX:T][:, ::-1], ibd[:, CTX:T][:, ::-1], hr[:, 0:1],
                                                                              ALU.mult, ALU.add), reads=[ibk, adk, "hr"], writes=["hr"])
            P.op("dve", lambda e: e.tensor_tensor(hr, hr, hf[:, 0:T], ALU.add), reads=["hr", "B3"], writes=["hr"])
            P.op("dve", lambda e, h=h: e.tensor_tensor(self.ygrp[:, h % 4, 0:CTX], hr[:, 0:CTX], xap[:, 2:2 + CTX], ALU.mult),
                 reads=["hr", "B0"], writes=[("og", h % 4)])
            P.op("dve", lambda e, h=h: e.tensor_tensor(self.ygrp[:, h % 4, CTX:T], hr[:, CTX:T], xap[:, 262:262 + SEQ], ALU.mult),
                 reads=["hr", "B0"], writes=[("og", h % 4)])
            if h % 4 == 3:
                self.out_proj(l, Wout[(h // 4) * 512:(h // 4 + 1) * 512, :], BLKS)
        self.barrier()
        self.s5_phase(l)
        self.barrier()


    def s5_phase(self, l):
        P = self.P
        e_ = l // 2
        Win = self.W["ev_w_in"][e_]
        Wout = self.W["ev_w_out"][e_]
        W = self.W
        F0, F1, F2 = self.F
        g_re, g_im, ub, B3 = self.B
        LW = self.LW
        LWb = LW[:].bitcast(BF16)
        sm = self.small
        tht, rmag = LW[:, 0:32], LW[:, 32:64]
        M16, Mrow, nMrow = LW[:, 64:72], LW[:, 72:74], LW[:, 74:76]
        Braw = [LW[:, 80:144], LW[:, 144:208]]
        Bbar = [LW[:, 208:272], LW[:, 272:336]]
        CT = LW[:, 336:464].rearrange("p (j q h) -> p j q h", j=4, q=2)
        dsk, glub = LW[:, 464:468], LW[:, 468:472]
        Eexp = LW[0:32, 472:600]
        Fc = LW[0:32, 600:856]
        lB = lambda j, q: LWb[:, 1712 + (j * 2 + q) * 128:1712 + (j * 2 + q + 1) * 128]
        lC = lambda j, v: LWb[:, 2736 + (j * 3 + v) * 128:2736 + (j * 3 + v + 1) * 128]
        Dd = LWb[:, 4272:4400]
        iota48 = sm[:, 208:256]
        hpi = sm[:, 87:88]
        tA0, tA1, tA2 = self.tmpA
        ld = lambda dst, src, key: P.dma(dst, src, writes=[key], allow_slow_non_contiguous=True)
        dve = lambda fn, r, w: P.op("dve", fn, reads=r, writes=w)
        act = lambda fn, r, w: P.op("act", fn, reads=r, writes=w)
        pool = lambda fn, r, w: P.op("pool", fn, reads=r, writes=w)
        pool(lambda e: e.iota(iota48, [[1, 48]], base=0, channel_multiplier=0, allow_small_or_imprecise_dtypes=True), [], ["iota48"])
        dve(lambda e: e.memset(hpi, math.pi / 2), [], ["hpi"])
        dve(lambda e: e.tensor_reduce(M16, self.identf[:].rearrange("p (c h) -> p c h", h=16), AX.X, ALU.add), ["identf"], ["M16"])
        dve(lambda e: e.tensor_reduce(Mrow, self.identf[:].rearrange("p (c h) -> p c h", h=64), AX.X, ALU.add), ["identf"], ["Mrow"])
        dve(lambda e: e.tensor_scalar_mul(nMrow, Mrow, -1.0), ["Mrow"], ["nMrow"])
        pool(lambda e: e.memset(LWb[:, 2736:4272], 0.0), [], ["lC"])
        ld(dsk, W["s5_d"][e_].rearrange("(t g) h -> (g h) t", g=8), "dsk")
        ld(glub, W["s5_glu_b"][e_].rearrange("(t p) -> p t", p=128), "glub")
        for i, nm in enumerate(["s5_lam_re", "s5_lam_im", "s5_log_dt"]):
            ld(tA0[:, i * 32:(i + 1) * 32], W[nm][e_].rearrange("d (gp gl) p -> (gl p) (d gp)", gl=2), "tA0")
        act(lambda e: e.activation(tA0[:, 64:96], tA0[:, 64:96], AF.Exp), ["tA0"], ["tA0"])
        dve(lambda e: e.tensor_tensor(tht, tA0[:, 32:64], tA0[:, 64:96], ALU.mult), ["tA0"], ["tht"])
        dve(lambda e: e.tensor_scalar_mul(tht, tht, 1.0 / TWO_PI), ["tht"], ["tht"])
        dve(lambda e: e.tensor_tensor(rmag, tA0[:, 0:32], tA0[:, 64:96], ALU.mult), ["tA0"], ["rmag"])
        act(lambda e: e.activation(rmag, rmag, AF.Exp), ["rmag"], ["rmag"])
        c = lambda i: F0[0:32, i * 128:(i + 1) * 128]
        ci = lambda i: F0[0:32, i * 128:(i + 1) * 128].bitcast(I32)
        for i, nm in enumerate(["s5_lam_re", "s5_lam_im", "s5_log_dt"]):
            ld(c(i).rearrange("g (d p) -> g d p", d=2), W[nm][e_].rearrange("d g p -> g d p"), "F0")
        K0 = ["F0"]
        act(lambda e: e.activation(c(2), c(2), AF.Exp), K0, K0)
        dve(lambda e: e.tensor_tensor(c(3), c(0), c(2), ALU.mult), K0, K0)
        dve(lambda e: e.tensor_tensor(c(4), c(1), c(2), ALU.mult), K0, K0)
        dve(lambda e: e.tensor_scalar_mul(c(4), c(4), 1.0 / TWO_PI), K0, K0)
        dve(lambda e: e.tensor_scalar_mul(c(10), c(4), 0.5), K0, K0)
        act(lambda e: e.activation(c(5), c(3), AF.Exp), K0, K0)
        act(lambda e: e.activation(c(6), c(3), AF.Tanh, scale=0.5), K0, K0)
        dve(lambda e: e.scalar_tensor_tensor(c(6), c(5), 1.0, c(6), ALU.add, ALU.mult), K0, K0)
        dve(lambda e: e.tensor_copy(ci(7), c(4)), K0, K0)
        dve(lambda e: e.tensor_tensor(c(4), c(4), ci(7), ALU.subtract), K0, K0)
        act(lambda e: e.activation(c(8), c(4), AF.Sin, scale=TWO_PI), K0, K0)
        dve(lambda e: e.scalar_tensor_tensor(c(9), c(4), -1.0, c(4), ALU.mult, ALU.max), K0, K0)
        act(lambda e: e.activation(c(9), c(9), AF.Sin, bias=hpi[0:32, :], scale=-TWO_PI), K0 + ["hpi"], K0)
        dve(lambda e: e.tensor_copy(ci(7), c(10)), K0, K0)
        dve(lambda e: e.tensor_tensor(c(10), c(10), ci(7), ALU.subtract), K0, K0)
        act(lambda e: e.activation(c(10), c(10), AF.Sin, scale=TWO_PI), K0, K0)
        dve(lambda e: e.tensor_tensor(c(10), c(10), c(10), ALU.mult), K0, K0)
        dve(lambda e: e.tensor_tensor(c(11), c(6), c(9), ALU.mult), K0, K0)
        dve(lambda e: e.scalar_tensor_tensor(c(11), c(10), -2.0, c(11), ALU.mult, ALU.add), K0, K0)
        dve(lambda e: e.tensor_tensor(c(12), c(5), c(8), ALU.mult), K0, K0)
        dve(lambda e: e.tensor_tensor(c(13), c(0), c(0), ALU.mult), K0, K0)
        dve(lambda e: e.tensor_tensor(c(14), c(1), c(1), ALU.mult), K0, K0)
        dve(lambda e: e.tensor_tensor(c(13), c(13), c(14), ALU.add), K0, K0)
        dve(lambda e: e.reciprocal(c(13), c(13)), K0, K0)
        dve(lambda e: e.tensor_tensor(c(14), c(11), c(0), ALU.mult), K0, K0)
        dve(lambda e: e.tensor_tensor(c(15), c(12), c(1), ALU.mult), K0, K0)
        dve(lambda e: e.tensor_tensor(c(14), c(14), c(15), ALU.add), K0, K0)
        dve(lambda e: e.tensor_tensor(Fc[:, 0:128], c(14), c(13), ALU.mult), K0, ["Fc"])
        dve(lambda e: e.tensor_tensor(c(14), c(12), c(0), ALU.mult), K0, K0)
        dve(lambda e: e.tensor_tensor(c(15), c(11), c(1), ALU.mult), K0, K0)
        dve(lambda e: e.tensor_tensor(c(14), c(14), c(15), ALU.subtract), K0, K0)
        dve(lambda e: e.tensor_tensor(Fc[:, 128:256], c(14), c(13), ALU.mult), K0, ["Fc"])
        nsrc = lambda k: self.nT[:, k, :]
        nkeys = [("nT", k) for k in range(NKT)]

        def tv(X, d, c0, c1):
            if d == 0:
                return X[:, c0:c1]
            if c0 < CTX:
                return X[:, 0:CTX][:, ::-1]
            return X[:, 2560 - c1:2560 - c0][:, ::-1]

        for ti in range(4):
            self.proj_banks = [7]
            wub, wubk = self.load_cast(Win[:, 2048 + ti * 128:2048 + (ti + 1) * 128], 8, 128)

            def ev_u(ps, pkey, c0, c1):
                act(lambda e: e.activation(ub[:, c0:c1], ps[:, 0:c1 - c0], AF.Copy), [pkey], ["B2"])
            self.proj_tile(wub, 8, 128, nsrc, nkeys, wubk, ev_u)
            pool(lambda e, ti=ti: e.tensor_copy(Eexp.rearrange("g (c h) -> g c h", h=16),
                                                self.identf[0:32, 8 * ti:8 * ti + 8].unsqueeze(2).to_broadcast([32, 8, 16])), ["identf"], ["Eexp"])
            pool(lambda e, ti=ti: e.tensor_scalar_mul(Dd, self.identb[:], dsk[:, ti:ti + 1]), ["identb", "dsk"], ["Dd"])
            for d in range(2):
                self.mm(self.ps[7][:, 0:256], Eexp, Fc, True, True, ["Eexp", "Fc"], ["ps7"])
                for q, nm in enumerate(["s5_b_re", "s5_b_im"]):
                    for g8 in range(8):
                        ld(Braw[q][16 * g8:16 * g8 + 16, :], W[nm][e_, d, 8 * ti + g8].rearrange("p h -> h p"), f"Braw{q}")
                for q, nm in enumerate(["s5_c_re", "s5_c_im"]):
                    for gl in range(2):
                        for j in range(4):
                            ld(CT[64 * gl:64 * gl + 64, j, q, :], W[nm][e_, d, 8 * ti + 2 * j + gl].rearrange("h p -> p h"), "CT")
                Fre = self.ps[7][:, d * 64:(d + 1) * 64]
                Fim = self.ps[7][:, 128 + d * 64:128 + (d + 1) * 64]
                t0_, t1_ = tA0[:, 0:64], tA0[:, 64:128]
                dve(lambda e, Fre=Fre: e.tensor_tensor(t0_, Fre, Braw[0], ALU.mult), ["ps7", "Braw0"], ["tA0", "prod0"])
                dve(lambda e, Fim=Fim: e.tensor_tensor(t1_, Fim, Braw[1], ALU.mult), ["ps7", "Braw1"], ["tA0", "prod0"])
                dve(lambda e: e.tensor_tensor(Bbar[0], t0_, t1_, ALU.subtract), ["tA0"], ["Bbar0"])
                dve(lambda e, Fre=Fre: e.tensor_tensor(t0_, Fre, Braw[1], ALU.mult), ["ps7", "Braw1"], ["tA0", "prod0"])
                dve(lambda e, Fim=Fim: e.tensor_tensor(t1_, Fim, Braw[0], ALU.mult), ["ps7", "Braw0"], ["tA0", "prod0"])
                dve(lambda e: e.tensor_tensor(Bbar[1], t0_, t1_, ALU.add), ["tA0"], ["Bbar1"])
                if ti == 0 and d == DBG_D and self.debug:
                    self.dump(3, Fc, "Fc")
                    dve(lambda e: e.tensor_copy(tA1[:, 0:256], self.ps[7][:, 0:256]), ["ps7"], ["tA1"])
                    self.dump(6, tA1[:, 0:256], "tA1")
                    self.dump(7, Braw[0], "Braw0")
                for j in range(4):
                    for q in range(2):
                        for gl in range(2):
                            act(lambda e, j=j, q=q, gl=gl: e.activation(
                                lB(j, q)[:, 64 * gl:64 * gl + 64], Bbar[q], AF.Copy, scale=M16[:, 2 * j + gl:2 * j + gl + 1]),
                                [f"Bbar{q}", "M16"], [("lB", j)])
                    for gl in range(2):
                        cs = slice(32 * j + 16 * gl, 32 * j + 16 * gl + 16)
                        act(lambda e, j=j, gl=gl, cs=cs: e.activation(lC(j, 0)[:, cs], CT[:, j, 0, :], AF.Copy, scale=Mrow[:, gl:gl + 1]), ["CT", "Mrow"], [("lC", j)])
                        act(lambda e, j=j, gl=gl, cs=cs: e.activation(lC(j, 1)[:, cs], CT[:, j, 0, :], AF.Copy, scale=nMrow[:, gl:gl + 1]), ["CT", "nMrow"], [("lC", j)])
                        act(lambda e, j=j, gl=gl, cs=cs: e.activation(lC(j, 2)[:, cs], CT[:, j, 1, :], AF.Copy, scale=nMrow[:, gl:gl + 1]), ["CT", "nMrow"], [("lC", j)])
                for j in range(4):
                    uidx = (ti * 2 + d) * 4 + j
                    if uidx == 0:
                        for st in self.s5_table_steps(0, 0, 0, 0, tht):
                            st()
                    ins = []
                    if uidx + 1 < 32:
                        n_ = uidx + 1
                        ins = self.s5_table_steps(n_ // 8, (n_ // 4) % 2, n_ % 4, n_, tht)
                    self.s5_unit_compute(ti, d, j, uidx, lB, lC, rmag, ins)
            for bi, (c0, c1) in enumerate(BLKS):
                w_ = c1 - c0
                y = self.ps[bi]
                yk = f"ps{bi}"
                self.mm(y[:, 0:w_], Dd, ub[:, c0:c1], False, True, ["Dd", "B2"], [yk])
                tg, tgk = (tA0, "tA0") if bi % 2 == 0 else (tA1, "tA1")
                act(lambda e, y=y, w_=w_, tg=tg: e.activation(tg[:, 0:w_], y[:, 0:w_], AF.Square), [yk], [tgk])
                dve(lambda e, w_=w_, tg=tg: e.tensor_scalar(tg[:, 0:w_], tg[:, 0:w_], 0.044715, 1.0, ALU.mult, ALU.add), [tgk], [tgk])
                dve(lambda e, y=y, w_=w_, tg=tg: e.tensor_tensor(tg[:, 0:w_], tg[:, 0:w_], y[:, 0:w_], ALU.mult), [tgk, yk], [tgk])
                act(lambda e, w_=w_, tg=tg: e.activation(tg[:, 0:w_], tg[:, 0:w_], AF.Sigmoid, scale=2.0 * math.sqrt(2.0 / math.pi)), [tgk], [tgk])
                dve(lambda e, y=y, w_=w_, ti=ti, c0=c0, c1=c1, tg=tg: e.tensor_tensor(self.ygrp[:, ti, c0:c1], y[:, 0:w_], tg[:, 0:w_], ALU.mult),
                    [yk, tgk], [("og", ti)])
        self.proj_banks = [4, 5, 6]
        ysrc = lambda k: self.ygrp[:, k, :]
        ykeys = [("og", k) for k in range(4)]
        for ot in range(4):
            wg, wgk = self.load_cast(W["s5_glu_w"][e_][:, ot * 128:(ot + 1) * 128], 4, 128)

            def ev_z(ps, pkey, c0, c1, ot=ot):
                act(lambda e: e.activation(self.B[ot][:, c0:c1], ps[:, 0:c1 - c0], AF.Sigmoid, bias=glub[:, ot:ot + 1], scale=1.0),
                    [pkey, "glub"], [f"B{ot}"])
            self.proj_tile(wg, 4, 128, ysrc, ykeys, wgk, ev_z)
        for ot in range(4):
            wgb, wgbk = self.load_cast(Win[:, 2560 + ot * 128:2560 + (ot + 1) * 128], 8, 128)

            def ev_gb(ps, pkey, c0, c1, ot=ot):
                sg = self.tmpB[self.ps_rr % 2]
                sk = f"tmpB{self.ps_rr % 2}"
                act(lambda e: e.activation(sg[:, 0:c1 - c0], ps[:, 0:c1 - c0], AF.Silu), [pkey], [sk])
                pool(lambda e: e.tensor_tensor(sg[:, 0:c1 - c0], sg[:, 0:c1 - c0], self.B[ot][:, c0:c1], ALU.mult), [sk, f"B{ot}"], [sk])
                pool(lambda e: e.tensor_tensor(self.ygrp[:, ot, c0:c1], self.ygrp[:, ot, c0:c1], sg[:, 0:c1 - c0], ALU.mult),
                     [sk, ("og", ot)], [("og", ot)])
            self.proj_tile(wgb, 8, 128, nsrc, nkeys, wgbk, ev_gb)
        self.out_proj(l, Wout[1024:1536, :], BLKS)


    def s5_views(self):
        F0b = self.F[0][:].bitcast(BF16)
        F1b = self.F[1][:].bitcast(BF16)
        F2b = self.F[2][:].bitcast(BF16)
        tabs = [(F0b[:, 0:T], F0b[:, T:2 * T]), (F1b[:, 0:T], F1b[:, T:2 * T])]
        gin = (F2b[:, 0:T], F2b[:, T:2 * T])
        B3f = self.B[3][:, 0:2304].bitcast(F32)
        tAb = [self.tmpA[0][:].bitcast(BF16), self.tmpA[1][:].bitcast(BF16)]
        prod = [tAb[0][:, 0:512], tAb[0][:, 512:1024], tAb[1][:, 0:512], tAb[1][:, 512:1024]]
        return tabs, gin, B3f, prod

    def s5_table_steps(self, ti, d, j, uidx, tht):
        P = self.P
        tabs, gin, B3f, prod = self.s5_views()
        cosT, sinT = tabs[uidx % 2]
        tk = f"tab{uidx % 2}"
        sm = self.small
        iota48 = sm[:, 208:256]
        hpi = sm[:, 87:88]
        tA2 = self.tmpA[2]
        idx0 = d * 16 + 4 * ti
        th4 = tht[:, idx0:idx0 + 4]
        Ap4 = tA2[:, 0:192]
        Bp4 = tA2[:, 192:384]
        t48_4 = tA2[:, 384:388]
        kk1_4 = tA2[:, 388:392].bitcast(I32)
        kk = tA2[:, 392:488].bitcast(I32)
        Ap, Bp = Ap4[:, j * 48:(j + 1) * 48], Bp4[:, j * 48:(j + 1) * 48]
        KA = ["tA2"]
        dve = lambda fn, r, w: P.op("dve", fn, reads=r, writes=w)
        sxs = [B3f[:, 0:384], B3f[:, 384:768]]
        sy = B3f[:, 768:1152].bitcast(I32)
        steps = []

        def tiny():
            io4 = iota48.unsqueeze(1).to_broadcast([128, 4, 48])
            dve(lambda e: e.tensor_scalar_mul(t48_4, th4, 48.0), ["tht"], KA)
            dve(lambda e: e.tensor_copy(kk1_4, t48_4), KA, KA)
            dve(lambda e: e.tensor_tensor(t48_4, t48_4, kk1_4, ALU.subtract), KA, KA)
            dve(lambda e: e.tensor_tensor(Ap4.rearrange("p (u i) -> p u i", u=4), io4,
                                          t48_4.unsqueeze(2).to_broadcast([128, 4, 48]), ALU.mult), KA + ["iota48"], KA)
            dve(lambda e: e.tensor_tensor(Bp4.rearrange("p (u i) -> p u i", u=4), io4,
                                          th4.unsqueeze(2).to_broadcast([128, 4, 48]), ALU.mult), ["tht", "iota48"] + KA, KA)
            for X in (Ap4, Bp4):
                for hh in range(2):
                    xs = X[:, hh * 96:(hh + 1) * 96]
                    dve(lambda e, xs=xs: e.tensor_copy(kk, xs), KA, KA)
                    dve(lambda e, xs=xs: e.tensor_tensor(xs, xs, kk, ALU.subtract), KA, KA)
        if j == 0:
            steps.append(tiny)
        for k in range(6):
            sx = sxs[k % 2]
            sk = f"sx{k % 2}"
            c0 = 384 * k

            def stepA(k=k, sx=sx, sk=sk):
                sxv = sx.rearrange("p (i j) -> p i j", j=48)
                P.op("dve", lambda e: e.tensor_tensor(sxv, Ap[:, 8 * k:8 * k + 8].unsqueeze(2).to_broadcast([128, 8, 48]),
                                                     Bp.unsqueeze(1).to_broadcast([128, 8, 48]), ALU.add), reads=KA, writes=[sk])
                dve(lambda e: e.tensor_copy(sy, sx), [sk], ["sy"])
                P.op("dve", lambda e: e.tensor_tensor(sx, sx, sy, ALU.subtract), reads=[sk, "sy"], writes=[sk])

            def stepB(sx=sx, sk=sk, c0=c0):
                P.op("act", lambda e: e.activation(sinT[:, c0:c0 + 384], sx, AF.Sin, scale=TWO_PI), reads=[sk], writes=[tk])

            def stepC(sx=sx, sk=sk, c0=c0):
                P.op("act", lambda e: e.activation(sx, sx, AF.Sin, scale=math.pi), reads=[sk], writes=[sk])
                P.op("act", lambda e: e.activation(sx, sx, AF.Square), reads=[sk], writes=[sk])
                P.op("act", lambda e: e.activation(cosT[:, c0:c0 + 384], sx, AF.Identity, bias=1.0, scale=-2.0), reads=[sk], writes=[tk])
            steps += [stepA, stepB, stepC]
        return steps

    def s5_unit_compute(self, ti, d, j, uidx, lB, lC, rmag, inserts):
        P = self.P
        tabs, gin, B3f, prod = self.s5_views()
        cosT, sinT = tabs[uidx % 2]
        tk = f"tab{uidx % 2}"
        g_re, g_im, ub, _ = self.B
        idx = d * 16 + 4 * ti + j
        rm = rmag[:, idx:idx + 1]
        bre, bim, tA, tB = self.tmpB
        dve = lambda fn, r, w: P.op("dve", fn, reads=r, writes=w)
        nslots = 12
        total = len(inserts)
        state = {"slot": 0, "done": 0}

        def slot_end():
            state["slot"] += 1
            target = (state["slot"] * total + nslots - 1) // nslots
            while state["done"] < min(target, total):
                inserts[state["done"]]()
                state["done"] += 1

        def tcols(k):
            s0, s1 = BLKS[k]
            if d == 0:
                return s0, s1, k, False
            if k == 0:
                return 0, CTX, 0, True
            return 2560 - s1, 2560 - s0, 5 - k, True

        for k, (c0, c1) in enumerate(BLKS):
            w_ = c1 - c0
            t0, t1, bt, rev = tcols(k)
            ubv = ub[:, t0:t1][:, ::-1] if rev else ub[:, t0:t1]
            br_, bi_ = 5 + (2 * k) % 3, 5 + (2 * k + 1) % 3
            if k % 2 == 0:
                cre, cim, kre, kim = bre, bim, "tmpB0", "tmpB1"
            else:
                cre, cim, kre, kim = prod[0], prod[1], "prod0", "prod1"
            self.mm(self.ps[br_][:, 0:w_], lB(j, 0), ubv, True, True, [("lB", j), "B2"], [f"ps{br_}"])
            self.mm(self.ps[bi_][:, 0:w_], lB(j, 1), ubv, True, True, [("lB", j), "B2"], [f"ps{bi_}"])
            P.op("act", lambda e, w_=w_, cre=cre, br_=br_: e.activation(cre[:, 0:w_], self.ps[br_][:, 0:w_], AF.Copy), reads=[f"ps{br_}"], writes=[kre])
            P.op("act", lambda e, w_=w_, cim=cim, bi_=bi_: e.activation(cim[:, 0:w_], self.ps[bi_][:, 0:w_], AF.Copy), reads=[f"ps{bi_}"], writes=[kim])
            cv, sv = cosT[:, c0:c1], sinT[:, c0:c1]
            tC, tD = prod[2], prod[3]
            dve(lambda e, w_=w_, cv=cv, cre=cre: e.tensor_tensor(tA[:, 0:w_], cre[:, 0:w_], cv, ALU.mult), [kre, tk], ["tmpB2"])
            dve(lambda e, w_=w_, sv=sv, cim=cim: e.tensor_tensor(tB[:, 0:w_], cim[:, 0:w_], sv, ALU.mult), [kim, tk], ["tmpB3"])
            dve(lambda e, w_=w_, cv=cv, cim=cim: e.tensor_tensor(tC[:, 0:w_], cim[:, 0:w_], cv, ALU.mult), [kim, tk], ["prod2"])
            dve(lambda e, w_=w_, sv=sv, cre=cre: e.tensor_tensor(tD[:, 0:w_], cre[:, 0:w_], sv, ALU.mult), [kre, tk], ["prod3"])
            dve(lambda e, w_=w_, c0=c0, c1=c1: e.tensor_tensor(gin[0][:, c0:c1], tA[:, 0:w_], tB[:, 0:w_], ALU.add), ["tmpB2", "tmpB3"], ["ginr"])
            dve(lambda e, w_=w_, c0=c0, c1=c1: e.tensor_tensor(gin[1][:, c0:c1], tC[:, 0:w_], tD[:, 0:w_], ALU.subtract), ["prod2", "prod3"], ["gini"])
            slot_end()
        for part, (dst, dk, gk) in enumerate([(g_re, "B0", "ginr"), (g_im, "B1", "gini")]):
            src = gin[part]
            dve(lambda e, dst=dst, src=src: e.tensor_tensor_scan(dst[:, 0:T], rm.to_broadcast([128, T]), src, 0.0, ALU.mult, ALU.add),
                [gk, "rmag"], [dk])
            slot_end()
        for k, (c0, c1) in enumerate(BLKS):
            w_ = c1 - c0
            t0, t1, bt, rev = tcols(k)
            cv, sv = cosT[:, c0:c1], sinT[:, c0:c1]
            plist = [(cv, g_re, "B0", 0), (sv, g_im, "B1", 1), (sv, g_re, "B0", 2), (cv, g_im, "B1", 2)]
            for pi, (tab, gg, gk, var) in enumerate(plist):
                pt = prod[pi]
                pk = f"prod{pi}"
                dve(lambda e, pt=pt, tab=tab, gg=gg, c0=c0, c1=c1, w_=w_: e.tensor_tensor(pt[:, 0:w_], tab, gg[:, c0:c1], ALU.mult),
                    [tk, gk], [pk])
                first = (d == 0 and j == 0 and pi == 0)
                rhs = pt[:, 0:w_][:, ::-1] if rev else pt[:, 0:w_]
                self.mm(self.ps[bt][:, 0:w_], lC(j, var), rhs, first, False, [("lC", j), pk], [f"ps{bt}"])
            slot_end()
        while state["done"] < total:
            inserts[state["done"]]()
            state["done"] += 1

    def dump(self, slot, ap, key, bf=False):
        if not self.debug:
            return
        dst = (self.dbgb if bf else self.dbgf)[slot, 0:ap.shape[0], 0:ap.shape[1]]
        self.out_toks.append(self.P.dma(dst, ap, reads=[key]))

    def barrier(self):
        P = self.P
        toks = [(P.sem[e], P.count[e]) for e in ENGS if P.count[e] > 0]
        toks += [(s, 16 * c) for s, c in zip(P.dma_sems, P.dma_cnt) if c > 0]
        for e in ENGS:
            P._emit_waits(e, toks)
        P.lastw.clear()
        P.readers.clear()

    def rope_tables(self):
        P = self.P
        yf = self.ygrp[:].rearrange("p s t -> p (s t)").bitcast(F32)
        posr = yf[0:64, 0:2048]
        posc = yf[0:64, 2048:4096]
        sm = self.small
        pidx = sm[0:64, 80:81].bitcast(I32)
        p16 = sm[0:64, 81:82].bitcast(I32)
        pb16 = sm[0:64, 82:83].bitcast(I32)
        invt = sm[0:64, 83:84]
        mA = sm[0:64, 84:85]
        mB = sm[0:64, 85:86]
        halfpi = sm[0:64, 86:87]
        LWb = self.LW[:].bitcast(BF16)
        self.cosT = LWb[0:64, 0:2048]
        self.sinT = LWb[0:64, 2048:4096]
        K = ["ropescr"]
        P.op("pool", lambda e: e.iota(posr.rearrange("p (r c) -> p r c", c=64), [[1, 32], [0, 64]], base=0, channel_multiplier=0,
                                      allow_small_or_imprecise_dtypes=True), writes=["posr"])
        P.op("pool", lambda e: e.iota(posc.rearrange("p (r c) -> p r c", c=64), [[0, 32], [1, 64]], base=0, channel_multiplier=0,
                                      allow_small_or_imprecise_dtypes=True), writes=["posc"])
        P.op("pool", lambda e: e.iota(pidx, [[0, 1]], base=0, channel_multiplier=1), writes=["pidx"])
        P.op("dve", lambda e: e.tensor_single_scalar(p16, pidx, 15, ALU.bitwise_and), reads=["pidx"], writes=["p16"])
        P.op("dve", lambda e: e.tensor_single_scalar(pb16, pidx, 16, ALU.bitwise_and), reads=["pidx"], writes=["pb16"])
        P.op("dve", lambda e: e.tensor_copy(invt, p16), reads=["p16"], writes=["invt"])
        P.op("act", lambda e: e.activation(invt, invt, AF.Exp, scale=-math.log(10000.0) / 16.0), reads=["invt"], writes=["invt"])
        P.op("dve", lambda e: e.tensor_scalar_mul(invt, invt, 1.0 / TWO_PI), reads=["invt"], writes=["invt"])
        P.op("dve", lambda e: e.tensor_copy(mB, pb16), reads=["pb16"], writes=["mB"])
        P.op("dve", lambda e: e.tensor_scalar_mul(mB, mB, 1.0 / 16.0), reads=["mB"], writes=["mB"])
        P.op("dve", lambda e: e.tensor_scalar(mA, mB, -1.0, 1.0, ALU.mult, ALU.add), reads=["mB"], writes=["mA"])
        P.op("dve", lambda e: e.memset(halfpi, math.pi / 2), writes=["halfpi"])
        P.op("dve", lambda e: e.tensor_scalar_mul(posr, posr, mA), reads=["posr", "mA"], writes=["posr"])
        P.op("dve", lambda e: e.scalar_tensor_tensor(posr, posc, mB, posr, ALU.mult, ALU.add), reads=["posr", "posc", "mB"], writes=["posr"])
        P.op("dve", lambda e: e.tensor_scalar_mul(posr, posr, invt), reads=["posr", "invt"], writes=["posr"])
        pci = posc.bitcast(I32)
        P.op("dve", lambda e: e.tensor_copy(pci, posr), reads=["posr"], writes=["posc"])
        P.op("dve", lambda e: e.tensor_tensor(posr, posr, pci, ALU.subtract), reads=["posr", "posc"], writes=["posr"])
        P.op("act", lambda e: e.activation(self.sinT, posr, AF.Sin, scale=TWO_PI), reads=["posr"], writes=["sinT"])
        P.op("dve", lambda e: e.scalar_tensor_tensor(posr, posr, -1.0, posr, ALU.mult, ALU.max), reads=["posr"], writes=["posr"])
        P.op("act", lambda e: e.activation(self.cosT, posr, AF.Sin, bias=halfpi, scale=-TWO_PI), reads=["posr", "halfpi"], writes=["cosT"])
        self.Rm = LWb[0:64, 4096:4160]
        P.op("pool", lambda e: e.tensor_scalar_mul(self.Rm[:, 0:32], self.identb[0:64, 32:64], -1.0), reads=["identb"], writes=["Rm"])
        P.op("pool", lambda e: e.tensor_copy(self.Rm[:, 32:64], self.identb[0:64, 0:32]), reads=["identb"], writes=["Rm"])

    def rope_apply(self, buf, key, blocks):
        P = self.P
        for (c0, c1) in blocks:
            w = c1 - c0
            ps = self.ps[7]
            self.mm(ps[0:64, 0:w], self.Rm, buf[0:64, c0:c1], True, True, [key, "Rm"], ["ps7"])
            t1 = self.tmpA[0]
            t2 = self.tmpA[1]
            P.op("pool", lambda e, c0=c0, c1=c1, w=w: e.tensor_tensor(t1[0:64, 0:w], buf[0:64, c0:c1], self.cosT[:, c0 - CTX:c1 - CTX], ALU.mult),
                 reads=[key, "cosT"], writes=["tmpA0"])
            P.op("dve", lambda e, c0=c0, c1=c1, w=w: e.tensor_tensor(t2[0:64, 0:w], ps[0:64, 0:w], self.sinT[:, c0 - CTX:c1 - CTX], ALU.mult),
                 reads=["ps7", "sinT"], writes=["tmpA1"])
            P.op("dve", lambda e, c0=c0, c1=c1, w=w: e.tensor_tensor(buf[0:64, c0:c1], t1[0:64, 0:w], t2[0:64, 0:w], ALU.add),
                 reads=["tmpA0", "tmpA1"], writes=[key])

    def proj_tile(self, w, nk, m, src_fn, src_keys, wkey, evac, blks=None):
        for bi, (c0, c1) in enumerate(blks or BLKS):
            bank = self.proj_banks[self.ps_rr % len(self.proj_banks)]
            self.ps_rr += 1
            ps = self.ps[bank]
            for k in range(nk):
                self.mm(ps[0:m, 0:c1 - c0], w[:, k, :], src_fn(k)[:, c0:c1], k == 0, k == nk - 1,
                        [wkey, src_keys[k]], [f"ps{bank}"])
            evac(ps, f"ps{bank}", c0, c1)

    def mla_layer(self, l):
        P = self.P
        o = l // 2
        with_ctx = l < DEPTH - 1
        Win = self.W["mla_w_in"][o]
        Wuq = self.W["mla_w_uq"][o]
        Wukv = self.W["mla_w_ukv"][o]
        Wout = self.W["mla_w_out"][o]
        self.ps_rr = 0
        self.proj_banks = [0, 1, 2, 3, 4, 5, 6]
        self.rope_tables()
        F0b = self.F[0][:].bitcast(BF16)
        F1b = self.F[1][:].bitcast(BF16)
        F2b = self.F[2][:].bitcast(BF16)
        cqn = [F0b[:, 0:T], F0b[:, T:2 * T], F1b[:, 0:T]]
        vh = F1b[:, T:2 * T].rearrange("p (t d) -> p t d", d=128)
        ckvn = [F2b[:, 0:T], F2b[:, T:2 * T]]
        kr, qn, qr, kn = self.B[0], self.B[1], self.B[2], self.B[3]
        yf = self.ygrp[:].rearrange("p s t -> p (s t)").bitcast(F32)
        nsrc = lambda k: self.nT[:, k, :]
        nkeys = [("nT", k) for k in range(NKT)]
        sm = self.small
        gq = sm[:, 96:99]
        gkv = sm[:, 99:101]
        P.dma(gq, self.W["mla_q_norm"][o].rearrange("(k p) -> p k", p=128), writes=["gq"], allow_slow_non_contiguous=True)
        P.dma(gkv, self.W["mla_kv_norm"][o].rearrange("(k p) -> p k", p=128), writes=["gkv"], allow_slow_non_contiguous=True)

        def copy_evac(dst, dkey, m=128, scale=None, eng="act"):
            def f(ps, pkey, c0, c1):
                if eng == "act":
                    if scale is None:
                        P.op("act", lambda e: e.activation(dst[0:m, c0:c1], ps[0:m, 0:c1 - c0], AF.Copy), reads=[pkey], writes=[dkey])
                    else:
                        P.op("act", lambda e: e.activation(dst[0:m, c0:c1], ps[0:m, 0:c1 - c0], AF.Copy, scale=scale), reads=[pkey], writes=[dkey])
                else:
                    P.op("dve", lambda e: e.tensor_copy(dst[0:m, c0:c1], ps[0:m, 0:c1 - c0]), reads=[pkey], writes=[dkey])
            return f

        for k in range(3):
            w, wk = self.load_cast(Win[:, k * 128:(k + 1) * 128], 8, 128)
            self.proj_tile(w, 8, 128, nsrc, nkeys, wk, copy_evac(cqn[k], f"cqn{k}", eng="act" if k % 2 == 0 else "dve"))
        for k in range(2):
            w, wk = self.load_cast(Win[:, 384 + k * 128:384 + (k + 1) * 128], 8, 128)
            self.proj_tile(w, 8, 128, nsrc, nkeys, wk, copy_evac(ckvn[k], f"ckvn{k}", eng="dve" if k % 2 == 0 else "act"))
        w, wk = self.load_cast(Win[:, 640:704], 8, 64)
        self.proj_tile(w, 8, 64, nsrc, nkeys, wk, copy_evac(kr, "B0", m=64))
        rq = yf[:, 0:T]
        rkv = yf[:, T:2 * T]
        self.rstd_into(rq, "rq", cqn, [f"cqn{k}" for k in range(3)], 384)
        self.rstd_into(rkv, "rkv", ckvn, [f"ckvn{k}" for k in range(2)], 256)
        for k in range(3):
            P.op("dve", lambda e, k=k: e.scalar_tensor_tensor(cqn[k], cqn[k], gq[:, k:k + 1], rq, ALU.mult, ALU.mult),
                 reads=[f"cqn{k}", "gq", "rq"], writes=[f"cqn{k}"])
        for k in range(2):
            P.op("dve", lambda e, k=k: e.scalar_tensor_tensor(ckvn[k], ckvn[k], gkv[:, k:k + 1], rkv, ALU.mult, ALU.mult),
                 reads=[f"ckvn{k}", "gkv", "rkv"], writes=[f"ckvn{k}"])
        self.rope_apply(kr, "B0", BLKS[1:])
        P.op("pool", lambda e: e.memset(kr[64:128, 0:T], 0.0), writes=["B0"])
        P.op("pool", lambda e: e.memset(qr[64:128, 0:T], 0.0), writes=["B2"])
        qblks = BLKS if with_ctx else BLKS[1:]
        cq_src = lambda k: cqn[k]
        cq_keys = [f"cqn{k}" for k in range(3)]
        kv_src = lambda k: ckvn[k]
        kv_keys = [f"ckvn{k}" for k in range(2)]
        self.proj_banks = [4, 5, 6]

        def head_loads(h, defer):
            r = [self.load_cast(Wuq[:, h * 192:(h + 1) * 192], 3, 192, defer=defer),
                 self.load_cast(Wukv[:, h * 256:(h + 1) * 256], 2, 256, defer=defer)]
            if not defer:
                r.append(self.load_cast(Win[:, 704 + h * 128:704 + (h + 1) * 128], 8, 128))
            return r
        pend = None
        for h in range(8):
            if pend is None:
                pend = head_loads(h, False)
            (wq, wqk), (wkv, wkvk), (wg, wgk) = [p[0:2] for p in pend]
            pend = None
            self.proj_tile(wq[:, :, 0:128], 3, 128, cq_src, cq_keys, wqk, copy_evac(qn, "B1", scale=MLA_SCALE))
            self.proj_tile(wq[:, :, 128:192], 3, 64, cq_src, cq_keys, wqk, copy_evac(qr, "B2", m=64, scale=MLA_SCALE))
            self.rope_apply(qr, "B2", BLKS[1:])
            self.proj_tile(wkv[:, :, 0:128], 2, 128, kv_src, kv_keys, wkvk, copy_evac(kn, "B3", eng="dve"))
            for t0 in range(0, 18, 4):
                nt = min(4, 18 - t0)
                bank = 4 + (self.ps_rr % 3)
                self.ps_rr += 1
                ps = self.ps[bank]
                for j in range(nt):
                    tt = t0 + j
                    for rk in range(2):
                        self.mm(ps[:, j * 128:(j + 1) * 128], ckvn[rk][:, tt * 128:(tt + 1) * 128], wkv[:, rk, 128:256],
                                rk == 0, rk == 1, [f"ckvn{rk}", wkvk], [f"ps{bank}"])
                P.op("dve", lambda e, ps=ps, t0=t0, nt=nt: e.tensor_copy(
                    vh[:, t0:t0 + nt, :], ps[:, 0:nt * 128].rearrange("p (t d) -> p t d", d=128)),
                    reads=[f"ps{bank}"], writes=["vh"])
            def ev_gate(ps, pkey, c0, c1, h=h):
                P.op("act", lambda e: e.activation(self.ygrp[:, h % 4, c0:c1], ps[:, 0:c1 - c0], AF.Silu), reads=[pkey],
                     writes=[("og", h % 4), "rq", "rkv"])
            self.proj_tile(wg, 8, 128, nsrc, nkeys, wgk, ev_gate, blks=qblks)
            if h + 1 < 8 and h % 4 != 3:
                pend = head_loads(h + 1, True)
            LOOK = 2
            for qi, (c0, c1) in enumerate(qblks):
                if qi == 1 and pend is not None:
                    for p in pend:
                        p[2]("act")
                    d3, k3, c3 = self.load_cast(Win[:, 704 + (h + 1) * 128:704 + (h + 2) * 128], 8, 128, defer=True)
                    c3("act")
                    pend.append((d3, k3))
                w_ = c1 - c0
                nkt = 2 if c0 == 0 else 18
                Ops = self.ps[qi % 2]
                Dps = self.ps[2 + qi % 2]
                okey, dkey = f"ps{qi % 2}", f"ps{2 + qi % 2}"
                Ebuf = {}

                def emit_S(kt, w_=w_, c0=c0, c1=c1):
                    sb = 4 + (self.ps_rr % 3)
                    self.ps_rr += 1
                    S = self.ps[sb]
                    ks = slice(kt * 128, (kt + 1) * 128)
                    self.mm(S[:, 0:w_], kn[:, ks], qn[:, c0:c1], True, False, ["B3", "B1"], [f"ps{sb}"])
                    self.mm(S[:, 0:w_], kr[:, ks], qr[:, c0:c1], False, True, ["B0", "B2"], [f"ps{sb}"])
                    ei = kt % 3
                    E = self.tmpB[ei]
                    P.op("act", lambda e, E=E, S=S, w_=w_: e.activation(E[:, 0:w_], S[:, 0:w_], AF.Exp),
                         reads=[f"ps{sb}"], writes=[f"tmpB{ei}"])
                    Ebuf[kt] = (E, f"tmpB{ei}")

                acc = self.tmpA[2]

                def emit_OD(kt, w_=w_, nkt=nkt, Ops=Ops, Dps=Dps, okey=okey, dkey=dkey):
                    E, ek = Ebuf[kt]
                    self.mm(Ops[:, 0:w_], vh[:, kt, :], E[:, 0:w_], kt == 0, kt == nkt - 1, ["vh", ek], [okey])
                    if kt % 2 == 1:
                        self.mm(Dps[:, 0:w_], self.onesb[:], E[:, 0:w_], kt == 1, False, ["onesb", ek], [dkey])
                    elif kt == 0:
                        P.op("dve", lambda e: e.tensor_copy(acc[:, 0:w_], E[:, 0:w_]), reads=[ek], writes=["tmpA2"])
                    else:
                        P.op("dve", lambda e: e.tensor_tensor(acc[:, 0:w_], acc[:, 0:w_], E[:, 0:w_], ALU.add), reads=[ek, "tmpA2"], writes=["tmpA2"])
                    if kt == nkt - 1:
                        accb = self.tmpB[3]
                        P.op("dve", lambda e: e.tensor_copy(accb[:, 0:w_], acc[:, 0:w_]), reads=["tmpA2"], writes=["tmpB3"])
                        self.mm(Dps[:, 0:w_], self.onesb[:], accb[:, 0:w_], False, True, ["onesb", "tmpB3"], [dkey])
                for kt in range(min(LOOK, nkt)):
                    emit_S(kt)
                for kt in range(nkt):
                    if kt + LOOK < nkt:
                        emit_S(kt + LOOK)
                    emit_OD(kt)
                rec = self.tmpA[0]
                o1 = self.tmpA[1]
                P.op("dve", lambda e, Dps=Dps, w_=w_: e.reciprocal(rec[:, 0:w_], Dps[:, 0:w_]), reads=[dkey], writes=["tmpA0"])
                P.op("dve", lambda e, Ops=Ops, w_=w_: e.tensor_tensor(o1[:, 0:w_], Ops[:, 0:w_], rec[:, 0:w_], ALU.mult),
                     reads=[okey, "tmpA0"], writes=["tmpA1"])
                P.op("pool", lambda e, h=h, c0=c0, c1=c1, w_=w_: e.tensor_tensor(self.ygrp[:, h % 4, c0:c1], o1[:, 0:w_], self.ygrp[:, h % 4, c0:c1], ALU.mult),
                     reads=["tmpA1", ("og", h % 4)], writes=[("og", h % 4)])
            if h % 4 == 3:
                self.out_proj(l, Wout[(h // 4) * 512:(h // 4 + 1) * 512, :], qblks)
        self.barrier()

    def out_proj(self, l, Wrows, blks, nk=4):
        P = self.P
        for ot in range(NKT):
            w, wk = self.load_cast(Wrows[:, ot * 128:(ot + 1) * 128], nk, 128)
            for (c0, c1) in blks:
                s = 1 if c0 == 0 else 0
                bank = self.proj_banks[self.ps_rr % len(self.proj_banks)]
                self.ps_rr += 1
                ps = self.ps[bank]
                for j in range(nk):
                    self.mm(ps[:, 0:c1 - c0], w[:, j, :], self.ygrp[:, j, c0:c1], j == 0, j == nk - 1, [wk, ("og", j)], [f"ps{bank}"])
                P.op("dve", lambda e, ot=ot, c0=c0, c1=c1, s=s, ps=ps: e.scalar_tensor_tensor(
                    self.xT[:, ot, c0:c1], ps[:, 0:c1 - c0], self.modT[:, l, 16 + ot, s:s + 1], self.xT[:, ot, c0:c1], ALU.mult, ALU.add),
                    reads=[f"ps{bank}", "modT", ("xT", ot)], writes=[("xT", ot)])


_NC_CACHE = {}


def _get_nc(layers, debug=False):
    key = (tuple(layers), debug)
    if key not in _NC_CACHE:
        _NC_CACHE[key] = Builder(layers, debug).build()
    return _NC_CACHE[key]


def kernel(_layers=(0, 1, 2, 3), _ncores=8, _debug=False, **inputs):
    nc = _get_nc(_layers, _debug)
    in_maps = []
    for b in range(_ncores):
        m = {}
        for name, shp in WSPEC:
            a = np.asarray(inputs[name], dtype=np.float32)
            if name in ("x", "c", "ctx"):
                a = a[b]
            m[name] = np.ascontiguousarray(a).reshape(shp)
        in_maps.append(m)
    res = run_bass_kernel_spmd(nc, in_maps, core_ids=list(range(_ncores)))
    if _debug:
        return res.results[0]
    return np.stack([np.asarray(r["out"], dtype=np.float32).reshape(SEQ, D) for r in res.results], axis=0)
```

```python
import math
import numpy as np
import concourse.bass as bass
import concourse.mybir as mybir
from concourse.bass_utils import run_bass_kernel_spmd

F32 = mybir.dt.float32
BF16 = mybir.dt.bfloat16
I32 = mybir.dt.int32
AF = mybir.ActivationFunctionType
ALU = mybir.AluOpType
AX = mybir.AxisListType

D = 1024
SEQ = 2048
CTX = 256
T = CTX + SEQ
NKT = 8
DEPTH = 4
EPS = 1e-6
BLKS = [(0, 256), (256, 768), (768, 1280), (1280, 1792), (1792, 2304)]
MLA_SCALE = 1.0 / math.sqrt(192.0)
TWO_PI = 2.0 * math.pi

DBG_D = 0
ENGS = ["pe", "act", "dve", "pool", "sp"]


class Prog:
    def __init__(self, nc):
        self.nc = nc
        self.streams = {e: [] for e in ENGS}
        self.count = {e: 0 for e in ENGS}
        self.sem = {e: nc.alloc_semaphore(name=f"prog_{e}") for e in ENGS}
        self.waited = {}
        self.lastw = {}
        self.readers = {}
        self.dma_sems = [nc.alloc_semaphore(name=f"dma_{i}") for i in range(16)]
        self.dma_cnt = [0] * 16
        self.dma_rr = 0

    def _deps(self, reads, writes):
        deps = []
        for k in reads:
            if k in self.lastw:
                deps.append(self.lastw[k])
        for k in writes:
            if k in self.lastw:
                deps.append(self.lastw[k])
            deps.extend(self.readers.get(k, []))
        return deps

    def _emit_waits(self, eng, deps):
        need = {}
        for (s, v) in deps:
            if eng == "pe" and s is self.sem["pe"]:
                continue
            key = id(s)
            if v > self.waited.get((eng, key), 0):
                if key not in need or need[key][1] < v:
                    need[key] = (s, v)
        for key, (s, v) in need.items():
            self.waited[(eng, key)] = v
            self.streams[eng].append(lambda e, s=s, v=v: e.wait_ge(s, v))

    def _commit(self, tok, reads, writes):
        for k in writes:
            self.lastw[k] = tok
            self.readers[k] = []
        for k in reads:
            if k not in writes:
                self.readers.setdefault(k, []).append(tok)

    def op(self, eng, fn, reads=(), writes=()):
        reads = list(reads)
        writes = list(writes)
        self._emit_waits(eng, self._deps(reads, writes))
        self.count[eng] += 1
        v = self.count[eng]
        s = self.sem[eng]
        self.streams[eng].append(lambda e, fn=fn, s=s: fn(e).then_inc(s, 1))
        tok = (s, v)
        self._commit(tok, reads, writes)
        return tok

    def dma(self, out, in_, reads=(), writes=(), eng="sp", **kw):
        reads = list(reads)
        writes = list(writes)
        i = self.dma_rr
        self.dma_rr = (self.dma_rr + 1) % len(self.dma_sems)
        s = self.dma_sems[i]
        deps = self._deps(reads, writes)
        if self.dma_cnt[i] > 0:
            deps.append((s, 16 * self.dma_cnt[i]))
        self._emit_waits(eng, deps)
        self.dma_cnt[i] += 1
        v = 16 * self.dma_cnt[i]
        self.streams[eng].append(
            lambda e, out=out, in_=in_, s=s, kw=kw: e.dma_start(out=out, in_=in_, **kw).then_inc(s, 16))
        tok = (s, v)
        self._commit(tok, reads, writes)
        return tok

    def finish(self, final_tokens):
        nc = self.nc
        self._emit_waits("sp", final_tokens)
        with nc.Block() as block:
            @block.tensor
            def _(e):
                for f in self.streams["pe"]:
                    f(e)

            @block.scalar
            def _(e):
                for f in self.streams["act"]:
                    f(e)

            @block.vector
            def _(e):
                for f in self.streams["dve"]:
                    f(e)

            @block.gpsimd
            def _(e):
                for f in self.streams["pool"]:
                    f(e)

            @block.sync
            def _(e):
                for f in self.streams["sp"]:
                    f(e)


WSPEC = [
    ("x", [SEQ, D]), ("c", [D]), ("ctx", [CTX, D]), ("c_ctx", [D]),
    ("norm_g", [4, D]), ("mod_w", [4, D, 3 * D]), ("mod_b", [4, 3 * D]),
    ("ev_w_in", [2, D, 3072]), ("lru_conv_w", [2, 4, D]), ("lru_conv_b", [2, D]),
    ("lru_wr", [2, 2, 8, 128, 128]), ("lru_br", [2, 2, D]),
    ("lru_wi", [2, 2, 8, 128, 128]), ("lru_bi", [2, 2, D]), ("lru_lam", [2, 2, D]),
    ("s5_lam_re", [2, 2, 32, 64]), ("s5_lam_im", [2, 2, 32, 64]), ("s5_log_dt", [2, 2, 32, 64]),
    ("s5_b_re", [2, 2, 32, 64, 16]), ("s5_b_im", [2, 2, 32, 64, 16]),
    ("s5_c_re", [2, 2, 32, 16, 64]), ("s5_c_im", [2, 2, 32, 16, 64]),
    ("s5_d", [2, 32, 16]), ("s5_glu_w", [2, 512, 512]), ("s5_glu_b", [2, 512]),
    ("ev_w_out", [2, 1536, D]),
    ("mla_w_in", [2, D, 1728]), ("mla_q_norm", [2, 384]), ("mla_w_uq", [2, 384, 1536]),
    ("mla_kv_norm", [2, 256]), ("mla_w_ukv", [2, 256, 2048]), ("mla_w_out", [2, D, D]),
    ("final_g", [D]),
]


class Builder:
    def __init__(self, layers=(0, 1, 2, 3), debug=False):
        self.layers = list(layers)
        nc = bass.Bass("TRN2", target_bir_lowering=False)
        self.nc = nc
        self.P = Prog(nc)
        self.W = {}
        for name, shp in WSPEC:
            self.W[name] = nc.dram_tensor(name, shp, F32, kind="ExternalInput").ap()
        self.out = nc.dram_tensor("out", [SEQ, D], F32, kind="ExternalOutput").ap()
        self.debug = debug
        if debug:
            self.dbgf = nc.dram_tensor("dbgf", [8, 128, T], F32, kind="ExternalOutput").ap()
            self.dbgb = nc.dram_tensor("dbgb", [8, 128, T], BF16, kind="ExternalOutput").ap()
        A = nc.alloc_sbuf_tensor
        self.xT = A("xT", [128, NKT, T], F32)
        self.nT = A("nT", [128, NKT, T], BF16)
        self.modT = A("modT", [128, DEPTH, 24, 2], F32)
        self.identf = A("identf", [128, 128], F32)
        self.identb = A("identb", [128, 128], BF16)
        self.onesb = A("onesb", [128, 128], BF16)
        self.ygrp = A("ygrp", [128, 4, T], BF16)
        self.F = [A(f"F{i}", [128, T], F32) for i in range(3)]
        self.B = [A(f"B{i}", [128, T + 8], BF16) for i in range(4)]
        self.stage = [A(f"stage{i}", [128, 1024], F32) for i in range(2)]
        self.stage_i = 0
        self.wbf = [A(f"wbf{i}", [128, 1024], BF16) for i in range(3)]
        self.wbf_i = 0
        self.LW = A("LW", [128, 2304], F32)
        self.small = A("small", [128, 256], F32)
        self.tmpA = [A(f"tmpA{i}", [128, 512], F32) for i in range(3)]
        self.tmpB = [A(f"tmpB{i}", [128, 512], BF16) for i in range(4)]
        self.ps = [nc.alloc_psum_tensor(f"ps{i}", [128, 512], F32) for i in range(8)]

    def load_cast(self, src_ap, kt, ncols, dst=None, dst_key=None, defer=False):
        P = self.P
        si = self.stage_i
        self.stage_i = (si + 1) % len(self.stage)
        st = self.stage[si]
        stv = st[:, 0:kt * ncols].rearrange("p (k c) -> p k c", k=kt)
        P.dma(stv, src_ap.rearrange("(k p) c -> p k c", p=128), writes=[f"stage{si}"])
        if dst is None:
            wi = self.wbf_i
            self.wbf_i = (wi + 1) % len(self.wbf)
            dst = self.wbf[wi][:, 0:kt * ncols].rearrange("p (k c) -> p k c", k=kt)
            dst_key = f"wbf{wi}"

        def cast(eng="pool"):
            if eng == "act":
                P.op("act", lambda e: e.activation(dst, stv, AF.Copy), reads=[f"stage{si}"], writes=[dst_key])
            else:
                P.op(eng, lambda e: e.tensor_copy(dst, stv), reads=[f"stage{si}"], writes=[dst_key])
        if defer:
            return dst, dst_key, cast
        cast()
        return dst, dst_key

    def mm(self, out, lhsT, rhs, start, stop, reads, writes):
        return self.P.op("pe", lambda e: e.matmul(out, lhsT, rhs, start=start, stop=stop),
                         reads=reads, writes=writes)

    def consts(self):
        P = self.P
        idf, idb, ob = self.identf, self.identb, self.onesb
        P.op("pool", lambda e: e.memset(idf[:], 0.0), writes=["identf"])
        P.op("pool", lambda e: e.affine_select(idf[:], idf[:], [[-1, 128]], ALU.not_equal, 1.0, base=0,
                                               channel_multiplier=1), reads=["identf"], writes=["identf"])
        P.op("pool", lambda e: e.tensor_copy(idb[:], idf[:]), reads=["identf"], writes=["identb"])
        P.op("pool", lambda e: e.memset(ob[:], 1.0), writes=["onesb"])

    def load_x(self):
        P = self.P
        for tt in range(T // 128):
            st = self.stage[tt % 2]
            skey = f"stage{tt % 2}"
            src = self.W["ctx"][tt * 128:(tt + 1) * 128, :] if tt < 2 else self.W["x"][(tt - 2) * 128:(tt - 1) * 128, :]
            P.dma(st[:], src, writes=[skey])
            for half in range(2):
                ps = self.ps[(tt * 2 + half) % 8]
                pkey = f"ps{(tt * 2 + half) % 8}"
                for q in range(4):
                    kt = half * 4 + q
                    P.op("pe", lambda e, ps=ps, q=q, kt=kt, st=st: e.transpose(
                        ps[:, q * 128:(q + 1) * 128], st[:, kt * 128:(kt + 1) * 128], self.identf[:]),
                        reads=[skey, "identf"], writes=[pkey])
                dst = self.xT[:, half * 4:half * 4 + 4, tt * 128:(tt + 1) * 128]
                srcp = ps[:].rearrange("p (q c) -> p q c", q=4)
                eng = "dve" if half == 0 else "act"
                if eng == "dve":
                    P.op("dve", lambda e, dst=dst, srcp=srcp: e.tensor_copy(dst, srcp), reads=[pkey],
                         writes=[("xT", half * 4 + q) for q in range(4)])
                else:
                    P.op("act", lambda e, dst=dst, srcp=srcp: e.activation(dst, srcp, AF.Copy), reads=[pkey],
                         writes=[("xT", half * 4 + q) for q in range(4)])

    def modulation(self):
        P = self.P
        sm = self.small
        cc = sm[:, 0:16]
        csb = sm[:, 16:24].bitcast(BF16)
        P.dma(cc[:, 0:8], self.W["c"].rearrange("(k p) -> p k", p=128), writes=["cc"], allow_slow_non_contiguous=True)
        P.dma(cc[:, 8:16], self.W["c_ctx"].rearrange("(k p) -> p k", p=128), writes=["cc"], allow_slow_non_contiguous=True)
        P.op("act", lambda e: e.activation(csb, cc, AF.Silu), reads=["cc"], writes=["cs"])
        csv = csb.rearrange("p (s k) -> p k s", s=2)
        nTf = self.nT[:].rearrange("p k t -> p (k t)").bitcast(F32)
        stg = [nTf[:, i * 1024:(i + 1) * 1024] for i in range(6)]
        nxt = 0
        for l in range(DEPTH):
            for kt in range(NKT):
                for h in range(3):
                    si = nxt % 6
                    nxt += 1
                    st = stg[si]
                    P.dma(st, self.W["mod_w"][l, kt * 128:(kt + 1) * 128, h * 1024:(h + 1) * 1024], writes=[f"mstg{si}"])
                    wi = self.wbf_i
                    self.wbf_i = (wi + 1) % len(self.wbf)
                    wb = self.wbf[wi]
                    eng = "act" if nxt % 2 == 0 else "dve"
                    if eng == "act":
                        P.op("act", lambda e, wb=wb, st=st: e.activation(wb[:], st, AF.Copy), reads=[f"mstg{si}"], writes=[f"wbf{wi}"])
                    else:
                        P.op("dve", lambda e, wb=wb, st=st: e.tensor_copy(wb[:], st), reads=[f"mstg{si}"], writes=[f"wbf{wi}"])
                    for q in range(2):
                        bank = h * 2 + q
                        self.mm(self.ps[bank][0:2, :], csv[:, kt, :], wb[:, q * 512:(q + 1) * 512],
                                kt == 0, kt == NKT - 1, ["cs", f"wbf{wi}"], [f"ps{bank}"])
            mrow = nTf[:, 6144:9216][0:2, :]
            bro = self.F[0][0:2, 0:2304]
            bro2 = self.F[1][0:2, 0:768]
            for s_ in range(2):
                P.dma(bro[s_:s_ + 1, :], self.W["mod_b"][l:l + 1, 0:2304], writes=["bro", "F0"])
                P.dma(bro2[s_:s_ + 1, :], self.W["mod_b"][l:l + 1, 2304:3072], writes=["bro", "F1"])
            for bank in range(6):
                c0 = bank * 512
                if c0 + 512 <= 2304:
                    bsrc = bro[:, c0:c0 + 512]
                    P.op("dve", lambda e, bank=bank, bsrc=bsrc, c0=c0: e.tensor_tensor(
                        mrow[:, c0:c0 + 512], self.ps[bank][0:2, :], bsrc, ALU.add), reads=[f"ps{bank}", "bro", "F0", "F1"], writes=["mrow"])
                else:
                    for (a0, a1) in [(c0, min(c0 + 512, 2304)), (max(c0, 2304), c0 + 512)]:
                        if a1 <= a0:
                            continue
                        bsrc = bro[:, a0:a1] if a1 <= 2304 else bro2[:, a0 - 2304:a1 - 2304]
                        P.op("dve", lambda e, bank=bank, bsrc=bsrc, a0=a0, a1=a1, c0=c0: e.tensor_tensor(
                            mrow[:, a0:a1], self.ps[bank][0:2, a0 - c0:a1 - c0], bsrc, ALU.add), reads=[f"ps{bank}", "bro", "F0", "F1"], writes=["mrow"])
            for j in range(24):
                self.mm(self.ps[6][:, 2 * j:2 * j + 2], mrow[:, j * 128:(j + 1) * 128], self.identf[0:2, 0:2],
                        True, True, ["mrow", "identf"], ["ps6"])
            P.op("dve", lambda e, l=l: e.tensor_copy(self.modT[:, l].rearrange("p j s -> p (j s)"), self.ps[6][:, 0:48]),
                 reads=["ps6"], writes=["modT"])

    def norm_mod(self, l):
        P = self.P
        sm = self.small
        g = sm[:, 32:40]
        gm = sm[:, 40:56]
        P.dma(g, self.W["norm_g"][l].rearrange("(k p) -> p k", p=128), writes=["g"], allow_slow_non_contiguous=True)
        for s in range(2):
            P.op("dve", lambda e, s=s: e.scalar_tensor_tensor(
                gm[:, s * 8:(s + 1) * 8], self.modT[:, l, 8:16, s], 1.0, g, ALU.add, ALU.mult),
                reads=["modT", "g"], writes=["gm"])
        rstd = self.F[1]
        self.rstd_into(rstd, "F1", [self.xT[:, kt, :] for kt in range(NKT)], [("xT", kt) for kt in range(NKT)], D)
        for kt in range(NKT):
            for s, (c0, c1) in enumerate([(CTX, T), (0, CTX)]):
                tmp = self.F[2]
                P.op("dve", lambda e, kt=kt, s=s, c0=c0, c1=c1: e.scalar_tensor_tensor(
                    tmp[:, c0:c1], self.xT[:, kt, c0:c1], gm[:, s * 8 + kt:s * 8 + kt + 1], rstd[:, c0:c1], ALU.mult, ALU.mult),
                    reads=[("xT", kt), "gm", "F1"], writes=["F2"])
                P.op("act", lambda e, kt=kt, s=s, c0=c0, c1=c1: e.activation(
                    self.nT[:, kt, c0:c1], tmp[:, c0:c1], AF.Identity, bias=self.modT[:, l, kt, s:s + 1], scale=1.0),
                    reads=["F2", "modT"], writes=[("nT", kt)])

    def rstd_into(self, dst, dst_key, srcs, src_keys, dim, cols=(0, T)):
        P = self.P
        blks = [b for b in BLKS if b[0] >= cols[0] and b[1] <= cols[1]]
        for bi, (c0, c1) in enumerate(blks):
            pb = self.ps[7]
            n = len(srcs)
            for i, (s_ap, sk) in enumerate(zip(srcs, src_keys)):
                sq = self.tmpB[i % 2]
                P.op("act", lambda e, sq=sq, s_ap=s_ap, c0=c0, c1=c1: e.activation(sq[:, 0:c1 - c0], s_ap[:, c0:c1], AF.Square),
                     reads=[sk], writes=[f"tmpB{i % 2}"])
                self.mm(pb[:, 0:c1 - c0], self.onesb[:], sq[:, 0:c1 - c0], i == 0, i == n - 1,
                        [f"tmpB{i % 2}", "onesb"], ["ps7"])
            P.op("act", lambda e, c0=c0, c1=c1: e.activation(dst[:, c0:c1], pb[:, 0:c1 - c0], AF.Sqrt, bias=EPS, scale=1.0 / dim),
                 reads=["ps7"], writes=[dst_key])
            P.op("dve", lambda e, c0=c0, c1=c1: e.reciprocal(dst[:, c0:c1], dst[:, c0:c1]), reads=[dst_key], writes=[dst_key])

    def final(self):
        P = self.P
        gb = self.F[0]
        P.dma(gb[:, 0:D], self.W["final_g"].partition_broadcast(128), writes=["F0"])
        ot = self.F[1]
        for tt in range(SEQ // 128):
            c0 = CTX + tt * 128
            ss = self.small[:, 64 + (tt % 2) * 2:64 + (tt % 2) * 2 + 2]
            sskey = f"ss{tt % 2}"
            pss = []
            P.op("pool", lambda e, ss=ss: e.memset(ss, 0.0), writes=[sskey + "0", sskey + "1"])
            for half in range(2):
                bank = (tt * 2 + half) % 4
                ps = self.ps[bank]
                for q in range(4):
                    kt = half * 4 + q
                    P.op("pe", lambda e, ps=ps, q=q, kt=kt, c0=c0: e.transpose(
                        ps[:, q * 128:(q + 1) * 128], self.xT[:, kt, c0:c0 + 128], self.identf[:]),
                        reads=[("xT", kt), "identf"], writes=[f"ps{bank}"])
                junk = self.tmpA[half]
                P.op("act", lambda e, ps=ps, junk=junk, half=half, ss=ss: e.activation(
                    junk[:], ps[:], AF.Square, accum_out=ss[:, half:half + 1]),
                    reads=[f"ps{bank}"], writes=[f"tmpA{half}", sskey + str(half)])
                pss.append((ps, bank))
            rs = self.small[:, 72 + (tt % 2):73 + (tt % 2)]
            rkey = f"rs{tt % 2}"
            P.op("dve", lambda e, ss=ss, rs=rs: e.tensor_tensor(rs, ss[:, 0:1], ss[:, 1:2], ALU.add),
                 reads=[sskey + "0", sskey + "1"], writes=[rkey])
            P.op("act", lambda e, rs=rs: e.activation(rs, rs, AF.Sqrt, bias=EPS, scale=1.0 / D), reads=[rkey], writes=[rkey])
            P.op("dve", lambda e, rs=rs: e.reciprocal(rs, rs), reads=[rkey], writes=[rkey])
            obuf = ot[:, (tt % 2) * 1024:(tt % 2) * 1024 + 1024]
            okey = f"ot{tt % 2}"
            for half, (ps, bank) in enumerate(pss):
                P.op("dve", lambda e, ps=ps, half=half, rs=rs, obuf=obuf: e.scalar_tensor_tensor(
                    obuf[:, half * 512:(half + 1) * 512], ps[:], rs, gb[:, half * 512:(half + 1) * 512], ALU.mult, ALU.mult),
                    reads=[f"ps{bank}", rkey, "F0"], writes=[okey + str(half)])
            self.out_toks.append(P.dma(self.out[tt * 128:(tt + 1) * 128, :], obuf, reads=[okey + "0", okey + "1"]))

    def build(self):
        self.out_toks = []
        self.consts()
        self.load_x()
        self.modulation()
        for l in self.layers:
            self.norm_mod(l)
            if l % 2 == 0:
                self.even_layer(l)
            else:
                self.mla_layer(l)
        self.final()
        self.P.finish(self.out_toks)
        return self.nc


    def even_layer(self, l):
        P = self.P
        e_ = l // 2
        Win = self.W["ev_w_in"][e_]
        Wout = self.W["ev_w_out"][e_]
        self.ps_rr = 0
        self.proj_banks = [0, 1, 2, 3, 4, 5, 6, 7]
        sm = self.small
        cw = sm[:, 104:136].rearrange("p (k t) -> p k t", k=4)
        cb = sm[:, 136:144]
        br = sm[:, 144:160].rearrange("p (d t) -> p d t", d=2)
        bi = sm[:, 160:176].rearrange("p (d t) -> p d t", d=2)
        coef = sm[:, 176:192].rearrange("p (d t) -> p d t", d=2)
        coef2 = sm[:, 192:208].rearrange("p (d t) -> p d t", d=2)
        ld = lambda dst, src, key: P.dma(dst, src, writes=[key], allow_slow_non_contiguous=True)
        for k in range(4):
            ld(cw[:, k, :], self.W["lru_conv_w"][e_, k].rearrange("(t p) -> p t", p=128), "cw")
        ld(cb, self.W["lru_conv_b"][e_].rearrange("(t p) -> p t", p=128), "cb")
        for d in range(2):
            ld(br[:, d, :], self.W["lru_br"][e_, d].rearrange("(t p) -> p t", p=128), "br")
            ld(bi[:, d, :], self.W["lru_bi"][e_, d].rearrange("(t p) -> p t", p=128), "bi")
            ld(coef[:, d, :], self.W["lru_lam"][e_, d].rearrange("(t p) -> p t", p=128), "coef")
        cf = sm[:, 176:192]
        cf2 = sm[:, 192:208]
        P.op("act", lambda e: e.activation(cf, cf, AF.Exp, scale=-1.0), reads=["coef"], writes=["coef"])
        P.op("act", lambda e: e.activation(cf, cf, AF.Ln, bias=1.0), reads=["coef"], writes=["coef"])
        P.op("dve", lambda e: e.tensor_scalar_mul(cf2, cf, -16.0), reads=["coef"], writes=["coef2"])
        P.op("dve", lambda e: e.tensor_scalar_mul(cf, cf, -8.0), reads=["coef", "coef2"], writes=["coef"])
        nsrc = lambda k: self.nT[:, k, :]
        nkeys = [("nT", k) for k in range(NKT)]
        xap, u, ib, hf = self.B
        F0, F1, F2 = self.F
        dg = [self.tmpB[0][:, k * 128:(k + 1) * 128] for k in range(4)]
        F2b = F2[:].bitcast(BF16)
        hr = F2b[:, 0:T]
        abuf = [(F1, "F1"), (self.LW, "LWa")]
        ibuf = [(ib, "B2"), (F2b[:, T:2 * T], "ibB")]
        for (a, b) in [(0, 2), (258, 262), (2310, 2312)]:
            P.op("pool", lambda e, a=a, b=b: e.memset(xap[:, a:b], 0.0), writes=["B0"])

        def xoff(c0):
            return c0 + 2 if c0 < CTX else c0 + 6

        def lru_loads(h):
            r = {"wxa": self.load_cast(Win[:, h * 128:(h + 1) * 128], 8, 128),
                 "wga": self.load_cast(Win[:, 1024 + h * 128:1024 + (h + 1) * 128], 8, 128)}
            gwb = self.tmpB[2 + h % 2]
            for d in range(2):
                for q, nm in enumerate(["lru_wr", "lru_wi"]):
                    off = (d * 2 + q) * 128
                    dst = gwb[:, off:off + 128].rearrange("p (k c) -> p k c", k=1)
                    r[(d, q)] = self.load_cast(self.W[nm][e_, d, h], 1, 128, dst=dst, dst_key=("gw", h % 2, d, q))
            return r

        pend = None
        xa_pending = None
        for h in range(8):
            if pend is None:
                pend = lru_loads(h)
            Wt = pend
            pend = None
            wxa, wxak = Wt["wxa"]
            wga, wgak = Wt["wga"]

            def ev_xa(ps, pkey, c0, c1):
                P.op("act", lambda e: e.activation(xap[:, xoff(c0):xoff(c0) + c1 - c0], ps[:, 0:c1 - c0], AF.Copy), reads=[pkey], writes=["B0"])
            if xa_pending is not None:
                for (ps_, pk_, c0_, c1_) in xa_pending:
                    ev_xa(ps_, pk_, c0_, c1_)
                xa_pending = None
                self.proj_banks = [0, 1, 2, 3, 4, 5, 6, 7]
            else:
                self.proj_tile(wxa, 8, 128, nsrc, nkeys, wxak, ev_xa)
            for k in range(4):
                P.op("pool", lambda e, k=k, h=h: e.tensor_scalar_mul(dg[k], self.identb[:], cw[:, k, h:h + 1]), reads=["identb", "cw"], writes=[f"dg{k}", "tmpB0"])
            for (c0, c1) in BLKS:
                bank = self.proj_banks[self.ps_rr % len(self.proj_banks)]
                self.ps_rr += 1
                ps = self.ps[bank]
                for k in range(4):
                    o0 = xoff(c0) + k - 2
                    self.mm(ps[:, 0:c1 - c0], dg[k], xap[:, o0:o0 + c1 - c0], k == 0, k == 3, [f"dg{k}", "B0"], [f"ps{bank}"])
                P.op("act", lambda e, ps=ps, c0=c0, c1=c1, h=h: e.activation(u[:, c0:c1], ps[:, 0:c1 - c0], AF.Identity, bias=cb[:, h:h + 1], scale=1.0),
                     reads=[f"ps{bank}", "cb"], writes=["B1"])
            def ev_g(ps, pkey, c0, c1):
                P.op("act", lambda e: e.activation(xap[:, xoff(c0):xoff(c0) + c1 - c0], ps[:, 0:c1 - c0], AF.Silu), reads=[pkey], writes=["B0"])
            usrc = lambda k: u
            for d in range(2):
                wr, wrk = Wt[(d, 0)]
                wi, wik = Wt[(d, 1)]
                ad, adk = abuf[d]
                ibd, ibk = ibuf[d]

                def ev_r(ps, pkey, c0, c1, d=d, h=h):
                    P.op("act", lambda e: e.activation(F0[:, c0:c1], ps[:, 0:c1 - c0], AF.Sigmoid, bias=br[:, d, h:h + 1], scale=1.0),
                         reads=[pkey, "br"], writes=["F0"])

                def ev_i(ps, pkey, c0, c1, d=d, h=h, ibd=ibd, ibk=ibk):
                    P.op("act", lambda e: e.activation(ibd[:, c0:c1], ps[:, 0:c1 - c0], AF.Sigmoid, bias=bi[:, d, h:h + 1], scale=1.0),
                         reads=[pkey, "bi"], writes=[ibk])
                self.proj_tile(wr, 1, 128, usrc, ["B1"], wrk, ev_r)
                self.proj_tile(wi, 1, 128, usrc, ["B1"], wik, ev_i)
                if d == 1 and pend is not None:
                    nwxa, nwxak = pend["wxa"]
                    xa_pending = []
                    for bi_, (c0_, c1_) in enumerate(BLKS):
                        bank_ = 3 + bi_
                        for k_ in range(NKT):
                            self.mm(self.ps[bank_][:, 0:c1_ - c0_], nwxa[:, k_, :], self.nT[:, k_, c0_:c1_], k_ == 0, k_ == NKT - 1,
                                    [nwxak, ("nT", k_)], [f"ps{bank_}"])
                        xa_pending.append((self.ps[bank_], f"ps{bank_}", c0_, c1_))
                    self.proj_banks = [0, 1, 2]
                P.op("act", lambda e, d=d, h=h, ad=ad: e.activation(ad[:, 0:T], F0[:, :], AF.Exp, scale=coef[:, d, h:h + 1]), reads=["F0", "coef"], writes=[adk])
                P.op("act", lambda e, d=d, h=h: e.activation(F0[:, :], F0[:, :], AF.Exp, scale=coef2[:, d, h:h + 1]), reads=["F0", "coef2"], writes=["F0"])
                P.op("act", lambda e: e.activation(F0[:, :], F0[:, :], AF.Sqrt, bias=1.0, scale=-1.0), reads=["F0"], writes=["F0"])
                P.op("dve", lambda e, ibd=ibd: e.tensor_tensor(ibd[:, 0:T], ibd[:, 0:T], u[:, 0:T], ALU.mult), reads=[ibk, "B1"], writes=[ibk])
                P.op("dve", lambda e, ibd=ibd: e.tensor_tensor(ibd[:, 0:T], F0[:, :], ibd[:, 0:T], ALU.mult), reads=["F0", ibk], writes=[ibk])
                if d == 0:
                    P.op("dve", lambda e, ad=ad, ibd=ibd: e.tensor_tensor_scan(hf[:, 0:T], ad[:, 0:T], ibd[:, 0:T], 0.0, ALU.mult, ALU.add),
                         reads=[ibk, adk], writes=["B3"])
                    self.proj_tile(wga, 8, 128, nsrc, nkeys, wgak, ev_g)
                    if h + 1 < 8 and h % 4 != 3:
                        pend = lru_loads(h + 1)
                else:
                    P.op("dve", lambda e, ad=ad, ibd=ibd: e.tensor_tensor_scan(hr[:, 0:CTX][:, ::-1], ad[:, 0:CTX][:, ::-1], ibd[:, 0:CTX][:, ::-1], 0.0,
                                                                              ALU.mult, ALU.add), reads=[ibk, adk], writes=["hr"])
                    P.op("dve", lambda e, ad=ad, ibd=ibd: e.tensor_tensor_scan(hr[:, CTX:T][:, ::-1], ad[:, CTX:T][:, ::-1], ibd[:, CTX:T][:, ::-1], hr[:, 0:1],
                                                                              ALU.mult, ALU.add), reads=[ibk, adk, "hr"], writes=["hr"])
            P.op("dve", lambda e: e.tensor_tensor(hr, hr, hf[:, 0:T], ALU.add), reads=["hr", "B3"], writes=["hr"])
            P.op("dve", lambda e, h=h: e.tensor_tensor(self.ygrp[:, h % 4, 0:CTX], hr[:, 0:CTX], xap[:, 2:2 + CTX], ALU.mult),
                 reads=["hr", "B0"], writes=[("og", h % 4)])
            P.op("dve", lambda e, h=h: e.tensor_tensor(self.ygrp[:, h % 4, CTX:T], hr[:, CTX:T], xap[:, 262:262 + SEQ], ALU.mult),
                 reads=["hr", "B0"], writes=[("og", h % 4)])
            if h % 4 == 3:
                self.out_proj(l, Wout[(h // 4) * 512:(h // 4 + 1) * 512, :], BLKS)
        self.barrier()
        self.s5_phase(l)
        self.barrier()


    def s5_phase(self, l):
        P = self.P
        e_ = l // 2
        Win = self.W["ev_w_in"][e_]
        Wout = self.W["ev_w_out"][e_]
        W = self.W
        F0, F1, F2 = self.F
        g_re, g_im, ub, B3 = self.B
        LW = self.LW
        LWb = LW[:].bitcast(BF16)
        sm = self.small
        tht, rmag = LW[:, 0:32], LW[:, 32:64]
        M16, Mrow, nMrow = LW[:, 64:72], LW[:, 72:74], LW[:, 74:76]
        Braw = [LW[:, 80:144], LW[:, 144:208]]
        Bbar = [LW[:, 208:272], LW[:, 272:336]]
        CT = LW[:, 336:464].rearrange("p (j q h) -> p j q h", j=4, q=2)
        dsk, glub = LW[:, 464:468], LW[:, 468:472]
        Eexp = LW[0:32, 472:600]
        Fc = LW[0:32, 600:856]
        lB = lambda j, q: LWb[:, 1712 + (j * 2 + q) * 128:1712 + (j * 2 + q + 1) * 128]
        lC = lambda j, v: LWb[:, 2736 + (j * 3 + v) * 128:2736 + (j * 3 + v + 1) * 128]
        Dd = LWb[:, 4272:4400]
        iota48 = sm[:, 208:256]
        hpi = sm[:, 87:88]
        tA0, tA1, tA2 = self.tmpA
        ld = lambda dst, src, key: P.dma(dst, src, writes=[key], allow_slow_non_contiguous=True)
        dve = lambda fn, r, w: P.op("dve", fn, reads=r, writes=w)
        act = lambda fn, r, w: P.op("act", fn, reads=r, writes=w)
        pool = lambda fn, r, w: P.op("pool", fn, reads=r, writes=w)
        pool(lambda e: e.iota(iota48, [[1, 48]], base=0, channel_multiplier=0, allow_small_or_imprecise_dtypes=True), [], ["iota48"])
        dve(lambda e: e.memset(hpi, math.pi / 2), [], ["hpi"])
        dve(lambda e: e.tensor_reduce(M16, self.identf[:].rearrange("p (c h) -> p c h", h=16), AX.X, ALU.add), ["identf"], ["M16"])
        dve(lambda e: e.tensor_reduce(Mrow, self.identf[:].rearrange("p (c h) -> p c h", h=64), AX.X, ALU.add), ["identf"], ["Mrow"])
        dve(lambda e: e.tensor_scalar_mul(nMrow, Mrow, -1.0), ["Mrow"], ["nMrow"])
        pool(lambda e: e.memset(LWb[:, 2736:4272], 0.0), [], ["lC"])
        ld(dsk, W["s5_d"][e_].rearrange("(t g) h -> (g h) t", g=8), "dsk")
        ld(glub, W["s5_glu_b"][e_].rearrange("(t p) -> p t", p=128), "glub")
        for i, nm in enumerate(["s5_lam_re", "s5_lam_im", "s5_log_dt"]):
            ld(tA0[:, i * 32:(i + 1) * 32], W[nm][e_].rearrange("d (gp gl) p -> (gl p) (d gp)", gl=2), "tA0")
        act(lambda e: e.activation(tA0[:, 64:96], tA0[:, 64:96], AF.Exp), ["tA0"], ["tA0"])
        dve(lambda e: e.tensor_tensor(tht, tA0[:, 32:64], tA0[:, 64:96], ALU.mult), ["tA0"], ["tht"])
        dve(lambda e: e.tensor_scalar_mul(tht, tht, 1.0 / TWO_PI), ["tht"], ["tht"])
        dve(lambda e: e.tensor_tensor(rmag, tA0[:, 0:32], tA0[:, 64:96], ALU.mult), ["tA0"], ["rmag"])
        act(lambda e: e.activation(rmag, rmag, AF.Exp), ["rmag"], ["rmag"])
        c = lambda i: F0[0:32, i * 128:(i + 1) * 128]
        ci = lambda i: F0[0:32, i * 128:(i + 1) * 128].bitcast(I32)
        for i, nm in enumerate(["s5_lam_re", "s5_lam_im", "s5_log_dt"]):
            ld(c(i).rearrange("g (d p) -> g d p", d=2), W[nm][e_].rearrange("d g p -> g d p"), "F0")
        K0 = ["F0"]
        act(lambda e: e.activation(c(2), c(2), AF.Exp), K0, K0)
        dve(lambda e: e.tensor_tensor(c(3), c(0), c(2), ALU.mult), K0, K0)
        dve(lambda e: e.tensor_tensor(c(4), c(1), c(2), ALU.mult), K0, K0)
        dve(lambda e: e.tensor_scalar_mul(c(4), c(4), 1.0 / TWO_PI), K0, K0)
        dve(lambda e: e.tensor_scalar_mul(c(10), c(4), 0.5), K0, K0)
        act(lambda e: e.activation(c(5), c(3), AF.Exp), K0, K0)
        act(lambda e: e.activation(c(6), c(3), AF.Tanh, scale=0.5), K0, K0)
        dve(lambda e: e.scalar_tensor_tensor(c(6), c(5), 1.0, c(6), ALU.add, ALU.mult), K0, K0)
        dve(lambda e: e.tensor_copy(ci(7), c(4)), K0, K0)
        dve(lambda e: e.tensor_tensor(c(4), c(4), ci(7), ALU.subtract), K0, K0)
        act(lambda e: e.activation(c(8), c(4), AF.Sin, scale=TWO_PI), K0, K0)
        dve(lambda e: e.scalar_tensor_tensor(c(9), c(4), -1.0, c(4), ALU.mult, ALU.max), K0, K0)
        act(lambda e: e.activation(c(9), c(9), AF.Sin, bias=hpi[0:32, :], scale=-TWO_PI), K0 + ["hpi"], K0)
        dve(lambda e: e.tensor_copy(ci(7), c(10)), K0, K0)
        dve(lambda e: e.tensor_tensor(c(10), c(10), ci(7), ALU.subtract), K0, K0)
        act(lambda e: e.activation(c(10), c(10), AF.Sin, scale=TWO_PI), K0, K0)
        dve(lambda e: e.tensor_tensor(c(10), c(10), c(10), ALU.mult), K0, K0)
        dve(lambda e: e.tensor_tensor(c(11), c(6), c(9), ALU.mult), K0, K0)
        dve(lambda e: e.scalar_tensor_tensor(c(11), c(10), -2.0, c(11), ALU.mult, ALU.add), K0, K0)
        dve(lambda e: e.tensor_tensor(c(12), c(5), c(8), ALU.mult), K0, K0)
        dve(lambda e: e.tensor_tensor(c(13), c(0), c(0), ALU.mult), K0, K0)
        dve(lambda e: e.tensor_tensor(c(14), c(1), c(1), ALU.mult), K0, K0)
        dve(lambda e: e.tensor_tensor(c(13), c(13), c(14), ALU.add), K0, K0)
        dve(lambda e: e.reciprocal(c(13), c(13)), K0, K0)
        dve(lambda e: e.tensor_tensor(c(14), c(11), c(0), ALU.mult), K0, K0)
        dve(lambda e: e.tensor_tensor(c(15), c(12), c(1), ALU.mult), K0, K0)
        dve(lambda e: e.tensor_tensor(c(14), c(14), c(15), ALU.add), K0, K0)
        dve(lambda e: e.tensor_tensor(Fc[:, 0:128], c(14), c(13), ALU.mult), K0, ["Fc"])
        dve(lambda e: e.tensor_tensor(c(14), c(12), c(0), ALU.mult), K0, K0)
        dve(lambda e: e.tensor_tensor(c(15), c(11), c(1), ALU.mult), K0, K0)
        dve(lambda e: e.tensor_tensor(c(14), c(14), c(15), ALU.subtract), K0, K0)
        dve(lambda e: e.tensor_tensor(Fc[:, 128:256], c(14), c(13), ALU.mult), K0, ["Fc"])
        nsrc = lambda k: self.nT[:, k, :]
        nkeys = [("nT", k) for k in range(NKT)]

        def tv(X, d, c0, c1):
            if d == 0:
                return X[:, c0:c1]
            if c0 < CTX:
                return X[:, 0:CTX][:, ::-1]
            return X[:, 2560 - c1:2560 - c0][:, ::-1]

        for ti in range(4):
            self.proj_banks = [7]
            wub, wubk = self.load_cast(Win[:, 2048 + ti * 128:2048 + (ti + 1) * 128], 8, 128)

            def ev_u(ps, pkey, c0, c1):
                act(lambda e: e.activation(ub[:, c0:c1], ps[:, 0:c1 - c0], AF.Copy), [pkey], ["B2"])
            self.proj_tile(wub, 8, 128, nsrc, nkeys, wubk, ev_u)
            pool(lambda e, ti=ti: e.tensor_copy(Eexp.rearrange("g (c h) -> g c h", h=16),
                                                self.identf[0:32, 8 * ti:8 * ti + 8].unsqueeze(2).to_broadcast([32, 8, 16])), ["identf"], ["Eexp"])
            pool(lambda e, ti=ti: e.tensor_scalar_mul(Dd, self.identb[:], dsk[:, ti:ti + 1]), ["identb", "dsk"], ["Dd"])
            for d in range(2):
                self.mm(self.ps[7][:, 0:256], Eexp, Fc, True, True, ["Eexp", "Fc"], ["ps7"])
                for q, nm in enumerate(["s5_b_re", "s5_b_im"]):
                    for g8 in range(8):
                        ld(Braw[q][16 * g8:16 * g8 + 16, :], W[nm][e_, d, 8 * ti + g8].rearrange("p h -> h p"), f"Braw{q}")
                for q, nm in enumerate(["s5_c_re", "s5_c_im"]):
                    for gl in range(2):
                        for j in range(4):
                            ld(CT[64 * gl:64 * gl + 64, j, q, :], W[nm][e_, d, 8 * ti + 2 * j + gl].rearrange("h p -> p h"), "CT")
                Fre = self.ps[7][:, d * 64:(d + 1) * 64]
                Fim = self.ps[7][:, 128 + d * 64:128 + (d + 1) * 64]
                t0_, t1_ = tA0[:, 0:64], tA0[:, 64:128]
                dve(lambda e, Fre=Fre: e.tensor_tensor(t0_, Fre, Braw[0], ALU.mult), ["ps7", "Braw0"], ["tA0", "prod0"])
                dve(lambda e, Fim=Fim: e.tensor_tensor(t1_, Fim, Braw[1], ALU.mult), ["ps7", "Braw1"], ["tA0", "prod0"])
                dve(lambda e: e.tensor_tensor(Bbar[0], t0_, t1_, ALU.subtract), ["tA0"], ["Bbar0"])
                dve(lambda e, Fre=Fre: e.tensor_tensor(t0_, Fre, Braw[1], ALU.mult), ["ps7", "Braw1"], ["tA0", "prod0"])
                dve(lambda e, Fim=Fim: e.tensor_tensor(t1_, Fim, Braw[0], ALU.mult), ["ps7", "Braw0"], ["tA0", "prod0"])
                dve(lambda e: e.tensor_tensor(Bbar[1], t0_, t1_, ALU.add), ["tA0"], ["Bbar1"])
                if ti == 0 and d == DBG_D and self.debug:
                    self.dump(3, Fc, "Fc")
                    dve(lambda e: e.tensor_copy(tA1[:, 0:256], self.ps[7][:, 0:256]), ["ps7"], ["tA1"])
                    self.dump(6, tA1[:, 0:256], "tA1")
                    self.dump(7, Braw[0], "Braw0")
                for j in range(4):
                    for q in range(2):
                        for gl in range(2):
                            act(lambda e, j=j, q=q, gl=gl: e.activation(
                                lB(j, q)[:, 64 * gl:64 * gl + 64], Bbar[q], AF.Copy, scale=M16[:, 2 * j + gl:2 * j + gl + 1]),
                                [f"Bbar{q}", "M16"], [("lB", j)])
                    for gl in range(2):
                        cs = slice(32 * j + 16 * gl, 32 * j + 16 * gl + 16)
                        act(lambda e, j=j, gl=gl, cs=cs: e.activation(lC(j, 0)[:, cs], CT[:, j, 0, :], AF.Copy, scale=Mrow[:, gl:gl + 1]), ["CT", "Mrow"], [("lC", j)])
                        act(lambda e, j=j, gl=gl, cs=cs: e.activation(lC(j, 1)[:, cs], CT[:, j, 0, :], AF.Copy, scale=nMrow[:, gl:gl + 1]), ["CT", "nMrow"], [("lC", j)])
                        act(lambda e, j=j, gl=gl, cs=cs: e.activation(lC(j, 2)[:, cs], CT[:, j, 1, :], AF.Copy, scale=nMrow[:, gl:gl + 1]), ["CT", "nMrow"], [("lC", j)])
                for j in range(4):
                    uidx = (ti * 2 + d) * 4 + j
                    if uidx == 0:
                        for st in self.s5_table_steps(0, 0, 0, 0, tht):
                            st()
                    ins = []
                    if uidx + 1 < 32:
                        n_ = uidx + 1
                        ins = self.s5_table_steps(n_ // 8, (n_ // 4) % 2, n_ % 4, n_, tht)
                    self.s5_unit_compute(ti, d, j, uidx, lB, lC, rmag, ins)
            for bi, (c0, c1) in enumerate(BLKS):
                w_ = c1 - c0
                y = self.ps[bi]
                yk = f"ps{bi}"
                self.mm(y[:, 0:w_], Dd, ub[:, c0:c1], False, True, ["Dd", "B2"], [yk])
                tg, tgk = (tA0, "tA0") if bi % 2 == 0 else (tA1, "tA1")
                act(lambda e, y=y, w_=w_, tg=tg: e.activation(tg[:, 0:w_], y[:, 0:w_], AF.Square), [yk], [tgk])
                dve(lambda e, w_=w_, tg=tg: e.tensor_scalar(tg[:, 0:w_], tg[:, 0:w_], 0.044715, 1.0, ALU.mult, ALU.add), [tgk], [tgk])
                dve(lambda e, y=y, w_=w_, tg=tg: e.tensor_tensor(tg[:, 0:w_], tg[:, 0:w_], y[:, 0:w_], ALU.mult), [tgk, yk], [tgk])
                act(lambda e, w_=w_, tg=tg: e.activation(tg[:, 0:w_], tg[:, 0:w_], AF.Sigmoid, scale=2.0 * math.sqrt(2.0 / math.pi)), [tgk], [tgk])
                dve(lambda e, y=y, w_=w_, ti=ti, c0=c0, c1=c1, tg=tg: e.tensor_tensor(self.ygrp[:, ti, c0:c1], y[:, 0:w_], tg[:, 0:w_], ALU.mult),
                    [yk, tgk], [("og", ti)])
        self.proj_banks = [4, 5, 6]
        ysrc = lambda k: self.ygrp[:, k, :]
        ykeys = [("og", k) for k in range(4)]
        for ot in range(4):
            wg, wgk = self.load_cast(W["s5_glu_w"][e_][:, ot * 128:(ot + 1) * 128], 4, 128)

            def ev_z(ps, pkey, c0, c1, ot=ot):
                act(lambda e: e.activation(self.B[ot][:, c0:c1], ps[:, 0:c1 - c0], AF.Sigmoid, bias=glub[:, ot:ot + 1], scale=1.0),
                    [pkey, "glub"], [f"B{ot}"])
            self.proj_tile(wg, 4, 128, ysrc, ykeys, wgk, ev_z)
        for ot in range(4):
            wgb, wgbk = self.load_cast(Win[:, 2560 + ot * 128:2560 + (ot + 1) * 128], 8, 128)

            def ev_gb(ps, pkey, c0, c1, ot=ot):
                sg = self.tmpB[self.ps_rr % 2]
                sk = f"tmpB{self.ps_rr % 2}"
                act(lambda e: e.activation(sg[:, 0:c1 - c0], ps[:, 0:c1 - c0], AF.Silu), [pkey], [sk])
                pool(lambda e: e.tensor_tensor(sg[:, 0:c1 - c0], sg[:, 0:c1 - c0], self.B[ot][:, c0:c1], ALU.mult), [sk, f"B{ot}"], [sk])
                pool(lambda e: e.tensor_tensor(self.ygrp[:, ot, c0:c1], self.ygrp[:, ot, c0:c1], sg[:, 0:c1 - c0], ALU.mult),
                     [sk, ("og", ot)], [("og", ot)])
            self.proj_tile(wgb, 8, 128, nsrc, nkeys, wgbk, ev_gb)
        self.out_proj(l, Wout[1024:1536, :], BLKS)


    def s5_views(self):
        F0b = self.F[0][:].bitcast(BF16)
        F1b = self.F[1][:].bitcast(BF16)
        F2b = self.F[2][:].bitcast(BF16)
        tabs = [(F0b[:, 0:T], F0b[:, T:2 * T]), (F1b[:, 0:T], F1b[:, T:2 * T])]
        gin = (F2b[:, 0:T], F2b[:, T:2 * T])
        B3f = self.B[3][:, 0:2304].bitcast(F32)
        tAb = [self.tmpA[0][:].bitcast(BF16), self.tmpA[1][:].bitcast(BF16)]
        prod = [tAb[0][:, 0:512], tAb[0][:, 512:1024], tAb[1][:, 0:512], tAb[1][:, 512:1024]]
        return tabs, gin, B3f, prod

    def s5_table_steps(self, ti, d, j, uidx, tht):
        P = self.P
        tabs, gin, B3f, prod = self.s5_views()
        cosT, sinT = tabs[uidx % 2]
        tk = f"tab{uidx % 2}"
        sm = self.small
        iota48 = sm[:, 208:256]
        hpi = sm[:, 87:88]
        tA2 = self.tmpA[2]
        idx0 = d * 16 + 4 * ti
        th4 = tht[:, idx0:idx0 + 4]
        Ap4 = tA2[:, 0:192]
        Bp4 = tA2[:, 192:384]
        t48_4 = tA2[:, 384:388]
        kk1_4 = tA2[:, 388:392].bitcast(I32)
        kk = tA2[:, 392:488].bitcast(I32)
        Ap, Bp = Ap4[:, j * 48:(j + 1) * 48], Bp4[:, j * 48:(j + 1) * 48]
        KA = ["tA2"]
        dve = lambda fn, r, w: P.op("dve", fn, reads=r, writes=w)
        sxs = [B3f[:, 0:384], B3f[:, 384:768]]
        sy = B3f[:, 768:1152].bitcast(I32)
        steps = []

        def tiny():
            io4 = iota48.unsqueeze(1).to_broadcast([128, 4, 48])
            dve(lambda e: e.tensor_scalar_mul(t48_4, th4, 48.0), ["tht"], KA)
            dve(lambda e: e.tensor_copy(kk1_4, t48_4), KA, KA)
            dve(lambda e: e.tensor_tensor(t48_4, t48_4, kk1_4, ALU.subtract), KA, KA)
            dve(lambda e: e.tensor_tensor(Ap4.rearrange("p (u i) -> p u i", u=4), io4,
                                          t48_4.unsqueeze(2).to_broadcast([128, 4, 48]), ALU.mult), KA + ["iota48"], KA)
            dve(lambda e: e.tensor_tensor(Bp4.rearrange("p (u i) -> p u i", u=4), io4,
                                          th4.unsqueeze(2).to_broadcast([128, 4, 48]), ALU.mult), ["tht", "iota48"] + KA, KA)
            for X in (Ap4, Bp4):
                for hh in range(2):
                    xs = X[:, hh * 96:(hh + 1) * 96]
                    dve(lambda e, xs=xs: e.tensor_copy(kk, xs), KA, KA)
                    dve(lambda e, xs=xs: e.tensor_tensor(xs, xs, kk, ALU.subtract), KA, KA)
        if j == 0:
            steps.append(tiny)
        for k in range(6):
            sx = sxs[k % 2]
            sk = f"sx{k % 2}"
            c0 = 384 * k

            def stepA1(k=k, sx=sx, sk=sk):
                sxv = sx.rearrange("p (i j) -> p i j", j=48)
                P.op("dve", lambda e: e.tensor_tensor(sxv, Ap[:, 8 * k:8 * k + 8].unsqueeze(2).to_broadcast([128, 8, 48]),
                                                     Bp.unsqueeze(1).to_broadcast([128, 8, 48]), ALU.add), reads=KA, writes=[sk])

            def stepA2(sx=sx, sk=sk):
                dve(lambda e: e.tensor_copy(sy, sx), [sk], ["sy"])

            def stepA3(sx=sx, sk=sk):
                P.op("dve", lambda e: e.tensor_tensor(sx, sx, sy, ALU.subtract), reads=[sk, "sy"], writes=[sk])

            def stepB(sx=sx, sk=sk, c0=c0):
                P.op("act", lambda e: e.activation(sinT[:, c0:c0 + 384], sx, AF.Sin, scale=TWO_PI), reads=[sk], writes=[tk])

            def stepC(sx=sx, sk=sk, c0=c0):
                P.op("act", lambda e: e.activation(sx, sx, AF.Sin, scale=math.pi), reads=[sk], writes=[sk])
                P.op("act", lambda e: e.activation(sx, sx, AF.Square), reads=[sk], writes=[sk])
                P.op("act", lambda e: e.activation(cosT[:, c0:c0 + 384], sx, AF.Identity, bias=1.0, scale=-2.0), reads=[sk], writes=[tk])
            steps += [stepA1, stepA2, stepA3, stepB, stepC]
        return steps

    def s5_unit_compute(self, ti, d, j, uidx, lB, lC, rmag, inserts):
        P = self.P
        tabs, gin, B3f, prod = self.s5_views()
        cosT, sinT = tabs[uidx % 2]
        tk = f"tab{uidx % 2}"
        g_re, g_im, ub, _ = self.B
        idx = d * 16 + 4 * ti + j
        rm = rmag[:, idx:idx + 1]
        bre, bim, tA, tB = self.tmpB
        dve = lambda fn, r, w: P.op("dve", fn, reads=r, writes=w)
        nslots = 27
        total = len(inserts)
        state = {"slot": 0, "done": 0}

        def slot_end():
            state["slot"] += 1
            target = (state["slot"] * total + nslots - 1) // nslots
            while state["done"] < min(target, total):
                inserts[state["done"]]()
                state["done"] += 1

        def tcols(k):
            s0, s1 = BLKS[k]
            if d == 0:
                return s0, s1, k, False
            if k == 0:
                return 0, CTX, 0, True
            return 2560 - s1, 2560 - s0, 5 - k, True

        for k, (c0, c1) in enumerate(BLKS):
            w_ = c1 - c0
            t0, t1, bt, rev = tcols(k)
            ubv = ub[:, t0:t1][:, ::-1] if rev else ub[:, t0:t1]
            br_, bi_ = 5 + (2 * k) % 3, 5 + (2 * k + 1) % 3
            if k % 2 == 0:
                cre, cim, kre, kim = bre, bim, "tmpB0", "tmpB1"
            else:
                cre, cim, kre, kim = prod[0], prod[1], "prod0", "prod1"
            self.mm(self.ps[br_][:, 0:w_], lB(j, 0), ubv, True, True, [("lB", j), "B2"], [f"ps{br_}"])
            self.mm(self.ps[bi_][:, 0:w_], lB(j, 1), ubv, True, True, [("lB", j), "B2"], [f"ps{bi_}"])
            P.op("act", lambda e, w_=w_, cre=cre, br_=br_: e.activation(cre[:, 0:w_], self.ps[br_][:, 0:w_], AF.Copy), reads=[f"ps{br_}"], writes=[kre])
            P.op("act", lambda e, w_=w_, cim=cim, bi_=bi_: e.activation(cim[:, 0:w_], self.ps[bi_][:, 0:w_], AF.Copy), reads=[f"ps{bi_}"], writes=[kim])
            cv, sv = cosT[:, c0:c1], sinT[:, c0:c1]
            tC, tD = prod[2], prod[3]
            dve(lambda e, w_=w_, cv=cv, cre=cre: e.tensor_tensor(tA[:, 0:w_], cre[:, 0:w_], cv, ALU.mult), [kre, tk], ["tmpB2"])
            dve(lambda e, w_=w_, sv=sv, cim=cim: e.tensor_tensor(tB[:, 0:w_], cim[:, 0:w_], sv, ALU.mult), [kim, tk], ["tmpB3"])
            slot_end()
            dve(lambda e, w_=w_, cv=cv, cim=cim: e.tensor_tensor(tC[:, 0:w_], cim[:, 0:w_], cv, ALU.mult), [kim, tk], ["prod2"])
            dve(lambda e, w_=w_, sv=sv, cre=cre: e.tensor_tensor(tD[:, 0:w_], cre[:, 0:w_], sv, ALU.mult), [kre, tk], ["prod3"])
            slot_end()
            dve(lambda e, w_=w_, c0=c0, c1=c1: e.tensor_tensor(gin[0][:, c0:c1], tA[:, 0:w_], tB[:, 0:w_], ALU.add), ["tmpB2", "tmpB3"], ["ginr"])
            dve(lambda e, w_=w_, c0=c0, c1=c1: e.tensor_tensor(gin[1][:, c0:c1], tC[:, 0:w_], tD[:, 0:w_], ALU.subtract), ["prod2", "prod3"], ["gini"])
            slot_end()
        for part, (dst, dk, gk) in enumerate([(g_re, "B0", "ginr"), (g_im, "B1", "gini")]):
            src = gin[part]
            dve(lambda e, dst=dst, src=src: e.tensor_tensor_scan(dst[:, 0:T], rm.to_broadcast([128, T]), src, 0.0, ALU.mult, ALU.add),
                [gk, "rmag"], [dk])
            slot_end()
        for k, (c0, c1) in enumerate(BLKS):
            w_ = c1 - c0
            t0, t1, bt, rev = tcols(k)
            cv, sv = cosT[:, c0:c1], sinT[:, c0:c1]
            plist = [(cv, g_re, "B0", 0), (sv, g_im, "B1", 1), (sv, g_re, "B0", 2), (cv, g_im, "B1", 2)]
            for pi, (tab, gg, gk, var) in enumerate(plist):
                pt, pk = (prod[pi], f"prod{pi}") if k % 2 == 0 else (self.tmpB[pi], f"tmpB{pi}")
                dve(lambda e, pt=pt, tab=tab, gg=gg, c0=c0, c1=c1, w_=w_: e.tensor_tensor(pt[:, 0:w_], tab, gg[:, c0:c1], ALU.mult),
                    [tk, gk], [pk])
                first = (d == 0 and j == 0 and pi == 0)
                rhs = pt[:, 0:w_][:, ::-1] if rev else pt[:, 0:w_]
                self.mm(self.ps[bt][:, 0:w_], lC(j, var), rhs, first, False, [("lC", j), pk], [f"ps{bt}"])
                if pi % 2 == 1:
                    slot_end()
        while state["done"] < total:
            inserts[state["done"]]()
            state["done"] += 1

    def dump(self, slot, ap, key, bf=False):
        if not self.debug:
            return
        dst = (self.dbgb if bf else self.dbgf)[slot, 0:ap.shape[0], 0:ap.shape[1]]
        self.out_toks.append(self.P.dma(dst, ap, reads=[key]))

    def barrier(self):
        P = self.P
        toks = [(P.sem[e], P.count[e]) for e in ENGS if P.count[e] > 0]
        toks += [(s, 16 * c) for s, c in zip(P.dma_sems, P.dma_cnt) if c > 0]
        for e in ENGS:
            P._emit_waits(e, toks)
        P.lastw.clear()
        P.readers.clear()

    def rope_tables(self):
        P = self.P
        yf = self.ygrp[:].rearrange("p s t -> p (s t)").bitcast(F32)
        posr = yf[0:64, 0:2048]
        posc = yf[0:64, 2048:4096]
        sm = self.small
        pidx = sm[0:64, 80:81].bitcast(I32)
        p16 = sm[0:64, 81:82].bitcast(I32)
        pb16 = sm[0:64, 82:83].bitcast(I32)
        invt = sm[0:64, 83:84]
        mA = sm[0:64, 84:85]
        mB = sm[0:64, 85:86]
        halfpi = sm[0:64, 86:87]
        LWb = self.LW[:].bitcast(BF16)
        self.cosT = LWb[0:64, 0:2048]
        self.sinT = LWb[0:64, 2048:4096]
        K = ["ropescr"]
        P.op("pool", lambda e: e.iota(posr.rearrange("p (r c) -> p r c", c=64), [[1, 32], [0, 64]], base=0, channel_multiplier=0,
                                      allow_small_or_imprecise_dtypes=True), writes=["posr"])
        P.op("pool", lambda e: e.iota(posc.rearrange("p (r c) -> p r c", c=64), [[0, 32], [1, 64]], base=0, channel_multiplier=0,
                                      allow_small_or_imprecise_dtypes=True), writes=["posc"])
        P.op("pool", lambda e: e.iota(pidx, [[0, 1]], base=0, channel_multiplier=1), writes=["pidx"])
        P.op("dve", lambda e: e.tensor_single_scalar(p16, pidx, 15, ALU.bitwise_and), reads=["pidx"], writes=["p16"])
        P.op("dve", lambda e: e.tensor_single_scalar(pb16, pidx, 16, ALU.bitwise_and), reads=["pidx"], writes=["pb16"])
        P.op("dve", lambda e: e.tensor_copy(invt, p16), reads=["p16"], writes=["invt"])
        P.op("act", lambda e: e.activation(invt, invt, AF.Exp, scale=-math.log(10000.0) / 16.0), reads=["invt"], writes=["invt"])
        P.op("dve", lambda e: e.tensor_scalar_mul(invt, invt, 1.0 / TWO_PI), reads=["invt"], writes=["invt"])
        P.op("dve", lambda e: e.tensor_copy(mB, pb16), reads=["pb16"], writes=["mB"])
        P.op("dve", lambda e: e.tensor_scalar_mul(mB, mB, 1.0 / 16.0), reads=["mB"], writes=["mB"])
        P.op("dve", lambda e: e.tensor_scalar(mA, mB, -1.0, 1.0, ALU.mult, ALU.add), reads=["mB"], writes=["mA"])
        P.op("dve", lambda e: e.memset(halfpi, math.pi / 2), writes=["halfpi"])
        P.op("dve", lambda e: e.tensor_scalar_mul(posr, posr, mA), reads=["posr", "mA"], writes=["posr"])
        P.op("dve", lambda e: e.scalar_tensor_tensor(posr, posc, mB, posr, ALU.mult, ALU.add), reads=["posr", "posc", "mB"], writes=["posr"])
        P.op("dve", lambda e: e.tensor_scalar_mul(posr, posr, invt), reads=["posr", "invt"], writes=["posr"])
        pci = posc.bitcast(I32)
        P.op("dve", lambda e: e.tensor_copy(pci, posr), reads=["posr"], writes=["posc"])
        P.op("dve", lambda e: e.tensor_tensor(posr, posr, pci, ALU.subtract), reads=["posr", "posc"], writes=["posr"])
        P.op("act", lambda e: e.activation(self.sinT, posr, AF.Sin, scale=TWO_PI), reads=["posr"], writes=["sinT"])
        P.op("dve", lambda e: e.scalar_tensor_tensor(posr, posr, -1.0, posr, ALU.mult, ALU.max), reads=["posr"], writes=["posr"])
        P.op("act", lambda e: e.activation(self.cosT, posr, AF.Sin, bias=halfpi, scale=-TWO_PI), reads=["posr", "halfpi"], writes=["cosT"])
        self.Rm = LWb[0:64, 4096:4160]
        P.op("pool", lambda e: e.tensor_scalar_mul(self.Rm[:, 0:32], self.identb[0:64, 32:64], -1.0), reads=["identb"], writes=["Rm"])
        P.op("pool", lambda e: e.tensor_copy(self.Rm[:, 32:64], self.identb[0:64, 0:32]), reads=["identb"], writes=["Rm"])

    def rope_apply(self, buf, key, blocks):
        P = self.P
        for (c0, c1) in blocks:
            w = c1 - c0
            ps = self.ps[7]
            self.mm(ps[0:64, 0:w], self.Rm, buf[0:64, c0:c1], True, True, [key, "Rm"], ["ps7"])
            t1 = self.tmpA[0]
            t2 = self.tmpA[1]
            P.op("pool", lambda e, c0=c0, c1=c1, w=w: e.tensor_tensor(t1[0:64, 0:w], buf[0:64, c0:c1], self.cosT[:, c0 - CTX:c1 - CTX], ALU.mult),
                 reads=[key, "cosT"], writes=["tmpA0"])
            P.op("dve", lambda e, c0=c0, c1=c1, w=w: e.tensor_tensor(t2[0:64, 0:w], ps[0:64, 0:w], self.sinT[:, c0 - CTX:c1 - CTX], ALU.mult),
                 reads=["ps7", "sinT"], writes=["tmpA1"])
            P.op("dve", lambda e, c0=c0, c1=c1, w=w: e.tensor_tensor(buf[0:64, c0:c1], t1[0:64, 0:w], t2[0:64, 0:w], ALU.add),
                 reads=["tmpA0", "tmpA1"], writes=[key])

    def proj_tile(self, w, nk, m, src_fn, src_keys, wkey, evac, blks=None):
        for bi, (c0, c1) in enumerate(blks or BLKS):
            bank = self.proj_banks[self.ps_rr % len(self.proj_banks)]
            self.ps_rr += 1
            ps = self.ps[bank]
            for k in range(nk):
                self.mm(ps[0:m, 0:c1 - c0], w[:, k, :], src_fn(k)[:, c0:c1], k == 0, k == nk - 1,
                        [wkey, src_keys[k]], [f"ps{bank}"])
            evac(ps, f"ps{bank}", c0, c1)

    def mla_layer(self, l):
        P = self.P
        o = l // 2
        with_ctx = l < DEPTH - 1
        Win = self.W["mla_w_in"][o]
        Wuq = self.W["mla_w_uq"][o]
        Wukv = self.W["mla_w_ukv"][o]
        Wout = self.W["mla_w_out"][o]
        self.ps_rr = 0
        self.proj_banks = [0, 1, 2, 3, 4, 5, 6]
        self.rope_tables()
        F0b = self.F[0][:].bitcast(BF16)
        F1b = self.F[1][:].bitcast(BF16)
        F2b = self.F[2][:].bitcast(BF16)
        cqn = [F0b[:, 0:T], F0b[:, T:2 * T], F1b[:, 0:T]]
        vh = F1b[:, T:2 * T].rearrange("p (t d) -> p t d", d=128)
        ckvn = [F2b[:, 0:T], F2b[:, T:2 * T]]
        kr, qn, qr, kn = self.B[0], self.B[1], self.B[2], self.B[3]
        yf = self.ygrp[:].rearrange("p s t -> p (s t)").bitcast(F32)
        nsrc = lambda k: self.nT[:, k, :]
        nkeys = [("nT", k) for k in range(NKT)]
        sm = self.small
        gq = sm[:, 96:99]
        gkv = sm[:, 99:101]
        P.dma(gq, self.W["mla_q_norm"][o].rearrange("(k p) -> p k", p=128), writes=["gq"], allow_slow_non_contiguous=True)
        P.dma(gkv, self.W["mla_kv_norm"][o].rearrange("(k p) -> p k", p=128), writes=["gkv"], allow_slow_non_contiguous=True)

        def copy_evac(dst, dkey, m=128, scale=None, eng="act"):
            def f(ps, pkey, c0, c1):
                if eng == "act":
                    if scale is None:
                        P.op("act", lambda e: e.activation(dst[0:m, c0:c1], ps[0:m, 0:c1 - c0], AF.Copy), reads=[pkey], writes=[dkey])
                    else:
                        P.op("act", lambda e: e.activation(dst[0:m, c0:c1], ps[0:m, 0:c1 - c0], AF.Copy, scale=scale), reads=[pkey], writes=[dkey])
                else:
                    P.op("dve", lambda e: e.tensor_copy(dst[0:m, c0:c1], ps[0:m, 0:c1 - c0]), reads=[pkey], writes=[dkey])
            return f

        for k in range(3):
            w, wk = self.load_cast(Win[:, k * 128:(k + 1) * 128], 8, 128)
            self.proj_tile(w, 8, 128, nsrc, nkeys, wk, copy_evac(cqn[k], f"cqn{k}", eng="act" if k % 2 == 0 else "dve"))
        for k in range(2):
            w, wk = self.load_cast(Win[:, 384 + k * 128:384 + (k + 1) * 128], 8, 128)
            self.proj_tile(w, 8, 128, nsrc, nkeys, wk, copy_evac(ckvn[k], f"ckvn{k}", eng="dve" if k % 2 == 0 else "act"))
        w, wk = self.load_cast(Win[:, 640:704], 8, 64)
        self.proj_tile(w, 8, 64, nsrc, nkeys, wk, copy_evac(kr, "B0", m=64))
        rq = yf[:, 0:T]
        rkv = yf[:, T:2 * T]
        self.rstd_into(rq, "rq", cqn, [f"cqn{k}" for k in range(3)], 384)
        self.rstd_into(rkv, "rkv", ckvn, [f"ckvn{k}" for k in range(2)], 256)
        for k in range(3):
            P.op("dve", lambda e, k=k: e.scalar_tensor_tensor(cqn[k], cqn[k], gq[:, k:k + 1], rq, ALU.mult, ALU.mult),
                 reads=[f"cqn{k}", "gq", "rq"], writes=[f"cqn{k}"])
        for k in range(2):
            P.op("dve", lambda e, k=k: e.scalar_tensor_tensor(ckvn[k], ckvn[k], gkv[:, k:k + 1], rkv, ALU.mult, ALU.mult),
                 reads=[f"ckvn{k}", "gkv", "rkv"], writes=[f"ckvn{k}"])
        self.rope_apply(kr, "B0", BLKS[1:])
        P.op("pool", lambda e: e.memset(kr[64:128, 0:T], 0.0), writes=["B0"])
        P.op("pool", lambda e: e.memset(qr[64:128, 0:T], 0.0), writes=["B2"])
        qblks = BLKS if with_ctx else BLKS[1:]
        cq_src = lambda k: cqn[k]
        cq_keys = [f"cqn{k}" for k in range(3)]
        kv_src = lambda k: ckvn[k]
        kv_keys = [f"ckvn{k}" for k in range(2)]
        self.proj_banks = [4, 5, 6]

        def head_loads(h, defer):
            r = [self.load_cast(Wuq[:, h * 192:(h + 1) * 192], 3, 192, defer=defer),
                 self.load_cast(Wukv[:, h * 256:(h + 1) * 256], 2, 256, defer=defer)]
            if not defer:
                r.append(self.load_cast(Win[:, 704 + h * 128:704 + (h + 1) * 128], 8, 128))
            return r
        pend = None
        for h in range(8):
            if pend is None:
                pend = head_loads(h, False)
            (wq, wqk), (wkv, wkvk), (wg, wgk) = [p[0:2] for p in pend]
            pend = None
            self.proj_tile(wq[:, :, 0:128], 3, 128, cq_src, cq_keys, wqk, copy_evac(qn, "B1", scale=MLA_SCALE))
            self.proj_tile(wq[:, :, 128:192], 3, 64, cq_src, cq_keys, wqk, copy_evac(qr, "B2", m=64, scale=MLA_SCALE))
            self.rope_apply(qr, "B2", BLKS[1:])
            self.proj_tile(wkv[:, :, 0:128], 2, 128, kv_src, kv_keys, wkvk, copy_evac(kn, "B3", eng="dve"))
            for t0 in range(0, 18, 4):
                nt = min(4, 18 - t0)
                bank = 4 + (self.ps_rr % 3)
                self.ps_rr += 1
                ps = self.ps[bank]
                for j in range(nt):
                    tt = t0 + j
                    for rk in range(2):
                        self.mm(ps[:, j * 128:(j + 1) * 128], ckvn[rk][:, tt * 128:(tt + 1) * 128], wkv[:, rk, 128:256],
                                rk == 0, rk == 1, [f"ckvn{rk}", wkvk], [f"ps{bank}"])
                P.op("dve", lambda e, ps=ps, t0=t0, nt=nt: e.tensor_copy(
                    vh[:, t0:t0 + nt, :], ps[:, 0:nt * 128].rearrange("p (t d) -> p t d", d=128)),
                    reads=[f"ps{bank}"], writes=["vh"])
            def ev_gate(ps, pkey, c0, c1, h=h):
                P.op("act", lambda e: e.activation(self.ygrp[:, h % 4, c0:c1], ps[:, 0:c1 - c0], AF.Silu), reads=[pkey],
                     writes=[("og", h % 4), "rq", "rkv"])
            self.proj_tile(wg, 8, 128, nsrc, nkeys, wgk, ev_gate, blks=qblks)
            if h + 1 < 8 and h % 4 != 3:
                pend = head_loads(h + 1, True)
            LOOK = 2
            for qi, (c0, c1) in enumerate(qblks):
                if qi == 1 and pend is not None:
                    for p in pend:
                        p[2]("act")
                    d3, k3, c3 = self.load_cast(Win[:, 704 + (h + 1) * 128:704 + (h + 2) * 128], 8, 128, defer=True)
                    c3("act")
                    pend.append((d3, k3))
                w_ = c1 - c0
                nkt = 2 if c0 == 0 else 18
                Ops = self.ps[qi % 2]
                Dps = self.ps[2 + qi % 2]
                okey, dkey = f"ps{qi % 2}", f"ps{2 + qi % 2}"
                Ebuf = {}

                def emit_S(kt, w_=w_, c0=c0, c1=c1):
                    sb = 4 + (self.ps_rr % 3)
                    self.ps_rr += 1
                    S = self.ps[sb]
                    ks = slice(kt * 128, (kt + 1) * 128)
                    self.mm(S[:, 0:w_], kn[:, ks], qn[:, c0:c1], True, False, ["B3", "B1"], [f"ps{sb}"])
                    self.mm(S[:, 0:w_], kr[:, ks], qr[:, c0:c1], False, True, ["B0", "B2"], [f"ps{sb}"])
                    ei = kt % 3
                    E = self.tmpB[ei]
                    P.op("act", lambda e, E=E, S=S, w_=w_: e.activation(E[:, 0:w_], S[:, 0:w_], AF.Exp),
                         reads=[f"ps{sb}"], writes=[f"tmpB{ei}"])
                    Ebuf[kt] = (E, f"tmpB{ei}")

                acc = self.tmpA[2]

                def emit_OD(kt, w_=w_, nkt=nkt, Ops=Ops, Dps=Dps, okey=okey, dkey=dkey):
                    E, ek = Ebuf[kt]
                    self.mm(Ops[:, 0:w_], vh[:, kt, :], E[:, 0:w_], kt == 0, kt == nkt - 1, ["vh", ek], [okey])
                    if kt % 2 == 1:
                        self.mm(Dps[:, 0:w_], self.onesb[:], E[:, 0:w_], kt == 1, False, ["onesb", ek], [dkey])
                    elif kt == 0:
                        P.op("dve", lambda e: e.tensor_copy(acc[:, 0:w_], E[:, 0:w_]), reads=[ek], writes=["tmpA2"])
                    else:
                        P.op("dve", lambda e: e.tensor_tensor(acc[:, 0:w_], acc[:, 0:w_], E[:, 0:w_], ALU.add), reads=[ek, "tmpA2"], writes=["tmpA2"])
                    if kt == nkt - 1:
                        accb = self.tmpB[3]
                        P.op("dve", lambda e: e.tensor_copy(accb[:, 0:w_], acc[:, 0:w_]), reads=["tmpA2"], writes=["tmpB3"])
                        self.mm(Dps[:, 0:w_], self.onesb[:], accb[:, 0:w_], False, True, ["onesb", "tmpB3"], [dkey])
                for kt in range(min(LOOK, nkt)):
                    emit_S(kt)
                for kt in range(nkt):
                    if kt + LOOK < nkt:
                        emit_S(kt + LOOK)
                    emit_OD(kt)
                rec = self.tmpA[0]
                o1 = self.tmpA[1]
                P.op("dve", lambda e, Dps=Dps, w_=w_: e.reciprocal(rec[:, 0:w_], Dps[:, 0:w_]), reads=[dkey], writes=["tmpA0"])
                P.op("dve", lambda e, Ops=Ops, w_=w_: e.tensor_tensor(o1[:, 0:w_], Ops[:, 0:w_], rec[:, 0:w_], ALU.mult),
                     reads=[okey, "tmpA0"], writes=["tmpA1"])
                P.op("pool", lambda e, h=h, c0=c0, c1=c1, w_=w_: e.tensor_tensor(self.ygrp[:, h % 4, c0:c1], o1[:, 0:w_], self.ygrp[:, h % 4, c0:c1], ALU.mult),
                     reads=["tmpA1", ("og", h % 4)], writes=[("og", h % 4)])
            if h % 4 == 3:
                self.out_proj(l, Wout[(h // 4) * 512:(h // 4 + 1) * 512, :], qblks)
        self.barrier()

    def out_proj(self, l, Wrows, blks, nk=4):
        P = self.P
        for ot in range(NKT):
            w, wk = self.load_cast(Wrows[:, ot * 128:(ot + 1) * 128], nk, 128)
            for (c0, c1) in blks:
                s = 1 if c0 == 0 else 0
                bank = self.proj_banks[self.ps_rr % len(self.proj_banks)]
                self.ps_rr += 1
                ps = self.ps[bank]
                for j in range(nk):
                    self.mm(ps[:, 0:c1 - c0], w[:, j, :], self.ygrp[:, j, c0:c1], j == 0, j == nk - 1, [wk, ("og", j)], [f"ps{bank}"])
                P.op("dve", lambda e, ot=ot, c0=c0, c1=c1, s=s, ps=ps: e.scalar_tensor_tensor(
                    self.xT[:, ot, c0:c1], ps[:, 0:c1 - c0], self.modT[:, l, 16 + ot, s:s + 1], self.xT[:, ot, c0:c1], ALU.mult, ALU.add),
                    reads=[f"ps{bank}", "modT", ("xT", ot)], writes=[("xT", ot)])


_NC_CACHE = {}


def _get_nc(layers, debug=False):
    key = (tuple(layers), debug)
    if key not in _NC_CACHE:
        _NC_CACHE[key] = Builder(layers, debug).build()
    return _NC_CACHE[key]


def kernel(_layers=(0, 1, 2, 3), _ncores=8, _debug=False, **inputs):
    nc = _get_nc(_layers, _debug)
    in_maps = []
    for b in range(_ncores):
        m = {}
        for name, shp in WSPEC:
            a = np.asarray(inputs[name], dtype=np.float32)
            if name in ("x", "c", "ctx"):
                a = a[b]
            m[name] = np.ascontiguousarray(a).reshape(shp)
        in_maps.append(m)
    res = run_bass_kernel_spmd(nc, in_maps, core_ids=list(range(_ncores)))
    if _debug:
        return res.results[0]
    return np.stack([np.asarray(r["out"], dtype=np.float32).reshape(SEQ, D) for r in res.results], axis=0)
```

```python
import math
import numpy as np
import concourse.bass as bass
import concourse.mybir as mybir
from concourse.bass_utils import run_bass_kernel_spmd

F32 = mybir.dt.float32
BF16 = mybir.dt.bfloat16
I32 = mybir.dt.int32
AF = mybir.ActivationFunctionType
ALU = mybir.AluOpType
AX = mybir.AxisListType

D = 1024
SEQ = 2048
CTX = 256
T = CTX + SEQ
NKT = 8
DEPTH = 4
EPS = 1e-6
BLKS = [(0, 256), (256, 768), (768, 1280), (1280, 1792), (1792, 2304)]
MLA_SCALE = 1.0 / math.sqrt(192.0)
TWO_PI = 2.0 * math.pi

DBG_D = 0
ENGS = ["pe", "act", "dve", "pool", "sp"]


class Prog:
    def __init__(self, nc):
        self.nc = nc
        self.streams = {e: [] for e in ENGS}
        self.count = {e: 0 for e in ENGS}
        self.sem = {e: nc.alloc_semaphore(name=f"prog_{e}") for e in ENGS}
        self.waited = {}
        self.lastw = {}
        self.readers = {}
        self.dma_sems = [nc.alloc_semaphore(name=f"dma_{i}") for i in range(16)]
        self.dma_cnt = [0] * 16
        self.dma_rr = 0

    def _deps(self, reads, writes):
        deps = []
        for k in reads:
            if k in self.lastw:
                deps.append(self.lastw[k])
        for k in writes:
            if k in self.lastw:
                deps.append(self.lastw[k])
            deps.extend(self.readers.get(k, []))
        return deps

    def _emit_waits(self, eng, deps):
        need = {}
        for (s, v) in deps:
            if eng == "pe" and s is self.sem["pe"]:
                continue
            key = id(s)
            if v > self.waited.get((eng, key), 0):
                if key not in need or need[key][1] < v:
                    need[key] = (s, v)
        for key, (s, v) in need.items():
            self.waited[(eng, key)] = v
            self.streams[eng].append(lambda e, s=s, v=v: e.wait_ge(s, v))

    def _commit(self, tok, reads, writes):
        for k in writes:
            self.lastw[k] = tok
            self.readers[k] = []
        for k in reads:
            if k not in writes:
                self.readers.setdefault(k, []).append(tok)

    def op(self, eng, fn, reads=(), writes=()):
        reads = list(reads)
        writes = list(writes)
        self._emit_waits(eng, self._deps(reads, writes))
        self.count[eng] += 1
        v = self.count[eng]
        s = self.sem[eng]
        self.streams[eng].append(lambda e, fn=fn, s=s: fn(e).then_inc(s, 1))
        tok = (s, v)
        self._commit(tok, reads, writes)
        return tok

    def dma(self, out, in_, reads=(), writes=(), eng="sp", **kw):
        reads = list(reads)
        writes = list(writes)
        i = self.dma_rr
        self.dma_rr = (self.dma_rr + 1) % len(self.dma_sems)
        s = self.dma_sems[i]
        deps = self._deps(reads, writes)
        if self.dma_cnt[i] > 0:
            deps.append((s, 16 * self.dma_cnt[i]))
        self._emit_waits(eng, deps)
        self.dma_cnt[i] += 1
        v = 16 * self.dma_cnt[i]
        self.streams[eng].append(
            lambda e, out=out, in_=in_, s=s, kw=kw: e.dma_start(out=out, in_=in_, **kw).then_inc(s, 16))
        tok = (s, v)
        self._commit(tok, reads, writes)
        return tok

    def finish(self, final_tokens):
        nc = self.nc
        self._emit_waits("sp", final_tokens)
        with nc.Block() as block:
            @block.tensor
            def _(e):
                for f in self.streams["pe"]:
                    f(e)

            @block.scalar
            def _(e):
                for f in self.streams["act"]:
                    f(e)

            @block.vector
            def _(e):
                for f in self.streams["dve"]:
                    f(e)

            @block.gpsimd
            def _(e):
                for f in self.streams["pool"]:
                    f(e)

            @block.sync
            def _(e):
                for f in self.streams["sp"]:
                    f(e)


WSPEC = [
    ("x", [SEQ, D]), ("c", [D]), ("ctx", [CTX, D]), ("c_ctx", [D]),
    ("norm_g", [4, D]), ("mod_w", [4, D, 3 * D]), ("mod_b", [4, 3 * D]),
    ("ev_w_in", [2, D, 3072]), ("lru_conv_w", [2, 4, D]), ("lru_conv_b", [2, D]),
    ("lru_wr", [2, 2, 8, 128, 128]), ("lru_br", [2, 2, D]),
    ("lru_wi", [2, 2, 8, 128, 128]), ("lru_bi", [2, 2, D]), ("lru_lam", [2, 2, D]),
    ("s5_lam_re", [2, 2, 32, 64]), ("s5_lam_im", [2, 2, 32, 64]), ("s5_log_dt", [2, 2, 32, 64]),
    ("s5_b_re", [2, 2, 32, 64, 16]), ("s5_b_im", [2, 2, 32, 64, 16]),
    ("s5_c_re", [2, 2, 32, 16, 64]), ("s5_c_im", [2, 2, 32, 16, 64]),
    ("s5_d", [2, 32, 16]), ("s5_glu_w", [2, 512, 512]), ("s5_glu_b", [2, 512]),
    ("ev_w_out", [2, 1536, D]),
    ("mla_w_in", [2, D, 1728]), ("mla_q_norm", [2, 384]), ("mla_w_uq", [2, 384, 1536]),
    ("mla_kv_norm", [2, 256]), ("mla_w_ukv", [2, 256, 2048]), ("mla_w_out", [2, D, D]),
    ("final_g", [D]),
]


class Builder:
    def __init__(self, layers=(0, 1, 2, 3), debug=False):
        self.layers = list(layers)
        nc = bass.Bass("TRN2", target_bir_lowering=False)
        self.nc = nc
        self.P = Prog(nc)
        self.W = {}
        for name, shp in WSPEC:
            self.W[name] = nc.dram_tensor(name, shp, F32, kind="ExternalInput").ap()
        self.out = nc.dram_tensor("out", [SEQ, D], F32, kind="ExternalOutput").ap()
        self.debug = debug
        if debug:
            self.dbgf = nc.dram_tensor("dbgf", [8, 128, T], F32, kind="ExternalOutput").ap()
            self.dbgb = nc.dram_tensor("dbgb", [8, 128, T], BF16, kind="ExternalOutput").ap()
        A = nc.alloc_sbuf_tensor
        self.xT = A("xT", [128, NKT, T], F32)
        self.nT = A("nT", [128, NKT, T], BF16)
        self.modT = A("modT", [128, DEPTH, 24, 2], F32)
        self.identf = A("identf", [128, 128], F32)
        self.identb = A("identb", [128, 128], BF16)
        self.onesb = A("onesb", [128, 128], BF16)
        self.ygrp = A("ygrp", [128, 4, T], BF16)
        self.F = [A(f"F{i}", [128, T], F32) for i in range(3)]
        self.B = [A(f"B{i}", [128, T + 8], BF16) for i in range(4)]
        self.stage = [A(f"stage{i}", [128, 1024], F32) for i in range(2)]
        self.stage_i = 0
        self.wbf = [A(f"wbf{i}", [128, 1024], BF16) for i in range(3)]
        self.wbf_i = 0
        self.LW = A("LW", [128, 2304], F32)
        self.small = A("small", [128, 256], F32)
        self.tmpA = [A(f"tmpA{i}", [128, 512], F32) for i in range(3)]
        self.tmpB = [A(f"tmpB{i}", [128, 512], BF16) for i in range(4)]
        self.ps = [nc.alloc_psum_tensor(f"ps{i}", [128, 512], F32) for i in range(8)]

    def load_cast(self, src_ap, kt, ncols, dst=None, dst_key=None, defer=False):
        P = self.P
        si = self.stage_i
        self.stage_i = (si + 1) % len(self.stage)
        st = self.stage[si]
        stv = st[:, 0:kt * ncols].rearrange("p (k c) -> p k c", k=kt)
        P.dma(stv, src_ap.rearrange("(k p) c -> p k c", p=128), writes=[f"stage{si}"])
        if dst is None:
            wi = self.wbf_i
            self.wbf_i = (wi + 1) % len(self.wbf)
            dst = self.wbf[wi][:, 0:kt * ncols].rearrange("p (k c) -> p k c", k=kt)
            dst_key = f"wbf{wi}"

        def cast(eng="pool"):
            if eng == "act":
                P.op("act", lambda e: e.activation(dst, stv, AF.Copy), reads=[f"stage{si}"], writes=[dst_key])
            else:
                P.op(eng, lambda e: e.tensor_copy(dst, stv), reads=[f"stage{si}"], writes=[dst_key])
        if defer:
            return dst, dst_key, cast
        cast()
        return dst, dst_key

    def mm(self, out, lhsT, rhs, start, stop, reads, writes):
        return self.P.op("pe", lambda e: e.matmul(out, lhsT, rhs, start=start, stop=stop),
                         reads=reads, writes=writes)

    def consts(self):
        P = self.P
        idf, idb, ob = self.identf, self.identb, self.onesb
        P.op("pool", lambda e: e.memset(idf[:], 0.0), writes=["identf"])
        P.op("pool", lambda e: e.affine_select(idf[:], idf[:], [[-1, 128]], ALU.not_equal, 1.0, base=0,
                                               channel_multiplier=1), reads=["identf"], writes=["identf"])
        P.op("pool", lambda e: e.tensor_copy(idb[:], idf[:]), reads=["identf"], writes=["identb"])
        P.op("pool", lambda e: e.memset(ob[:], 1.0), writes=["onesb"])

    def load_x(self):
        P = self.P
        for tt in range(T // 128):
            st = self.stage[tt % 2]
            skey = f"stage{tt % 2}"
            src = self.W["ctx"][tt * 128:(tt + 1) * 128, :] if tt < 2 else self.W["x"][(tt - 2) * 128:(tt - 1) * 128, :]
            P.dma(st[:], src, writes=[skey])
            for half in range(2):
                ps = self.ps[(tt * 2 + half) % 8]
                pkey = f"ps{(tt * 2 + half) % 8}"
                for q in range(4):
                    kt = half * 4 + q
                    P.op("pe", lambda e, ps=ps, q=q, kt=kt, st=st: e.transpose(
                        ps[:, q * 128:(q + 1) * 128], st[:, kt * 128:(kt + 1) * 128], self.identf[:]),
                        reads=[skey, "identf"], writes=[pkey])
                dst = self.xT[:, half * 4:half * 4 + 4, tt * 128:(tt + 1) * 128]
                srcp = ps[:].rearrange("p (q c) -> p q c", q=4)
                eng = "dve" if half == 0 else "act"
                if eng == "dve":
                    P.op("dve", lambda e, dst=dst, srcp=srcp: e.tensor_copy(dst, srcp), reads=[pkey],
                         writes=[("xT", half * 4 + q) for q in range(4)])
                else:
                    P.op("act", lambda e, dst=dst, srcp=srcp: e.activation(dst, srcp, AF.Copy), reads=[pkey],
                         writes=[("xT", half * 4 + q) for q in range(4)])

    def modulation(self):
        P = self.P
        sm = self.small
        cc = sm[:, 0:16]
        csb = sm[:, 16:24].bitcast(BF16)
        P.dma(cc[:, 0:8], self.W["c"].rearrange("(k p) -> p k", p=128), writes=["cc"], allow_slow_non_contiguous=True)
        P.dma(cc[:, 8:16], self.W["c_ctx"].rearrange("(k p) -> p k", p=128), writes=["cc"], allow_slow_non_contiguous=True)
        P.op("act", lambda e: e.activation(csb, cc, AF.Silu), reads=["cc"], writes=["cs"])
        csv = csb.rearrange("p (s k) -> p k s", s=2)
        nTf = self.nT[:].rearrange("p k t -> p (k t)").bitcast(F32)
        stg = [nTf[:, i * 1024:(i + 1) * 1024] for i in range(6)]
        nxt = 0
        for l in range(DEPTH):
            for kt in range(NKT):
                for h in range(3):
                    si = nxt % 6
                    nxt += 1
                    st = stg[si]
                    P.dma(st, self.W["mod_w"][l, kt * 128:(kt + 1) * 128, h * 1024:(h + 1) * 1024], writes=[f"mstg{si}"])
                    wi = self.wbf_i
                    self.wbf_i = (wi + 1) % len(self.wbf)
                    wb = self.wbf[wi]
                    eng = "act" if nxt % 2 == 0 else "dve"
                    if eng == "act":
                        P.op("act", lambda e, wb=wb, st=st: e.activation(wb[:], st, AF.Copy), reads=[f"mstg{si}"], writes=[f"wbf{wi}"])
                    else:
                        P.op("dve", lambda e, wb=wb, st=st: e.tensor_copy(wb[:], st), reads=[f"mstg{si}"], writes=[f"wbf{wi}"])
                    for q in range(2):
                        bank = h * 2 + q
                        self.mm(self.ps[bank][0:2, :], csv[:, kt, :], wb[:, q * 512:(q + 1) * 512],
                                kt == 0, kt == NKT - 1, ["cs", f"wbf{wi}"], [f"ps{bank}"])
            mrow = nTf[:, 6144:9216][0:2, :]
            bro = self.F[0][0:2, 0:2304]
            bro2 = self.F[1][0:2, 0:768]
            for s_ in range(2):
                P.dma(bro[s_:s_ + 1, :], self.W["mod_b"][l:l + 1, 0:2304], writes=["bro", "F0"])
                P.dma(bro2[s_:s_ + 1, :], self.W["mod_b"][l:l + 1, 2304:3072], writes=["bro", "F1"])
            for bank in range(6):
                c0 = bank * 512
                if c0 + 512 <= 2304:
                    bsrc = bro[:, c0:c0 + 512]
                    P.op("dve", lambda e, bank=bank, bsrc=bsrc, c0=c0: e.tensor_tensor(
                        mrow[:, c0:c0 + 512], self.ps[bank][0:2, :], bsrc, ALU.add), reads=[f"ps{bank}", "bro", "F0", "F1"], writes=["mrow"])
                else:
                    for (a0, a1) in [(c0, min(c0 + 512, 2304)), (max(c0, 2304), c0 + 512)]:
                        if a1 <= a0:
                            continue
                        bsrc = bro[:, a0:a1] if a1 <= 2304 else bro2[:, a0 - 2304:a1 - 2304]
                        P.op("dve", lambda e, bank=bank, bsrc=bsrc, a0=a0, a1=a1, c0=c0: e.tensor_tensor(
                            mrow[:, a0:a1], self.ps[bank][0:2, a0 - c0:a1 - c0], bsrc, ALU.add), reads=[f"ps{bank}", "bro", "F0", "F1"], writes=["mrow"])
            for j in range(24):
                self.mm(self.ps[6][:, 2 * j:2 * j + 2], mrow[:, j * 128:(j + 1) * 128], self.identf[0:2, 0:2],
                        True, True, ["mrow", "identf"], ["ps6"])
            P.op("dve", lambda e, l=l: e.tensor_copy(self.modT[:, l].rearrange("p j s -> p (j s)"), self.ps[6][:, 0:48]),
                 reads=["ps6"], writes=["modT"])

    def norm_mod(self, l):
        P = self.P
        sm = self.small
        g = sm[:, 32:40]
        gm = sm[:, 40:56]
        P.dma(g, self.W["norm_g"][l].rearrange("(k p) -> p k", p=128), writes=["g"], allow_slow_non_contiguous=True)
        for s in range(2):
            P.op("dve", lambda e, s=s: e.scalar_tensor_tensor(
                gm[:, s * 8:(s + 1) * 8], self.modT[:, l, 8:16, s], 1.0, g, ALU.add, ALU.mult),
                reads=["modT", "g"], writes=["gm"])
        rstd = self.F[1]
        self.rstd_into(rstd, "F1", [self.xT[:, kt, :] for kt in range(NKT)], [("xT", kt) for kt in range(NKT)], D)
        for kt in range(NKT):
            for s, (c0, c1) in enumerate([(CTX, T), (0, CTX)]):
                tmp = self.F[2]
                P.op("dve", lambda e, kt=kt, s=s, c0=c0, c1=c1: e.scalar_tensor_tensor(
                    tmp[:, c0:c1], self.xT[:, kt, c0:c1], gm[:, s * 8 + kt:s * 8 + kt + 1], rstd[:, c0:c1], ALU.mult, ALU.mult),
                    reads=[("xT", kt), "gm", "F1"], writes=["F2"])
                P.op("act", lambda e, kt=kt, s=s, c0=c0, c1=c1: e.activation(
                    self.nT[:, kt, c0:c1], tmp[:, c0:c1], AF.Identity, bias=self.modT[:, l, kt, s:s + 1], scale=1.0),
                    reads=["F2", "modT"], writes=[("nT", kt)])

    def rstd_into(self, dst, dst_key, srcs, src_keys, dim, cols=(0, T)):
        P = self.P
        blks = [b for b in BLKS if b[0] >= cols[0] and b[1] <= cols[1]]
        for bi, (c0, c1) in enumerate(blks):
            pb = self.ps[7]
            n = len(srcs)
            for i, (s_ap, sk) in enumerate(zip(srcs, src_keys)):
                sq = self.tmpB[i % 2]
                P.op("act", lambda e, sq=sq, s_ap=s_ap, c0=c0, c1=c1: e.activation(sq[:, 0:c1 - c0], s_ap[:, c0:c1], AF.Square),
                     reads=[sk], writes=[f"tmpB{i % 2}"])
                self.mm(pb[:, 0:c1 - c0], self.onesb[:], sq[:, 0:c1 - c0], i == 0, i == n - 1,
                        [f"tmpB{i % 2}", "onesb"], ["ps7"])
            P.op("act", lambda e, c0=c0, c1=c1: e.activation(dst[:, c0:c1], pb[:, 0:c1 - c0], AF.Sqrt, bias=EPS, scale=1.0 / dim),
                 reads=["ps7"], writes=[dst_key])
            P.op("dve", lambda e, c0=c0, c1=c1: e.reciprocal(dst[:, c0:c1], dst[:, c0:c1]), reads=[dst_key], writes=[dst_key])

    def final(self):
        P = self.P
        gb = self.F[0]
        P.dma(gb[:, 0:D], self.W["final_g"].partition_broadcast(128), writes=["F0"])
        ot = self.F[1]
        for tt in range(SEQ // 128):
            c0 = CTX + tt * 128
            ss = self.small[:, 64 + (tt % 2) * 2:64 + (tt % 2) * 2 + 2]
            sskey = f"ss{tt % 2}"
            pss = []
            P.op("pool", lambda e, ss=ss: e.memset(ss, 0.0), writes=[sskey + "0", sskey + "1"])
            for half in range(2):
                bank = (tt * 2 + half) % 4
                ps = self.ps[bank]
                for q in range(4):
                    kt = half * 4 + q
                    P.op("pe", lambda e, ps=ps, q=q, kt=kt, c0=c0: e.transpose(
                        ps[:, q * 128:(q + 1) * 128], self.xT[:, kt, c0:c0 + 128], self.identf[:]),
                        reads=[("xT", kt), "identf"], writes=[f"ps{bank}"])
                junk = self.tmpA[half]
                P.op("act", lambda e, ps=ps, junk=junk, half=half, ss=ss: e.activation(
                    junk[:], ps[:], AF.Square, accum_out=ss[:, half:half + 1]),
                    reads=[f"ps{bank}"], writes=[f"tmpA{half}", sskey + str(half)])
                pss.append((ps, bank))
            rs = self.small[:, 72 + (tt % 2):73 + (tt % 2)]
            rkey = f"rs{tt % 2}"
            P.op("dve", lambda e, ss=ss, rs=rs: e.tensor_tensor(rs, ss[:, 0:1], ss[:, 1:2], ALU.add),
                 reads=[sskey + "0", sskey + "1"], writes=[rkey])
            P.op("act", lambda e, rs=rs: e.activation(rs, rs, AF.Sqrt, bias=EPS, scale=1.0 / D), reads=[rkey], writes=[rkey])
            P.op("dve", lambda e, rs=rs: e.reciprocal(rs, rs), reads=[rkey], writes=[rkey])
            obuf = ot[:, (tt % 2) * 1024:(tt % 2) * 1024 + 1024]
            okey = f"ot{tt % 2}"
            for half, (ps, bank) in enumerate(pss):
                P.op("dve", lambda e, ps=ps, half=half, rs=rs, obuf=obuf: e.scalar_tensor_tensor(
                    obuf[:, half * 512:(half + 1) * 512], ps[:], rs, gb[:, half * 512:(half + 1) * 512], ALU.mult, ALU.mult),
                    reads=[f"ps{bank}", rkey, "F0"], writes=[okey + str(half)])
            self.out_toks.append(P.dma(self.out[tt * 128:(tt + 1) * 128, :], obuf, reads=[okey + "0", okey + "1"]))

    def build(self):
        self.out_toks = []
        self.consts()
        self.load_x()
        self.modulation()
        for l in self.layers:
            self.norm_mod(l)
            if l % 2 == 0:
                self.even_layer(l)
            else:
                self.mla_layer(l)
        self.final()
        self.P.finish(self.out_toks)
        return self.nc


    def even_layer(self, l):
        P = self.P
        e_ = l // 2
        Win = self.W["ev_w_in"][e_]
        Wout = self.W["ev_w_out"][e_]
        self.ps_rr = 0
        self.proj_banks = [0, 1, 2, 3, 4, 5, 6, 7]
        sm = self.small
        cw = sm[:, 104:136].rearrange("p (k t) -> p k t", k=4)
        cb = sm[:, 136:144]
        br = sm[:, 144:160].rearrange("p (d t) -> p d t", d=2)
        bi = sm[:, 160:176].rearrange("p (d t) -> p d t", d=2)
        coef = sm[:, 176:192].rearrange("p (d t) -> p d t", d=2)
        coef2 = sm[:, 192:208].rearrange("p (d t) -> p d t", d=2)
        ld = lambda dst, src, key: P.dma(dst, src, writes=[key], allow_slow_non_contiguous=True)
        for k in range(4):
            ld(cw[:, k, :], self.W["lru_conv_w"][e_, k].rearrange("(t p) -> p t", p=128), "cw")
        ld(cb, self.W["lru_conv_b"][e_].rearrange("(t p) -> p t", p=128), "cb")
        for d in range(2):
            ld(br[:, d, :], self.W["lru_br"][e_, d].rearrange("(t p) -> p t", p=128), "br")
            ld(bi[:, d, :], self.W["lru_bi"][e_, d].rearrange("(t p) -> p t", p=128), "bi")
            ld(coef[:, d, :], self.W["lru_lam"][e_, d].rearrange("(t p) -> p t", p=128), "coef")
        cf = sm[:, 176:192]
        cf2 = sm[:, 192:208]
        P.op("act", lambda e: e.activation(cf, cf, AF.Exp, scale=-1.0), reads=["coef"], writes=["coef"])
        P.op("act", lambda e: e.activation(cf, cf, AF.Ln, bias=1.0), reads=["coef"], writes=["coef"])
        P.op("dve", lambda e: e.tensor_scalar_mul(cf2, cf, -16.0), reads=["coef"], writes=["coef2"])
        P.op("dve", lambda e: e.tensor_scalar_mul(cf, cf, -8.0), reads=["coef", "coef2"], writes=["coef"])
        nsrc = lambda k: self.nT[:, k, :]
        nkeys = [("nT", k) for k in range(NKT)]
        xap, u, ib, hf = self.B
        F0, F1, F2 = self.F
        dg = [self.tmpB[0][:, k * 128:(k + 1) * 128] for k in range(4)]
        F2b = F2[:].bitcast(BF16)
        hr = F2b[:, 0:T]
        abuf = [(F1, "F1"), (self.LW, "LWa")]
        ibuf = [(ib, "B2"), (F2b[:, T:2 * T], "ibB")]
        for (a, b) in [(0, 2), (258, 262), (2310, 2312)]:
            P.op("pool", lambda e, a=a, b=b: e.memset(xap[:, a:b], 0.0), writes=["B0"])

        def xoff(c0):
            return c0 + 2 if c0 < CTX else c0 + 6

        def lru_loads(h):
            r = {"wxa": self.load_cast(Win[:, h * 128:(h + 1) * 128], 8, 128),
                 "wga": self.load_cast(Win[:, 1024 + h * 128:1024 + (h + 1) * 128], 8, 128)}
            gwb = self.tmpB[2 + h % 2]
            for d in range(2):
                for q, nm in enumerate(["lru_wr", "lru_wi"]):
                    off = (d * 2 + q) * 128
                    dst = gwb[:, off:off + 128].rearrange("p (k c) -> p k c", k=1)
                    r[(d, q)] = self.load_cast(self.W[nm][e_, d, h], 1, 128, dst=dst, dst_key=("gw", h % 2, d, q))
            return r

        pend = None
        xa_pending = None
        for h in range(8):
            if pend is None:
                pend = lru_loads(h)
            Wt = pend
            pend = None
            wxa, wxak = Wt["wxa"]
            wga, wgak = Wt["wga"]

            def ev_xa(ps, pkey, c0, c1):
                P.op("act", lambda e: e.activation(xap[:, xoff(c0):xoff(c0) + c1 - c0], ps[:, 0:c1 - c0], AF.Copy), reads=[pkey], writes=["B0"])
            if xa_pending is not None:
                for (ps_, pk_, c0_, c1_) in xa_pending:
                    ev_xa(ps_, pk_, c0_, c1_)
                xa_pending = None
                self.proj_banks = [0, 1, 2, 3, 4, 5, 6, 7]
            else:
                self.proj_tile(wxa, 8, 128, nsrc, nkeys, wxak, ev_xa)
            for k in range(4):
                P.op("pool", lambda e, k=k, h=h: e.tensor_scalar_mul(dg[k], self.identb[:], cw[:, k, h:h + 1]), reads=["identb", "cw"], writes=[f"dg{k}", "tmpB0"])
            for (c0, c1) in BLKS:
                bank = self.proj_banks[self.ps_rr % len(self.proj_banks)]
                self.ps_rr += 1
                ps = self.ps[bank]
                for k in range(4):
                    o0 = xoff(c0) + k - 2
                    self.mm(ps[:, 0:c1 - c0], dg[k], xap[:, o0:o0 + c1 - c0], k == 0, k == 3, [f"dg{k}", "B0"], [f"ps{bank}"])
                P.op("act", lambda e, ps=ps, c0=c0, c1=c1, h=h: e.activation(u[:, c0:c1], ps[:, 0:c1 - c0], AF.Identity, bias=cb[:, h:h + 1], scale=1.0),
                     reads=[f"ps{bank}", "cb"], writes=["B1"])
            def ev_g(ps, pkey, c0, c1):
                P.op("act", lambda e: e.activation(xap[:, xoff(c0):xoff(c0) + c1 - c0], ps[:, 0:c1 - c0], AF.Silu), reads=[pkey], writes=["B0"])
            usrc = lambda k: u
            for d in range(2):
                wr, wrk = Wt[(d, 0)]
                wi, wik = Wt[(d, 1)]
                ad, adk = abuf[d]
                ibd, ibk = ibuf[d]

                def ev_r(ps, pkey, c0, c1, d=d, h=h):
                    P.op("act", lambda e: e.activation(F0[:, c0:c1], ps[:, 0:c1 - c0], AF.Sigmoid, bias=br[:, d, h:h + 1], scale=1.0),
                         reads=[pkey, "br"], writes=["F0"])

                def ev_i(ps, pkey, c0, c1, d=d, h=h, ibd=ibd, ibk=ibk):
                    P.op("act", lambda e: e.activation(ibd[:, c0:c1], ps[:, 0:c1 - c0], AF.Sigmoid, bias=bi[:, d, h:h + 1], scale=1.0),
                         reads=[pkey, "bi"], writes=[ibk])
                self.proj_tile(wr, 1, 128, usrc, ["B1"], wrk, ev_r)
                self.proj_tile(wi, 1, 128, usrc, ["B1"], wik, ev_i)
                if d == 1 and pend is not None:
                    nwxa, nwxak = pend["wxa"]
                    xa_pending = []
                    for bi_, (c0_, c1_) in enumerate(BLKS):
                        bank_ = 3 + bi_
                        for k_ in range(NKT):
                            self.mm(self.ps[bank_][:, 0:c1_ - c0_], nwxa[:, k_, :], self.nT[:, k_, c0_:c1_], k_ == 0, k_ == NKT - 1,
                                    [nwxak, ("nT", k_)], [f"ps{bank_}"])
                        xa_pending.append((self.ps[bank_], f"ps{bank_}", c0_, c1_))
                    self.proj_banks = [0, 1, 2]
                P.op("act", lambda e, d=d, h=h, ad=ad: e.activation(ad[:, 0:T], F0[:, :], AF.Exp, scale=coef[:, d, h:h + 1]), reads=["F0", "coef"], writes=[adk])
                P.op("act", lambda e, d=d, h=h: e.activation(F0[:, :], F0[:, :], AF.Exp, scale=coef2[:, d, h:h + 1]), reads=["F0", "coef2"], writes=["F0"])
                P.op("act", lambda e: e.activation(F0[:, :], F0[:, :], AF.Sqrt, bias=1.0, scale=-1.0), reads=["F0"], writes=["F0"])
                P.op("dve", lambda e, ibd=ibd: e.tensor_tensor(ibd[:, 0:T], ibd[:, 0:T], u[:, 0:T], ALU.mult), reads=[ibk, "B1"], writes=[ibk])
                P.op("dve", lambda e, ibd=ibd: e.tensor_tensor(ibd[:, 0:T], F0[:, :], ibd[:, 0:T], ALU.mult), reads=["F0", ibk], writes=[ibk])
                if d == 0:
                    P.op("dve", lambda e, ad=ad, ibd=ibd: e.tensor_tensor_scan(hf[:, 0:T], ad[:, 0:T], ibd[:, 0:T], 0.0, ALU.mult, ALU.add),
                         reads=[ibk, adk], writes=["B3"])
                    self.proj_tile(wga, 8, 128, nsrc, nkeys, wgak, ev_g)
                    if h + 1 < 8 and h % 4 != 3:
                        pend = lru_loads(h + 1)
                else:
                    P.op("dve", lambda e, ad=ad, ibd=ibd: e.tensor_tensor_scan(hr[:, 0:CTX][:, ::-1], ad[:, 0:CTX][:, ::-1], ibd[:, 0:CTX][:, ::-1], 0.0,
                                                                              ALU.mult, ALU.add), reads=[ibk, adk], writes=["hr"])
                    P.op("dve", lambda e, ad=ad, ibd=ibd: e.tensor_tensor_scan(hr[:, CTX:T][:, ::-1], ad[:, CTX:T][:, ::-1], ibd[:, CTX:T][:, ::-1], hr[:, 0:1],
                                                                              ALU.mult, ALU.add), reads=[ibk, adk, "hr"], writes=["hr"])
            P.op("dve", lambda e: e.tensor_tensor(hr, hr, hf[:, 0:T], ALU.add), reads=["hr", "B3"], writes=["hr"])
            P.op("dve", lambda e, h=h: e.tensor_tensor(self.ygrp[:, h % 4, 0:CTX], hr[:, 0:CTX], xap[:, 2:2 + CTX], ALU.mult),
                 reads=["hr", "B0"], writes=[("og", h % 4)])
            P.op("dve", lambda e, h=h: e.tensor_tensor(self.ygrp[:, h % 4, CTX:T], hr[:, CTX:T], xap[:, 262:262 + SEQ], ALU.mult),
                 reads=["hr", "B0"], writes=[("og", h % 4)])
            if h % 4 == 3:
                self.out_proj(l, Wout[(h // 4) * 512:(h // 4 + 1) * 512, :], BLKS)
        self.barrier()
        self.s5_phase(l)
        self.barrier()


    def s5_phase(self, l):
        P = self.P
        e_ = l // 2
        Win = self.W["ev_w_in"][e_]
        Wout = self.W["ev_w_out"][e_]
        W = self.W
        F0, F1, F2 = self.F
        g_re, g_im, ub, B3 = self.B
        LW = self.LW
        LWb = LW[:].bitcast(BF16)
        sm = self.small
        tht, rmag = LW[:, 0:32], LW[:, 32:64]
        M16, Mrow, nMrow = LW[:, 64:72], LW[:, 72:74], LW[:, 74:76]
        Braw = [LW[:, 80:144], LW[:, 144:208]]
        Bbar = [LW[:, 208:272], LW[:, 272:336]]
        CT = LW[:, 336:464].rearrange("p (j q h) -> p j q h", j=4, q=2)
        dsk, glub = LW[:, 464:468], LW[:, 468:472]
        Eexp = LW[0:32, 472:600]
        Fc = LW[0:32, 600:856]
        lB = lambda j, q: LWb[:, 1712 + (j * 2 + q) * 128:1712 + (j * 2 + q + 1) * 128]
        lC = lambda j, v: LWb[:, 2736 + (j * 3 + v) * 128:2736 + (j * 3 + v + 1) * 128]
        Dd = LWb[:, 4272:4400]
        iota48 = sm[:, 208:256]
        hpi = sm[:, 87:88]
        tA0, tA1, tA2 = self.tmpA
        ld = lambda dst, src, key: P.dma(dst, src, writes=[key], allow_slow_non_contiguous=True)
        dve = lambda fn, r, w: P.op("dve", fn, reads=r, writes=w)
        act = lambda fn, r, w: P.op("act", fn, reads=r, writes=w)
        pool = lambda fn, r, w: P.op("pool", fn, reads=r, writes=w)
        pool(lambda e: e.iota(iota48, [[1, 48]], base=0, channel_multiplier=0, allow_small_or_imprecise_dtypes=True), [], ["iota48"])
        dve(lambda e: e.memset(hpi, math.pi / 2), [], ["hpi"])
        dve(lambda e: e.tensor_reduce(M16, self.identf[:].rearrange("p (c h) -> p c h", h=16), AX.X, ALU.add), ["identf"], ["M16"])
        dve(lambda e: e.tensor_reduce(Mrow, self.identf[:].rearrange("p (c h) -> p c h", h=64), AX.X, ALU.add), ["identf"], ["Mrow"])
        dve(lambda e: e.tensor_scalar_mul(nMrow, Mrow, -1.0), ["Mrow"], ["nMrow"])
        pool(lambda e: e.memset(LWb[:, 2736:4272], 0.0), [], ["lC"])
        ld(dsk, W["s5_d"][e_].rearrange("(t g) h -> (g h) t", g=8), "dsk")
        ld(glub, W["s5_glu_b"][e_].rearrange("(t p) -> p t", p=128), "glub")
        for i, nm in enumerate(["s5_lam_re", "s5_lam_im", "s5_log_dt"]):
            ld(tA0[:, i * 32:(i + 1) * 32], W[nm][e_].rearrange("d (gp gl) p -> (gl p) (d gp)", gl=2), "tA0")
        act(lambda e: e.activation(tA0[:, 64:96], tA0[:, 64:96], AF.Exp), ["tA0"], ["tA0"])
        dve(lambda e: e.tensor_tensor(tht, tA0[:, 32:64], tA0[:, 64:96], ALU.mult), ["tA0"], ["tht"])
        dve(lambda e: e.tensor_scalar_mul(tht, tht, 1.0 / TWO_PI), ["tht"], ["tht"])
        dve(lambda e: e.tensor_tensor(rmag, tA0[:, 0:32], tA0[:, 64:96], ALU.mult), ["tA0"], ["rmag"])
        act(lambda e: e.activation(rmag, rmag, AF.Exp), ["rmag"], ["rmag"])
        c = lambda i: F0[0:32, i * 128:(i + 1) * 128]
        ci = lambda i: F0[0:32, i * 128:(i + 1) * 128].bitcast(I32)
        for i, nm in enumerate(["s5_lam_re", "s5_lam_im", "s5_log_dt"]):
            ld(c(i).rearrange("g (d p) -> g d p", d=2), W[nm][e_].rearrange("d g p -> g d p"), "F0")
        K0 = ["F0"]
        act(lambda e: e.activation(c(2), c(2), AF.Exp), K0, K0)
        dve(lambda e: e.tensor_tensor(c(3), c(0), c(2), ALU.mult), K0, K0)
        dve(lambda e: e.tensor_tensor(c(4), c(1), c(2), ALU.mult), K0, K0)
        dve(lambda e: e.tensor_scalar_mul(c(4), c(4), 1.0 / TWO_PI), K0, K0)
        dve(lambda e: e.tensor_scalar_mul(c(10), c(4), 0.5), K0, K0)
        act(lambda e: e.activation(c(5), c(3), AF.Exp), K0, K0)
        act(lambda e: e.activation(c(6), c(3), AF.Tanh, scale=0.5), K0, K0)
        dve(lambda e: e.scalar_tensor_tensor(c(6), c(5), 1.0, c(6), ALU.add, ALU.mult), K0, K0)
        dve(lambda e: e.tensor_copy(ci(7), c(4)), K0, K0)
        dve(lambda e: e.tensor_tensor(c(4), c(4), ci(7), ALU.subtract), K0, K0)
        act(lambda e: e.activation(c(8), c(4), AF.Sin, scale=TWO_PI), K0, K0)
        dve(lambda e: e.scalar_tensor_tensor(c(9), c(4), -1.0, c(4), ALU.mult, ALU.max), K0, K0)
        act(lambda e: e.activation(c(9), c(9), AF.Sin, bias=hpi[0:32, :], scale=-TWO_PI), K0 + ["hpi"], K0)
        dve(lambda e: e.tensor_copy(ci(7), c(10)), K0, K0)
        dve(lambda e: e.tensor_tensor(c(10), c(10), ci(7), ALU.subtract), K0, K0)
        act(lambda e: e.activation(c(10), c(10), AF.Sin, scale=TWO_PI), K0, K0)
        dve(lambda e: e.tensor_tensor(c(10), c(10), c(10), ALU.mult), K0, K0)
        dve(lambda e: e.tensor_tensor(c(11), c(6), c(9), ALU.mult), K0, K0)
        dve(lambda e: e.scalar_tensor_tensor(c(11), c(10), -2.0, c(11), ALU.mult, ALU.add), K0, K0)
        dve(lambda e: e.tensor_tensor(c(12), c(5), c(8), ALU.mult), K0, K0)
        dve(lambda e: e.tensor_tensor(c(13), c(0), c(0), ALU.mult), K0, K0)
        dve(lambda e: e.tensor_tensor(c(14), c(1), c(1), ALU.mult), K0, K0)
        dve(lambda e: e.tensor_tensor(c(13), c(13), c(14), ALU.add), K0, K0)
        dve(lambda e: e.reciprocal(c(13), c(13)), K0, K0)
        dve(lambda e: e.tensor_tensor(c(14), c(11), c(0), ALU.mult), K0, K0)
        dve(lambda e: e.tensor_tensor(c(15), c(12), c(1), ALU.mult), K0, K0)
        dve(lambda e: e.tensor_tensor(c(14), c(14), c(15), ALU.add), K0, K0)
        dve(lambda e: e.tensor_tensor(Fc[:, 0:128], c(14), c(13), ALU.mult), K0, ["Fc"])
        dve(lambda e: e.tensor_tensor(c(14), c(12), c(0), ALU.mult), K0, K0)
        dve(lambda e: e.tensor_tensor(c(15), c(11), c(1), ALU.mult), K0, K0)
        dve(lambda e: e.tensor_tensor(c(14), c(14), c(15), ALU.subtract), K0, K0)
        dve(lambda e: e.tensor_tensor(Fc[:, 128:256], c(14), c(13), ALU.mult), K0, ["Fc"])
        nsrc = lambda k: self.nT[:, k, :]
        nkeys = [("nT", k) for k in range(NKT)]

        def tv(X, d, c0, c1):
            if d == 0:
                return X[:, c0:c1]
            if c0 < CTX:
                return X[:, 0:CTX][:, ::-1]
            return X[:, 2560 - c1:2560 - c0][:, ::-1]

        for ti in range(4):
            self.proj_banks = [7]
            wub, wubk = self.load_cast(Win[:, 2048 + ti * 128:2048 + (ti + 1) * 128], 8, 128)

            def ev_u(ps, pkey, c0, c1):
                act(lambda e: e.activation(ub[:, c0:c1], ps[:, 0:c1 - c0], AF.Copy), [pkey], ["B2"])
            self.proj_tile(wub, 8, 128, nsrc, nkeys, wubk, ev_u)
            pool(lambda e, ti=ti: e.tensor_copy(Eexp.rearrange("g (c h) -> g c h", h=16),
                                                self.identf[0:32, 8 * ti:8 * ti + 8].unsqueeze(2).to_broadcast([32, 8, 16])), ["identf"], ["Eexp"])
            pool(lambda e, ti=ti: e.tensor_scalar_mul(Dd, self.identb[:], dsk[:, ti:ti + 1]), ["identb", "dsk"], ["Dd"])
            for d in range(2):
                self.mm(self.ps[7][:, 0:256], Eexp, Fc, True, True, ["Eexp", "Fc"], ["ps7"])
                for q, nm in enumerate(["s5_b_re", "s5_b_im"]):
                    for g8 in range(8):
                        ld(Braw[q][16 * g8:16 * g8 + 16, :], W[nm][e_, d, 8 * ti + g8].rearrange("p h -> h p"), f"Braw{q}")
                for q, nm in enumerate(["s5_c_re", "s5_c_im"]):
                    for gl in range(2):
                        for j in range(4):
                            ld(CT[64 * gl:64 * gl + 64, j, q, :], W[nm][e_, d, 8 * ti + 2 * j + gl].rearrange("h p -> p h"), "CT")
                Fre = self.ps[7][:, d * 64:(d + 1) * 64]
                Fim = self.ps[7][:, 128 + d * 64:128 + (d + 1) * 64]
                t0_, t1_ = tA0[:, 0:64], tA0[:, 64:128]
                dve(lambda e, Fre=Fre: e.tensor_tensor(t0_, Fre, Braw[0], ALU.mult), ["ps7", "Braw0"], ["tA0", "prod0"])
                dve(lambda e, Fim=Fim: e.tensor_tensor(t1_, Fim, Braw[1], ALU.mult), ["ps7", "Braw1"], ["tA0", "prod0"])
                dve(lambda e: e.tensor_tensor(Bbar[0], t0_, t1_, ALU.subtract), ["tA0"], ["Bbar0"])
                dve(lambda e, Fre=Fre: e.tensor_tensor(t0_, Fre, Braw[1], ALU.mult), ["ps7", "Braw1"], ["tA0", "prod0"])
                dve(lambda e, Fim=Fim: e.tensor_tensor(t1_, Fim, Braw[0], ALU.mult), ["ps7", "Braw0"], ["tA0", "prod0"])
                dve(lambda e: e.tensor_tensor(Bbar[1], t0_, t1_, ALU.add), ["tA0"], ["Bbar1"])
                if ti == 0 and d == DBG_D and self.debug:
                    self.dump(3, Fc, "Fc")
                    dve(lambda e: e.tensor_copy(tA1[:, 0:256], self.ps[7][:, 0:256]), ["ps7"], ["tA1"])
                    self.dump(6, tA1[:, 0:256], "tA1")
                    self.dump(7, Braw[0], "Braw0")
                for j in range(4):
                    for q in range(2):
                        for gl in range(2):
                            act(lambda e, j=j, q=q, gl=gl: e.activation(
                                lB(j, q)[:, 64 * gl:64 * gl + 64], Bbar[q], AF.Copy, scale=M16[:, 2 * j + gl:2 * j + gl + 1]),
                                [f"Bbar{q}", "M16"], [("lB", j)])
                    for gl in range(2):
                        cs = slice(32 * j + 16 * gl, 32 * j + 16 * gl + 16)
                        act(lambda e, j=j, gl=gl, cs=cs: e.activation(lC(j, 0)[:, cs], CT[:, j, 0, :], AF.Copy, scale=Mrow[:, gl:gl + 1]), ["CT", "Mrow"], [("lC", j)])
                        act(lambda e, j=j, gl=gl, cs=cs: e.activation(lC(j, 1)[:, cs], CT[:, j, 0, :], AF.Copy, scale=nMrow[:, gl:gl + 1]), ["CT", "nMrow"], [("lC", j)])
                        act(lambda e, j=j, gl=gl, cs=cs: e.activation(lC(j, 2)[:, cs], CT[:, j, 1, :], AF.Copy, scale=nMrow[:, gl:gl + 1]), ["CT", "nMrow"], [("lC", j)])
                for j in range(4):
                    uidx = (ti * 2 + d) * 4 + j
                    if uidx == 0:
                        for st in self.s5_table_steps(0, 0, 0, 0, tht):
                            st()
                    ins = []
                    if uidx + 1 < 32:
                        n_ = uidx + 1
                        ins = self.s5_table_steps(n_ // 8, (n_ // 4) % 2, n_ % 4, n_, tht)
                    self.s5_unit_compute(ti, d, j, uidx, lB, lC, rmag, ins)
            for bi, (c0, c1) in enumerate(BLKS):
                w_ = c1 - c0
                y = self.ps[bi]
                yk = f"ps{bi}"
                self.mm(y[:, 0:w_], Dd, ub[:, c0:c1], False, True, ["Dd", "B2"], [yk])
                tg, tgk = (tA0, "tA0") if bi % 2 == 0 else (tA1, "tA1")
                act(lambda e, y=y, w_=w_, tg=tg: e.activation(tg[:, 0:w_], y[:, 0:w_], AF.Square), [yk], [tgk])
                dve(lambda e, w_=w_, tg=tg: e.tensor_scalar(tg[:, 0:w_], tg[:, 0:w_], 0.044715, 1.0, ALU.mult, ALU.add), [tgk], [tgk])
                dve(lambda e, y=y, w_=w_, tg=tg: e.tensor_tensor(tg[:, 0:w_], tg[:, 0:w_], y[:, 0:w_], ALU.mult), [tgk, yk], [tgk])
                act(lambda e, w_=w_, tg=tg: e.activation(tg[:, 0:w_], tg[:, 0:w_], AF.Sigmoid, scale=2.0 * math.sqrt(2.0 / math.pi)), [tgk], [tgk])
                dve(lambda e, y=y, w_=w_, ti=ti, c0=c0, c1=c1, tg=tg: e.tensor_tensor(self.ygrp[:, ti, c0:c1], y[:, 0:w_], tg[:, 0:w_], ALU.mult),
                    [yk, tgk], [("og", ti)])
        self.proj_banks = [4, 5, 6]
        ysrc = lambda k: self.ygrp[:, k, :]
        ykeys = [("og", k) for k in range(4)]
        for ot in range(4):
            wg, wgk = self.load_cast(W["s5_glu_w"][e_][:, ot * 128:(ot + 1) * 128], 4, 128)

            def ev_z(ps, pkey, c0, c1, ot=ot):
                act(lambda e: e.activation(self.B[ot][:, c0:c1], ps[:, 0:c1 - c0], AF.Sigmoid, bias=glub[:, ot:ot + 1], scale=1.0),
                    [pkey, "glub"], [f"B{ot}"])
            self.proj_tile(wg, 4, 128, ysrc, ykeys, wgk, ev_z)
        for ot in range(4):
            wgb, wgbk = self.load_cast(Win[:, 2560 + ot * 128:2560 + (ot + 1) * 128], 8, 128)

            def ev_gb(ps, pkey, c0, c1, ot=ot):
                sg = self.tmpB[self.ps_rr % 2]
                sk = f"tmpB{self.ps_rr % 2}"
                act(lambda e: e.activation(sg[:, 0:c1 - c0], ps[:, 0:c1 - c0], AF.Silu), [pkey], [sk])
                pool(lambda e: e.tensor_tensor(sg[:, 0:c1 - c0], sg[:, 0:c1 - c0], self.B[ot][:, c0:c1], ALU.mult), [sk, f"B{ot}"], [sk])
                pool(lambda e: e.tensor_tensor(self.ygrp[:, ot, c0:c1], self.ygrp[:, ot, c0:c1], sg[:, 0:c1 - c0], ALU.mult),
                     [sk, ("og", ot)], [("og", ot)])
            self.proj_tile(wgb, 8, 128, nsrc, nkeys, wgbk, ev_gb)
        self.out_proj(l, Wout[1024:1536, :], BLKS)


    def s5_views(self):
        F0b = self.F[0][:].bitcast(BF16)
        F1b = self.F[1][:].bitcast(BF16)
        F2b = self.F[2][:].bitcast(BF16)
        tabs = [(F0b[:, 0:T], F0b[:, T:2 * T]), (F1b[:, 0:T], F1b[:, T:2 * T])]
        gin = (F2b[:, 0:T], F2b[:, T:2 * T])
        B3f = self.B[3][:, 0:2304].bitcast(F32)
        tAb = [self.tmpA[0][:].bitcast(BF16), self.tmpA[1][:].bitcast(BF16)]
        prod = [tAb[0][:, 0:512], tAb[0][:, 512:1024], tAb[1][:, 0:512], tAb[1][:, 512:1024]]
        return tabs, gin, B3f, prod

    def s5_table_steps(self, ti, d, j, uidx, tht):
        P = self.P
        tabs, gin, B3f, prod = self.s5_views()
        cosT, sinT = tabs[uidx % 2]
        tk = f"tab{uidx % 2}"
        sm = self.small
        iota48 = sm[:, 208:256]
        hpi = sm[:, 87:88]
        tA2 = self.tmpA[2]
        idx0 = d * 16 + 4 * ti
        th4 = tht[:, idx0:idx0 + 4]
        Ap4 = tA2[:, 0:192]
        Bp4 = tA2[:, 192:384]
        t48_4 = tA2[:, 384:388]
        kk1_4 = tA2[:, 388:392].bitcast(I32)
        kk = tA2[:, 392:488].bitcast(I32)
        Ap, Bp = Ap4[:, j * 48:(j + 1) * 48], Bp4[:, j * 48:(j + 1) * 48]
        KA = ["tA2"]
        dve = lambda fn, r, w: P.op("dve", fn, reads=r, writes=w)
        sxs = [B3f[:, 0:384], B3f[:, 384:768]]
        sy = B3f[:, 768:1152].bitcast(I32)
        steps = []

        io4 = iota48.unsqueeze(1).to_broadcast([128, 4, 48])
        tiny_ops = [
            lambda: dve(lambda e: e.tensor_scalar_mul(t48_4, th4, 48.0), ["tht"], KA),
            lambda: dve(lambda e: e.tensor_tensor(Bp4.rearrange("p (u i) -> p u i", u=4), io4,
                                                  th4.unsqueeze(2).to_broadcast([128, 4, 48]), ALU.mult), ["tht", "iota48"] + KA, KA),
            lambda: dve(lambda e: e.tensor_copy(kk1_4, t48_4), KA, KA),
            lambda: dve(lambda e: e.tensor_tensor(t48_4, t48_4, kk1_4, ALU.subtract), KA, KA),
            lambda: dve(lambda e: e.tensor_tensor(Ap4.rearrange("p (u i) -> p u i", u=4), io4,
                                                  t48_4.unsqueeze(2).to_broadcast([128, 4, 48]), ALU.mult), KA + ["iota48"], KA),
        ]
        for X in (Bp4, Ap4):
            for hh in range(2):
                xs = X[:, hh * 96:(hh + 1) * 96]
                tiny_ops.append(lambda xs=xs: dve(lambda e: e.tensor_copy(kk, xs), KA, KA))
                tiny_ops.append(lambda xs=xs: dve(lambda e: e.tensor_tensor(xs, xs, kk, ALU.subtract), KA, KA))
        if j == 0:
            steps += tiny_ops
        for k in range(6):
            sx = sxs[k % 2]
            sk = f"sx{k % 2}"
            c0 = 384 * k

            def stepA1(k=k, sx=sx, sk=sk):
                sxv = sx.rearrange("p (i j) -> p i j", j=48)
                P.op("dve", lambda e: e.tensor_tensor(sxv, Ap[:, 8 * k:8 * k + 8].unsqueeze(2).to_broadcast([128, 8, 48]),
                                                     Bp.unsqueeze(1).to_broadcast([128, 8, 48]), ALU.add), reads=KA, writes=[sk])

            def stepA2(sx=sx, sk=sk):
                dve(lambda e: e.tensor_copy(sy, sx), [sk], ["sy"])

            def stepA3(sx=sx, sk=sk):
                P.op("dve", lambda e: e.tensor_tensor(sx, sx, sy, ALU.subtract), reads=[sk, "sy"], writes=[sk])

            def stepB(sx=sx, sk=sk, c0=c0):
                P.op("act", lambda e: e.activation(sinT[:, c0:c0 + 384], sx, AF.Sin, scale=TWO_PI), reads=[sk], writes=[tk])

            def stepC(sx=sx, sk=sk, c0=c0):
                P.op("act", lambda e: e.activation(sx, sx, AF.Sin, scale=math.pi), reads=[sk], writes=[sk])
                P.op("act", lambda e: e.activation(sx, sx, AF.Square), reads=[sk], writes=[sk])
                P.op("act", lambda e: e.activation(cosT[:, c0:c0 + 384], sx, AF.Identity, bias=1.0, scale=-2.0), reads=[sk], writes=[tk])
            steps += [stepA1, stepA2, stepA3, stepB, stepC]
        return steps

    def s5_unit_compute(self, ti, d, j, uidx, lB, lC, rmag, inserts):
        P = self.P
        tabs, gin, B3f, prod = self.s5_views()
        cosT, sinT = tabs[uidx % 2]
        tk = f"tab{uidx % 2}"
        g_re, g_im, ub, _ = self.B
        idx = d * 16 + 4 * ti + j
        rm = rmag[:, idx:idx + 1]
        bre, bim, tA, tB = self.tmpB
        dve = lambda fn, r, w: P.op("dve", fn, reads=r, writes=w)
        nslots = 27
        total = len(inserts)
        state = {"slot": 0, "done": 0}

        def slot_end():
            state["slot"] += 1
            target = (state["slot"] * total + nslots - 1) // nslots
            while state["done"] < min(target, total):
                inserts[state["done"]]()
                state["done"] += 1

        def tcols(k):
            s0, s1 = BLKS[k]
            if d == 0:
                return s0, s1, k, False
            if k == 0:
                return 0, CTX, 0, True
            return 2560 - s1, 2560 - s0, 5 - k, True

        for k, (c0, c1) in enumerate(BLKS):
            w_ = c1 - c0
            t0, t1, bt, rev = tcols(k)
            ubv = ub[:, t0:t1][:, ::-1] if rev else ub[:, t0:t1]
            br_, bi_ = 5 + (2 * k) % 3, 5 + (2 * k + 1) % 3
            if k % 2 == 0:
                cre, cim, kre, kim = bre, bim, "tmpB0", "tmpB1"
            else:
                cre, cim, kre, kim = prod[0], prod[1], "prod0", "prod1"
            self.mm(self.ps[br_][:, 0:w_], lB(j, 0), ubv, True, True, [("lB", j), "B2"], [f"ps{br_}"])
            self.mm(self.ps[bi_][:, 0:w_], lB(j, 1), ubv, True, True, [("lB", j), "B2"], [f"ps{bi_}"])
            P.op("act", lambda e, w_=w_, cre=cre, br_=br_: e.activation(cre[:, 0:w_], self.ps[br_][:, 0:w_], AF.Copy), reads=[f"ps{br_}"], writes=[kre])
            P.op("act", lambda e, w_=w_, cim=cim, bi_=bi_: e.activation(cim[:, 0:w_], self.ps[bi_][:, 0:w_], AF.Copy), reads=[f"ps{bi_}"], writes=[kim])
            cv, sv = cosT[:, c0:c1], sinT[:, c0:c1]
            tC, tD = prod[2], prod[3]
            dve(lambda e, w_=w_, cv=cv, cre=cre: e.tensor_tensor(tA[:, 0:w_], cre[:, 0:w_], cv, ALU.mult), [kre, tk], ["tmpB2"])
            dve(lambda e, w_=w_, sv=sv, cim=cim: e.tensor_tensor(tB[:, 0:w_], cim[:, 0:w_], sv, ALU.mult), [kim, tk], ["tmpB3"])
            slot_end()
            dve(lambda e, w_=w_, cv=cv, cim=cim: e.tensor_tensor(tC[:, 0:w_], cim[:, 0:w_], cv, ALU.mult), [kim, tk], ["prod2"])
            dve(lambda e, w_=w_, sv=sv, cre=cre: e.tensor_tensor(tD[:, 0:w_], cre[:, 0:w_], sv, ALU.mult), [kre, tk], ["prod3"])
            slot_end()
            dve(lambda e, w_=w_, c0=c0, c1=c1: e.tensor_tensor(gin[0][:, c0:c1], tA[:, 0:w_], tB[:, 0:w_], ALU.add), ["tmpB2", "tmpB3"], ["ginr"])
            dve(lambda e, w_=w_, c0=c0, c1=c1: e.tensor_tensor(gin[1][:, c0:c1], tC[:, 0:w_], tD[:, 0:w_], ALU.subtract), ["prod2", "prod3"], ["gini"])
            slot_end()
        for part, (dst, dk, gk) in enumerate([(g_re, "B0", "ginr"), (g_im, "B1", "gini")]):
            src = gin[part]
            dve(lambda e, dst=dst, src=src: e.tensor_tensor_scan(dst[:, 0:T], rm.to_broadcast([128, T]), src, 0.0, ALU.mult, ALU.add),
                [gk, "rmag"], [dk])
            slot_end()
        for k, (c0, c1) in enumerate(BLKS):
            w_ = c1 - c0
            t0, t1, bt, rev = tcols(k)
            cv, sv = cosT[:, c0:c1], sinT[:, c0:c1]
            plist = [(cv, g_re, "B0", 0), (sv, g_im, "B1", 1), (sv, g_re, "B0", 2), (cv, g_im, "B1", 2)]
            for pi, (tab, gg, gk, var) in enumerate(plist):
                pt, pk = (prod[pi], f"prod{pi}") if k % 2 == 0 else (self.tmpB[pi], f"tmpB{pi}")
                dve(lambda e, pt=pt, tab=tab, gg=gg, c0=c0, c1=c1, w_=w_: e.tensor_tensor(pt[:, 0:w_], tab, gg[:, c0:c1], ALU.mult),
                    [tk, gk], [pk])
                first = (d == 0 and j == 0 and pi == 0)
                rhs = pt[:, 0:w_][:, ::-1] if rev else pt[:, 0:w_]
                self.mm(self.ps[bt][:, 0:w_], lC(j, var), rhs, first, False, [("lC", j), pk], [f"ps{bt}"])
                if pi % 2 == 1:
                    slot_end()
        while state["done"] < total:
            inserts[state["done"]]()
            state["done"] += 1

    def dump(self, slot, ap, key, bf=False):
        if not self.debug:
            return
        dst = (self.dbgb if bf else self.dbgf)[slot, 0:ap.shape[0], 0:ap.shape[1]]
        self.out_toks.append(self.P.dma(dst, ap, reads=[key]))

    def barrier(self):
        P = self.P
        toks = [(P.sem[e], P.count[e]) for e in ENGS if P.count[e] > 0]
        toks += [(s, 16 * c) for s, c in zip(P.dma_sems, P.dma_cnt) if c > 0]
        for e in ENGS:
            P._emit_waits(e, toks)
        P.lastw.clear()
        P.readers.clear()

    def rope_tables(self):
        P = self.P
        yf = self.ygrp[:].rearrange("p s t -> p (s t)").bitcast(F32)
        posr = yf[0:64, 0:2048]
        posc = yf[0:64, 2048:4096]
        sm = self.small
        pidx = sm[0:64, 80:81].bitcast(I32)
        p16 = sm[0:64, 81:82].bitcast(I32)
        pb16 = sm[0:64, 82:83].bitcast(I32)
        invt = sm[0:64, 83:84]
        mA = sm[0:64, 84:85]
        mB = sm[0:64, 85:86]
        halfpi = sm[0:64, 86:87]
        LWb = self.LW[:].bitcast(BF16)
        self.cosT = LWb[0:64, 0:2048]
        self.sinT = LWb[0:64, 2048:4096]
        K = ["ropescr"]
        P.op("pool", lambda e: e.iota(posr.rearrange("p (r c) -> p r c", c=64), [[1, 32], [0, 64]], base=0, channel_multiplier=0,
                                      allow_small_or_imprecise_dtypes=True), writes=["posr"])
        P.op("pool", lambda e: e.iota(posc.rearrange("p (r c) -> p r c", c=64), [[0, 32], [1, 64]], base=0, channel_multiplier=0,
                                      allow_small_or_imprecise_dtypes=True), writes=["posc"])
        P.op("pool", lambda e: e.iota(pidx, [[0, 1]], base=0, channel_multiplier=1), writes=["pidx"])
        P.op("dve", lambda e: e.tensor_single_scalar(p16, pidx, 15, ALU.bitwise_and), reads=["pidx"], writes=["p16"])
        P.op("dve", lambda e: e.tensor_single_scalar(pb16, pidx, 16, ALU.bitwise_and), reads=["pidx"], writes=["pb16"])
        P.op("dve", lambda e: e.tensor_copy(invt, p16), reads=["p16"], writes=["invt"])
        P.op("act", lambda e: e.activation(invt, invt, AF.Exp, scale=-math.log(10000.0) / 16.0), reads=["invt"], writes=["invt"])
        P.op("dve", lambda e: e.tensor_scalar_mul(invt, invt, 1.0 / TWO_PI), reads=["invt"], writes=["invt"])
        P.op("dve", lambda e: e.tensor_copy(mB, pb16), reads=["pb16"], writes=["mB"])
        P.op("dve", lambda e: e.tensor_scalar_mul(mB, mB, 1.0 / 16.0), reads=["mB"], writes=["mB"])
        P.op("dve", lambda e: e.tensor_scalar(mA, mB, -1.0, 1.0, ALU.mult, ALU.add), reads=["mB"], writes=["mA"])
        P.op("dve", lambda e: e.memset(halfpi, math.pi / 2), writes=["halfpi"])
        P.op("dve", lambda e: e.tensor_scalar_mul(posr, posr, mA), reads=["posr", "mA"], writes=["posr"])
        P.op("dve", lambda e: e.scalar_tensor_tensor(posr, posc, mB, posr, ALU.mult, ALU.add), reads=["posr", "posc", "mB"], writes=["posr"])
        P.op("dve", lambda e: e.tensor_scalar_mul(posr, posr, invt), reads=["posr", "invt"], writes=["posr"])
        pci = posc.bitcast(I32)
        P.op("dve", lambda e: e.tensor_copy(pci, posr), reads=["posr"], writes=["posc"])
        P.op("dve", lambda e: e.tensor_tensor(posr, posr, pci, ALU.subtract), reads=["posr", "posc"], writes=["posr"])
        P.op("act", lambda e: e.activation(self.sinT, posr, AF.Sin, scale=TWO_PI), reads=["posr"], writes=["sinT"])
        P.op("dve", lambda e: e.scalar_tensor_tensor(posr, posr, -1.0, posr, ALU.mult, ALU.max), reads=["posr"], writes=["posr"])
        P.op("act", lambda e: e.activation(self.cosT, posr, AF.Sin, bias=halfpi, scale=-TWO_PI), reads=["posr", "halfpi"], writes=["cosT"])
        self.Rm = LWb[0:64, 4096:4160]
        P.op("pool", lambda e: e.tensor_scalar_mul(self.Rm[:, 0:32], self.identb[0:64, 32:64], -1.0), reads=["identb"], writes=["Rm"])
        P.op("pool", lambda e: e.tensor_copy(self.Rm[:, 32:64], self.identb[0:64, 0:32]), reads=["identb"], writes=["Rm"])

    def rope_apply(self, buf, key, blocks):
        P = self.P
        for (c0, c1) in blocks:
            w = c1 - c0
            ps = self.ps[7]
            self.mm(ps[0:64, 0:w], self.Rm, buf[0:64, c0:c1], True, True, [key, "Rm"], ["ps7"])
            t1 = self.tmpA[0]
            t2 = self.tmpA[1]
            P.op("pool", lambda e, c0=c0, c1=c1, w=w: e.tensor_tensor(t1[0:64, 0:w], buf[0:64, c0:c1], self.cosT[:, c0 - CTX:c1 - CTX], ALU.mult),
                 reads=[key, "cosT"], writes=["tmpA0"])
            P.op("dve", lambda e, c0=c0, c1=c1, w=w: e.tensor_tensor(t2[0:64, 0:w], ps[0:64, 0:w], self.sinT[:, c0 - CTX:c1 - CTX], ALU.mult),
                 reads=["ps7", "sinT"], writes=["tmpA1"])
            P.op("dve", lambda e, c0=c0, c1=c1, w=w: e.tensor_tensor(buf[0:64, c0:c1], t1[0:64, 0:w], t2[0:64, 0:w], ALU.add),
                 reads=["tmpA0", "tmpA1"], writes=[key])

    def proj_tile(self, w, nk, m, src_fn, src_keys, wkey, evac, blks=None):
        for bi, (c0, c1) in enumerate(blks or BLKS):
            bank = self.proj_banks[self.ps_rr % len(self.proj_banks)]
            self.ps_rr += 1
            ps = self.ps[bank]
            for k in range(nk):
                self.mm(ps[0:m, 0:c1 - c0], w[:, k, :], src_fn(k)[:, c0:c1], k == 0, k == nk - 1,
                        [wkey, src_keys[k]], [f"ps{bank}"])
            evac(ps, f"ps{bank}", c0, c1)

    def mla_layer(self, l):
        P = self.P
        o = l // 2
        with_ctx = l < DEPTH - 1
        Win = self.W["mla_w_in"][o]
        Wuq = self.W["mla_w_uq"][o]
        Wukv = self.W["mla_w_ukv"][o]
        Wout = self.W["mla_w_out"][o]
        self.ps_rr = 0
        self.proj_banks = [0, 1, 2, 3, 4, 5, 6]
        self.rope_tables()
        F0b = self.F[0][:].bitcast(BF16)
        F1b = self.F[1][:].bitcast(BF16)
        F2b = self.F[2][:].bitcast(BF16)
        cqn = [F0b[:, 0:T], F0b[:, T:2 * T], F1b[:, 0:T]]
        vh = F1b[:, T:2 * T].rearrange("p (t d) -> p t d", d=128)
        ckvn = [F2b[:, 0:T], F2b[:, T:2 * T]]
        kr, qn, qr, kn = self.B[0], self.B[1], self.B[2], self.B[3]
        yf = self.ygrp[:].rearrange("p s t -> p (s t)").bitcast(F32)
        nsrc = lambda k: self.nT[:, k, :]
        nkeys = [("nT", k) for k in range(NKT)]
        sm = self.small
        gq = sm[:, 96:99]
        gkv = sm[:, 99:101]
        P.dma(gq, self.W["mla_q_norm"][o].rearrange("(k p) -> p k", p=128), writes=["gq"], allow_slow_non_contiguous=True)
        P.dma(gkv, self.W["mla_kv_norm"][o].rearrange("(k p) -> p k", p=128), writes=["gkv"], allow_slow_non_contiguous=True)

        def copy_evac(dst, dkey, m=128, scale=None, eng="act"):
            def f(ps, pkey, c0, c1):
                if eng == "act":
                    if scale is None:
                        P.op("act", lambda e: e.activation(dst[0:m, c0:c1], ps[0:m, 0:c1 - c0], AF.Copy), reads=[pkey], writes=[dkey])
                    else:
                        P.op("act", lambda e: e.activation(dst[0:m, c0:c1], ps[0:m, 0:c1 - c0], AF.Copy, scale=scale), reads=[pkey], writes=[dkey])
                else:
                    P.op("dve", lambda e: e.tensor_copy(dst[0:m, c0:c1], ps[0:m, 0:c1 - c0]), reads=[pkey], writes=[dkey])
            return f

        for k in range(3):
            w, wk = self.load_cast(Win[:, k * 128:(k + 1) * 128], 8, 128)
            self.proj_tile(w, 8, 128, nsrc, nkeys, wk, copy_evac(cqn[k], f"cqn{k}", eng="act" if k % 2 == 0 else "dve"))
        for k in range(2):
            w, wk = self.load_cast(Win[:, 384 + k * 128:384 + (k + 1) * 128], 8, 128)
            self.proj_tile(w, 8, 128, nsrc, nkeys, wk, copy_evac(ckvn[k], f"ckvn{k}", eng="dve" if k % 2 == 0 else "act"))
        w, wk = self.load_cast(Win[:, 640:704], 8, 64)
        self.proj_tile(w, 8, 64, nsrc, nkeys, wk, copy_evac(kr, "B0", m=64))
        rq = yf[:, 0:T]
        rkv = yf[:, T:2 * T]
        self.rstd_into(rq, "rq", cqn, [f"cqn{k}" for k in range(3)], 384)
        self.rstd_into(rkv, "rkv", ckvn, [f"ckvn{k}" for k in range(2)], 256)
        for k in range(3):
            P.op("dve", lambda e, k=k: e.scalar_tensor_tensor(cqn[k], cqn[k], gq[:, k:k + 1], rq, ALU.mult, ALU.mult),
                 reads=[f"cqn{k}", "gq", "rq"], writes=[f"cqn{k}"])
        for k in range(2):
            P.op("dve", lambda e, k=k: e.scalar_tensor_tensor(ckvn[k], ckvn[k], gkv[:, k:k + 1], rkv, ALU.mult, ALU.mult),
                 reads=[f"ckvn{k}", "gkv", "rkv"], writes=[f"ckvn{k}"])
        self.rope_apply(kr, "B0", BLKS[1:])
        P.op("pool", lambda e: e.memset(kr[64:128, 0:T], 0.0), writes=["B0"])
        P.op("pool", lambda e: e.memset(qr[64:128, 0:T], 0.0), writes=["B2"])
        qblks = BLKS if with_ctx else BLKS[1:]
        cq_src = lambda k: cqn[k]
        cq_keys = [f"cqn{k}" for k in range(3)]
        kv_src = lambda k: ckvn[k]
        kv_keys = [f"ckvn{k}" for k in range(2)]
        self.proj_banks = [4, 5, 6]

        def head_loads(h, defer):
            r = [self.load_cast(Wuq[:, h * 192:(h + 1) * 192], 3, 192, defer=defer),
                 self.load_cast(Wukv[:, h * 256:(h + 1) * 256], 2, 256, defer=defer)]
            if not defer:
                r.append(self.load_cast(Win[:, 704 + h * 128:704 + (h + 1) * 128], 8, 128))
            return r
        pend = None
        for h in range(8):
            if pend is None:
                pend = head_loads(h, False)
            (wq, wqk), (wkv, wkvk), (wg, wgk) = [p[0:2] for p in pend]
            pend = None
            self.proj_tile(wq[:, :, 0:128], 3, 128, cq_src, cq_keys, wqk, copy_evac(qn, "B1", scale=MLA_SCALE))
            self.proj_tile(wq[:, :, 128:192], 3, 64, cq_src, cq_keys, wqk, copy_evac(qr, "B2", m=64, scale=MLA_SCALE))
            self.rope_apply(qr, "B2", BLKS[1:])
            self.proj_tile(wkv[:, :, 0:128], 2, 128, kv_src, kv_keys, wkvk, copy_evac(kn, "B3", eng="dve"))
            for t0 in range(0, 18, 4):
                nt = min(4, 18 - t0)
                bank = 4 + (self.ps_rr % 3)
                self.ps_rr += 1
                ps = self.ps[bank]
                for j in range(nt):
                    tt = t0 + j
                    for rk in range(2):
                        self.mm(ps[:, j * 128:(j + 1) * 128], ckvn[rk][:, tt * 128:(tt + 1) * 128], wkv[:, rk, 128:256],
                                rk == 0, rk == 1, [f"ckvn{rk}", wkvk], [f"ps{bank}"])
                P.op("dve", lambda e, ps=ps, t0=t0, nt=nt: e.tensor_copy(
                    vh[:, t0:t0 + nt, :], ps[:, 0:nt * 128].rearrange("p (t d) -> p t d", d=128)),
                    reads=[f"ps{bank}"], writes=["vh"])
            def ev_gate(ps, pkey, c0, c1, h=h):
                P.op("act", lambda e: e.activation(self.ygrp[:, h % 4, c0:c1], ps[:, 0:c1 - c0], AF.Silu), reads=[pkey],
                     writes=[("og", h % 4), "rq", "rkv"])
            self.proj_tile(wg, 8, 128, nsrc, nkeys, wgk, ev_gate, blks=qblks)
            if h + 1 < 8 and h % 4 != 3:
                pend = head_loads(h + 1, True)
            LOOK = 2
            for qi, (c0, c1) in enumerate(qblks):
                if qi == 1 and pend is not None:
                    for p in pend:
                        p[2]("act")
                    d3, k3, c3 = self.load_cast(Win[:, 704 + (h + 1) * 128:704 + (h + 2) * 128], 8, 128, defer=True)
                    c3("act")
                    pend.append((d3, k3))
                w_ = c1 - c0
                nkt = 2 if c0 == 0 else 18
                Ops = self.ps[qi % 2]
                Dps = self.ps[2 + qi % 2]
                okey, dkey = f"ps{qi % 2}", f"ps{2 + qi % 2}"
                Ebuf = {}

                def emit_S(kt, w_=w_, c0=c0, c1=c1):
                    sb = 4 + (self.ps_rr % 3)
                    self.ps_rr += 1
                    S = self.ps[sb]
                    ks = slice(kt * 128, (kt + 1) * 128)
                    self.mm(S[:, 0:w_], kn[:, ks], qn[:, c0:c1], True, False, ["B3", "B1"], [f"ps{sb}"])
                    self.mm(S[:, 0:w_], kr[:, ks], qr[:, c0:c1], False, True, ["B0", "B2"], [f"ps{sb}"])
                    ei = kt % 3
                    E = self.tmpB[ei]
                    P.op("act", lambda e, E=E, S=S, w_=w_: e.activation(E[:, 0:w_], S[:, 0:w_], AF.Exp),
                         reads=[f"ps{sb}"], writes=[f"tmpB{ei}"])
                    Ebuf[kt] = (E, f"tmpB{ei}")

                acc = self.tmpA[2]

                def emit_OD(kt, w_=w_, nkt=nkt, Ops=Ops, Dps=Dps, okey=okey, dkey=dkey):
                    E, ek = Ebuf[kt]
                    self.mm(Ops[:, 0:w_], vh[:, kt, :], E[:, 0:w_], kt == 0, kt == nkt - 1, ["vh", ek], [okey])
                    if kt % 2 == 1:
                        self.mm(Dps[:, 0:w_], self.onesb[:], E[:, 0:w_], kt == 1, False, ["onesb", ek], [dkey])
                    elif kt == 0:
                        P.op("dve", lambda e: e.tensor_copy(acc[:, 0:w_], E[:, 0:w_]), reads=[ek], writes=["tmpA2"])
                    else:
                        P.op("dve", lambda e: e.tensor_tensor(acc[:, 0:w_], acc[:, 0:w_], E[:, 0:w_], ALU.add), reads=[ek, "tmpA2"], writes=["tmpA2"])
                    if kt == nkt - 1:
                        accb = self.tmpB[3]
                        P.op("dve", lambda e: e.tensor_copy(accb[:, 0:w_], acc[:, 0:w_]), reads=["tmpA2"], writes=["tmpB3"])
                        self.mm(Dps[:, 0:w_], self.onesb[:], accb[:, 0:w_], False, True, ["onesb", "tmpB3"], [dkey])
                for kt in range(min(LOOK, nkt)):
                    emit_S(kt)
                for kt in range(nkt):
                    if kt + LOOK < nkt:
                        emit_S(kt + LOOK)
                    emit_OD(kt)
                rec = self.tmpA[0]
                o1 = self.tmpA[1]
                P.op("dve", lambda e, Dps=Dps, w_=w_: e.reciprocal(rec[:, 0:w_], Dps[:, 0:w_]), reads=[dkey], writes=["tmpA0"])
                P.op("dve", lambda e, Ops=Ops, w_=w_: e.tensor_tensor(o1[:, 0:w_], Ops[:, 0:w_], rec[:, 0:w_], ALU.mult),
                     reads=[okey, "tmpA0"], writes=["tmpA1"])
                P.op("pool", lambda e, h=h, c0=c0, c1=c1, w_=w_: e.tensor_tensor(self.ygrp[:, h % 4, c0:c1], o1[:, 0:w_], self.ygrp[:, h % 4, c0:c1], ALU.mult),
                     reads=["tmpA1", ("og", h % 4)], writes=[("og", h % 4)])
            if h % 4 == 3:
                self.out_proj(l, Wout[(h // 4) * 512:(h // 4 + 1) * 512, :], qblks)
        self.barrier()

    def out_proj(self, l, Wrows, blks, nk=4):
        P = self.P
        for ot in range(NKT):
            w, wk = self.load_cast(Wrows[:, ot * 128:(ot + 1) * 128], nk, 128)
            for (c0, c1) in blks:
                s = 1 if c0 == 0 else 0
                bank = self.proj_banks[self.ps_rr % len(self.proj_banks)]
                self.ps_rr += 1
                ps = self.ps[bank]
                for j in range(nk):
                    self.mm(ps[:, 0:c1 - c0], w[:, j, :], self.ygrp[:, j, c0:c1], j == 0, j == nk - 1, [wk, ("og", j)], [f"ps{bank}"])
                P.op("dve", lambda e, ot=ot, c0=c0, c1=c1, s=s, ps=ps: e.scalar_tensor_tensor(
                    self.xT[:, ot, c0:c1], ps[:, 0:c1 - c0], self.modT[:, l, 16 + ot, s:s + 1], self.xT[:, ot, c0:c1], ALU.mult, ALU.add),
                    reads=[f"ps{bank}", "modT", ("xT", ot)], writes=[("xT", ot)])


_NC_CACHE = {}


def _get_nc(layers, debug=False):
    key = (tuple(layers), debug)
    if key not in _NC_CACHE:
        _NC_CACHE[key] = Builder(layers, debug).build()
    return _NC_CACHE[key]


def kernel(_layers=(0, 1, 2, 3), _ncores=8, _debug=False, **inputs):
    nc = _get_nc(_layers, _debug)
    in_maps = []
    for b in range(_ncores):
        m = {}
        for name, shp in WSPEC:
            a = np.asarray(inputs[name], dtype=np.float32)
            if name in ("x", "c", "ctx"):
                a = a[b]
            m[name] = np.ascontiguousarray(a).reshape(shp)
        in_maps.append(m)
    res = run_bass_kernel_spmd(nc, in_maps, core_ids=list(range(_ncores)))
    if _debug:
        return res.results[0]
    return np.stack([np.asarray(r["out"], dtype=np.float32).reshape(SEQ, D) for r in res.results], axis=0)
```

```python
import math
import numpy as np
import concourse.bass as bass
import concourse.mybir as mybir
from concourse.bass_utils import run_bass_kernel_spmd

F32 = mybir.dt.float32
BF16 = mybir.dt.bfloat16
I32 = mybir.dt.int32
AF = mybir.ActivationFunctionType
ALU = mybir.AluOpType
AX = mybir.AxisListType

D = 1024
SEQ = 2048
CTX = 256
T = CTX + SEQ
NKT = 8
DEPTH = 4
EPS = 1e-6
BLKS = [(0, 256), (256, 768), (768, 1280), (1280, 1792), (1792, 2304)]
MLA_SCALE = 1.0 / math.sqrt(192.0)
TWO_PI = 2.0 * math.pi

DBG_D = 0
ENGS = ["pe", "act", "dve", "pool", "sp"]


class Prog:
    def __init__(self, nc):
        self.nc = nc
        self.streams = {e: [] for e in ENGS}
        self.count = {e: 0 for e in ENGS}
        self.sem = {e: nc.alloc_semaphore(name=f"prog_{e}") for e in ENGS}
        self.waited = {}
        self.lastw = {}
        self.readers = {}
        self.dma_sems = [nc.alloc_semaphore(name=f"dma_{i}") for i in range(16)]
        self.dma_cnt = [0] * 16
        self.dma_rr = 0

    def _deps(self, reads, writes):
        deps = []
        for k in reads:
            if k in self.lastw:
                deps.append(self.lastw[k])
        for k in writes:
            if k in self.lastw:
                deps.append(self.lastw[k])
            deps.extend(self.readers.get(k, []))
        return deps

    def _emit_waits(self, eng, deps):
        need = {}
        for (s, v) in deps:
            if eng == "pe" and s is self.sem["pe"]:
                continue
            key = id(s)
            if v > self.waited.get((eng, key), 0):
                if key not in need or need[key][1] < v:
                    need[key] = (s, v)
        for key, (s, v) in need.items():
            self.waited[(eng, key)] = v
            self.streams[eng].append(lambda e, s=s, v=v: e.wait_ge(s, v))

    def _commit(self, tok, reads, writes):
        for k in writes:
            self.lastw[k] = tok
            self.readers[k] = []
        for k in reads:
            if k not in writes:
                self.readers.setdefault(k, []).append(tok)

    def op(self, eng, fn, reads=(), writes=()):
        reads = list(reads)
        writes = list(writes)
        self._emit_waits(eng, self._deps(reads, writes))
        self.count[eng] += 1
        v = self.count[eng]
        s = self.sem[eng]
        self.streams[eng].append(lambda e, fn=fn, s=s: fn(e).then_inc(s, 1))
        tok = (s, v)
        self._commit(tok, reads, writes)
        return tok

    def dma(self, out, in_, reads=(), writes=(), eng="sp", **kw):
        reads = list(reads)
        writes = list(writes)
        i = self.dma_rr
        self.dma_rr = (self.dma_rr + 1) % len(self.dma_sems)
        s = self.dma_sems[i]
        deps = self._deps(reads, writes)
        if self.dma_cnt[i] > 0:
            deps.append((s, 16 * self.dma_cnt[i]))
        self._emit_waits(eng, deps)
        self.dma_cnt[i] += 1
        v = 16 * self.dma_cnt[i]
        self.streams[eng].append(
            lambda e, out=out, in_=in_, s=s, kw=kw: e.dma_start(out=out, in_=in_, **kw).then_inc(s, 16))
        tok = (s, v)
        self._commit(tok, reads, writes)
        return tok

    def finish(self, final_tokens):
        nc = self.nc
        self._emit_waits("sp", final_tokens)
        with nc.Block() as block:
            @block.tensor
            def _(e):
                for f in self.streams["pe"]:
                    f(e)

            @block.scalar
            def _(e):
                for f in self.streams["act"]:
                    f(e)

            @block.vector
            def _(e):
                for f in self.streams["dve"]:
                    f(e)

            @block.gpsimd
            def _(e):
                for f in self.streams["pool"]:
                    f(e)

            @block.sync
            def _(e):
                for f in self.streams["sp"]:
                    f(e)


WSPEC = [
    ("x", [SEQ, D]), ("c", [D]), ("ctx", [CTX, D]), ("c_ctx", [D]),
    ("norm_g", [4, D]), ("mod_w", [4, D, 3 * D]), ("mod_b", [4, 3 * D]),
    ("ev_w_in", [2, D, 3072]), ("lru_conv_w", [2, 4, D]), ("lru_conv_b", [2, D]),
    ("lru_wr", [2, 2, 8, 128, 128]), ("lru_br", [2, 2, D]),
    ("lru_wi", [2, 2, 8, 128, 128]), ("lru_bi", [2, 2, D]), ("lru_lam", [2, 2, D]),
    ("s5_lam_re", [2, 2, 32, 64]), ("s5_lam_im", [2, 2, 32, 64]), ("s5_log_dt", [2, 2, 32, 64]),
    ("s5_b_re", [2, 2, 32, 64, 16]), ("s5_b_im", [2, 2, 32, 64, 16]),
    ("s5_c_re", [2, 2, 32, 16, 64]), ("s5_c_im", [2, 2, 32, 16, 64]),
    ("s5_d", [2, 32, 16]), ("s5_glu_w", [2, 512, 512]), ("s5_glu_b", [2, 512]),
    ("ev_w_out", [2, 1536, D]),
    ("mla_w_in", [2, D, 1728]), ("mla_q_norm", [2, 384]), ("mla_w_uq", [2, 384, 1536]),
    ("mla_kv_norm", [2, 256]), ("mla_w_ukv", [2, 256, 2048]), ("mla_w_out", [2, D, D]),
    ("final_g", [D]),
]


class Builder:
    def __init__(self, layers=(0, 1, 2, 3), debug=False):
        self.layers = list(layers)
        nc = bass.Bass("TRN2", target_bir_lowering=False)
        self.nc = nc
        self.P = Prog(nc)
        self.W = {}
        for name, shp in WSPEC:
            self.W[name] = nc.dram_tensor(name, shp, F32, kind="ExternalInput").ap()
        self.out = nc.dram_tensor("out", [SEQ, D], F32, kind="ExternalOutput").ap()
        self.debug = debug
        if debug:
            self.dbgf = nc.dram_tensor("dbgf", [8, 128, T], F32, kind="ExternalOutput").ap()
            self.dbgb = nc.dram_tensor("dbgb", [8, 128, T], BF16, kind="ExternalOutput").ap()
        A = nc.alloc_sbuf_tensor
        self.xT = A("xT", [128, NKT, T], F32)
        self.nT = A("nT", [128, NKT, T], BF16)
        self.modT = A("modT", [128, DEPTH, 24, 2], F32)
        self.identf = A("identf", [128, 128], F32)
        self.identb = A("identb", [128, 128], BF16)
        self.onesb = A("onesb", [128, 128], BF16)
        self.ygrp = A("ygrp", [128, 4, T], BF16)
        self.F = [A(f"F{i}", [128, T], F32) for i in range(3)]
        self.B = [A(f"B{i}", [128, T + 8], BF16) for i in range(4)]
        self.stage = [A(f"stage{i}", [128, 1024], F32) for i in range(2)]
        self.stage_i = 0
        self.wbf = [A(f"wbf{i}", [128, 1024], BF16) for i in range(3)]
        self.wbf_i = 0
        self.LW = A("LW", [128, 2304], F32)
        self.small = A("small", [128, 256], F32)
        self.tmpA = [A(f"tmpA{i}", [128, 512], F32) for i in range(3)]
        self.tmpB = [A(f"tmpB{i}", [128, 512], BF16) for i in range(4)]
        self.ps = [nc.alloc_psum_tensor(f"ps{i}", [128, 512], F32) for i in range(8)]

    def load_cast(self, src_ap, kt, ncols, dst=None, dst_key=None, defer=False):
        P = self.P
        si = self.stage_i
        self.stage_i = (si + 1) % len(self.stage)
        st = self.stage[si]
        stv = st[:, 0:kt * ncols].rearrange("p (k c) -> p k c", k=kt)
        P.dma(stv, src_ap.rearrange("(k p) c -> p k c", p=128), writes=[f"stage{si}"])
        if dst is None:
            wi = self.wbf_i
            self.wbf_i = (wi + 1) % len(self.wbf)
            dst = self.wbf[wi][:, 0:kt * ncols].rearrange("p (k c) -> p k c", k=kt)
            dst_key = f"wbf{wi}"

        def cast(eng="pool"):
            if eng == "act":
                P.op("act", lambda e: e.activation(dst, stv, AF.Copy), reads=[f"stage{si}"], writes=[dst_key])
            else:
                P.op(eng, lambda e: e.tensor_copy(dst, stv), reads=[f"stage{si}"], writes=[dst_key])
        if defer:
            return dst, dst_key, cast
        cast()
        return dst, dst_key

    def mm(self, out, lhsT, rhs, start, stop, reads, writes):
        return self.P.op("pe", lambda e: e.matmul(out, lhsT, rhs, start=start, stop=stop),
                         reads=reads, writes=writes)

    def consts(self):
        P = self.P
        idf, idb, ob = self.identf, self.identb, self.onesb
        P.op("pool", lambda e: e.memset(idf[:], 0.0), writes=["identf"])
        P.op("pool", lambda e: e.affine_select(idf[:], idf[:], [[-1, 128]], ALU.not_equal, 1.0, base=0,
                                               channel_multiplier=1), reads=["identf"], writes=["identf"])
        P.op("pool", lambda e: e.tensor_copy(idb[:], idf[:]), reads=["identf"], writes=["identb"])
        P.op("pool", lambda e: e.memset(ob[:], 1.0), writes=["onesb"])

    def load_x(self):
        P = self.P
        for tt in range(T // 128):
            st = self.stage[tt % 2]
            skey = f"stage{tt % 2}"
            src = self.W["ctx"][tt * 128:(tt + 1) * 128, :] if tt < 2 else self.W["x"][(tt - 2) * 128:(tt - 1) * 128, :]
            P.dma(st[:], src, writes=[skey])
            for half in range(2):
                ps = self.ps[(tt * 2 + half) % 8]
                pkey = f"ps{(tt * 2 + half) % 8}"
                for q in range(4):
                    kt = half * 4 + q
                    P.op("pe", lambda e, ps=ps, q=q, kt=kt, st=st: e.transpose(
                        ps[:, q * 128:(q + 1) * 128], st[:, kt * 128:(kt + 1) * 128], self.identf[:]),
                        reads=[skey, "identf"], writes=[pkey])
                dst = self.xT[:, half * 4:half * 4 + 4, tt * 128:(tt + 1) * 128]
                srcp = ps[:].rearrange("p (q c) -> p q c", q=4)
                eng = "dve" if half == 0 else "act"
                if eng == "dve":
                    P.op("dve", lambda e, dst=dst, srcp=srcp: e.tensor_copy(dst, srcp), reads=[pkey],
                         writes=[("xT", half * 4 + q) for q in range(4)])
                else:
                    P.op("act", lambda e, dst=dst, srcp=srcp: e.activation(dst, srcp, AF.Copy), reads=[pkey],
                         writes=[("xT", half * 4 + q) for q in range(4)])

    def modulation(self):
        P = self.P
        sm = self.small
        cc = sm[:, 0:16]
        csb = sm[:, 16:24].bitcast(BF16)
        P.dma(cc[:, 0:8], self.W["c"].rearrange("(k p) -> p k", p=128), writes=["cc"], allow_slow_non_contiguous=True)
        P.dma(cc[:, 8:16], self.W["c_ctx"].rearrange("(k p) -> p k", p=128), writes=["cc"], allow_slow_non_contiguous=True)
        P.op("act", lambda e: e.activation(csb, cc, AF.Silu), reads=["cc"], writes=["cs"])
        csv = csb.rearrange("p (s k) -> p k s", s=2)
        nTf = self.nT[:].rearrange("p k t -> p (k t)").bitcast(F32)
        stg = [nTf[:, i * 1024:(i + 1) * 1024] for i in range(6)]
        nxt = 0
        for l in range(DEPTH):
            for kt in range(NKT):
                for h in range(3):
                    si = nxt % 6
                    nxt += 1
                    st = stg[si]
                    P.dma(st, self.W["mod_w"][l, kt * 128:(kt + 1) * 128, h * 1024:(h + 1) * 1024], writes=[f"mstg{si}"])
                    wi = self.wbf_i
                    self.wbf_i = (wi + 1) % len(self.wbf)
                    wb = self.wbf[wi]
                    eng = "act" if nxt % 2 == 0 else "dve"
                    if eng == "act":
                        P.op("act", lambda e, wb=wb, st=st: e.activation(wb[:], st, AF.Copy), reads=[f"mstg{si}"], writes=[f"wbf{wi}"])
                    else:
                        P.op("dve", lambda e, wb=wb, st=st: e.tensor_copy(wb[:], st), reads=[f"mstg{si}"], writes=[f"wbf{wi}"])
                    for q in range(2):
                        bank = h * 2 + q
                        self.mm(self.ps[bank][0:2, :], csv[:, kt, :], wb[:, q * 512:(q + 1) * 512],
                                kt == 0, kt == NKT - 1, ["cs", f"wbf{wi}"], [f"ps{bank}"])
            mrow = nTf[:, 6144:9216][0:2, :]
            bro = self.F[0][0:2, 0:2304]
            bro2 = self.F[1][0:2, 0:768]
            for s_ in range(2):
                P.dma(bro[s_:s_ + 1, :], self.W["mod_b"][l:l + 1, 0:2304], writes=["bro", "F0"])
                P.dma(bro2[s_:s_ + 1, :], self.W["mod_b"][l:l + 1, 2304:3072], writes=["bro", "F1"])
            for bank in range(6):
                c0 = bank * 512
                if c0 + 512 <= 2304:
                    bsrc = bro[:, c0:c0 + 512]
                    P.op("dve", lambda e, bank=bank, bsrc=bsrc, c0=c0: e.tensor_tensor(
                        mrow[:, c0:c0 + 512], self.ps[bank][0:2, :], bsrc, ALU.add), reads=[f"ps{bank}", "bro", "F0", "F1"], writes=["mrow"])
                else:
                    for (a0, a1) in [(c0, min(c0 + 512, 2304)), (max(c0, 2304), c0 + 512)]:
                        if a1 <= a0:
                            continue
                        bsrc = bro[:, a0:a1] if a1 <= 2304 else bro2[:, a0 - 2304:a1 - 2304]
                        P.op("dve", lambda e, bank=bank, bsrc=bsrc, a0=a0, a1=a1, c0=c0: e.tensor_tensor(
                            mrow[:, a0:a1], self.ps[bank][0:2, a0 - c0:a1 - c0], bsrc, ALU.add), reads=[f"ps{bank}", "bro", "F0", "F1"], writes=["mrow"])
            for j in range(24):
                self.mm(self.ps[6][:, 2 * j:2 * j + 2], mrow[:, j * 128:(j + 1) * 128], self.identf[0:2, 0:2],
                        True, True, ["mrow", "identf"], ["ps6"])
            P.op("dve", lambda e, l=l: e.tensor_copy(self.modT[:, l].rearrange("p j s -> p (j s)"), self.ps[6][:, 0:48]),
                 reads=["ps6"], writes=["modT"])

    def norm_mod(self, l):
        P = self.P
        sm = self.small
        g = sm[:, 32:40]
        gm = sm[:, 40:56]
        P.dma(g, self.W["norm_g"][l].rearrange("(k p) -> p k", p=128), writes=["g"], allow_slow_non_contiguous=True)
        for s in range(2):
            P.op("dve", lambda e, s=s: e.scalar_tensor_tensor(
                gm[:, s * 8:(s + 1) * 8], self.modT[:, l, 8:16, s], 1.0, g, ALU.add, ALU.mult),
                reads=["modT", "g"], writes=["gm"])
        rstd = self.F[1]
        self.rstd_into(rstd, "F1", [self.xT[:, kt, :] for kt in range(NKT)], [("xT", kt) for kt in range(NKT)], D)
        for kt in range(NKT):
            for s, (c0, c1) in enumerate([(CTX, T), (0, CTX)]):
                tmp = self.F[2]
                P.op("dve", lambda e, kt=kt, s=s, c0=c0, c1=c1: e.scalar_tensor_tensor(
                    tmp[:, c0:c1], self.xT[:, kt, c0:c1], gm[:, s * 8 + kt:s * 8 + kt + 1], rstd[:, c0:c1], ALU.mult, ALU.mult),
                    reads=[("xT", kt), "gm", "F1"], writes=["F2"])
                P.op("act", lambda e, kt=kt, s=s, c0=c0, c1=c1: e.activation(
                    self.nT[:, kt, c0:c1], tmp[:, c0:c1], AF.Identity, bias=self.modT[:, l, kt, s:s + 1], scale=1.0),
                    reads=["F2", "modT"], writes=[("nT", kt)])

    def rstd_into(self, dst, dst_key, srcs, src_keys, dim, cols=(0, T)):
        P = self.P
        blks = [b for b in BLKS if b[0] >= cols[0] and b[1] <= cols[1]]
        for bi, (c0, c1) in enumerate(blks):
            pb = self.ps[7]
            n = len(srcs)
            for i, (s_ap, sk) in enumerate(zip(srcs, src_keys)):
                sq = self.tmpB[i % 2]
                P.op("act", lambda e, sq=sq, s_ap=s_ap, c0=c0, c1=c1: e.activation(sq[:, 0:c1 - c0], s_ap[:, c0:c1], AF.Square),
                     reads=[sk], writes=[f"tmpB{i % 2}"])
                self.mm(pb[:, 0:c1 - c0], self.onesb[:], sq[:, 0:c1 - c0], i == 0, i == n - 1,
                        [f"tmpB{i % 2}", "onesb"], ["ps7"])
            P.op("act", lambda e, c0=c0, c1=c1: e.activation(dst[:, c0:c1], pb[:, 0:c1 - c0], AF.Sqrt, bias=EPS, scale=1.0 / dim),
                 reads=["ps7"], writes=[dst_key])
            P.op("dve", lambda e, c0=c0, c1=c1: e.reciprocal(dst[:, c0:c1], dst[:, c0:c1]), reads=[dst_key], writes=[dst_key])

    def final(self):
        P = self.P
        gb = self.F[0]
        P.dma(gb[:, 0:D], self.W["final_g"].partition_broadcast(128), writes=["F0"])
        ot = self.F[1]
        for tt in range(SEQ // 128):
            c0 = CTX + tt * 128
            ss = self.small[:, 64 + (tt % 2) * 2:64 + (tt % 2) * 2 + 2]
            sskey = f"ss{tt % 2}"
            pss = []
            P.op("pool", lambda e, ss=ss: e.memset(ss, 0.0), writes=[sskey + "0", sskey + "1"])
            for half in range(2):
                bank = (tt * 2 + half) % 4
                ps = self.ps[bank]
                for q in range(4):
                    kt = half * 4 + q
                    P.op("pe", lambda e, ps=ps, q=q, kt=kt, c0=c0: e.transpose(
                        ps[:, q * 128:(q + 1) * 128], self.xT[:, kt, c0:c0 + 128], self.identf[:]),
                        reads=[("xT", kt), "identf"], writes=[f"ps{bank}"])
                junk = self.tmpA[half]
                P.op("act", lambda e, ps=ps, junk=junk, half=half, ss=ss: e.activation(
                    junk[:], ps[:], AF.Square, accum_out=ss[:, half:half + 1]),
                    reads=[f"ps{bank}"], writes=[f"tmpA{half}", sskey + str(half)])
                pss.append((ps, bank))
            rs = self.small[:, 72 + (tt % 2):73 + (tt % 2)]
            rkey = f"rs{tt % 2}"
            P.op("dve", lambda e, ss=ss, rs=rs: e.tensor_tensor(rs, ss[:, 0:1], ss[:, 1:2], ALU.add),
                 reads=[sskey + "0", sskey + "1"], writes=[rkey])
            P.op("act", lambda e, rs=rs: e.activation(rs, rs, AF.Sqrt, bias=EPS, scale=1.0 / D), reads=[rkey], writes=[rkey])
            P.op("dve", lambda e, rs=rs: e.reciprocal(rs, rs), reads=[rkey], writes=[rkey])
            obuf = ot[:, (tt % 2) * 1024:(tt % 2) * 1024 + 1024]
            okey = f"ot{tt % 2}"
            for half, (ps, bank) in enumerate(pss):
                P.op("dve", lambda e, ps=ps, half=half, rs=rs, obuf=obuf: e.scalar_tensor_tensor(
                    obuf[:, half * 512:(half + 1) * 512], ps[:], rs, gb[:, half * 512:(half + 1) * 512], ALU.mult, ALU.mult),
                    reads=[f"ps{bank}", rkey, "F0"], writes=[okey + str(half)])
            self.out_toks.append(P.dma(self.out[tt * 128:(tt + 1) * 128, :], obuf, reads=[okey + "0", okey + "1"]))

    def build(self):
        self.out_toks = []
        self.consts()
        self.load_x()
        self.modulation()
        for l in self.layers:
            self.norm_mod(l)
            if l % 2 == 0:
                self.even_layer(l)
            else:
                self.mla_layer(l)
        self.final()
        self.P.finish(self.out_toks)
        return self.nc


    def even_layer(self, l):
        P = self.P
        e_ = l // 2
        Win = self.W["ev_w_in"][e_]
        Wout = self.W["ev_w_out"][e_]
        self.ps_rr = 0
        self.proj_banks = [0, 1, 2, 3, 4, 5, 6, 7]
        sm = self.small
        cw = sm[:, 104:136].rearrange("p (k t) -> p k t", k=4)
        cb = sm[:, 136:144]
        br = sm[:, 144:160].rearrange("p (d t) -> p d t", d=2)
        bi = sm[:, 160:176].rearrange("p (d t) -> p d t", d=2)
        coef = sm[:, 176:192].rearrange("p (d t) -> p d t", d=2)
        coef2 = sm[:, 192:208].rearrange("p (d t) -> p d t", d=2)
        ld = lambda dst, src, key: P.dma(dst, src, writes=[key], allow_slow_non_contiguous=True)
        for k in range(4):
            ld(cw[:, k, :], self.W["lru_conv_w"][e_, k].rearrange("(t p) -> p t", p=128), "cw")
        ld(cb, self.W["lru_conv_b"][e_].rearrange("(t p) -> p t", p=128), "cb")
        for d in range(2):
            ld(br[:, d, :], self.W["lru_br"][e_, d].rearrange("(t p) -> p t", p=128), "br")
            ld(bi[:, d, :], self.W["lru_bi"][e_, d].rearrange("(t p) -> p t", p=128), "bi")
            ld(coef[:, d, :], self.W["lru_lam"][e_, d].rearrange("(t p) -> p t", p=128), "coef")
        cf = sm[:, 176:192]
        cf2 = sm[:, 192:208]
        P.op("act", lambda e: e.activation(cf, cf, AF.Exp, scale=-1.0), reads=["coef"], writes=["coef"])
        P.op("act", lambda e: e.activation(cf, cf, AF.Ln, bias=1.0), reads=["coef"], writes=["coef"])
        P.op("dve", lambda e: e.tensor_scalar_mul(cf2, cf, -16.0), reads=["coef"], writes=["coef2"])
        P.op("dve", lambda e: e.tensor_scalar_mul(cf, cf, -8.0), reads=["coef", "coef2"], writes=["coef"])
        nsrc = lambda k: self.nT[:, k, :]
        nkeys = [("nT", k) for k in range(NKT)]
        xap, u, ib, hf = self.B
        F0, F1, F2 = self.F
        dg = [self.tmpB[0][:, k * 128:(k + 1) * 128] for k in range(4)]
        F2b = F2[:].bitcast(BF16)
        hr = F2b[:, 0:T]
        abuf = [(F1, "F1"), (self.LW, "LWa")]
        ibuf = [(ib, "B2"), (F2b[:, T:2 * T], "ibB")]
        for (a, b) in [(0, 2), (258, 262), (2310, 2312)]:
            P.op("pool", lambda e, a=a, b=b: e.memset(xap[:, a:b], 0.0), writes=["B0"])

        def xoff(c0):
            return c0 + 2 if c0 < CTX else c0 + 6

        def lru_loads(h):
            r = {"wxa": self.load_cast(Win[:, h * 128:(h + 1) * 128], 8, 128),
                 "wga": self.load_cast(Win[:, 1024 + h * 128:1024 + (h + 1) * 128], 8, 128)}
            gwb = self.tmpB[2 + h % 2]
            for d in range(2):
                for q, nm in enumerate(["lru_wr", "lru_wi"]):
                    off = (d * 2 + q) * 128
                    dst = gwb[:, off:off + 128].rearrange("p (k c) -> p k c", k=1)
                    r[(d, q)] = self.load_cast(self.W[nm][e_, d, h], 1, 128, dst=dst, dst_key=("gw", h % 2, d, q))
            return r

        pend = None
        xa_pending = None
        for h in range(8):
            if pend is None:
                pend = lru_loads(h)
            Wt = pend
            pend = None
            wxa, wxak = Wt["wxa"]
            wga, wgak = Wt["wga"]

            def ev_xa(ps, pkey, c0, c1):
                P.op("act", lambda e: e.activation(xap[:, xoff(c0):xoff(c0) + c1 - c0], ps[:, 0:c1 - c0], AF.Copy), reads=[pkey], writes=["B0"])
            if xa_pending is not None:
                for (ps_, pk_, c0_, c1_) in xa_pending:
                    ev_xa(ps_, pk_, c0_, c1_)
                xa_pending = None
                self.proj_banks = [0, 1, 2, 3, 4, 5, 6, 7]
            else:
                self.proj_tile(wxa, 8, 128, nsrc, nkeys, wxak, ev_xa)
            for k in range(4):
                P.op("pool", lambda e, k=k, h=h: e.tensor_scalar_mul(dg[k], self.identb[:], cw[:, k, h:h + 1]), reads=["identb", "cw"], writes=[f"dg{k}", "tmpB0"])
            for (c0, c1) in BLKS:
                bank = self.proj_banks[self.ps_rr % len(self.proj_banks)]
                self.ps_rr += 1
                ps = self.ps[bank]
                for k in range(4):
                    o0 = xoff(c0) + k - 2
                    self.mm(ps[:, 0:c1 - c0], dg[k], xap[:, o0:o0 + c1 - c0], k == 0, k == 3, [f"dg{k}", "B0"], [f"ps{bank}"])
                P.op("act", lambda e, ps=ps, c0=c0, c1=c1, h=h: e.activation(u[:, c0:c1], ps[:, 0:c1 - c0], AF.Identity, bias=cb[:, h:h + 1], scale=1.0),
                     reads=[f"ps{bank}", "cb"], writes=["B1"])
            def ev_g(ps, pkey, c0, c1):
                P.op("act", lambda e: e.activation(xap[:, xoff(c0):xoff(c0) + c1 - c0], ps[:, 0:c1 - c0], AF.Silu), reads=[pkey], writes=["B0"])
            usrc = lambda k: u
            for d in range(2):
                wr, wrk = Wt[(d, 0)]
                wi, wik = Wt[(d, 1)]
                ad, adk = abuf[d]
                ibd, ibk = ibuf[d]

                def ev_r(ps, pkey, c0, c1, d=d, h=h):
                    P.op("act", lambda e: e.activation(F0[:, c0:c1], ps[:, 0:c1 - c0], AF.Sigmoid, bias=br[:, d, h:h + 1], scale=1.0),
                         reads=[pkey, "br"], writes=["F0"])

                def ev_i(ps, pkey, c0, c1, d=d, h=h, ibd=ibd, ibk=ibk):
                    P.op("act", lambda e: e.activation(ibd[:, c0:c1], ps[:, 0:c1 - c0], AF.Sigmoid, bias=bi[:, d, h:h + 1], scale=1.0),
                         reads=[pkey, "bi"], writes=[ibk])
                self.proj_tile(wr, 1, 128, usrc, ["B1"], wrk, ev_r)
                self.proj_tile(wi, 1, 128, usrc, ["B1"], wik, ev_i)
                if d == 1 and pend is not None:
                    nwxa, nwxak = pend["wxa"]
                    xa_pending = []
                    for bi_, (c0_, c1_) in enumerate(BLKS):
                        bank_ = 3 + bi_
                        for k_ in range(NKT):
                            self.mm(self.ps[bank_][:, 0:c1_ - c0_], nwxa[:, k_, :], self.nT[:, k_, c0_:c1_], k_ == 0, k_ == NKT - 1,
                                    [nwxak, ("nT", k_)], [f"ps{bank_}"])
                        xa_pending.append((self.ps[bank_], f"ps{bank_}", c0_, c1_))
                    self.proj_banks = [0, 1, 2]
                P.op("act", lambda e, d=d, h=h, ad=ad: e.activation(ad[:, 0:T], F0[:, :], AF.Exp, scale=coef[:, d, h:h + 1]), reads=["F0", "coef"], writes=[adk])
                P.op("act", lambda e, d=d, h=h: e.activation(F0[:, :], F0[:, :], AF.Exp, scale=coef2[:, d, h:h + 1]), reads=["F0", "coef2"], writes=["F0"])
                P.op("act", lambda e: e.activation(F0[:, :], F0[:, :], AF.Sqrt, bias=1.0, scale=-1.0), reads=["F0"], writes=["F0"])
                P.op("dve", lambda e, ibd=ibd: e.tensor_tensor(ibd[:, 0:T], ibd[:, 0:T], u[:, 0:T], ALU.mult), reads=[ibk, "B1"], writes=[ibk])
                P.op("dve", lambda e, ibd=ibd: e.tensor_tensor(ibd[:, 0:T], F0[:, :], ibd[:, 0:T], ALU.mult), reads=["F0", ibk], writes=[ibk])
                if d == 0:
                    P.op("dve", lambda e, ad=ad, ibd=ibd: e.tensor_tensor_scan(hf[:, 0:T], ad[:, 0:T], ibd[:, 0:T], 0.0, ALU.mult, ALU.add),
                         reads=[ibk, adk], writes=["B3"])
                    self.proj_tile(wga, 8, 128, nsrc, nkeys, wgak, ev_g)
                    if h + 1 < 8 and h % 4 != 3:
                        pend = lru_loads(h + 1)
                else:
                    P.op("dve", lambda e, ad=ad, ibd=ibd: e.tensor_tensor_scan(hr[:, 0:CTX][:, ::-1], ad[:, 0:CTX][:, ::-1], ibd[:, 0:CTX][:, ::-1], 0.0,
                                                                              ALU.mult, ALU.add), reads=[ibk, adk], writes=["hr"])
                    P.op("dve", lambda e, ad=ad, ibd=ibd: e.tensor_tensor_scan(hr[:, CTX:T][:, ::-1], ad[:, CTX:T][:, ::-1], ibd[:, CTX:T][:, ::-1], hr[:, 0:1],
                                                                              ALU.mult, ALU.add), reads=[ibk, adk, "hr"], writes=["hr"])
            P.op("dve", lambda e: e.tensor_tensor(hr, hr, hf[:, 0:T], ALU.add), reads=["hr", "B3"], writes=["hr"])
            P.op("dve", lambda e, h=h: e.tensor_tensor(self.ygrp[:, h % 4, 0:CTX], hr[:, 0:CTX], xap[:, 2:2 + CTX], ALU.mult),
                 reads=["hr", "B0"], writes=[("og", h % 4)])
            P.op("dve", lambda e, h=h: e.tensor_tensor(self.ygrp[:, h % 4, CTX:T], hr[:, CTX:T], xap[:, 262:262 + SEQ], ALU.mult),
                 reads=["hr", "B0"], writes=[("og", h % 4)])
            if h % 4 == 3:
                self.out_proj(l, Wout[(h // 4) * 512:(h // 4 + 1) * 512, :], BLKS)
        self.barrier()
        self.s5_phase(l)
        self.barrier()


    def s5_phase(self, l):
        P = self.P
        e_ = l // 2
        Win = self.W["ev_w_in"][e_]
        Wout = self.W["ev_w_out"][e_]
        W = self.W
        F0, F1, F2 = self.F
        g_re, g_im, ub, B3 = self.B
        LW = self.LW
        LWb = LW[:].bitcast(BF16)
        sm = self.small
        tht, rmag = LW[:, 0:32], LW[:, 32:64]
        M16, Mrow, nMrow = LW[:, 64:72], LW[:, 72:74], LW[:, 74:76]
        Braw = [LW[:, 80:144], LW[:, 144:208]]
        Bbar = [LW[:, 208:272], LW[:, 272:336]]
        CT = LW[:, 336:464].rearrange("p (j q h) -> p j q h", j=4, q=2)
        dsk, glub = LW[:, 464:468], LW[:, 468:472]
        Eexp = LW[0:32, 472:600]
        Fc = LW[0:32, 600:856]
        lB = lambda j, q: LWb[:, 1712 + (j * 2 + q) * 128:1712 + (j * 2 + q + 1) * 128]
        lC = lambda j, v: LWb[:, 2736 + (j * 3 + v) * 128:2736 + (j * 3 + v + 1) * 128]
        Dd = LWb[:, 4272:4400]
        iota48 = sm[:, 208:256]
        hpi = sm[:, 87:88]
        tA0, tA1, tA2 = self.tmpA
        ld = lambda dst, src, key: P.dma(dst, src, writes=[key], allow_slow_non_contiguous=True)
        dve = lambda fn, r, w: P.op("dve", fn, reads=r, writes=w)
        act = lambda fn, r, w: P.op("act", fn, reads=r, writes=w)
        pool = lambda fn, r, w: P.op("pool", fn, reads=r, writes=w)
        pool(lambda e: e.iota(iota48, [[1, 48]], base=0, channel_multiplier=0, allow_small_or_imprecise_dtypes=True), [], ["iota48"])
        dve(lambda e: e.memset(hpi, math.pi / 2), [], ["hpi"])
        dve(lambda e: e.tensor_reduce(M16, self.identf[:].rearrange("p (c h) -> p c h", h=16), AX.X, ALU.add), ["identf"], ["M16"])
        dve(lambda e: e.tensor_reduce(Mrow, self.identf[:].rearrange("p (c h) -> p c h", h=64), AX.X, ALU.add), ["identf"], ["Mrow"])
        dve(lambda e: e.tensor_scalar_mul(nMrow, Mrow, -1.0), ["Mrow"], ["nMrow"])
        pool(lambda e: e.memset(LWb[:, 2736:4272], 0.0), [], [("lC", j_, v_, g_) for j_ in range(4) for v_ in range(3) for g_ in range(2)])
        ld(dsk, W["s5_d"][e_].rearrange("(t g) h -> (g h) t", g=8), "dsk")
        ld(glub, W["s5_glu_b"][e_].rearrange("(t p) -> p t", p=128), "glub")
        for i, nm in enumerate(["s5_lam_re", "s5_lam_im", "s5_log_dt"]):
            ld(tA0[:, i * 32:(i + 1) * 32], W[nm][e_].rearrange("d (gp gl) p -> (gl p) (d gp)", gl=2), "tA0")
        act(lambda e: e.activation(tA0[:, 64:96], tA0[:, 64:96], AF.Exp), ["tA0"], ["tA0"])
        dve(lambda e: e.tensor_tensor(tht, tA0[:, 32:64], tA0[:, 64:96], ALU.mult), ["tA0"], ["tht"])
        dve(lambda e: e.tensor_scalar_mul(tht, tht, 1.0 / TWO_PI), ["tht"], ["tht"])
        dve(lambda e: e.tensor_tensor(rmag, tA0[:, 0:32], tA0[:, 64:96], ALU.mult), ["tA0"], ["rmag"])
        act(lambda e: e.activation(rmag, rmag, AF.Exp), ["rmag"], ["rmag"])
        c = lambda i: F0[0:32, i * 128:(i + 1) * 128]
        ci = lambda i: F0[0:32, i * 128:(i + 1) * 128].bitcast(I32)
        for i, nm in enumerate(["s5_lam_re", "s5_lam_im", "s5_log_dt"]):
            ld(c(i).rearrange("g (d p) -> g d p", d=2), W[nm][e_].rearrange("d g p -> g d p"), "F0")
        K0 = ["F0"]
        act(lambda e: e.activation(c(2), c(2), AF.Exp), K0, K0)
        dve(lambda e: e.tensor_tensor(c(3), c(0), c(2), ALU.mult), K0, K0)
        dve(lambda e: e.tensor_tensor(c(4), c(1), c(2), ALU.mult), K0, K0)
        dve(lambda e: e.tensor_scalar_mul(c(4), c(4), 1.0 / TWO_PI), K0, K0)
        dve(lambda e: e.tensor_scalar_mul(c(10), c(4), 0.5), K0, K0)
        act(lambda e: e.activation(c(5), c(3), AF.Exp), K0, K0)
        act(lambda e: e.activation(c(6), c(3), AF.Tanh, scale=0.5), K0, K0)
        dve(lambda e: e.scalar_tensor_tensor(c(6), c(5), 1.0, c(6), ALU.add, ALU.mult), K0, K0)
        dve(lambda e: e.tensor_copy(ci(7), c(4)), K0, K0)
        dve(lambda e: e.tensor_tensor(c(4), c(4), ci(7), ALU.subtract), K0, K0)
        act(lambda e: e.activation(c(8), c(4), AF.Sin, scale=TWO_PI), K0, K0)
        dve(lambda e: e.scalar_tensor_tensor(c(9), c(4), -1.0, c(4), ALU.mult, ALU.max), K0, K0)
        act(lambda e: e.activation(c(9), c(9), AF.Sin, bias=hpi[0:32, :], scale=-TWO_PI), K0 + ["hpi"], K0)
        dve(lambda e: e.tensor_copy(ci(7), c(10)), K0, K0)
        dve(lambda e: e.tensor_tensor(c(10), c(10), ci(7), ALU.subtract), K0, K0)
        act(lambda e: e.activation(c(10), c(10), AF.Sin, scale=TWO_PI), K0, K0)
        dve(lambda e: e.tensor_tensor(c(10), c(10), c(10), ALU.mult), K0, K0)
        dve(lambda e: e.tensor_tensor(c(11), c(6), c(9), ALU.mult), K0, K0)
        dve(lambda e: e.scalar_tensor_tensor(c(11), c(10), -2.0, c(11), ALU.mult, ALU.add), K0, K0)
        dve(lambda e: e.tensor_tensor(c(12), c(5), c(8), ALU.mult), K0, K0)
        dve(lambda e: e.tensor_tensor(c(13), c(0), c(0), ALU.mult), K0, K0)
        dve(lambda e: e.tensor_tensor(c(14), c(1), c(1), ALU.mult), K0, K0)
        dve(lambda e: e.tensor_tensor(c(13), c(13), c(14), ALU.add), K0, K0)
        dve(lambda e: e.reciprocal(c(13), c(13)), K0, K0)
        dve(lambda e: e.tensor_tensor(c(14), c(11), c(0), ALU.mult), K0, K0)
        dve(lambda e: e.tensor_tensor(c(15), c(12), c(1), ALU.mult), K0, K0)
        dve(lambda e: e.tensor_tensor(c(14), c(14), c(15), ALU.add), K0, K0)
        dve(lambda e: e.tensor_tensor(Fc[:, 0:128], c(14), c(13), ALU.mult), K0, ["Fc"])
        dve(lambda e: e.tensor_tensor(c(14), c(12), c(0), ALU.mult), K0, K0)
        dve(lambda e: e.tensor_tensor(c(15), c(11), c(1), ALU.mult), K0, K0)
        dve(lambda e: e.tensor_tensor(c(14), c(14), c(15), ALU.subtract), K0, K0)
        dve(lambda e: e.tensor_tensor(Fc[:, 128:256], c(14), c(13), ALU.mult), K0, ["Fc"])
        nsrc = lambda k: self.nT[:, k, :]
        nkeys = [("nT", k) for k in range(NKT)]

        def tv(X, d, c0, c1):
            if d == 0:
                return X[:, c0:c1]
            if c0 < CTX:
                return X[:, 0:CTX][:, ::-1]
            return X[:, 2560 - c1:2560 - c0][:, ::-1]

        for ti in range(4):
            self.proj_banks = [7]
            wub, wubk = self.load_cast(Win[:, 2048 + ti * 128:2048 + (ti + 1) * 128], 8, 128)

            def ev_u(ps, pkey, c0, c1):
                act(lambda e: e.activation(ub[:, c0:c1], ps[:, 0:c1 - c0], AF.Copy), [pkey], ["B2"])
            self.proj_tile(wub, 8, 128, nsrc, nkeys, wubk, ev_u)
            pool(lambda e, ti=ti: e.tensor_copy(Eexp.rearrange("g (c h) -> g c h", h=16),
                                                self.identf[0:32, 8 * ti:8 * ti + 8].unsqueeze(2).to_broadcast([32, 8, 16])), ["identf"], ["Eexp"])
            pool(lambda e, ti=ti: e.tensor_scalar_mul(Dd, self.identb[:], dsk[:, ti:ti + 1]), ["identb", "dsk"], ["Dd"])
            for d in range(2):
                self.mm(self.ps[7][:, 0:256], Eexp, Fc, True, True, ["Eexp", "Fc"], ["ps7"])
                for q, nm in enumerate(["s5_b_re", "s5_b_im"]):
                    for g8 in range(8):
                        ld(Braw[q][16 * g8:16 * g8 + 16, :], W[nm][e_, d, 8 * ti + g8].rearrange("p h -> h p"), f"Braw{q}")
                for q, nm in enumerate(["s5_c_re", "s5_c_im"]):
                    for gl in range(2):
                        for j in range(4):
                            ld(CT[64 * gl:64 * gl + 64, j, q, :], W[nm][e_, d, 8 * ti + 2 * j + gl].rearrange("h p -> p h"), "CT")
                Fre = self.ps[7][:, d * 64:(d + 1) * 64]
                Fim = self.ps[7][:, 128 + d * 64:128 + (d + 1) * 64]
                t0_, t1_ = tA0[:, 0:64], tA0[:, 64:128]
                dve(lambda e, Fre=Fre: e.tensor_tensor(t0_, Fre, Braw[0], ALU.mult), ["ps7", "Braw0"], ["tA0", "prod0"])
                dve(lambda e, Fim=Fim: e.tensor_tensor(t1_, Fim, Braw[1], ALU.mult), ["ps7", "Braw1"], ["tA0", "prod0"])
                dve(lambda e: e.tensor_tensor(Bbar[0], t0_, t1_, ALU.subtract), ["tA0"], ["Bbar0"])
                dve(lambda e, Fre=Fre: e.tensor_tensor(t0_, Fre, Braw[1], ALU.mult), ["ps7", "Braw1"], ["tA0", "prod0"])
                dve(lambda e, Fim=Fim: e.tensor_tensor(t1_, Fim, Braw[0], ALU.mult), ["ps7", "Braw0"], ["tA0", "prod0"])
                dve(lambda e: e.tensor_tensor(Bbar[1], t0_, t1_, ALU.add), ["tA0"], ["Bbar1"])
                if ti == 0 and d == DBG_D and self.debug:
                    self.dump(3, Fc, "Fc")
                    dve(lambda e: e.tensor_copy(tA1[:, 0:256], self.ps[7][:, 0:256]), ["ps7"], ["tA1"])
                    self.dump(6, tA1[:, 0:256], "tA1")
                    self.dump(7, Braw[0], "Braw0")
                for j in range(4):
                    for q in range(2):
                        for gl in range(2):
                            act(lambda e, j=j, q=q, gl=gl: e.activation(
                                lB(j, q)[:, 64 * gl:64 * gl + 64], Bbar[q], AF.Copy, scale=M16[:, 2 * j + gl:2 * j + gl + 1]),
                                [f"Bbar{q}", "M16"], [("lB", j, q, gl)])
                    for gl in range(2):
                        cs = slice(32 * j + 16 * gl, 32 * j + 16 * gl + 16)
                        act(lambda e, j=j, gl=gl, cs=cs: e.activation(lC(j, 0)[:, cs], CT[:, j, 0, :], AF.Copy, scale=Mrow[:, gl:gl + 1]), ["CT", "Mrow"], [("lC", j, 0, gl)])
                        act(lambda e, j=j, gl=gl, cs=cs: e.activation(lC(j, 1)[:, cs], CT[:, j, 0, :], AF.Copy, scale=nMrow[:, gl:gl + 1]), ["CT", "nMrow"], [("lC", j, 1, gl)])
                        act(lambda e, j=j, gl=gl, cs=cs: e.activation(lC(j, 2)[:, cs], CT[:, j, 1, :], AF.Copy, scale=nMrow[:, gl:gl + 1]), ["CT", "nMrow"], [("lC", j, 2, gl)])
                for j in range(4):
                    uidx = (ti * 2 + d) * 4 + j
                    if uidx == 0:
                        for st in self.s5_table_steps(0, 0, 0, 0, tht):
                            st()
                    ins = []
                    if uidx + 1 < 32:
                        n_ = uidx + 1
                        ins = self.s5_table_steps(n_ // 8, (n_ // 4) % 2, n_ % 4, n_, tht)
                    self.s5_unit_compute(ti, d, j, uidx, lB, lC, rmag, ins)
            for bi, (c0, c1) in enumerate(BLKS):
                w_ = c1 - c0
                y = self.ps[bi]
                yk = f"ps{bi}"
                self.mm(y[:, 0:w_], Dd, ub[:, c0:c1], False, True, ["Dd", "B2"], [yk])
                tg, tgk = (tA0, "tA0") if bi % 2 == 0 else (tA1, "tA1")
                act(lambda e, y=y, w_=w_, tg=tg: e.activation(tg[:, 0:w_], y[:, 0:w_], AF.Square), [yk], [tgk])
                dve(lambda e, w_=w_, tg=tg: e.tensor_scalar(tg[:, 0:w_], tg[:, 0:w_], 0.044715, 1.0, ALU.mult, ALU.add), [tgk], [tgk])
                dve(lambda e, y=y, w_=w_, tg=tg: e.tensor_tensor(tg[:, 0:w_], tg[:, 0:w_], y[:, 0:w_], ALU.mult), [tgk, yk], [tgk])
                act(lambda e, w_=w_, tg=tg: e.activation(tg[:, 0:w_], tg[:, 0:w_], AF.Sigmoid, scale=2.0 * math.sqrt(2.0 / math.pi)), [tgk], [tgk])
                dve(lambda e, y=y, w_=w_, ti=ti, c0=c0, c1=c1, tg=tg: e.tensor_tensor(self.ygrp[:, ti, c0:c1], y[:, 0:w_], tg[:, 0:w_], ALU.mult),
                    [yk, tgk], [("og", ti)])
        self.proj_banks = [4, 5, 6]
        ysrc = lambda k: self.ygrp[:, k, :]
        ykeys = [("og", k) for k in range(4)]
        for ot in range(4):
            wg, wgk = self.load_cast(W["s5_glu_w"][e_][:, ot * 128:(ot + 1) * 128], 4, 128)

            def ev_z(ps, pkey, c0, c1, ot=ot):
                act(lambda e: e.activation(self.B[ot][:, c0:c1], ps[:, 0:c1 - c0], AF.Sigmoid, bias=glub[:, ot:ot + 1], scale=1.0),
                    [pkey, "glub"], [f"B{ot}"])
            self.proj_tile(wg, 4, 128, ysrc, ykeys, wgk, ev_z)
        for ot in range(4):
            wgb, wgbk = self.load_cast(Win[:, 2560 + ot * 128:2560 + (ot + 1) * 128], 8, 128)

            def ev_gb(ps, pkey, c0, c1, ot=ot):
                sg = self.tmpB[self.ps_rr % 2]
                sk = f"tmpB{self.ps_rr % 2}"
                act(lambda e: e.activation(sg[:, 0:c1 - c0], ps[:, 0:c1 - c0], AF.Silu), [pkey], [sk])
                pool(lambda e: e.tensor_tensor(sg[:, 0:c1 - c0], sg[:, 0:c1 - c0], self.B[ot][:, c0:c1], ALU.mult), [sk, f"B{ot}"], [sk])
                pool(lambda e: e.tensor_tensor(self.ygrp[:, ot, c0:c1], self.ygrp[:, ot, c0:c1], sg[:, 0:c1 - c0], ALU.mult),
                     [sk, ("og", ot)], [("og", ot)])
            self.proj_tile(wgb, 8, 128, nsrc, nkeys, wgbk, ev_gb)
        self.out_proj(l, Wout[1024:1536, :], BLKS)


    def s5_views(self):
        F0b = self.F[0][:].bitcast(BF16)
        F1b = self.F[1][:].bitcast(BF16)
        F2b = self.F[2][:].bitcast(BF16)
        tabs = [(F0b[:, 0:T], F0b[:, T:2 * T]), (F1b[:, 0:T], F1b[:, T:2 * T])]
        gin = (F2b[:, 0:T], F2b[:, T:2 * T])
        B3f = self.B[3][:, 0:2304].bitcast(F32)
        tAb = [self.tmpA[0][:].bitcast(BF16), self.tmpA[1][:].bitcast(BF16)]
        prod = [tAb[0][:, 0:512], tAb[0][:, 512:1024], tAb[1][:, 0:512], tAb[1][:, 512:1024]]
        return tabs, gin, B3f, prod

    def s5_table_steps(self, ti, d, j, uidx, tht):
        P = self.P
        tabs, gin, B3f, prod = self.s5_views()
        cosT, sinT = tabs[uidx % 2]
        tk = f"tab{uidx % 2}"
        sm = self.small
        iota48 = sm[:, 208:256]
        hpi = sm[:, 87:88]
        tA2 = self.tmpA[2]
        idx0 = d * 16 + 4 * ti
        th4 = tht[:, idx0:idx0 + 4]
        Ap4 = tA2[:, 0:192]
        Bp4 = tA2[:, 192:384]
        t48_4 = tA2[:, 384:388]
        kk1_4 = tA2[:, 388:392].bitcast(I32)
        kk = tA2[:, 392:488].bitcast(I32)
        Ap, Bp = Ap4[:, j * 48:(j + 1) * 48], Bp4[:, j * 48:(j + 1) * 48]
        KA = ["tA2"]
        dve = lambda fn, r, w: P.op("dve", fn, reads=r, writes=w)
        sxs = [B3f[:, 0:384], B3f[:, 384:768]]
        sy = B3f[:, 768:1152].bitcast(I32)
        steps = []

        def tiny():
            io4 = iota48.unsqueeze(1).to_broadcast([128, 4, 48])
            dve(lambda e: e.tensor_scalar_mul(t48_4, th4, 48.0), ["tht"], KA)
            dve(lambda e: e.tensor_copy(kk1_4, t48_4), KA, KA)
            dve(lambda e: e.tensor_tensor(t48_4, t48_4, kk1_4, ALU.subtract), KA, KA)
            dve(lambda e: e.tensor_tensor(Ap4.rearrange("p (u i) -> p u i", u=4), io4,
                                          t48_4.unsqueeze(2).to_broadcast([128, 4, 48]), ALU.mult), KA + ["iota48"], KA)
            dve(lambda e: e.tensor_tensor(Bp4.rearrange("p (u i) -> p u i", u=4), io4,
                                          th4.unsqueeze(2).to_broadcast([128, 4, 48]), ALU.mult), ["tht", "iota48"] + KA, KA)
            for X in (Ap4, Bp4):
                for hh in range(2):
                    xs = X[:, hh * 96:(hh + 1) * 96]
                    dve(lambda e, xs=xs: e.tensor_copy(kk, xs), KA, KA)
                    dve(lambda e, xs=xs: e.tensor_tensor(xs, xs, kk, ALU.subtract), KA, KA)
        if j == 0:
            steps.append(tiny)
        for k in range(6):
            sx = sxs[k % 2]
            sk = f"sx{k % 2}"
            c0 = 384 * k

            def stepA1(k=k, sx=sx, sk=sk):
                sxv = sx.rearrange("p (i j) -> p i j", j=48)
                P.op("dve", lambda e: e.tensor_tensor(sxv, Ap[:, 8 * k:8 * k + 8].unsqueeze(2).to_broadcast([128, 8, 48]),
                                                     Bp.unsqueeze(1).to_broadcast([128, 8, 48]), ALU.add), reads=KA, writes=[sk])

            def stepA2(sx=sx, sk=sk):
                dve(lambda e: e.tensor_copy(sy, sx), [sk], ["sy"])

            def stepA3(sx=sx, sk=sk):
                P.op("dve", lambda e: e.tensor_tensor(sx, sx, sy, ALU.subtract), reads=[sk, "sy"], writes=[sk])

            def stepB(sx=sx, sk=sk, c0=c0):
                P.op("act", lambda e: e.activation(sinT[:, c0:c0 + 384], sx, AF.Sin, scale=TWO_PI), reads=[sk], writes=[tk])

            def stepC(sx=sx, sk=sk, c0=c0):
                P.op("act", lambda e: e.activation(sx, sx, AF.Sin, scale=math.pi), reads=[sk], writes=[sk])
                P.op("act", lambda e: e.activation(sx, sx, AF.Square), reads=[sk], writes=[sk])
                P.op("act", lambda e: e.activation(cosT[:, c0:c0 + 384], sx, AF.Identity, bias=1.0, scale=-2.0), reads=[sk], writes=[tk])
            steps += [stepA1, stepA2, stepA3, stepB, stepC]
        return steps

    def s5_unit_compute(self, ti, d, j, uidx, lB, lC, rmag, inserts):
        P = self.P
        tabs, gin, B3f, prod = self.s5_views()
        cosT, sinT = tabs[uidx % 2]
        tk = f"tab{uidx % 2}"
        g_re, g_im, ub, _ = self.B
        idx = d * 16 + 4 * ti + j
        rm = rmag[:, idx:idx + 1]
        bre, bim, tA, tB = self.tmpB
        dve = lambda fn, r, w: P.op("dve", fn, reads=r, writes=w)
        nslots = 27
        total = len(inserts)
        state = {"slot": 0, "done": 0}

        def slot_end():
            state["slot"] += 1
            target = (state["slot"] * total + nslots - 1) // nslots
            while state["done"] < min(target, total):
                inserts[state["done"]]()
                state["done"] += 1

        def tcols(k):
            s0, s1 = BLKS[k]
            if d == 0:
                return s0, s1, k, False
            if k == 0:
                return 0, CTX, 0, True
            return 2560 - s1, 2560 - s0, 5 - k, True

        for k, (c0, c1) in enumerate(BLKS):
            w_ = c1 - c0
            t0, t1, bt, rev = tcols(k)
            ubv = ub[:, t0:t1][:, ::-1] if rev else ub[:, t0:t1]
            br_, bi_ = 5 + (2 * k) % 3, 5 + (2 * k + 1) % 3
            if k % 2 == 0:
                cre, cim, kre, kim = bre, bim, "tmpB0", "tmpB1"
            else:
                cre, cim, kre, kim = prod[0], prod[1], "prod0", "prod1"
            self.mm(self.ps[br_][:, 0:w_], lB(j, 0), ubv, True, True, [("lB", j, 0, 0), ("lB", j, 0, 1), "B2"], [f"ps{br_}"])
            self.mm(self.ps[bi_][:, 0:w_], lB(j, 1), ubv, True, True, [("lB", j, 1, 0), ("lB", j, 1, 1), "B2"], [f"ps{bi_}"])
            P.op("act", lambda e, w_=w_, cre=cre, br_=br_: e.activation(cre[:, 0:w_], self.ps[br_][:, 0:w_], AF.Copy), reads=[f"ps{br_}"], writes=[kre])
            P.op("act", lambda e, w_=w_, cim=cim, bi_=bi_: e.activation(cim[:, 0:w_], self.ps[bi_][:, 0:w_], AF.Copy), reads=[f"ps{bi_}"], writes=[kim])
            cv, sv = cosT[:, c0:c1], sinT[:, c0:c1]
            tC, tD = prod[2], prod[3]
            dve(lambda e, w_=w_, cv=cv, cre=cre: e.tensor_tensor(tA[:, 0:w_], cre[:, 0:w_], cv, ALU.mult), [kre, tk], ["tmpB2"])
            dve(lambda e, w_=w_, sv=sv, cim=cim: e.tensor_tensor(tB[:, 0:w_], cim[:, 0:w_], sv, ALU.mult), [kim, tk], ["tmpB3"])
            slot_end()
            dve(lambda e, w_=w_, cv=cv, cim=cim: e.tensor_tensor(tC[:, 0:w_], cim[:, 0:w_], cv, ALU.mult), [kim, tk], ["prod2"])
            dve(lambda e, w_=w_, sv=sv, cre=cre: e.tensor_tensor(tD[:, 0:w_], cre[:, 0:w_], sv, ALU.mult), [kre, tk], ["prod3"])
            slot_end()
            dve(lambda e, w_=w_, c0=c0, c1=c1: e.tensor_tensor(gin[0][:, c0:c1], tA[:, 0:w_], tB[:, 0:w_], ALU.add), ["tmpB2", "tmpB3"], ["ginr"])
            dve(lambda e, w_=w_, c0=c0, c1=c1: e.tensor_tensor(gin[1][:, c0:c1], tC[:, 0:w_], tD[:, 0:w_], ALU.subtract), ["prod2", "prod3"], ["gini"])
            slot_end()
        for part, (dst, dk, gk) in enumerate([(g_re, "B0", "ginr"), (g_im, "B1", "gini")]):
            src = gin[part]
            dve(lambda e, dst=dst, src=src: e.tensor_tensor_scan(dst[:, 0:T], rm.to_broadcast([128, T]), src, 0.0, ALU.mult, ALU.add),
                [gk, "rmag"], [dk])
            slot_end()
        for k, (c0, c1) in enumerate(BLKS):
            w_ = c1 - c0
            t0, t1, bt, rev = tcols(k)
            cv, sv = cosT[:, c0:c1], sinT[:, c0:c1]
            plist = [(cv, g_re, "B0", 0), (sv, g_im, "B1", 1), (sv, g_re, "B0", 2), (cv, g_im, "B1", 2)]
            for pi, (tab, gg, gk, var) in enumerate(plist):
                pt, pk = (prod[pi], f"prod{pi}") if k % 2 == 0 else (self.tmpB[pi], f"tmpB{pi}")
                dve(lambda e, pt=pt, tab=tab, gg=gg, c0=c0, c1=c1, w_=w_: e.tensor_tensor(pt[:, 0:w_], tab, gg[:, c0:c1], ALU.mult),
                    [tk, gk], [pk])
                first = (d == 0 and j == 0 and pi == 0)
                rhs = pt[:, 0:w_][:, ::-1] if rev else pt[:, 0:w_]
                self.mm(self.ps[bt][:, 0:w_], lC(j, var), rhs, first, False, [("lC", j, var, 0), ("lC", j, var, 1), pk], [f"ps{bt}"])
                if pi % 2 == 1:
                    slot_end()
        while state["done"] < total:
            inserts[state["done"]]()
            state["done"] += 1

    def dump(self, slot, ap, key, bf=False):
        if not self.debug:
            return
        dst = (self.dbgb if bf else self.dbgf)[slot, 0:ap.shape[0], 0:ap.shape[1]]
        self.out_toks.append(self.P.dma(dst, ap, reads=[key]))

    def barrier(self):
        P = self.P
        toks = [(P.sem[e], P.count[e]) for e in ENGS if P.count[e] > 0]
        toks += [(s, 16 * c) for s, c in zip(P.dma_sems, P.dma_cnt) if c > 0]
        for e in ENGS:
            P._emit_waits(e, toks)
        P.lastw.clear()
        P.readers.clear()

    def rope_tables(self):
        P = self.P
        yf = self.ygrp[:].rearrange("p s t -> p (s t)").bitcast(F32)
        posr = yf[0:64, 0:2048]
        posc = yf[0:64, 2048:4096]
        sm = self.small
        pidx = sm[0:64, 80:81].bitcast(I32)
        p16 = sm[0:64, 81:82].bitcast(I32)
        pb16 = sm[0:64, 82:83].bitcast(I32)
        invt = sm[0:64, 83:84]
        mA = sm[0:64, 84:85]
        mB = sm[0:64, 85:86]
        halfpi = sm[0:64, 86:87]
        LWb = self.LW[:].bitcast(BF16)
        self.cosT = LWb[0:64, 0:2048]
        self.sinT = LWb[0:64, 2048:4096]
        K = ["ropescr"]
        P.op("pool", lambda e: e.iota(posr.rearrange("p (r c) -> p r c", c=64), [[1, 32], [0, 64]], base=0, channel_multiplier=0,
                                      allow_small_or_imprecise_dtypes=True), writes=["posr"])
        P.op("pool", lambda e: e.iota(posc.rearrange("p (r c) -> p r c", c=64), [[0, 32], [1, 64]], base=0, channel_multiplier=0,
                                      allow_small_or_imprecise_dtypes=True), writes=["posc"])
        P.op("pool", lambda e: e.iota(pidx, [[0, 1]], base=0, channel_multiplier=1), writes=["pidx"])
        P.op("dve", lambda e: e.tensor_single_scalar(p16, pidx, 15, ALU.bitwise_and), reads=["pidx"], writes=["p16"])
        P.op("dve", lambda e: e.tensor_single_scalar(pb16, pidx, 16, ALU.bitwise_and), reads=["pidx"], writes=["pb16"])
        P.op("dve", lambda e: e.tensor_copy(invt, p16), reads=["p16"], writes=["invt"])
        P.op("act", lambda e: e.activation(invt, invt, AF.Exp, scale=-math.log(10000.0) / 16.0), reads=["invt"], writes=["invt"])
        P.op("dve", lambda e: e.tensor_scalar_mul(invt, invt, 1.0 / TWO_PI), reads=["invt"], writes=["invt"])
        P.op("dve", lambda e: e.tensor_copy(mB, pb16), reads=["pb16"], writes=["mB"])
        P.op("dve", lambda e: e.tensor_scalar_mul(mB, mB, 1.0 / 16.0), reads=["mB"], writes=["mB"])
        P.op("dve", lambda e: e.tensor_scalar(mA, mB, -1.0, 1.0, ALU.mult, ALU.add), reads=["mB"], writes=["mA"])
        P.op("dve", lambda e: e.memset(halfpi, math.pi / 2), writes=["halfpi"])
        P.op("dve", lambda e: e.tensor_scalar_mul(posr, posr, mA), reads=["posr", "mA"], writes=["posr"])
        P.op("dve", lambda e: e.scalar_tensor_tensor(posr, posc, mB, posr, ALU.mult, ALU.add), reads=["posr", "posc", "mB"], writes=["posr"])
        P.op("dve", lambda e: e.tensor_scalar_mul(posr, posr, invt), reads=["posr", "invt"], writes=["posr"])
        pci = posc.bitcast(I32)
        P.op("dve", lambda e: e.tensor_copy(pci, posr), reads=["posr"], writes=["posc"])
        P.op("dve", lambda e: e.tensor_tensor(posr, posr, pci, ALU.subtract), reads=["posr", "posc"], writes=["posr"])
        P.op("act", lambda e: e.activation(self.sinT, posr, AF.Sin, scale=TWO_PI), reads=["posr"], writes=["sinT"])
        P.op("dve", lambda e: e.scalar_tensor_tensor(posr, posr, -1.0, posr, ALU.mult, ALU.max), reads=["posr"], writes=["posr"])
        P.op("act", lambda e: e.activation(self.cosT, posr, AF.Sin, bias=halfpi, scale=-TWO_PI), reads=["posr", "halfpi"], writes=["cosT"])
        self.Rm = LWb[0:64, 4096:4160]
        P.op("pool", lambda e: e.tensor_scalar_mul(self.Rm[:, 0:32], self.identb[0:64, 32:64], -1.0), reads=["identb"], writes=["Rm"])
        P.op("pool", lambda e: e.tensor_copy(self.Rm[:, 32:64], self.identb[0:64, 0:32]), reads=["identb"], writes=["Rm"])

    def rope_apply(self, buf, key, blocks):
        P = self.P
        for (c0, c1) in blocks:
            w = c1 - c0
            ps = self.ps[7]
            self.mm(ps[0:64, 0:w], self.Rm, buf[0:64, c0:c1], True, True, [key, "Rm"], ["ps7"])
            t1 = self.tmpA[0]
            t2 = self.tmpA[1]
            P.op("pool", lambda e, c0=c0, c1=c1, w=w: e.tensor_tensor(t1[0:64, 0:w], buf[0:64, c0:c1], self.cosT[:, c0 - CTX:c1 - CTX], ALU.mult),
                 reads=[key, "cosT"], writes=["tmpA0"])
            P.op("dve", lambda e, c0=c0, c1=c1, w=w: e.tensor_tensor(t2[0:64, 0:w], ps[0:64, 0:w], self.sinT[:, c0 - CTX:c1 - CTX], ALU.mult),
                 reads=["ps7", "sinT"], writes=["tmpA1"])
            P.op("dve", lambda e, c0=c0, c1=c1, w=w: e.tensor_tensor(buf[0:64, c0:c1], t1[0:64, 0:w], t2[0:64, 0:w], ALU.add),
                 reads=["tmpA0", "tmpA1"], writes=[key])

    def proj_tile(self, w, nk, m, src_fn, src_keys, wkey, evac, blks=None):
        for bi, (c0, c1) in enumerate(blks or BLKS):
            bank = self.proj_banks[self.ps_rr % len(self.proj_banks)]
            self.ps_rr += 1
            ps = self.ps[bank]
            for k in range(nk):
                self.mm(ps[0:m, 0:c1 - c0], w[:, k, :], src_fn(k)[:, c0:c1], k == 0, k == nk - 1,
                        [wkey, src_keys[k]], [f"ps{bank}"])
            evac(ps, f"ps{bank}", c0, c1)

    def mla_layer(self, l):
        P = self.P
        o = l // 2
        with_ctx = l < DEPTH - 1
        Win = self.W["mla_w_in"][o]
        Wuq = self.W["mla_w_uq"][o]
        Wukv = self.W["mla_w_ukv"][o]
        Wout = self.W["mla_w_out"][o]
        self.ps_rr = 0
        self.proj_banks = [0, 1, 2, 3, 4, 5, 6]
        self.rope_tables()
        F0b = self.F[0][:].bitcast(BF16)
        F1b = self.F[1][:].bitcast(BF16)
        F2b = self.F[2][:].bitcast(BF16)
        cqn = [F0b[:, 0:T], F0b[:, T:2 * T], F1b[:, 0:T]]
        vh = F1b[:, T:2 * T].rearrange("p (t d) -> p t d", d=128)
        ckvn = [F2b[:, 0:T], F2b[:, T:2 * T]]
        kr, qn, qr, kn = self.B[0], self.B[1], self.B[2], self.B[3]
        yf = self.ygrp[:].rearrange("p s t -> p (s t)").bitcast(F32)
        nsrc = lambda k: self.nT[:, k, :]
        nkeys = [("nT", k) for k in range(NKT)]
        sm = self.small
        gq = sm[:, 96:99]
        gkv = sm[:, 99:101]
        P.dma(gq, self.W["mla_q_norm"][o].rearrange("(k p) -> p k", p=128), writes=["gq"], allow_slow_non_contiguous=True)
        P.dma(gkv, self.W["mla_kv_norm"][o].rearrange("(k p) -> p k", p=128), writes=["gkv"], allow_slow_non_contiguous=True)

        def copy_evac(dst, dkey, m=128, scale=None, eng="act"):
            def f(ps, pkey, c0, c1):
                if eng == "act":
                    if scale is None:
                        P.op("act", lambda e: e.activation(dst[0:m, c0:c1], ps[0:m, 0:c1 - c0], AF.Copy), reads=[pkey], writes=[dkey])
                    else:
                        P.op("act", lambda e: e.activation(dst[0:m, c0:c1], ps[0:m, 0:c1 - c0], AF.Copy, scale=scale), reads=[pkey], writes=[dkey])
                else:
                    P.op("dve", lambda e: e.tensor_copy(dst[0:m, c0:c1], ps[0:m, 0:c1 - c0]), reads=[pkey], writes=[dkey])
            return f

        for k in range(3):
            w, wk = self.load_cast(Win[:, k * 128:(k + 1) * 128], 8, 128)
            self.proj_tile(w, 8, 128, nsrc, nkeys, wk, copy_evac(cqn[k], f"cqn{k}", eng="act" if k % 2 == 0 else "dve"))
        for k in range(2):
            w, wk = self.load_cast(Win[:, 384 + k * 128:384 + (k + 1) * 128], 8, 128)
            self.proj_tile(w, 8, 128, nsrc, nkeys, wk, copy_evac(ckvn[k], f"ckvn{k}", eng="dve" if k % 2 == 0 else "act"))
        w, wk = self.load_cast(Win[:, 640:704], 8, 64)
        self.proj_tile(w, 8, 64, nsrc, nkeys, wk, copy_evac(kr, "B0", m=64))
        rq = yf[:, 0:T]
        rkv = yf[:, T:2 * T]
        self.rstd_into(rq, "rq", cqn, [f"cqn{k}" for k in range(3)], 384)
        self.rstd_into(rkv, "rkv", ckvn, [f"ckvn{k}" for k in range(2)], 256)
        for k in range(3):
            P.op("dve", lambda e, k=k: e.scalar_tensor_tensor(cqn[k], cqn[k], gq[:, k:k + 1], rq, ALU.mult, ALU.mult),
                 reads=[f"cqn{k}", "gq", "rq"], writes=[f"cqn{k}"])
        for k in range(2):
            P.op("dve", lambda e, k=k: e.scalar_tensor_tensor(ckvn[k], ckvn[k], gkv[:, k:k + 1], rkv, ALU.mult, ALU.mult),
                 reads=[f"ckvn{k}", "gkv", "rkv"], writes=[f"ckvn{k}"])
        self.rope_apply(kr, "B0", BLKS[1:])
        P.op("pool", lambda e: e.memset(kr[64:128, 0:T], 0.0), writes=["B0"])
        P.op("pool", lambda e: e.memset(qr[64:128, 0:T], 0.0), writes=["B2"])
        qblks = BLKS if with_ctx else BLKS[1:]
        cq_src = lambda k: cqn[k]
        cq_keys = [f"cqn{k}" for k in range(3)]
        kv_src = lambda k: ckvn[k]
        kv_keys = [f"ckvn{k}" for k in range(2)]
        self.proj_banks = [4, 5, 6]

        def head_loads(h, defer):
            r = [self.load_cast(Wuq[:, h * 192:(h + 1) * 192], 3, 192, defer=defer),
                 self.load_cast(Wukv[:, h * 256:(h + 1) * 256], 2, 256, defer=defer)]
            if not defer:
                r.append(self.load_cast(Win[:, 704 + h * 128:704 + (h + 1) * 128], 8, 128))
            return r
        pend = None
        for h in range(8):
            if pend is None:
                pend = head_loads(h, False)
            (wq, wqk), (wkv, wkvk), (wg, wgk) = [p[0:2] for p in pend]
            pend = None
            self.proj_tile(wq[:, :, 0:128], 3, 128, cq_src, cq_keys, wqk, copy_evac(qn, "B1", scale=MLA_SCALE))
            self.proj_tile(wq[:, :, 128:192], 3, 64, cq_src, cq_keys, wqk, copy_evac(qr, "B2", m=64, scale=MLA_SCALE))
            self.rope_apply(qr, "B2", BLKS[1:])
            self.proj_tile(wkv[:, :, 0:128], 2, 128, kv_src, kv_keys, wkvk, copy_evac(kn, "B3", eng="dve"))
            for t0 in range(0, 18, 4):
                nt = min(4, 18 - t0)
                bank = 4 + (self.ps_rr % 3)
                self.ps_rr += 1
                ps = self.ps[bank]
                for j in range(nt):
                    tt = t0 + j
                    for rk in range(2):
                        self.mm(ps[:, j * 128:(j + 1) * 128], ckvn[rk][:, tt * 128:(tt + 1) * 128], wkv[:, rk, 128:256],
                                rk == 0, rk == 1, [f"ckvn{rk}", wkvk], [f"ps{bank}"])
                P.op("dve", lambda e, ps=ps, t0=t0, nt=nt: e.tensor_copy(
                    vh[:, t0:t0 + nt, :], ps[:, 0:nt * 128].rearrange("p (t d) -> p t d", d=128)),
                    reads=[f"ps{bank}"], writes=["vh"])
            def ev_gate(ps, pkey, c0, c1, h=h):
                P.op("act", lambda e: e.activation(self.ygrp[:, h % 4, c0:c1], ps[:, 0:c1 - c0], AF.Silu), reads=[pkey],
                     writes=[("og", h % 4), "rq", "rkv"])
            self.proj_tile(wg, 8, 128, nsrc, nkeys, wgk, ev_gate, blks=qblks)
            if h + 1 < 8 and h % 4 != 3:
                pend = head_loads(h + 1, True)
            LOOK = 2
            for qi, (c0, c1) in enumerate(qblks):
                if qi == 1 and pend is not None:
                    for p in pend:
                        p[2]("act")
                    d3, k3, c3 = self.load_cast(Win[:, 704 + (h + 1) * 128:704 + (h + 2) * 128], 8, 128, defer=True)
                    c3("act")
                    pend.append((d3, k3))
                w_ = c1 - c0
                nkt = 2 if c0 == 0 else 18
                Ops = self.ps[qi % 2]
                Dps = self.ps[2 + qi % 2]
                okey, dkey = f"ps{qi % 2}", f"ps{2 + qi % 2}"
                Ebuf = {}

                def emit_S(kt, w_=w_, c0=c0, c1=c1):
                    sb = 4 + (self.ps_rr % 3)
                    self.ps_rr += 1
                    S = self.ps[sb]
                    ks = slice(kt * 128, (kt + 1) * 128)
                    self.mm(S[:, 0:w_], kn[:, ks], qn[:, c0:c1], True, False, ["B3", "B1"], [f"ps{sb}"])
                    self.mm(S[:, 0:w_], kr[:, ks], qr[:, c0:c1], False, True, ["B0", "B2"], [f"ps{sb}"])
                    ei = kt % 3
                    E = self.tmpB[ei]
                    P.op("act", lambda e, E=E, S=S, w_=w_: e.activation(E[:, 0:w_], S[:, 0:w_], AF.Exp),
                         reads=[f"ps{sb}"], writes=[f"tmpB{ei}"])
                    Ebuf[kt] = (E, f"tmpB{ei}")

                acc = self.tmpA[2]

                def emit_OD(kt, w_=w_, nkt=nkt, Ops=Ops, Dps=Dps, okey=okey, dkey=dkey):
                    E, ek = Ebuf[kt]
                    self.mm(Ops[:, 0:w_], vh[:, kt, :], E[:, 0:w_], kt == 0, kt == nkt - 1, ["vh", ek], [okey])
                    if kt % 2 == 1:
                        self.mm(Dps[:, 0:w_], self.onesb[:], E[:, 0:w_], kt == 1, False, ["onesb", ek], [dkey])
                    elif kt == 0:
                        P.op("dve", lambda e: e.tensor_copy(acc[:, 0:w_], E[:, 0:w_]), reads=[ek], writes=["tmpA2"])
                    else:
                        P.op("dve", lambda e: e.tensor_tensor(acc[:, 0:w_], acc[:, 0:w_], E[:, 0:w_], ALU.add), reads=[ek, "tmpA2"], writes=["tmpA2"])
                    if kt == nkt - 1:
                        accb = self.tmpB[3]
                        P.op("dve", lambda e: e.tensor_copy(accb[:, 0:w_], acc[:, 0:w_]), reads=["tmpA2"], writes=["tmpB3"])
                        self.mm(Dps[:, 0:w_], self.onesb[:], accb[:, 0:w_], False, True, ["onesb", "tmpB3"], [dkey])
                for kt in range(min(LOOK, nkt)):
                    emit_S(kt)
                for kt in range(nkt):
                    if kt + LOOK < nkt:
                        emit_S(kt + LOOK)
                    emit_OD(kt)
                rec = self.tmpA[0]
                o1 = self.tmpA[1]
                P.op("dve", lambda e, Dps=Dps, w_=w_: e.reciprocal(rec[:, 0:w_], Dps[:, 0:w_]), reads=[dkey], writes=["tmpA0"])
                P.op("dve", lambda e, Ops=Ops, w_=w_: e.tensor_tensor(o1[:, 0:w_], Ops[:, 0:w_], rec[:, 0:w_], ALU.mult),
                     reads=[okey, "tmpA0"], writes=["tmpA1"])
                P.op("pool", lambda e, h=h, c0=c0, c1=c1, w_=w_: e.tensor_tensor(self.ygrp[:, h % 4, c0:c1], o1[:, 0:w_], self.ygrp[:, h % 4, c0:c1], ALU.mult),
                     reads=["tmpA1", ("og", h % 4)], writes=[("og", h % 4)])
            if h % 4 == 3:
                self.out_proj(l, Wout[(h // 4) * 512:(h // 4 + 1) * 512, :], qblks)
        self.barrier()

    def out_proj(self, l, Wrows, blks, nk=4):
        P = self.P
        for ot in range(NKT):
            w, wk = self.load_cast(Wrows[:, ot * 128:(ot + 1) * 128], nk, 128)
            for (c0, c1) in blks:
                s = 1 if c0 == 0 else 0
                bank = self.proj_banks[self.ps_rr % len(self.proj_banks)]
                self.ps_rr += 1
                ps = self.ps[bank]
                for j in range(nk):
                    self.mm(ps[:, 0:c1 - c0], w[:, j, :], self.ygrp[:, j, c0:c1], j == 0, j == nk - 1, [wk, ("og", j)], [f"ps{bank}"])
                P.op("dve", lambda e, ot=ot, c0=c0, c1=c1, s=s, ps=ps: e.scalar_tensor_tensor(
                    self.xT[:, ot, c0:c1], ps[:, 0:c1 - c0], self.modT[:, l, 16 + ot, s:s + 1], self.xT[:, ot, c0:c1], ALU.mult, ALU.add),
                    reads=[f"ps{bank}", "modT", ("xT", ot)], writes=[("xT", ot)])


_NC_CACHE = {}


def _get_nc(layers, debug=False):
    key = (tuple(layers), debug)
    if key not in _NC_CACHE:
        _NC_CACHE[key] = Builder(layers, debug).build()
    return _NC_CACHE[key]


def kernel(_layers=(0, 1, 2, 3), _ncores=8, _debug=False, **inputs):
    nc = _get_nc(_layers, _debug)
    in_maps = []
    for b in range(_ncores):
        m = {}
        for name, shp in WSPEC:
            a = np.asarray(inputs[name], dtype=np.float32)
            if name in ("x", "c", "ctx"):
                a = a[b]
            m[name] = np.ascontiguousarray(a).reshape(shp)
        in_maps.append(m)
    res = run_bass_kernel_spmd(nc, in_maps, core_ids=list(range(_ncores)))
    if _debug:
        return res.results[0]
    return np.stack([np.asarray(r["out"], dtype=np.float32).reshape(SEQ, D) for r in res.results], axis=0)
```

```python
import math
import numpy as np
import concourse.bass as bass
import concourse.mybir as mybir
from concourse.bass_utils import run_bass_kernel_spmd

F32 = mybir.dt.float32
BF16 = mybir.dt.bfloat16
I32 = mybir.dt.int32
AF = mybir.ActivationFunctionType
ALU = mybir.AluOpType
AX = mybir.AxisListType

D = 1024
SEQ = 2048
CTX = 256
T = CTX + SEQ
NKT = 8
DEPTH = 4
EPS = 1e-6
BLKS = [(0, 256), (256, 768), (768, 1280), (1280, 1792), (1792, 2304)]
MLA_SCALE = 1.0 / math.sqrt(192.0)
TWO_PI = 2.0 * math.pi

DBG_D = 0
ENGS = ["pe", "act", "dve", "pool", "sp"]


class Prog:
    def __init__(self, nc):
        self.nc = nc
        self.streams = {e: [] for e in ENGS}
        self.count = {e: 0 for e in ENGS}
        self.sem = {e: nc.alloc_semaphore(name=f"prog_{e}") for e in ENGS}
        self.waited = {}
        self.lastw = {}
        self.readers = {}
        self.dma_sems = [nc.alloc_semaphore(name=f"dma_{i}") for i in range(16)]
        self.dma_cnt = [0] * 16
        self.dma_rr = 0

    def _deps(self, reads, writes, eng=None):
        deps = []
        for k in reads:
            if k in self.lastw:
                deps.append(self.lastw[k])
        own = self.sem.get(eng)
        for k in writes:
            if k in self.lastw:
                lw = self.lastw[k]
                if not (own is not None and lw[0] is own and k not in reads):
                    deps.append(lw)
            deps.extend(self.readers.get(k, []))
        return deps

    def _emit_waits(self, eng, deps):
        need = {}
        for (s, v) in deps:
            if eng == "pe" and s is self.sem["pe"]:
                continue
            key = id(s)
            if v > self.waited.get((eng, key), 0):
                if key not in need or need[key][1] < v:
                    need[key] = (s, v)
        for key, (s, v) in need.items():
            self.waited[(eng, key)] = v
            self.streams[eng].append(lambda e, s=s, v=v: e.wait_ge(s, v))

    def _commit(self, tok, reads, writes):
        for k in writes:
            self.lastw[k] = tok
            self.readers[k] = []
        for k in reads:
            if k not in writes:
                self.readers.setdefault(k, []).append(tok)

    def op(self, eng, fn, reads=(), writes=()):
        reads = list(reads)
        writes = list(writes)
        self._emit_waits(eng, self._deps(reads, writes, eng))
        self.count[eng] += 1
        v = self.count[eng]
        s = self.sem[eng]
        self.streams[eng].append(lambda e, fn=fn, s=s: fn(e).then_inc(s, 1))
        tok = (s, v)
        self._commit(tok, reads, writes)
        return tok

    def dma(self, out, in_, reads=(), writes=(), eng="sp", **kw):
        reads = list(reads)
        writes = list(writes)
        i = self.dma_rr
        self.dma_rr = (self.dma_rr + 1) % len(self.dma_sems)
        s = self.dma_sems[i]
        deps = self._deps(reads, writes)
        if self.dma_cnt[i] > 0:
            deps.append((s, 16 * self.dma_cnt[i]))
        self._emit_waits(eng, deps)
        self.dma_cnt[i] += 1
        v = 16 * self.dma_cnt[i]
        self.streams[eng].append(
            lambda e, out=out, in_=in_, s=s, kw=kw: e.dma_start(out=out, in_=in_, **kw).then_inc(s, 16))
        tok = (s, v)
        self._commit(tok, reads, writes)
        return tok

    def finish(self, final_tokens):
        nc = self.nc
        self._emit_waits("sp", final_tokens)
        with nc.Block() as block:
            @block.tensor
            def _(e):
                for f in self.streams["pe"]:
                    f(e)

            @block.scalar
            def _(e):
                for f in self.streams["act"]:
                    f(e)

            @block.vector
            def _(e):
                for f in self.streams["dve"]:
                    f(e)

            @block.gpsimd
            def _(e):
                for f in self.streams["pool"]:
                    f(e)

            @block.sync
            def _(e):
                for f in self.streams["sp"]:
                    f(e)


WSPEC = [
    ("x", [SEQ, D]), ("c", [D]), ("ctx", [CTX, D]), ("c_ctx", [D]),
    ("norm_g", [4, D]), ("mod_w", [4, D, 3 * D]), ("mod_b", [4, 3 * D]),
    ("ev_w_in", [2, D, 3072]), ("lru_conv_w", [2, 4, D]), ("lru_conv_b", [2, D]),
    ("lru_wr", [2, 2, 8, 128, 128]), ("lru_br", [2, 2, D]),
    ("lru_wi", [2, 2, 8, 128, 128]), ("lru_bi", [2, 2, D]), ("lru_lam", [2, 2, D]),
    ("s5_lam_re", [2, 2, 32, 64]), ("s5_lam_im", [2, 2, 32, 64]), ("s5_log_dt", [2, 2, 32, 64]),
    ("s5_b_re", [2, 2, 32, 64, 16]), ("s5_b_im", [2, 2, 32, 64, 16]),
    ("s5_c_re", [2, 2, 32, 16, 64]), ("s5_c_im", [2, 2, 32, 16, 64]),
    ("s5_d", [2, 32, 16]), ("s5_glu_w", [2, 512, 512]), ("s5_glu_b", [2, 512]),
    ("ev_w_out", [2, 1536, D]),
    ("mla_w_in", [2, D, 1728]), ("mla_q_norm", [2, 384]), ("mla_w_uq", [2, 384, 1536]),
    ("mla_kv_norm", [2, 256]), ("mla_w_ukv", [2, 256, 2048]), ("mla_w_out", [2, D, D]),
    ("final_g", [D]),
]


class Builder:
    def __init__(self, layers=(0, 1, 2, 3), debug=False):
        self.layers = list(layers)
        nc = bass.Bass("TRN2", target_bir_lowering=False)
        self.nc = nc
        self.P = Prog(nc)
        self.W = {}
        for name, shp in WSPEC:
            self.W[name] = nc.dram_tensor(name, shp, F32, kind="ExternalInput").ap()
        self.out = nc.dram_tensor("out", [SEQ, D], F32, kind="ExternalOutput").ap()
        self.debug = debug
        if debug:
            self.dbgf = nc.dram_tensor("dbgf", [8, 128, T], F32, kind="ExternalOutput").ap()
            self.dbgb = nc.dram_tensor("dbgb", [8, 128, T], BF16, kind="ExternalOutput").ap()
        A = nc.alloc_sbuf_tensor
        self.xT = A("xT", [128, NKT, T], F32)
        self.nT = A("nT", [128, NKT, T], BF16)
        self.modT = A("modT", [128, DEPTH, 24, 2], F32)
        self.identf = A("identf", [128, 128], F32)
        self.identb = A("identb", [128, 128], BF16)
        self.onesb = A("onesb", [128, 128], BF16)
        self.ygrp = A("ygrp", [128, 4, T], BF16)
        self.F = [A(f"F{i}", [128, T], F32) for i in range(3)]
        self.B = [A(f"B{i}", [128, T + 8], BF16) for i in range(4)]
        self.stage = [A(f"stage{i}", [128, 1024], F32) for i in range(2)]
        self.stage_i = 0
        self.wbf = [A(f"wbf{i}", [128, 1024], BF16) for i in range(3)]
        self.wbf_i = 0
        self.LW = A("LW", [128, 2304], F32)
        self.small = A("small", [128, 256], F32)
        self.tmpA = [A(f"tmpA{i}", [128, 512], F32) for i in range(3)]
        self.tmpB = [A(f"tmpB{i}", [128, 512], BF16) for i in range(4)]
        self.ps = [nc.alloc_psum_tensor(f"ps{i}", [128, 512], F32) for i in range(8)]

    def load_cast(self, src_ap, kt, ncols, dst=None, dst_key=None, defer=False):
        P = self.P
        si = self.stage_i
        self.stage_i = (si + 1) % len(self.stage)
        st = self.stage[si]
        stv = st[:, 0:kt * ncols].rearrange("p (k c) -> p k c", k=kt)
        P.dma(stv, src_ap.rearrange("(k p) c -> p k c", p=128), writes=[f"stage{si}"])
        if dst is None:
            wi = self.wbf_i
            self.wbf_i = (wi + 1) % len(self.wbf)
            dst = self.wbf[wi][:, 0:kt * ncols].rearrange("p (k c) -> p k c", k=kt)
            dst_key = f"wbf{wi}"

        def cast(eng="pool"):
            if eng == "act":
                P.op("act", lambda e: e.activation(dst, stv, AF.Copy), reads=[f"stage{si}"], writes=[dst_key])
            else:
                P.op(eng, lambda e: e.tensor_copy(dst, stv), reads=[f"stage{si}"], writes=[dst_key])
        if defer:
            return dst, dst_key, cast
        cast()
        return dst, dst_key

    def mm(self, out, lhsT, rhs, start, stop, reads, writes):
        return self.P.op("pe", lambda e: e.matmul(out, lhsT, rhs, start=start, stop=stop),
                         reads=reads, writes=writes)

    def consts(self):
        P = self.P
        idf, idb, ob = self.identf, self.identb, self.onesb
        P.op("pool", lambda e: e.memset(idf[:], 0.0), writes=["identf"])
        P.op("pool", lambda e: e.affine_select(idf[:], idf[:], [[-1, 128]], ALU.not_equal, 1.0, base=0,
                                               channel_multiplier=1), reads=["identf"], writes=["identf"])
        P.op("pool", lambda e: e.tensor_copy(idb[:], idf[:]), reads=["identf"], writes=["identb"])
        P.op("pool", lambda e: e.memset(ob[:], 1.0), writes=["onesb"])

    def load_x(self):
        P = self.P
        for tt in range(T // 128):
            st = self.stage[tt % 2]
            skey = f"stage{tt % 2}"
            src = self.W["ctx"][tt * 128:(tt + 1) * 128, :] if tt < 2 else self.W["x"][(tt - 2) * 128:(tt - 1) * 128, :]
            P.dma(st[:], src, writes=[skey])
            for half in range(2):
                ps = self.ps[(tt * 2 + half) % 8]
                pkey = f"ps{(tt * 2 + half) % 8}"
                for q in range(4):
                    kt = half * 4 + q
                    P.op("pe", lambda e, ps=ps, q=q, kt=kt, st=st: e.transpose(
                        ps[:, q * 128:(q + 1) * 128], st[:, kt * 128:(kt + 1) * 128], self.identf[:]),
                        reads=[skey, "identf"], writes=[pkey])
                dst = self.xT[:, half * 4:half * 4 + 4, tt * 128:(tt + 1) * 128]
                srcp = ps[:].rearrange("p (q c) -> p q c", q=4)
                eng = "dve" if half == 0 else "act"
                if eng == "dve":
                    P.op("dve", lambda e, dst=dst, srcp=srcp: e.tensor_copy(dst, srcp), reads=[pkey],
                         writes=[("xT", half * 4 + q) for q in range(4)])
                else:
                    P.op("act", lambda e, dst=dst, srcp=srcp: e.activation(dst, srcp, AF.Copy), reads=[pkey],
                         writes=[("xT", half * 4 + q) for q in range(4)])

    def modulation(self):
        P = self.P
        sm = self.small
        cc = sm[:, 0:16]
        csb = sm[:, 16:24].bitcast(BF16)
        P.dma(cc[:, 0:8], self.W["c"].rearrange("(k p) -> p k", p=128), writes=["cc"], allow_slow_non_contiguous=True)
        P.dma(cc[:, 8:16], self.W["c_ctx"].rearrange("(k p) -> p k", p=128), writes=["cc"], allow_slow_non_contiguous=True)
        P.op("act", lambda e: e.activation(csb, cc, AF.Silu), reads=["cc"], writes=["cs"])
        csv = csb.rearrange("p (s k) -> p k s", s=2)
        nTf = self.nT[:].rearrange("p k t -> p (k t)").bitcast(F32)
        stg = [nTf[:, i * 1024:(i + 1) * 1024] for i in range(6)]
        nxt = 0
        for l in range(DEPTH):
            for kt in range(NKT):
                for h in range(3):
                    si = nxt % 6
                    nxt += 1
                    st = stg[si]
                    P.dma(st, self.W["mod_w"][l, kt * 128:(kt + 1) * 128, h * 1024:(h + 1) * 1024], writes=[f"mstg{si}"])
                    wi = self.wbf_i
                    self.wbf_i = (wi + 1) % len(self.wbf)
                    wb = self.wbf[wi]
                    eng = "act" if nxt % 2 == 0 else "dve"
                    if eng == "act":
                        P.op("act", lambda e, wb=wb, st=st: e.activation(wb[:], st, AF.Copy), reads=[f"mstg{si}"], writes=[f"wbf{wi}"])
                    else:
                        P.op("dve", lambda e, wb=wb, st=st: e.tensor_copy(wb[:], st), reads=[f"mstg{si}"], writes=[f"wbf{wi}"])
                    for q in range(2):
                        bank = h * 2 + q
                        self.mm(self.ps[bank][0:2, :], csv[:, kt, :], wb[:, q * 512:(q + 1) * 512],
                                kt == 0, kt == NKT - 1, ["cs", f"wbf{wi}"], [f"ps{bank}"])
            mrow = nTf[:, 6144:9216][0:2, :]
            bro = self.F[0][0:2, 0:2304]
            bro2 = self.F[1][0:2, 0:768]
            for s_ in range(2):
                P.dma(bro[s_:s_ + 1, :], self.W["mod_b"][l:l + 1, 0:2304], writes=["bro", "F0"])
                P.dma(bro2[s_:s_ + 1, :], self.W["mod_b"][l:l + 1, 2304:3072], writes=["bro", "F1"])
            for bank in range(6):
                c0 = bank * 512
                if c0 + 512 <= 2304:
                    bsrc = bro[:, c0:c0 + 512]
                    P.op("dve", lambda e, bank=bank, bsrc=bsrc, c0=c0: e.tensor_tensor(
                        mrow[:, c0:c0 + 512], self.ps[bank][0:2, :], bsrc, ALU.add), reads=[f"ps{bank}", "bro", "F0", "F1"], writes=["mrow"])
                else:
                    for (a0, a1) in [(c0, min(c0 + 512, 2304)), (max(c0, 2304), c0 + 512)]:
                        if a1 <= a0:
                            continue
                        bsrc = bro[:, a0:a1] if a1 <= 2304 else bro2[:, a0 - 2304:a1 - 2304]
                        P.op("dve", lambda e, bank=bank, bsrc=bsrc, a0=a0, a1=a1, c0=c0: e.tensor_tensor(
                            mrow[:, a0:a1], self.ps[bank][0:2, a0 - c0:a1 - c0], bsrc, ALU.add), reads=[f"ps{bank}", "bro", "F0", "F1"], writes=["mrow"])
            for j in range(24):
                self.mm(self.ps[6][:, 2 * j:2 * j + 2], mrow[:, j * 128:(j + 1) * 128], self.identf[0:2, 0:2],
                        True, True, ["mrow", "identf"], ["ps6"])
            P.op("dve", lambda e, l=l: e.tensor_copy(self.modT[:, l].rearrange("p j s -> p (j s)"), self.ps[6][:, 0:48]),
                 reads=["ps6"], writes=["modT"])

    def norm_mod(self, l):
        P = self.P
        sm = self.small
        g = sm[:, 32:40]
        gm = sm[:, 40:56]
        P.dma(g, self.W["norm_g"][l].rearrange("(k p) -> p k", p=128), writes=["g"], allow_slow_non_contiguous=True)
        for s in range(2):
            P.op("dve", lambda e, s=s: e.scalar_tensor_tensor(
                gm[:, s * 8:(s + 1) * 8], self.modT[:, l, 8:16, s], 1.0, g, ALU.add, ALU.mult),
                reads=["modT", "g"], writes=["gm"])
        rstd = self.F[1]
        self.rstd_into(rstd, "F1", [self.xT[:, kt, :] for kt in range(NKT)], [("xT", kt) for kt in range(NKT)], D)
        for kt in range(NKT):
            for s, (c0, c1) in enumerate([(CTX, T), (0, CTX)]):
                tmp = self.F[2]
                P.op("dve", lambda e, kt=kt, s=s, c0=c0, c1=c1: e.scalar_tensor_tensor(
                    tmp[:, c0:c1], self.xT[:, kt, c0:c1], gm[:, s * 8 + kt:s * 8 + kt + 1], rstd[:, c0:c1], ALU.mult, ALU.mult),
                    reads=[("xT", kt), "gm", "F1"], writes=["F2"])
                P.op("act", lambda e, kt=kt, s=s, c0=c0, c1=c1: e.activation(
                    self.nT[:, kt, c0:c1], tmp[:, c0:c1], AF.Identity, bias=self.modT[:, l, kt, s:s + 1], scale=1.0),
                    reads=["F2", "modT"], writes=[("nT", kt)])

    def rstd_into(self, dst, dst_key, srcs, src_keys, dim, cols=(0, T)):
        P = self.P
        blks = [b for b in BLKS if b[0] >= cols[0] and b[1] <= cols[1]]
        for bi, (c0, c1) in enumerate(blks):
            pb = self.ps[7]
            n = len(srcs)
            for i, (s_ap, sk) in enumerate(zip(srcs, src_keys)):
                sq = self.tmpB[i % 2]
                P.op("act", lambda e, sq=sq, s_ap=s_ap, c0=c0, c1=c1: e.activation(sq[:, 0:c1 - c0], s_ap[:, c0:c1], AF.Square),
                     reads=[sk], writes=[f"tmpB{i % 2}"])
                self.mm(pb[:, 0:c1 - c0], self.onesb[:], sq[:, 0:c1 - c0], i == 0, i == n - 1,
                        [f"tmpB{i % 2}", "onesb"], ["ps7"])
            P.op("act", lambda e, c0=c0, c1=c1: e.activation(dst[:, c0:c1], pb[:, 0:c1 - c0], AF.Sqrt, bias=EPS, scale=1.0 / dim),
                 reads=["ps7"], writes=[dst_key])
            P.op("dve", lambda e, c0=c0, c1=c1: e.reciprocal(dst[:, c0:c1], dst[:, c0:c1]), reads=[dst_key], writes=[dst_key])

    def final(self):
        P = self.P
        gb = self.F[0]
        P.dma(gb[:, 0:D], self.W["final_g"].partition_broadcast(128), writes=["F0"])
        ot = self.F[1]
        for tt in range(SEQ // 128):
            c0 = CTX + tt * 128
            ss = self.small[:, 64 + (tt % 2) * 2:64 + (tt % 2) * 2 + 2]
            sskey = f"ss{tt % 2}"
            pss = []
            P.op("pool", lambda e, ss=ss: e.memset(ss, 0.0), writes=[sskey + "0", sskey + "1"])
            for half in range(2):
                bank = (tt * 2 + half) % 4
                ps = self.ps[bank]
                for q in range(4):
                    kt = half * 4 + q
                    P.op("pe", lambda e, ps=ps, q=q, kt=kt, c0=c0: e.transpose(
                        ps[:, q * 128:(q + 1) * 128], self.xT[:, kt, c0:c0 + 128], self.identf[:]),
                        reads=[("xT", kt), "identf"], writes=[f"ps{bank}"])
                junk = self.tmpA[half]
                P.op("act", lambda e, ps=ps, junk=junk, half=half, ss=ss: e.activation(
                    junk[:], ps[:], AF.Square, accum_out=ss[:, half:half + 1]),
                    reads=[f"ps{bank}"], writes=[f"tmpA{half}", sskey + str(half)])
                pss.append((ps, bank))
            rs = self.small[:, 72 + (tt % 2):73 + (tt % 2)]
            rkey = f"rs{tt % 2}"
            P.op("dve", lambda e, ss=ss, rs=rs: e.tensor_tensor(rs, ss[:, 0:1], ss[:, 1:2], ALU.add),
                 reads=[sskey + "0", sskey + "1"], writes=[rkey])
            P.op("act", lambda e, rs=rs: e.activation(rs, rs, AF.Sqrt, bias=EPS, scale=1.0 / D), reads=[rkey], writes=[rkey])
            P.op("dve", lambda e, rs=rs: e.reciprocal(rs, rs), reads=[rkey], writes=[rkey])
            obuf = ot[:, (tt % 2) * 1024:(tt % 2) * 1024 + 1024]
            okey = f"ot{tt % 2}"
            for half, (ps, bank) in enumerate(pss):
                P.op("dve", lambda e, ps=ps, half=half, rs=rs, obuf=obuf: e.scalar_tensor_tensor(
                    obuf[:, half * 512:(half + 1) * 512], ps[:], rs, gb[:, half * 512:(half + 1) * 512], ALU.mult, ALU.mult),
                    reads=[f"ps{bank}", rkey, "F0"], writes=[okey + str(half)])
            self.out_toks.append(P.dma(self.out[tt * 128:(tt + 1) * 128, :], obuf, reads=[okey + "0", okey + "1"]))

    def build(self):
        self.out_toks = []
        self.consts()
        self.load_x()
        self.modulation()
        for l in self.layers:
            self.norm_mod(l)
            if l % 2 == 0:
                self.even_layer(l)
            else:
                self.mla_layer(l)
        self.final()
        self.P.finish(self.out_toks)
        return self.nc


    def even_layer(self, l):
        P = self.P
        e_ = l // 2
        Win = self.W["ev_w_in"][e_]
        Wout = self.W["ev_w_out"][e_]
        self.ps_rr = 0
        self.proj_banks = [0, 1, 2, 3, 4, 5, 6, 7]
        sm = self.small
        cw = sm[:, 104:136].rearrange("p (k t) -> p k t", k=4)
        cb = sm[:, 136:144]
        br = sm[:, 144:160].rearrange("p (d t) -> p d t", d=2)
        bi = sm[:, 160:176].rearrange("p (d t) -> p d t", d=2)
        coef = sm[:, 176:192].rearrange("p (d t) -> p d t", d=2)
        coef2 = sm[:, 192:208].rearrange("p (d t) -> p d t", d=2)
        ld = lambda dst, src, key: P.dma(dst, src, writes=[key], allow_slow_non_contiguous=True)
        for k in range(4):
            ld(cw[:, k, :], self.W["lru_conv_w"][e_, k].rearrange("(t p) -> p t", p=128), "cw")
        ld(cb, self.W["lru_conv_b"][e_].rearrange("(t p) -> p t", p=128), "cb")
        for d in range(2):
            ld(br[:, d, :], self.W["lru_br"][e_, d].rearrange("(t p) -> p t", p=128), "br")
            ld(bi[:, d, :], self.W["lru_bi"][e_, d].rearrange("(t p) -> p t", p=128), "bi")
            ld(coef[:, d, :], self.W["lru_lam"][e_, d].rearrange("(t p) -> p t", p=128), "coef")
        cf = sm[:, 176:192]
        cf2 = sm[:, 192:208]
        P.op("act", lambda e: e.activation(cf, cf, AF.Exp, scale=-1.0), reads=["coef"], writes=["coef"])
        P.op("act", lambda e: e.activation(cf, cf, AF.Ln, bias=1.0), reads=["coef"], writes=["coef"])
        P.op("dve", lambda e: e.tensor_scalar_mul(cf2, cf, -16.0), reads=["coef"], writes=["coef2"])
        P.op("dve", lambda e: e.tensor_scalar_mul(cf, cf, -8.0), reads=["coef", "coef2"], writes=["coef"])
        nsrc = lambda k: self.nT[:, k, :]
        nkeys = [("nT", k) for k in range(NKT)]
        xap, u, ib, hf = self.B
        F0, F1, F2 = self.F
        dg = [self.tmpB[0][:, k * 128:(k + 1) * 128] for k in range(4)]
        F2b = F2[:].bitcast(BF16)
        hr = F2b[:, 0:T]
        abuf = [(F1, "F1"), (self.LW, "LWa")]
        ibuf = [(ib, "B2"), (F2b[:, T:2 * T], "ibB")]
        for (a, b) in [(0, 2), (258, 262), (2310, 2312)]:
            P.op("pool", lambda e, a=a, b=b: e.memset(xap[:, a:b], 0.0), writes=["B0"])

        def xoff(c0):
            return c0 + 2 if c0 < CTX else c0 + 6

        def lru_loads(h):
            r = {"wxa": self.load_cast(Win[:, h * 128:(h + 1) * 128], 8, 128),
                 "wga": self.load_cast(Win[:, 1024 + h * 128:1024 + (h + 1) * 128], 8, 128)}
            gwb = self.tmpB[2 + h % 2]
            for d in range(2):
                for q, nm in enumerate(["lru_wr", "lru_wi"]):
                    off = (d * 2 + q) * 128
                    dst = gwb[:, off:off + 128].rearrange("p (k c) -> p k c", k=1)
                    r[(d, q)] = self.load_cast(self.W[nm][e_, d, h], 1, 128, dst=dst, dst_key=("gw", h % 2, d, q))
            return r

        pend = None
        xa_pending = None
        for h in range(8):
            if pend is None:
                pend = lru_loads(h)
            Wt = pend
            pend = None
            wxa, wxak = Wt["wxa"]
            wga, wgak = Wt["wga"]

            def ev_xa(ps, pkey, c0, c1):
                P.op("act", lambda e: e.activation(xap[:, xoff(c0):xoff(c0) + c1 - c0], ps[:, 0:c1 - c0], AF.Copy), reads=[pkey], writes=["B0"])
            if xa_pending is not None:
                for (ps_, pk_, c0_, c1_) in xa_pending:
                    ev_xa(ps_, pk_, c0_, c1_)
                xa_pending = None
                self.proj_banks = [0, 1, 2, 3, 4, 5, 6, 7]
            else:
                self.proj_tile(wxa, 8, 128, nsrc, nkeys, wxak, ev_xa)
            for k in range(4):
                P.op("pool", lambda e, k=k, h=h: e.tensor_scalar_mul(dg[k], self.identb[:], cw[:, k, h:h + 1]), reads=["identb", "cw"], writes=[f"dg{k}", "tmpB0"])
            for (c0, c1) in BLKS:
                bank = self.proj_banks[self.ps_rr % len(self.proj_banks)]
                self.ps_rr += 1
                ps = self.ps[bank]
                for k in range(4):
                    o0 = xoff(c0) + k - 2
                    self.mm(ps[:, 0:c1 - c0], dg[k], xap[:, o0:o0 + c1 - c0], k == 0, k == 3, [f"dg{k}", "B0"], [f"ps{bank}"])
                P.op("act", lambda e, ps=ps, c0=c0, c1=c1, h=h: e.activation(u[:, c0:c1], ps[:, 0:c1 - c0], AF.Identity, bias=cb[:, h:h + 1], scale=1.0),
                     reads=[f"ps{bank}", "cb"], writes=["B1"])
            def ev_g(ps, pkey, c0, c1):
                P.op("act", lambda e: e.activation(xap[:, xoff(c0):xoff(c0) + c1 - c0], ps[:, 0:c1 - c0], AF.Silu), reads=[pkey], writes=["B0"])
            usrc = lambda k: u
            for d in range(2):
                wr, wrk = Wt[(d, 0)]
                wi, wik = Wt[(d, 1)]
                ad, adk = abuf[d]
                ibd, ibk = ibuf[d]

                def ev_r(ps, pkey, c0, c1, d=d, h=h):
                    P.op("act", lambda e: e.activation(F0[:, c0:c1], ps[:, 0:c1 - c0], AF.Sigmoid, bias=br[:, d, h:h + 1], scale=1.0),
                         reads=[pkey, "br"], writes=["F0"])

                def ev_i(ps, pkey, c0, c1, d=d, h=h, ibd=ibd, ibk=ibk):
                    P.op("act", lambda e: e.activation(ibd[:, c0:c1], ps[:, 0:c1 - c0], AF.Sigmoid, bias=bi[:, d, h:h + 1], scale=1.0),
                         reads=[pkey, "bi"], writes=[ibk])
                self.proj_tile(wr, 1, 128, usrc, ["B1"], wrk, ev_r)
                self.proj_tile(wi, 1, 128, usrc, ["B1"], wik, ev_i)
                if d == 1 and pend is not None:
                    nwxa, nwxak = pend["wxa"]
                    xa_pending = []
                    for bi_, (c0_, c1_) in enumerate(BLKS):
                        bank_ = 3 + bi_
                        for k_ in range(NKT):
                            self.mm(self.ps[bank_][:, 0:c1_ - c0_], nwxa[:, k_, :], self.nT[:, k_, c0_:c1_], k_ == 0, k_ == NKT - 1,
                                    [nwxak, ("nT", k_)], [f"ps{bank_}"])
                        xa_pending.append((self.ps[bank_], f"ps{bank_}", c0_, c1_))
                    self.proj_banks = [0, 1, 2]
                P.op("act", lambda e, d=d, h=h, ad=ad: e.activation(ad[:, 0:T], F0[:, :], AF.Exp, scale=coef[:, d, h:h + 1]), reads=["F0", "coef"], writes=[adk])
                P.op("act", lambda e, d=d, h=h: e.activation(F0[:, :], F0[:, :], AF.Exp, scale=coef2[:, d, h:h + 1]), reads=["F0", "coef2"], writes=["F0"])
                P.op("act", lambda e: e.activation(F0[:, :], F0[:, :], AF.Sqrt, bias=1.0, scale=-1.0), reads=["F0"], writes=["F0"])
                P.op("dve", lambda e, ibd=ibd: e.tensor_tensor(ibd[:, 0:T], ibd[:, 0:T], u[:, 0:T], ALU.mult), reads=[ibk, "B1"], writes=[ibk])
                P.op("dve", lambda e, ibd=ibd: e.tensor_tensor(ibd[:, 0:T], F0[:, :], ibd[:, 0:T], ALU.mult), reads=["F0", ibk], writes=[ibk])
                if d == 0:
                    P.op("dve", lambda e, ad=ad, ibd=ibd: e.tensor_tensor_scan(hf[:, 0:T], ad[:, 0:T], ibd[:, 0:T], 0.0, ALU.mult, ALU.add),
                         reads=[ibk, adk], writes=["B3"])
                    self.proj_tile(wga, 8, 128, nsrc, nkeys, wgak, ev_g)
                    if h + 1 < 8 and h % 4 != 3:
                        pend = lru_loads(h + 1)
                else:
                    P.op("dve", lambda e, ad=ad, ibd=ibd: e.tensor_tensor_scan(hr[:, 0:CTX][:, ::-1], ad[:, 0:CTX][:, ::-1], ibd[:, 0:CTX][:, ::-1], 0.0,
                                                                              ALU.mult, ALU.add), reads=[ibk, adk], writes=["hr"])
                    P.op("dve", lambda e, ad=ad, ibd=ibd: e.tensor_tensor_scan(hr[:, CTX:T][:, ::-1], ad[:, CTX:T][:, ::-1], ibd[:, CTX:T][:, ::-1], hr[:, 0:1],
                                                                              ALU.mult, ALU.add), reads=[ibk, adk, "hr"], writes=["hr"])
            P.op("dve", lambda e: e.tensor_tensor(hr, hr, hf[:, 0:T], ALU.add), reads=["hr", "B3"], writes=["hr"])
            P.op("dve", lambda e, h=h: e.tensor_tensor(self.ygrp[:, h % 4, 0:CTX], hr[:, 0:CTX], xap[:, 2:2 + CTX], ALU.mult),
                 reads=["hr", "B0"], writes=[("og", h % 4)])
            P.op("dve", lambda e, h=h: e.tensor_tensor(self.ygrp[:, h % 4, CTX:T], hr[:, CTX:T], xap[:, 262:262 + SEQ], ALU.mult),
                 reads=["hr", "B0"], writes=[("og", h % 4)])
            if h % 4 == 3:
                self.out_proj(l, Wout[(h // 4) * 512:(h // 4 + 1) * 512, :], BLKS)
        self.barrier()
        self.s5_phase(l)
        self.barrier()


    def s5_phase(self, l):
        P = self.P
        e_ = l // 2
        Win = self.W["ev_w_in"][e_]
        Wout = self.W["ev_w_out"][e_]
        W = self.W
        F0, F1, F2 = self.F
        g_re, g_im, ub, B3 = self.B
        LW = self.LW
        LWb = LW[:].bitcast(BF16)
        sm = self.small
        tht, rmag = LW[:, 0:32], LW[:, 32:64]
        M16, Mrow, nMrow = LW[:, 64:72], LW[:, 72:74], LW[:, 74:76]
        Braw = [LW[:, 80:144], LW[:, 144:208]]
        Bbar = [LW[:, 208:272], LW[:, 272:336]]
        CT = LW[:, 336:464].rearrange("p (j q h) -> p j q h", j=4, q=2)
        dsk, glub = LW[:, 464:468], LW[:, 468:472]
        Eexp = LW[0:32, 472:600]
        Fc = LW[0:32, 600:856]
        lB = lambda j, q: LWb[:, 1712 + (j * 2 + q) * 128:1712 + (j * 2 + q + 1) * 128]
        lC = lambda j, v: LWb[:, 2736 + (j * 3 + v) * 128:2736 + (j * 3 + v + 1) * 128]
        Dd = LWb[:, 4272:4400]
        iota48 = sm[:, 208:256]
        hpi = sm[:, 87:88]
        tA0, tA1, tA2 = self.tmpA
        ld = lambda dst, src, key: P.dma(dst, src, writes=[key], allow_slow_non_contiguous=True)
        dve = lambda fn, r, w: P.op("dve", fn, reads=r, writes=w)
        act = lambda fn, r, w: P.op("act", fn, reads=r, writes=w)
        pool = lambda fn, r, w: P.op("pool", fn, reads=r, writes=w)
        pool(lambda e: e.iota(iota48, [[1, 48]], base=0, channel_multiplier=0, allow_small_or_imprecise_dtypes=True), [], ["iota48"])
        dve(lambda e: e.memset(hpi, math.pi / 2), [], ["hpi"])
        dve(lambda e: e.tensor_reduce(M16, self.identf[:].rearrange("p (c h) -> p c h", h=16), AX.X, ALU.add), ["identf"], ["M16"])
        dve(lambda e: e.tensor_reduce(Mrow, self.identf[:].rearrange("p (c h) -> p c h", h=64), AX.X, ALU.add), ["identf"], ["Mrow"])
        dve(lambda e: e.tensor_scalar_mul(nMrow, Mrow, -1.0), ["Mrow"], ["nMrow"])
        pool(lambda e: e.memset(LWb[:, 2736:4272], 0.0), [], [("lC", j_, v_, g_) for j_ in range(4) for v_ in range(3) for g_ in range(2)])
        ld(dsk, W["s5_d"][e_].rearrange("(t g) h -> (g h) t", g=8), "dsk")
        ld(glub, W["s5_glu_b"][e_].rearrange("(t p) -> p t", p=128), "glub")
        for i, nm in enumerate(["s5_lam_re", "s5_lam_im", "s5_log_dt"]):
            ld(tA0[:, i * 32:(i + 1) * 32], W[nm][e_].rearrange("d (gp gl) p -> (gl p) (d gp)", gl=2), "tA0")
        act(lambda e: e.activation(tA0[:, 64:96], tA0[:, 64:96], AF.Exp), ["tA0"], ["tA0"])
        dve(lambda e: e.tensor_tensor(tht, tA0[:, 32:64], tA0[:, 64:96], ALU.mult), ["tA0"], ["tht"])
        dve(lambda e: e.tensor_scalar_mul(tht, tht, 1.0 / TWO_PI), ["tht"], ["tht"])
        dve(lambda e: e.tensor_tensor(rmag, tA0[:, 0:32], tA0[:, 64:96], ALU.mult), ["tA0"], ["rmag"])
        act(lambda e: e.activation(rmag, rmag, AF.Exp), ["rmag"], ["rmag"])
        c = lambda i: F0[0:32, i * 128:(i + 1) * 128]
        ci = lambda i: F0[0:32, i * 128:(i + 1) * 128].bitcast(I32)
        for i, nm in enumerate(["s5_lam_re", "s5_lam_im", "s5_log_dt"]):
            ld(c(i).rearrange("g (d p) -> g d p", d=2), W[nm][e_].rearrange("d g p -> g d p"), "F0")
        K0 = ["F0"]
        act(lambda e: e.activation(c(2), c(2), AF.Exp), K0, K0)
        dve(lambda e: e.tensor_tensor(c(3), c(0), c(2), ALU.mult), K0, K0)
        dve(lambda e: e.tensor_tensor(c(4), c(1), c(2), ALU.mult), K0, K0)
        dve(lambda e: e.tensor_scalar_mul(c(4), c(4), 1.0 / TWO_PI), K0, K0)
        dve(lambda e: e.tensor_scalar_mul(c(10), c(4), 0.5), K0, K0)
        act(lambda e: e.activation(c(5), c(3), AF.Exp), K0, K0)
        act(lambda e: e.activation(c(6), c(3), AF.Tanh, scale=0.5), K0, K0)
        dve(lambda e: e.scalar_tensor_tensor(c(6), c(5), 1.0, c(6), ALU.add, ALU.mult), K0, K0)
        dve(lambda e: e.tensor_copy(ci(7), c(4)), K0, K0)
        dve(lambda e: e.tensor_tensor(c(4), c(4), ci(7), ALU.subtract), K0, K0)
        act(lambda e: e.activation(c(8), c(4), AF.Sin, scale=TWO_PI), K0, K0)
        dve(lambda e: e.scalar_tensor_tensor(c(9), c(4), -1.0, c(4), ALU.mult, ALU.max), K0, K0)
        act(lambda e: e.activation(c(9), c(9), AF.Sin, bias=hpi[0:32, :], scale=-TWO_PI), K0 + ["hpi"], K0)
        dve(lambda e: e.tensor_copy(ci(7), c(10)), K0, K0)
        dve(lambda e: e.tensor_tensor(c(10), c(10), ci(7), ALU.subtract), K0, K0)
        act(lambda e: e.activation(c(10), c(10), AF.Sin, scale=TWO_PI), K0, K0)
        dve(lambda e: e.tensor_tensor(c(10), c(10), c(10), ALU.mult), K0, K0)
        dve(lambda e: e.tensor_tensor(c(11), c(6), c(9), ALU.mult), K0, K0)
        dve(lambda e: e.scalar_tensor_tensor(c(11), c(10), -2.0, c(11), ALU.mult, ALU.add), K0, K0)
        dve(lambda e: e.tensor_tensor(c(12), c(5), c(8), ALU.mult), K0, K0)
        dve(lambda e: e.tensor_tensor(c(13), c(0), c(0), ALU.mult), K0, K0)
        dve(lambda e: e.tensor_tensor(c(14), c(1), c(1), ALU.mult), K0, K0)
        dve(lambda e: e.tensor_tensor(c(13), c(13), c(14), ALU.add), K0, K0)
        dve(lambda e: e.reciprocal(c(13), c(13)), K0, K0)
        dve(lambda e: e.tensor_tensor(c(14), c(11), c(0), ALU.mult), K0, K0)
        dve(lambda e: e.tensor_tensor(c(15), c(12), c(1), ALU.mult), K0, K0)
        dve(lambda e: e.tensor_tensor(c(14), c(14), c(15), ALU.add), K0, K0)
        dve(lambda e: e.tensor_tensor(Fc[:, 0:128], c(14), c(13), ALU.mult), K0, ["Fc"])
        dve(lambda e: e.tensor_tensor(c(14), c(12), c(0), ALU.mult), K0, K0)
        dve(lambda e: e.tensor_tensor(c(15), c(11), c(1), ALU.mult), K0, K0)
        dve(lambda e: e.tensor_tensor(c(14), c(14), c(15), ALU.subtract), K0, K0)
        dve(lambda e: e.tensor_tensor(Fc[:, 128:256], c(14), c(13), ALU.mult), K0, ["Fc"])
        nsrc = lambda k: self.nT[:, k, :]
        nkeys = [("nT", k) for k in range(NKT)]

        def tv(X, d, c0, c1):
            if d == 0:
                return X[:, c0:c1]
            if c0 < CTX:
                return X[:, 0:CTX][:, ::-1]
            return X[:, 2560 - c1:2560 - c0][:, ::-1]

        for ti in range(4):
            self.proj_banks = [7]
            wub, wubk = self.load_cast(Win[:, 2048 + ti * 128:2048 + (ti + 1) * 128], 8, 128)

            def ev_u(ps, pkey, c0, c1):
                act(lambda e: e.activation(ub[:, c0:c1], ps[:, 0:c1 - c0], AF.Copy), [pkey], ["B2"])
            self.proj_tile(wub, 8, 128, nsrc, nkeys, wubk, ev_u)
            pool(lambda e, ti=ti: e.tensor_copy(Eexp.rearrange("g (c h) -> g c h", h=16),
                                                self.identf[0:32, 8 * ti:8 * ti + 8].unsqueeze(2).to_broadcast([32, 8, 16])), ["identf"], ["Eexp"])
            pool(lambda e, ti=ti: e.tensor_scalar_mul(Dd, self.identb[:], dsk[:, ti:ti + 1]), ["identb", "dsk"], ["Dd"])
            for d in range(2):
                self.mm(self.ps[7][:, 0:256], Eexp, Fc, True, True, ["Eexp", "Fc"], ["ps7"])
                for q, nm in enumerate(["s5_b_re", "s5_b_im"]):
                    for g8 in range(8):
                        ld(Braw[q][16 * g8:16 * g8 + 16, :], W[nm][e_, d, 8 * ti + g8].rearrange("p h -> h p"), f"Braw{q}")
                for q, nm in enumerate(["s5_c_re", "s5_c_im"]):
                    for gl in range(2):
                        for j in range(4):
                            ld(CT[64 * gl:64 * gl + 64, j, q, :], W[nm][e_, d, 8 * ti + 2 * j + gl].rearrange("h p -> p h"), "CT")
                Fre = self.ps[7][:, d * 64:(d + 1) * 64]
                Fim = self.ps[7][:, 128 + d * 64:128 + (d + 1) * 64]
                t0_, t1_ = tA0[:, 0:64], tA0[:, 64:128]
                dve(lambda e, Fre=Fre: e.tensor_tensor(t0_, Fre, Braw[0], ALU.mult), ["ps7", "Braw0"], ["tA0", "prod0"])
                dve(lambda e, Fim=Fim: e.tensor_tensor(t1_, Fim, Braw[1], ALU.mult), ["ps7", "Braw1"], ["tA0", "prod0"])
                dve(lambda e: e.tensor_tensor(Bbar[0], t0_, t1_, ALU.subtract), ["tA0"], ["Bbar0"])
                dve(lambda e, Fre=Fre: e.tensor_tensor(t0_, Fre, Braw[1], ALU.mult), ["ps7", "Braw1"], ["tA0", "prod0"])
                dve(lambda e, Fim=Fim: e.tensor_tensor(t1_, Fim, Braw[0], ALU.mult), ["ps7", "Braw0"], ["tA0", "prod0"])
                dve(lambda e: e.tensor_tensor(Bbar[1], t0_, t1_, ALU.add), ["tA0"], ["Bbar1"])
                if ti == 0 and d == DBG_D and self.debug:
                    self.dump(3, Fc, "Fc")
                    dve(lambda e: e.tensor_copy(tA1[:, 0:256], self.ps[7][:, 0:256]), ["ps7"], ["tA1"])
                    self.dump(6, tA1[:, 0:256], "tA1")
                    self.dump(7, Braw[0], "Braw0")
                for j in range(4):
                    for q in range(2):
                        for gl in range(2):
                            act(lambda e, j=j, q=q, gl=gl: e.activation(
                                lB(j, q)[:, 64 * gl:64 * gl + 64], Bbar[q], AF.Copy, scale=M16[:, 2 * j + gl:2 * j + gl + 1]),
                                [f"Bbar{q}", "M16"], [("lB", j, q, gl)])
                    for gl in range(2):
                        cs = slice(32 * j + 16 * gl, 32 * j + 16 * gl + 16)
                        act(lambda e, j=j, gl=gl, cs=cs: e.activation(lC(j, 0)[:, cs], CT[:, j, 0, :], AF.Copy, scale=Mrow[:, gl:gl + 1]), ["CT", "Mrow"], [("lC", j, 0, gl)])
                        act(lambda e, j=j, gl=gl, cs=cs: e.activation(lC(j, 1)[:, cs], CT[:, j, 0, :], AF.Copy, scale=nMrow[:, gl:gl + 1]), ["CT", "nMrow"], [("lC", j, 1, gl)])
                        act(lambda e, j=j, gl=gl, cs=cs: e.activation(lC(j, 2)[:, cs], CT[:, j, 1, :], AF.Copy, scale=nMrow[:, gl:gl + 1]), ["CT", "nMrow"], [("lC", j, 2, gl)])
                for j in range(4):
                    uidx = (ti * 2 + d) * 4 + j
                    if uidx == 0:
                        for st in self.s5_table_steps(0, 0, 0, 0, tht):
                            st()
                    ins = []
                    if uidx + 1 < 32:
                        n_ = uidx + 1
                        ins = self.s5_table_steps(n_ // 8, (n_ // 4) % 2, n_ % 4, n_, tht)
                    self.s5_unit_compute(ti, d, j, uidx, lB, lC, rmag, ins)
            for bi, (c0, c1) in enumerate(BLKS):
                w_ = c1 - c0
                y = self.ps[bi]
                yk = f"ps{bi}"
                self.mm(y[:, 0:w_], Dd, ub[:, c0:c1], False, True, ["Dd", "B2"], [yk])
                tg, tgk = (tA0, "tA0") if bi % 2 == 0 else (tA1, "tA1")
                act(lambda e, y=y, w_=w_, tg=tg: e.activation(tg[:, 0:w_], y[:, 0:w_], AF.Square), [yk], [tgk])
                dve(lambda e, w_=w_, tg=tg: e.tensor_scalar(tg[:, 0:w_], tg[:, 0:w_], 0.044715, 1.0, ALU.mult, ALU.add), [tgk], [tgk])
                dve(lambda e, y=y, w_=w_, tg=tg: e.tensor_tensor(tg[:, 0:w_], tg[:, 0:w_], y[:, 0:w_], ALU.mult), [tgk, yk], [tgk])
                act(lambda e, w_=w_, tg=tg: e.activation(tg[:, 0:w_], tg[:, 0:w_], AF.Sigmoid, scale=2.0 * math.sqrt(2.0 / math.pi)), [tgk], [tgk])
                dve(lambda e, y=y, w_=w_, ti=ti, c0=c0, c1=c1, tg=tg: e.tensor_tensor(self.ygrp[:, ti, c0:c1], y[:, 0:w_], tg[:, 0:w_], ALU.mult),
                    [yk, tgk], [("og", ti)])
        self.proj_banks = [4, 5, 6]
        ysrc = lambda k: self.ygrp[:, k, :]
        ykeys = [("og", k) for k in range(4)]
        for ot in range(4):
            wg, wgk = self.load_cast(W["s5_glu_w"][e_][:, ot * 128:(ot + 1) * 128], 4, 128)

            def ev_z(ps, pkey, c0, c1, ot=ot):
                act(lambda e: e.activation(self.B[ot][:, c0:c1], ps[:, 0:c1 - c0], AF.Sigmoid, bias=glub[:, ot:ot + 1], scale=1.0),
                    [pkey, "glub"], [f"B{ot}"])
            self.proj_tile(wg, 4, 128, ysrc, ykeys, wgk, ev_z)
        for ot in range(4):
            wgb, wgbk = self.load_cast(Win[:, 2560 + ot * 128:2560 + (ot + 1) * 128], 8, 128)

            def ev_gb(ps, pkey, c0, c1, ot=ot):
                sg = self.tmpB[self.ps_rr % 2]
                sk = f"tmpB{self.ps_rr % 2}"
                act(lambda e: e.activation(sg[:, 0:c1 - c0], ps[:, 0:c1 - c0], AF.Silu), [pkey], [sk])
                pool(lambda e: e.tensor_tensor(sg[:, 0:c1 - c0], sg[:, 0:c1 - c0], self.B[ot][:, c0:c1], ALU.mult), [sk, f"B{ot}"], [sk])
                pool(lambda e: e.tensor_tensor(self.ygrp[:, ot, c0:c1], self.ygrp[:, ot, c0:c1], sg[:, 0:c1 - c0], ALU.mult),
                     [sk, ("og", ot)], [("og", ot)])
            self.proj_tile(wgb, 8, 128, nsrc, nkeys, wgbk, ev_gb)
        self.out_proj(l, Wout[1024:1536, :], BLKS)


    def s5_views(self):
        F0b = self.F[0][:].bitcast(BF16)
        F1b = self.F[1][:].bitcast(BF16)
        F2b = self.F[2][:].bitcast(BF16)
        tabs = [(F0b[:, 0:T], F0b[:, T:2 * T]), (F1b[:, 0:T], F1b[:, T:2 * T])]
        gin = (F2b[:, 0:T], F2b[:, T:2 * T])
        B3f = self.B[3][:, 0:2304].bitcast(F32)
        tAb = [self.tmpA[0][:].bitcast(BF16), self.tmpA[1][:].bitcast(BF16)]
        prod = [tAb[0][:, 0:512], tAb[0][:, 512:1024], tAb[1][:, 0:512], tAb[1][:, 512:1024]]
        return tabs, gin, B3f, prod

    def s5_table_steps(self, ti, d, j, uidx, tht):
        P = self.P
        tabs, gin, B3f, prod = self.s5_views()
        cosT, sinT = tabs[uidx % 2]
        tk = f"tab{uidx % 2}"
        sm = self.small
        iota48 = sm[:, 208:256]
        hpi = sm[:, 87:88]
        tA2 = self.tmpA[2]
        idx0 = d * 16 + 4 * ti
        th4 = tht[:, idx0:idx0 + 4]
        Ap4 = tA2[:, 0:192]
        Bp4 = tA2[:, 192:384]
        t48_4 = tA2[:, 384:388]
        kk1_4 = tA2[:, 388:392].bitcast(I32)
        kk = tA2[:, 392:488].bitcast(I32)
        Ap, Bp = Ap4[:, j * 48:(j + 1) * 48], Bp4[:, j * 48:(j + 1) * 48]
        KA = ["tA2"]
        dve = lambda fn, r, w: P.op("dve", fn, reads=r, writes=w)
        sxs = [B3f[:, 0:384], B3f[:, 384:768]]
        sy = B3f[:, 768:1152].bitcast(I32)
        steps = []

        def tiny():
            io4 = iota48.unsqueeze(1).to_broadcast([128, 4, 48])
            dve(lambda e: e.tensor_scalar_mul(t48_4, th4, 48.0), ["tht"], KA)
            dve(lambda e: e.tensor_copy(kk1_4, t48_4), KA, KA)
            dve(lambda e: e.tensor_tensor(t48_4, t48_4, kk1_4, ALU.subtract), KA, KA)
            dve(lambda e: e.tensor_tensor(Ap4.rearrange("p (u i) -> p u i", u=4), io4,
                                          t48_4.unsqueeze(2).to_broadcast([128, 4, 48]), ALU.mult), KA + ["iota48"], KA)
            dve(lambda e: e.tensor_tensor(Bp4.rearrange("p (u i) -> p u i", u=4), io4,
                                          th4.unsqueeze(2).to_broadcast([128, 4, 48]), ALU.mult), ["tht", "iota48"] + KA, KA)
            for X in (Ap4, Bp4):
                for hh in range(2):
                    xs = X[:, hh * 96:(hh + 1) * 96]
                    dve(lambda e, xs=xs: e.tensor_copy(kk, xs), KA, KA)
                    dve(lambda e, xs=xs: e.tensor_tensor(xs, xs, kk, ALU.subtract), KA, KA)
        if j == 0:
            steps.append(tiny)
        for k in range(6):
            sx = sxs[k % 2]
            sk = f"sx{k % 2}"
            c0 = 384 * k

            def stepA1(k=k, sx=sx, sk=sk):
                sxv = sx.rearrange("p (i j) -> p i j", j=48)
                P.op("dve", lambda e: e.tensor_tensor(sxv, Ap[:, 8 * k:8 * k + 8].unsqueeze(2).to_broadcast([128, 8, 48]),
                                                     Bp.unsqueeze(1).to_broadcast([128, 8, 48]), ALU.add), reads=KA, writes=[sk])

            def stepA2(sx=sx, sk=sk):
                dve(lambda e: e.tensor_copy(sy, sx), [sk], ["sy"])

            def stepA3(sx=sx, sk=sk):
                P.op("dve", lambda e: e.tensor_tensor(sx, sx, sy, ALU.subtract), reads=[sk, "sy"], writes=[sk])

            def stepB(sx=sx, sk=sk, c0=c0):
                P.op("act", lambda e: e.activation(sinT[:, c0:c0 + 384], sx, AF.Sin, scale=TWO_PI), reads=[sk], writes=[tk])

            def stepC(sx=sx, sk=sk, c0=c0):
                P.op("act", lambda e: e.activation(sx, sx, AF.Sin, scale=math.pi), reads=[sk], writes=[sk])
                P.op("act", lambda e: e.activation(sx, sx, AF.Square), reads=[sk], writes=[sk])
                P.op("act", lambda e: e.activation(cosT[:, c0:c0 + 384], sx, AF.Identity, bias=1.0, scale=-2.0), reads=[sk], writes=[tk])
            steps += [stepA1, stepA2, stepA3, stepB, stepC]
        return steps

    def s5_unit_compute(self, ti, d, j, uidx, lB, lC, rmag, inserts):
        P = self.P
        tabs, gin, B3f, prod = self.s5_views()
        cosT, sinT = tabs[uidx % 2]
        tk = f"tab{uidx % 2}"
        g_re, g_im, ub, _ = self.B
        idx = d * 16 + 4 * ti + j
        rm = rmag[:, idx:idx + 1]
        bre, bim, tA, tB = self.tmpB
        dve = lambda fn, r, w: P.op("dve", fn, reads=r, writes=w)
        nslots = 27
        total = len(inserts)
        state = {"slot": 0, "done": 0}

        def slot_end():
            state["slot"] += 1
            target = (state["slot"] * total + nslots - 1) // nslots
            while state["done"] < min(target, total):
                inserts[state["done"]]()
                state["done"] += 1

        def tcols(k):
            s0, s1 = BLKS[k]
            if d == 0:
                return s0, s1, k, False
            if k == 0:
                return 0, CTX, 0, True
            return 2560 - s1, 2560 - s0, 5 - k, True

        for k, (c0, c1) in enumerate(BLKS):
            w_ = c1 - c0
            t0, t1, bt, rev = tcols(k)
            ubv = ub[:, t0:t1][:, ::-1] if rev else ub[:, t0:t1]
            br_, bi_ = 5 + (2 * k) % 3, 5 + (2 * k + 1) % 3
            if k % 2 == 0:
                cre, cim, kre, kim = bre, bim, "tmpB0", "tmpB1"
            else:
                cre, cim, kre, kim = prod[0], prod[1], "prod0", "prod1"
            self.mm(self.ps[br_][:, 0:w_], lB(j, 0), ubv, True, True, [("lB", j, 0, 0), ("lB", j, 0, 1), "B2"], [f"ps{br_}"])
            self.mm(self.ps[bi_][:, 0:w_], lB(j, 1), ubv, True, True, [("lB", j, 1, 0), ("lB", j, 1, 1), "B2"], [f"ps{bi_}"])
            P.op("act", lambda e, w_=w_, cre=cre, br_=br_: e.activation(cre[:, 0:w_], self.ps[br_][:, 0:w_], AF.Copy), reads=[f"ps{br_}"], writes=[kre])
            P.op("act", lambda e, w_=w_, cim=cim, bi_=bi_: e.activation(cim[:, 0:w_], self.ps[bi_][:, 0:w_], AF.Copy), reads=[f"ps{bi_}"], writes=[kim])
            cv, sv = cosT[:, c0:c1], sinT[:, c0:c1]
            tC, tD = prod[2], prod[3]
            dve(lambda e, w_=w_, cv=cv, cre=cre: e.tensor_tensor(tA[:, 0:w_], cre[:, 0:w_], cv, ALU.mult), [kre, tk], ["tmpB2"])
            dve(lambda e, w_=w_, sv=sv, cim=cim: e.tensor_tensor(tB[:, 0:w_], cim[:, 0:w_], sv, ALU.mult), [kim, tk], ["tmpB3"])
            slot_end()
            dve(lambda e, w_=w_, cv=cv, cim=cim: e.tensor_tensor(tC[:, 0:w_], cim[:, 0:w_], cv, ALU.mult), [kim, tk], ["prod2"])
            dve(lambda e, w_=w_, sv=sv, cre=cre: e.tensor_tensor(tD[:, 0:w_], cre[:, 0:w_], sv, ALU.mult), [kre, tk], ["prod3"])
            slot_end()
            dve(lambda e, w_=w_, c0=c0, c1=c1: e.tensor_tensor(gin[0][:, c0:c1], tA[:, 0:w_], tB[:, 0:w_], ALU.add), ["tmpB2", "tmpB3"], ["ginr"])
            dve(lambda e, w_=w_, c0=c0, c1=c1: e.tensor_tensor(gin[1][:, c0:c1], tC[:, 0:w_], tD[:, 0:w_], ALU.subtract), ["prod2", "prod3"], ["gini"])
            slot_end()
        for part, (dst, dk, gk) in enumerate([(g_re, "B0", "ginr"), (g_im, "B1", "gini")]):
            src = gin[part]
            dve(lambda e, dst=dst, src=src: e.tensor_tensor_scan(dst[:, 0:T], rm.to_broadcast([128, T]), src, 0.0, ALU.mult, ALU.add),
                [gk, "rmag"], [dk])
            slot_end()
        for k, (c0, c1) in enumerate(BLKS):
            w_ = c1 - c0
            t0, t1, bt, rev = tcols(k)
            cv, sv = cosT[:, c0:c1], sinT[:, c0:c1]
            plist = [(cv, g_re, "B0", 0), (sv, g_im, "B1", 1), (sv, g_re, "B0", 2), (cv, g_im, "B1", 2)]
            for pi, (tab, gg, gk, var) in enumerate(plist):
                pt, pk = (prod[pi], f"prod{pi}") if k % 2 == 0 else (self.tmpB[pi], f"tmpB{pi}")
                dve(lambda e, pt=pt, tab=tab, gg=gg, c0=c0, c1=c1, w_=w_: e.tensor_tensor(pt[:, 0:w_], tab, gg[:, c0:c1], ALU.mult),
                    [tk, gk], [pk])
                first = (d == 0 and j == 0 and pi == 0)
                rhs = pt[:, 0:w_][:, ::-1] if rev else pt[:, 0:w_]
                self.mm(self.ps[bt][:, 0:w_], lC(j, var), rhs, first, False, [("lC", j, var, 0), ("lC", j, var, 1), pk], [f"ps{bt}"])
                if pi % 2 == 1:
                    slot_end()
        while state["done"] < total:
            inserts[state["done"]]()
            state["done"] += 1

    def dump(self, slot, ap, key, bf=False):
        if not self.debug:
            return
        dst = (self.dbgb if bf else self.dbgf)[slot, 0:ap.shape[0], 0:ap.shape[1]]
        self.out_toks.append(self.P.dma(dst, ap, reads=[key]))

    def barrier(self):
        P = self.P
        toks = [(P.sem[e], P.count[e]) for e in ENGS if P.count[e] > 0]
        toks += [(s, 16 * c) for s, c in zip(P.dma_sems, P.dma_cnt) if c > 0]
        for e in ENGS:
            P._emit_waits(e, toks)
        P.lastw.clear()
        P.readers.clear()

    def rope_tables(self):
        P = self.P
        yf = self.ygrp[:].rearrange("p s t -> p (s t)").bitcast(F32)
        posr = yf[0:64, 0:2048]
        posc = yf[0:64, 2048:4096]
        sm = self.small
        pidx = sm[0:64, 80:81].bitcast(I32)
        p16 = sm[0:64, 81:82].bitcast(I32)
        pb16 = sm[0:64, 82:83].bitcast(I32)
        invt = sm[0:64, 83:84]
        mA = sm[0:64, 84:85]
        mB = sm[0:64, 85:86]
        halfpi = sm[0:64, 86:87]
        LWb = self.LW[:].bitcast(BF16)
        self.cosT = LWb[0:64, 0:2048]
        self.sinT = LWb[0:64, 2048:4096]
        K = ["ropescr"]
        P.op("pool", lambda e: e.iota(posr.rearrange("p (r c) -> p r c", c=64), [[1, 32], [0, 64]], base=0, channel_multiplier=0,
                                      allow_small_or_imprecise_dtypes=True), writes=["posr"])
        P.op("pool", lambda e: e.iota(posc.rearrange("p (r c) -> p r c", c=64), [[0, 32], [1, 64]], base=0, channel_multiplier=0,
                                      allow_small_or_imprecise_dtypes=True), writes=["posc"])
        P.op("pool", lambda e: e.iota(pidx, [[0, 1]], base=0, channel_multiplier=1), writes=["pidx"])
        P.op("dve", lambda e: e.tensor_single_scalar(p16, pidx, 15, ALU.bitwise_and), reads=["pidx"], writes=["p16"])
        P.op("dve", lambda e: e.tensor_single_scalar(pb16, pidx, 16, ALU.bitwise_and), reads=["pidx"], writes=["pb16"])
        P.op("dve", lambda e: e.tensor_copy(invt, p16), reads=["p16"], writes=["invt"])
        P.op("act", lambda e: e.activation(invt, invt, AF.Exp, scale=-math.log(10000.0) / 16.0), reads=["invt"], writes=["invt"])
        P.op("dve", lambda e: e.tensor_scalar_mul(invt, invt, 1.0 / TWO_PI), reads=["invt"], writes=["invt"])
        P.op("dve", lambda e: e.tensor_copy(mB, pb16), reads=["pb16"], writes=["mB"])
        P.op("dve", lambda e: e.tensor_scalar_mul(mB, mB, 1.0 / 16.0), reads=["mB"], writes=["mB"])
        P.op("dve", lambda e: e.tensor_scalar(mA, mB, -1.0, 1.0, ALU.mult, ALU.add), reads=["mB"], writes=["mA"])
        P.op("dve", lambda e: e.memset(halfpi, math.pi / 2), writes=["halfpi"])
        P.op("dve", lambda e: e.tensor_scalar_mul(posr, posr, mA), reads=["posr", "mA"], writes=["posr"])
        P.op("dve", lambda e: e.scalar_tensor_tensor(posr, posc, mB, posr, ALU.mult, ALU.add), reads=["posr", "posc", "mB"], writes=["posr"])
        P.op("dve", lambda e: e.tensor_scalar_mul(posr, posr, invt), reads=["posr", "invt"], writes=["posr"])
        pci = posc.bitcast(I32)
        P.op("dve", lambda e: e.tensor_copy(pci, posr), reads=["posr"], writes=["posc"])
        P.op("dve", lambda e: e.tensor_tensor(posr, posr, pci, ALU.subtract), reads=["posr", "posc"], writes=["posr"])
        P.op("act", lambda e: e.activation(self.sinT, posr, AF.Sin, scale=TWO_PI), reads=["posr"], writes=["sinT"])
        P.op("dve", lambda e: e.scalar_tensor_tensor(posr, posr, -1.0, posr, ALU.mult, ALU.max), reads=["posr"], writes=["posr"])
        P.op("act", lambda e: e.activation(self.cosT, posr, AF.Sin, bias=halfpi, scale=-TWO_PI), reads=["posr", "halfpi"], writes=["cosT"])
        self.Rm = LWb[0:64, 4096:4160]
        P.op("pool", lambda e: e.tensor_scalar_mul(self.Rm[:, 0:32], self.identb[0:64, 32:64], -1.0), reads=["identb"], writes=["Rm"])
        P.op("pool", lambda e: e.tensor_copy(self.Rm[:, 32:64], self.identb[0:64, 0:32]), reads=["identb"], writes=["Rm"])

    def rope_apply(self, buf, key, blocks):
        P = self.P
        for (c0, c1) in blocks:
            w = c1 - c0
            ps = self.ps[7]
            self.mm(ps[0:64, 0:w], self.Rm, buf[0:64, c0:c1], True, True, [key, "Rm"], ["ps7"])
            t1 = self.tmpA[0]
            t2 = self.tmpA[1]
            P.op("pool", lambda e, c0=c0, c1=c1, w=w: e.tensor_tensor(t1[0:64, 0:w], buf[0:64, c0:c1], self.cosT[:, c0 - CTX:c1 - CTX], ALU.mult),
                 reads=[key, "cosT"], writes=["tmpA0"])
            P.op("dve", lambda e, c0=c0, c1=c1, w=w: e.tensor_tensor(t2[0:64, 0:w], ps[0:64, 0:w], self.sinT[:, c0 - CTX:c1 - CTX], ALU.mult),
                 reads=["ps7", "sinT"], writes=["tmpA1"])
            P.op("dve", lambda e, c0=c0, c1=c1, w=w: e.tensor_tensor(buf[0:64, c0:c1], t1[0:64, 0:w], t2[0:64, 0:w], ALU.add),
                 reads=["tmpA0", "tmpA1"], writes=[key])

    def proj_tile(self, w, nk, m, src_fn, src_keys, wkey, evac, blks=None):
        for bi, (c0, c1) in enumerate(blks or BLKS):
            bank = self.proj_banks[self.ps_rr % len(self.proj_banks)]
            self.ps_rr += 1
            ps = self.ps[bank]
            for k in range(nk):
                self.mm(ps[0:m, 0:c1 - c0], w[:, k, :], src_fn(k)[:, c0:c1], k == 0, k == nk - 1,
                        [wkey, src_keys[k]], [f"ps{bank}"])
            evac(ps, f"ps{bank}", c0, c1)

    def mla_layer(self, l):
        P = self.P
        o = l // 2
        with_ctx = l < DEPTH - 1
        Win = self.W["mla_w_in"][o]
        Wuq = self.W["mla_w_uq"][o]
        Wukv = self.W["mla_w_ukv"][o]
        Wout = self.W["mla_w_out"][o]
        self.ps_rr = 0
        self.proj_banks = [0, 1, 2, 3, 4, 5, 6]
        self.rope_tables()
        F0b = self.F[0][:].bitcast(BF16)
        F1b = self.F[1][:].bitcast(BF16)
        F2b = self.F[2][:].bitcast(BF16)
        cqn = [F0b[:, 0:T], F0b[:, T:2 * T], F1b[:, 0:T]]
        vh = F1b[:, T:2 * T].rearrange("p (t d) -> p t d", d=128)
        ckvn = [F2b[:, 0:T], F2b[:, T:2 * T]]
        kr, qn, qr, kn = self.B[0], self.B[1], self.B[2], self.B[3]
        yf = self.ygrp[:].rearrange("p s t -> p (s t)").bitcast(F32)
        nsrc = lambda k: self.nT[:, k, :]
        nkeys = [("nT", k) for k in range(NKT)]
        sm = self.small
        gq = sm[:, 96:99]
        gkv = sm[:, 99:101]
        P.dma(gq, self.W["mla_q_norm"][o].rearrange("(k p) -> p k", p=128), writes=["gq"], allow_slow_non_contiguous=True)
        P.dma(gkv, self.W["mla_kv_norm"][o].rearrange("(k p) -> p k", p=128), writes=["gkv"], allow_slow_non_contiguous=True)

        def copy_evac(dst, dkey, m=128, scale=None, eng="act"):
            def f(ps, pkey, c0, c1):
                if eng == "act":
                    if scale is None:
                        P.op("act", lambda e: e.activation(dst[0:m, c0:c1], ps[0:m, 0:c1 - c0], AF.Copy), reads=[pkey], writes=[dkey])
                    else:
                        P.op("act", lambda e: e.activation(dst[0:m, c0:c1], ps[0:m, 0:c1 - c0], AF.Copy, scale=scale), reads=[pkey], writes=[dkey])
                else:
                    P.op("dve", lambda e: e.tensor_copy(dst[0:m, c0:c1], ps[0:m, 0:c1 - c0]), reads=[pkey], writes=[dkey])
            return f

        for k in range(3):
            w, wk = self.load_cast(Win[:, k * 128:(k + 1) * 128], 8, 128)
            self.proj_tile(w, 8, 128, nsrc, nkeys, wk, copy_evac(cqn[k], f"cqn{k}", eng="act" if k % 2 == 0 else "dve"))
        for k in range(2):
            w, wk = self.load_cast(Win[:, 384 + k * 128:384 + (k + 1) * 128], 8, 128)
            self.proj_tile(w, 8, 128, nsrc, nkeys, wk, copy_evac(ckvn[k], f"ckvn{k}", eng="dve" if k % 2 == 0 else "act"))
        w, wk = self.load_cast(Win[:, 640:704], 8, 64)
        self.proj_tile(w, 8, 64, nsrc, nkeys, wk, copy_evac(kr, "B0", m=64))
        rq = yf[:, 0:T]
        rkv = yf[:, T:2 * T]
        self.rstd_into(rq, "rq", cqn, [f"cqn{k}" for k in range(3)], 384)
        self.rstd_into(rkv, "rkv", ckvn, [f"ckvn{k}" for k in range(2)], 256)
        for k in range(3):
            P.op("dve", lambda e, k=k: e.scalar_tensor_tensor(cqn[k], cqn[k], gq[:, k:k + 1], rq, ALU.mult, ALU.mult),
                 reads=[f"cqn{k}", "gq", "rq"], writes=[f"cqn{k}"])
        for k in range(2):
            P.op("dve", lambda e, k=k: e.scalar_tensor_tensor(ckvn[k], ckvn[k], gkv[:, k:k + 1], rkv, ALU.mult, ALU.mult),
                 reads=[f"ckvn{k}", "gkv", "rkv"], writes=[f"ckvn{k}"])
        self.rope_apply(kr, "B0", BLKS[1:])
        P.op("pool", lambda e: e.memset(kr[64:128, 0:T], 0.0), writes=["B0"])
        P.op("pool", lambda e: e.memset(qr[64:128, 0:T], 0.0), writes=["B2"])
        qblks = BLKS if with_ctx else BLKS[1:]
        cq_src = lambda k: cqn[k]
        cq_keys = [f"cqn{k}" for k in range(3)]
        kv_src = lambda k: ckvn[k]
        kv_keys = [f"ckvn{k}" for k in range(2)]
        self.proj_banks = [4, 5, 6]

        def head_loads(h, defer):
            r = [self.load_cast(Wuq[:, h * 192:(h + 1) * 192], 3, 192, defer=defer),
                 self.load_cast(Wukv[:, h * 256:(h + 1) * 256], 2, 256, defer=defer)]
            if not defer:
                r.append(self.load_cast(Win[:, 704 + h * 128:704 + (h + 1) * 128], 8, 128))
            return r
        pend = None
        for h in range(8):
            if pend is None:
                pend = head_loads(h, False)
            (wq, wqk), (wkv, wkvk), (wg, wgk) = [p[0:2] for p in pend]
            pend = None
            self.proj_tile(wq[:, :, 0:128], 3, 128, cq_src, cq_keys, wqk, copy_evac(qn, "B1", scale=MLA_SCALE))
            self.proj_tile(wq[:, :, 128:192], 3, 64, cq_src, cq_keys, wqk, copy_evac(qr, "B2", m=64, scale=MLA_SCALE))
            self.rope_apply(qr, "B2", BLKS[1:])
            self.proj_tile(wkv[:, :, 0:128], 2, 128, kv_src, kv_keys, wkvk, copy_evac(kn, "B3", eng="dve"))
            for t0 in range(0, 18, 4):
                nt = min(4, 18 - t0)
                bank = 4 + (self.ps_rr % 3)
                self.ps_rr += 1
                ps = self.ps[bank]
                for j in range(nt):
                    tt = t0 + j
                    for rk in range(2):
                        self.mm(ps[:, j * 128:(j + 1) * 128], ckvn[rk][:, tt * 128:(tt + 1) * 128], wkv[:, rk, 128:256],
                                rk == 0, rk == 1, [f"ckvn{rk}", wkvk], [f"ps{bank}"])
                P.op("dve", lambda e, ps=ps, t0=t0, nt=nt: e.tensor_copy(
                    vh[:, t0:t0 + nt, :], ps[:, 0:nt * 128].rearrange("p (t d) -> p t d", d=128)),
                    reads=[f"ps{bank}"], writes=["vh"])
            def ev_gate(ps, pkey, c0, c1, h=h):
                P.op("act", lambda e: e.activation(self.ygrp[:, h % 4, c0:c1], ps[:, 0:c1 - c0], AF.Silu), reads=[pkey],
                     writes=[("og", h % 4), "rq", "rkv"])
            self.proj_tile(wg, 8, 128, nsrc, nkeys, wgk, ev_gate, blks=qblks)
            if h + 1 < 8 and h % 4 != 3:
                pend = head_loads(h + 1, True)
            LOOK = 2
            for qi, (c0, c1) in enumerate(qblks):
                if qi == 1 and pend is not None:
                    for p in pend:
                        p[2]("act")
                    d3, k3, c3 = self.load_cast(Win[:, 704 + (h + 1) * 128:704 + (h + 2) * 128], 8, 128, defer=True)
                    c3("act")
                    pend.append((d3, k3))
                w_ = c1 - c0
                nkt = 2 if c0 == 0 else 18
                Ops = self.ps[qi % 2]
                Dps = self.ps[2 + qi % 2]
                okey, dkey = f"ps{qi % 2}", f"ps{2 + qi % 2}"
                Ebuf = {}

                def emit_S(kt, w_=w_, c0=c0, c1=c1):
                    sb = 4 + (self.ps_rr % 3)
                    self.ps_rr += 1
                    S = self.ps[sb]
                    ks = slice(kt * 128, (kt + 1) * 128)
                    self.mm(S[:, 0:w_], kn[:, ks], qn[:, c0:c1], True, False, ["B3", "B1"], [f"ps{sb}"])
                    self.mm(S[:, 0:w_], kr[:, ks], qr[:, c0:c1], False, True, ["B0", "B2"], [f"ps{sb}"])
                    ei = kt % 3
                    E = self.tmpB[ei]
                    P.op("act", lambda e, E=E, S=S, w_=w_: e.activation(E[:, 0:w_], S[:, 0:w_], AF.Exp),
                         reads=[f"ps{sb}"], writes=[f"tmpB{ei}"])
                    Ebuf[kt] = (E, f"tmpB{ei}")

                acc = self.tmpA[2]

                def emit_OD(kt, w_=w_, nkt=nkt, Ops=Ops, Dps=Dps, okey=okey, dkey=dkey):
                    E, ek = Ebuf[kt]
                    self.mm(Ops[:, 0:w_], vh[:, kt, :], E[:, 0:w_], kt == 0, kt == nkt - 1, ["vh", ek], [okey])
                    if kt % 2 == 1:
                        self.mm(Dps[:, 0:w_], self.onesb[:], E[:, 0:w_], kt == 1, False, ["onesb", ek], [dkey])
                    elif kt == 0:
                        P.op("dve", lambda e: e.tensor_copy(acc[:, 0:w_], E[:, 0:w_]), reads=[ek], writes=["tmpA2"])
                    else:
                        P.op("dve", lambda e: e.tensor_tensor(acc[:, 0:w_], acc[:, 0:w_], E[:, 0:w_], ALU.add), reads=[ek, "tmpA2"], writes=["tmpA2"])
                    if kt == nkt - 1:
                        accb = self.tmpB[3]
                        P.op("dve", lambda e: e.tensor_copy(accb[:, 0:w_], acc[:, 0:w_]), reads=["tmpA2"], writes=["tmpB3"])
                        self.mm(Dps[:, 0:w_], self.onesb[:], accb[:, 0:w_], False, True, ["onesb", "tmpB3"], [dkey])
                for kt in range(min(LOOK, nkt)):
                    emit_S(kt)
                for kt in range(nkt):
                    if kt + LOOK < nkt:
                        emit_S(kt + LOOK)
                    emit_OD(kt)
                rec = self.tmpA[0]
                o1 = self.tmpA[1]
                P.op("dve", lambda e, Dps=Dps, w_=w_: e.reciprocal(rec[:, 0:w_], Dps[:, 0:w_]), reads=[dkey], writes=["tmpA0"])
                P.op("dve", lambda e, Ops=Ops, w_=w_: e.tensor_tensor(o1[:, 0:w_], Ops[:, 0:w_], rec[:, 0:w_], ALU.mult),
                     reads=[okey, "tmpA0"], writes=["tmpA1"])
                P.op("pool", lambda e, h=h, c0=c0, c1=c1, w_=w_: e.tensor_tensor(self.ygrp[:, h % 4, c0:c1], o1[:, 0:w_], self.ygrp[:, h % 4, c0:c1], ALU.mult),
                     reads=["tmpA1", ("og", h % 4)], writes=[("og", h % 4)])
            if h % 4 == 3:
                self.out_proj(l, Wout[(h // 4) * 512:(h // 4 + 1) * 512, :], qblks)
        self.barrier()

    def out_proj(self, l, Wrows, blks, nk=4):
        P = self.P
        for ot in range(NKT):
            w, wk = self.load_cast(Wrows[:, ot * 128:(ot + 1) * 128], nk, 128)
            for (c0, c1) in blks:
                s = 1 if c0 == 0 else 0
                bank = self.proj_banks[self.ps_rr % len(self.proj_banks)]
                self.ps_rr += 1
                ps = self.ps[bank]
                for j in range(nk):
                    self.mm(ps[:, 0:c1 - c0], w[:, j, :], self.ygrp[:, j, c0:c1], j == 0, j == nk - 1, [wk, ("og", j)], [f"ps{bank}"])
                P.op("dve", lambda e, ot=ot, c0=c0, c1=c1, s=s, ps=ps: e.scalar_tensor_tensor(
                    self.xT[:, ot, c0:c1], ps[:, 0:c1 - c0], self.modT[:, l, 16 + ot, s:s + 1], self.xT[:, ot, c0:c1], ALU.mult, ALU.add),
                    reads=[f"ps{bank}", "modT", ("xT", ot)], writes=[("xT", ot)])


_NC_CACHE = {}


def _get_nc(layers, debug=False):
    key = (tuple(layers), debug)
    if key not in _NC_CACHE:
        _NC_CACHE[key] = Builder(layers, debug).build()
    return _NC_CACHE[key]


def kernel(_layers=(0, 1, 2, 3), _ncores=8, _debug=False, **inputs):
    nc = _get_nc(_layers, _debug)
    in_maps = []
    for b in range(_ncores):
        m = {}
        for name, shp in WSPEC:
            a = np.asarray(inputs[name], dtype=np.float32)
            if name in ("x", "c", "ctx"):
                a = a[b]
            m[name] = np.ascontiguousarray(a).reshape(shp)
        in_maps.append(m)
    res = run_bass_kernel_spmd(nc, in_maps, core_ids=list(range(_ncores)))
    if _debug:
        return res.results[0]
    return np.stack([np.asarray(r["out"], dtype=np.float32).reshape(SEQ, D) for r in res.results], axis=0)
```
